# Optimizing a Trainium2 kernel written in Bass

```python
import jax, jax.numpy as jnp
from jax import lax
import numpy as np

D_MODEL = 1024
BATCH = 1
SEQ = 16384
DEPTH = 4

N_EVEN = (DEPTH + 1) // 2
N_ODD = DEPTH // 2
D_FF = 2816

POOL_WINDOWS = (2, 4, 8, 16)
POOL_GROUPS = len(POOL_WINDOWS)
POOL_DIM = D_MODEL // 2
POOL_GROUP_DIM = POOL_DIM // POOL_GROUPS

MLA_HEADS = 8
QK_NOPE_DIM = D_MODEL // 16
QK_ROPE_DIM = D_MODEL // 32
QK_HEAD_DIM = QK_NOPE_DIM + QK_ROPE_DIM
V_HEAD_DIM = D_MODEL // 16
Q_LORA_RANK = 3 * D_MODEL // 8
KV_LORA_RANK = D_MODEL // 4
ROPE_THETA = 10000.0
Q_BLOCK = 128

MIX_IN_EVEN = POOL_DIM + Q_LORA_RANK + KV_LORA_RANK + QK_ROPE_DIM
MIX_OUT_EVEN = POOL_DIM + MLA_HEADS * V_HEAD_DIM

CONV_DIM = D_MODEL
CONV_WIDTH = 3

NORM_EPS = 1e-6

kernel_name = "hybrid_pool_mla_shortconv_macaron"


def rms_norm(x, g):
    xf = x.astype(jnp.float32)
    y = xf * lax.rsqrt(jnp.mean(xf * xf, axis=-1, keepdims=True) + NORM_EPS)
    return (y * g.astype(jnp.float32)).astype(x.dtype)


def swiglu(x, w_gate, w_up, w_down):
    return (jax.nn.silu(x @ w_gate) * (x @ w_up)) @ w_down


def rotary(x, cos, sin):
    x1, x2 = jnp.split(x, 2, axis=-1)
    c = cos[None, :, None, :]
    s = sin[None, :, None, :]
    return jnp.concatenate([x1 * c - x2 * s, x2 * c + x1 * s], axis=-1)


def causal_multiscale_pool(u):
    s = u.shape[1]
    uf = u.astype(jnp.float32)
    cs = jnp.pad(jnp.cumsum(uf, axis=1), ((0, 0), (1, 0), (0, 0), (0, 0)))
    t = jnp.arange(1, s + 1)
    means = []
    for g, w in enumerate(POOL_WINDOWS):
        lo = jnp.maximum(t - w, 0)
        wsum = cs[:, 1:, g] - cs[:, lo, g]
        cnt = jnp.minimum(t, w).astype(jnp.float32)[None, :, None]
        means.append(wsum / cnt)
    mean = jnp.stack(means, axis=2)
    return (mean - uf).astype(u.dtype)


def causal_attention(q, k, v):
    b, s, h, dqk = q.shape
    dv = v.shape[-1]
    scale = dqk ** -0.5
    kpos = jnp.arange(s)

    def block(i):
        start = i * Q_BLOCK
        qb = lax.dynamic_slice_in_dim(q, start, Q_BLOCK, axis=1)
        sc = jnp.einsum('bqhd,bkhd->bhqk', qb, k,
                        preferred_element_type=jnp.float32) * scale
        qpos = start + jnp.arange(Q_BLOCK)
        sc = jnp.where(kpos[None, :] <= qpos[:, None], sc, -jnp.inf)
        p = jax.nn.softmax(sc, axis=-1)
        return jnp.einsum('bhqk,bkhd->bqhd', p.astype(v.dtype), v)

    out = lax.map(block, jnp.arange(s // Q_BLOCK))
    return jnp.moveaxis(out, 0, 1).reshape(b, s, h, dv)


def pool_mla_mixer(hn, cos, sin, w_in, q_a_norm, w_q_up, kv_a_norm, w_kv_up,
                   q_head_norm, k_head_norm, w_pool, pool_scale, w_out):
    b, s, _ = hn.shape
    z = hn @ w_in
    c1 = POOL_DIM
    c2 = c1 + Q_LORA_RANK
    c3 = c2 + KV_LORA_RANK
    u, q_lat, kv_lat, k_rope = jnp.split(z, [c1, c2, c3], axis=-1)

    u = u.reshape(b, s, POOL_GROUPS, POOL_GROUP_DIM)
    pooled = causal_multiscale_pool(u)
    pool_out = jnp.einsum('bsgc,gcd->bsgd', pooled, w_pool).reshape(b, s, POOL_DIM) * pool_scale

    q = (rms_norm(q_lat, q_a_norm) @ w_q_up).reshape(b, s, MLA_HEADS, QK_HEAD_DIM)
    kv = (rms_norm(kv_lat, kv_a_norm) @ w_kv_up).reshape(b, s, MLA_HEADS, QK_NOPE_DIM + V_HEAD_DIM)
    k_nope, v = jnp.split(kv, [QK_NOPE_DIM], axis=-1)
    k_rope_h = jnp.broadcast_to(k_rope[:, :, None, :], (b, s, MLA_HEADS, QK_ROPE_DIM))
    k = jnp.concatenate([k_nope, k_rope_h], axis=-1)
    q = rms_norm(q, q_head_norm)
    k = rms_norm(k, k_head_norm)
    q = jnp.concatenate([q[..., :QK_NOPE_DIM], rotary(q[..., QK_NOPE_DIM:], cos, sin)], axis=-1)
    k = jnp.concatenate([k[..., :QK_NOPE_DIM], rotary(k[..., QK_NOPE_DIM:], cos, sin)], axis=-1)
    attn = causal_attention(q, k, v).reshape(b, s, MLA_HEADS * V_HEAD_DIM)

    return jnp.concatenate([pool_out, attn], axis=-1) @ w_out


def gated_conv_mixer(hn, w_in, conv_w, w_out):
    gb, gc, hh = jnp.split(hn @ w_in, 3, axis=-1)
    u = gc * hh
    s = u.shape[1]
    up = jnp.pad(u, ((0, 0), (CONV_WIDTH - 1, 0), (0, 0)))
    y = conv_w[0] * up[:, 0:s]
    for j in range(1, CONV_WIDTH):
        y = y + conv_w[j] * up[:, j:j + s]
    return (gb * y) @ w_out


def setup_inputs(seed: int = 0) -> dict:
    key = jax.random.key(seed)
    ks = iter(jax.random.split(key, 32))

    def dense(shape, fan_in):
        return jax.random.normal(next(ks), shape, jnp.float32) * (fan_in ** -0.5)

    def gain(shape):
        return 1.0 + 0.05 * jax.random.normal(next(ks), shape, jnp.float32)

    return {
        "x": jax.random.normal(next(ks), (BATCH, SEQ, D_MODEL), jnp.float32),
        "ffn1_norm": gain((DEPTH, D_MODEL)),
        "ffn1_w_gate": dense((DEPTH, D_MODEL, D_FF), D_MODEL),
        "ffn1_w_up": dense((DEPTH, D_MODEL, D_FF), D_MODEL),
        "ffn1_w_down": dense((DEPTH, D_FF, D_MODEL), D_FF),
        "mix_norm": gain((DEPTH, D_MODEL)),
        "ffn2_norm": gain((DEPTH, D_MODEL)),
        "ffn2_w_gate": dense((DEPTH, D_MODEL, D_FF), D_MODEL),
        "ffn2_w_up": dense((DEPTH, D_MODEL, D_FF), D_MODEL),
        "ffn2_w_down": dense((DEPTH, D_FF, D_MODEL), D_FF),
        "a_w_in": dense((N_EVEN, D_MODEL, MIX_IN_EVEN), D_MODEL),
        "a_q_a_norm": gain((N_EVEN, Q_LORA_RANK)),
        "a_w_q_up": dense((N_EVEN, Q_LORA_RANK, MLA_HEADS * QK_HEAD_DIM), Q_LORA_RANK),
        "a_kv_a_norm": gain((N_EVEN, KV_LORA_RANK)),
        "a_w_kv_up": dense((N_EVEN, KV_LORA_RANK, MLA_HEADS * (QK_NOPE_DIM + V_HEAD_DIM)), KV_LORA_RANK),
        "a_q_head_norm": gain((N_EVEN, QK_HEAD_DIM)),
        "a_k_head_norm": gain((N_EVEN, QK_HEAD_DIM)),
        "a_w_pool": dense((N_EVEN, POOL_GROUPS, POOL_GROUP_DIM, POOL_GROUP_DIM), POOL_GROUP_DIM),
        "a_pool_scale": gain((N_EVEN, POOL_DIM)),
        "a_w_out": dense((N_EVEN, MIX_OUT_EVEN, D_MODEL), MIX_OUT_EVEN),
        "c_w_in": dense((N_ODD, D_MODEL, 3 * CONV_DIM), D_MODEL),
        "c_conv_w": dense((N_ODD, CONV_WIDTH, CONV_DIM), CONV_WIDTH),
        "c_w_out": dense((N_ODD, CONV_DIM, D_MODEL), CONV_DIM),
    }


def reference(x, ffn1_norm, ffn1_w_gate, ffn1_w_up, ffn1_w_down, mix_norm,
              ffn2_norm, ffn2_w_gate, ffn2_w_up, ffn2_w_down,
              a_w_in, a_q_a_norm, a_w_q_up, a_kv_a_norm, a_w_kv_up,
              a_q_head_norm, a_k_head_norm, a_w_pool, a_pool_scale, a_w_out,
              c_w_in, c_conv_w, c_w_out):
    s = x.shape[1]
    pos = jnp.arange(s, dtype=jnp.float32)
    inv_freq = ROPE_THETA ** (-jnp.arange(0, QK_ROPE_DIM, 2, dtype=jnp.float32) / QK_ROPE_DIM)
    ang = pos[:, None] * inv_freq[None, :]
    cos = jnp.cos(ang).astype(x.dtype)
    sin = jnp.sin(ang).astype(x.dtype)

    for layer in range(DEPTH):
        x = x + 0.5 * swiglu(rms_norm(x, ffn1_norm[layer]),
                             ffn1_w_gate[layer], ffn1_w_up[layer], ffn1_w_down[layer])
        hn = rms_norm(x, mix_norm[layer])
        i = layer // 2
        if layer % 2 == 0:
            x = x + pool_mla_mixer(hn, cos, sin, a_w_in[i], a_q_a_norm[i], a_w_q_up[i],
                                   a_kv_a_norm[i], a_w_kv_up[i], a_q_head_norm[i],
                                   a_k_head_norm[i], a_w_pool[i], a_pool_scale[i], a_w_out[i])
        else:
            x = x + gated_conv_mixer(hn, c_w_in[i], c_conv_w[i], c_w_out[i])
        x = x + 0.5 * swiglu(rms_norm(x, ffn2_norm[layer]),
                             ffn2_w_gate[layer], ffn2_w_up[layer], ffn2_w_down[layer])
    return x
```

```python
import numpy as np
import ml_dtypes
import concourse.bass as bass
import concourse.mybir as mybir
from concourse.bass_utils import run_bass_kernel_spmd

F32 = mybir.dt.float32
BF16 = mybir.dt.bfloat16
U8 = mybir.dt.uint8
AF = mybir.ActivationFunctionType
ALU = mybir.AluOpType

NC = 8
S = 16384
D = 1024
T = S // NC
NT = T // 128
DFF = 2816
NFC = DFF // 128
EPS = 1e-6
H = 8
DQK = 96
DV = 64
QL = 384
KVL = 256
MIXIN = 1184
ENGS = ("pe", "act", "dve", "pool", "sp")
FF_GROUPS = [(0, 3), (3, 6), (6, 9), (9, 12), (12, 15), (15, 18), (18, 20), (20, 22)]


class _Op:
    __slots__ = ("idx", "eng", "fn", "deps", "dma", "slot", "ms", "tok")

    def __init__(self, idx, eng, fn, deps, dma, slot):
        self.idx, self.eng, self.fn, self.deps, self.dma, self.slot = idx, eng, fn, deps, dma, slot
        self.ms = False
        self.tok = None


class Prog:
    def __init__(self, nc):
        self.nc = nc
        self.ops = []
        self.last_writer = {}
        self.readers = {}
        self.since_barrier = []

    def op(self, eng, fn, reads=(), writes=(), dma=0, slot=None):
        bank_reads = [k for k in reads if k[0] == "B"]
        if bank_reads:
            reads = [k for k in reads if k[0] != "B"]
            writes = list(writes) + bank_reads
        deps = set()
        for k in reads:
            w = self.last_writer.get(k)
            if w is not None:
                deps.add(w)
        for k in writes:
            w = self.last_writer.get(k)
            if w is not None:
                deps.add(w)
            deps.update(self.readers.get(k, ()))
        idx = len(self.ops)
        o = _Op(idx, eng, fn, deps, dma, slot)
        self.ops.append(o)
        self.since_barrier.append(idx)
        for k in writes:
            self.last_writer[k] = idx
            self.readers[k] = []
        for k in reads:
            self.readers.setdefault(k, []).append(idx)
        return idx

    def barrier(self):
        prev = self.since_barrier
        self.since_barrier = []
        last = {}
        dmas = set()
        for i in prev:
            o = self.ops[i]
            if o.fn is None:
                continue
            if o.dma:
                dmas.add(i)
            else:
                last[o.eng] = i
        for e in ENGS:
            deps = set(dmas)
            for e2, i in last.items():
                if e2 != e:
                    deps.add(i)
            idx = len(self.ops)
            self.ops.append(_Op(idx, e, None, deps, 0, None))
        self.last_writer = {}
        self.readers = {}

    def emit(self):
        nc = self.nc
        ops = self.ops
        for o in ops:
            for d in o.deps:
                q = ops[d]
                if q.dma or q.fn is None:
                    continue
                if q.eng == "pe" and o.eng == "pe" and not o.dma and o.fn is not None:
                    continue
                q.ms = True
        cnt = {e: 0 for e in ENGS}
        slots = {}
        for o in ops:
            if o.fn is None:
                continue
            if o.dma:
                c = slots.get(o.slot, 0) + 16 * o.dma
                slots[o.slot] = c
                o.tok = ("dma:" + o.slot, c)
            elif o.ms:
                cnt[o.eng] += 1
                o.tok = ("eng:" + o.eng, cnt[o.eng])
        names = ["eng:" + e for e in ENGS if cnt[e] > 0] + ["dma:" + s for s in slots]
        sems = {n: nc.alloc_semaphore(name="s%d" % i) for i, n in enumerate(names)}
        by_eng = {e: [o for o in ops if o.eng == e] for e in ENGS}

        def run(en, eng):
            seen = {}
            for o in by_eng[en]:
                need = {}
                for d in o.deps:
                    q = ops[d]
                    if q.tok is None:
                        continue
                    sn, c = q.tok
                    if c > need.get(sn, 0):
                        need[sn] = c
                for sn, c in need.items():
                    if seen.get(sn, 0) >= c:
                        continue
                    eng.wait_ge(sems[sn], c)
                    seen[sn] = c
                if o.fn is None:
                    continue
                if o.dma:
                    o.fn(eng, sems[o.tok[0]])
                else:
                    ins = o.fn(eng)
                    if o.ms:
                        ins.then_inc(sems[o.tok[0]], 1)

        with nc.Block() as block:
            @block.tensor
            def _(e):
                run("pe", e)

            @block.scalar
            def _(e):
                run("act", e)

            @block.vector
            def _(e):
                run("dve", e)

            @block.gpsimd
            def _(e):
                run("pool", e)

            @block.sync
            def _(e):
                run("sp", e)


class Arena:
    def __init__(self, ar, nbytes):
        self.ar = ar
        self.n = nbytes
        self.top = 0

    def alloc(self, shape, dt, parts=128):
        esz = 2 if dt == BF16 else 4
        free = int(np.prod(shape[1:]))
        nb = free * esz
        off = (self.top + 63) // 64 * 64
        assert off + nb <= self.n, ("SBUF arena overflow", off, nb, self.n)
        self.top = off + nb
        v = self.ar[0:shape[0], off:off + nb].bitcast(dt)
        if len(shape) == 3:
            v = v.rearrange("p (a b) -> p a b", a=shape[1])
        elif len(shape) == 4:
            v = v.rearrange("p (a b c) -> p a b c", a=shape[1], b=shape[2])
        return v


class Ctx:
    pass


def _dma(C, eng, out, in_, reads, writes, slot):
    C.p.op(eng, lambda e, s: e.dma_start(out=out, in_=in_).then_inc(s, 16), reads=reads, writes=writes, dma=1, slot=slot)


def xk(tb, k):
    return ["xnT%d_%d" % (4 * tb + i, k) for i in range(4)]


def ph_load_x(C, x_d):
    for j in range(NT):
        _dma(C, "sp", C.xs[:, j, :], x_d[:, j, :], [], ["xs%d" % j], "xs%d" % j)


def ph_store_x(C, x_d, keyname):
    for j in range(NT):
        _dma(C, "sp", x_d[:, j, :], C.xs[:, j, :], ["xs%d" % j], [keyname + "%d" % j], "xo%d" % (j % 4))
    C.outkeys += [keyname + "%d" % j for j in range(NT)]


def ph_consts(C):
    p = C.p
    A = C.A
    C.ident = A.alloc([128, 128], BF16)
    C.ones = A.alloc([128, 128], BF16)
    _dma(C, "pool", C.ident, C.d["ident"], [], ["ident"], "c_ident")
    _dma(C, "pool", C.ones, C.d["ones"], [], ["ones"], "c_ones")


def ph_norm(C, g_d, tag):
    p = C.p
    xs, xnT = C.xs, C.xnT
    gT = C.gT
    _dma(C, "sp", gT, g_d, [], ["gT"], "gT")
    ss, rstd, junk = C.ss, C.rstd, C.junk
    for j in range(NT):
        p.op("act", lambda e, j=j: e.activation(out=junk, in_=xs[:, j, :], func=AF.Square, accum_out=ss[:, j:j + 1]),
             reads=["xs%d" % j], writes=["junk", "ss"])
    p.op("act", lambda e: e.activation(out=rstd, in_=ss, func=AF.Ln, bias=C.epsb, scale=1.0 / D), reads=["ss", "epsb"], writes=["rstd"])
    p.op("act", lambda e: e.activation(out=rstd, in_=rstd, func=AF.Exp, scale=-0.5), reads=["rstd"], writes=["rstd"])
    for j in range(NT):
        xb = C.xb[j % 2]
        bt = C.BT[j % 2]
        p.op("dve", lambda e, j=j, xb=xb: e.tensor_scalar(out=xb, in0=xs[:, j, :], scalar1=rstd[:, j:j + 1], scalar2=None, op0=ALU.mult),
             reads=["xs%d" % j, "rstd"], writes=["xb%d" % (j % 2)])
        for k in range(8):
            p.op("pe", lambda e, k=k, xb=xb, bt=bt: e.transpose(out=bt[:, k * 128:(k + 1) * 128], in_=xb[:, k * 128:(k + 1) * 128], identity=C.ident),
                 reads=["xb%d" % (j % 2), "ident"], writes=["BT%d" % (j % 2)])
        for k in range(8):
            if j % 2 == 0:
                p.op("act", lambda e, k=k, j=j, bt=bt: e.activation(out=xnT[:, k, j * 128:(j + 1) * 128], in_=bt[:, k * 128:(k + 1) * 128],
                                                                func=AF.Copy, scale=gT[:, k:k + 1]),
                     reads=["BT%d" % (j % 2), "gT"], writes=["xnT%d_%d" % (j, k)])
            else:
                p.op("dve", lambda e, k=k, j=j, bt=bt: e.tensor_scalar(out=xnT[:, k, j * 128:(j + 1) * 128], in0=bt[:, k * 128:(k + 1) * 128],
                                                                   scalar1=gT[:, k:k + 1], scalar2=None, op0=ALU.mult),
                     reads=["BT%d" % (j % 2), "gT"], writes=["xnT%d_%d" % (j, k)])


def ph_ffn(C, g_d, wg_d, wu_d, wd_d, tag):
    p = C.p
    ph_norm(C, g_d, tag)
    xs, xnT = C.xs, C.xnT
    for gi, (c0, c1) in enumerate(FF_GROUPS):
        gs = c1 - c0
        sl = gi % 2
        wg, wu, wd, hT = C.wg[sl], C.wu[sl], C.wd[sl], C.hT[sl]
        _dma(C, "pool", wg[:, :, 0:gs * 128], wg_d[:, :, c0 * 128:c1 * 128], [], ["wg%d" % sl], "wg%d" % sl)
        _dma(C, "pool", wu[:, :, 0:gs * 128], wu_d[:, :, c0 * 128:c1 * 128], [], ["wu%d" % sl], "wu%d" % sl)
        _dma(C, "pool", wd[:, 0:gs, :], wd_d[:, c0:c1, :], [], ["wd%d" % sl], "wd%d" % sl)
        n = 0
        for c in range(gs):
            for tb in range(4):
                b = n % 2
                n += 1
                psg, psu = C.B[b], C.B[2 + b]
                for k in range(8):
                    p.op("pe", lambda e, k=k, c=c, tb=tb, psg=psg, wg=wg: e.matmul(psg, wg[:, k, c * 128:(c + 1) * 128], xnT[:, k, tb * 512:(tb + 1) * 512], start=(k == 0), stop=(k == 7)),
                         reads=["wg%d" % sl] + xk(tb, k), writes=["B%d" % b])
                for k in range(8):
                    p.op("pe", lambda e, k=k, c=c, tb=tb, psu=psu, wu=wu: e.matmul(psu, wu[:, k, c * 128:(c + 1) * 128], xnT[:, k, tb * 512:(tb + 1) * 512], start=(k == 0), stop=(k == 7)),
                         reads=["wu%d" % sl] + xk(tb, k), writes=["B%d" % (2 + b)])
                sg = C.sg[b]
                p.op("act", lambda e, sg=sg, psg=psg: e.activation(out=sg, in_=psg, func=AF.Silu), reads=["B%d" % b], writes=["sg%d" % b])
                p.op("dve", lambda e, sg=sg, psu=psu, hT=hT, c=c, tb=tb: e.tensor_tensor(out=hT[:, c, tb * 512:(tb + 1) * 512], in0=sg, in1=psu, op=ALU.mult),
                     reads=["sg%d" % b, "B%d" % (2 + b)], writes=["hT%d_%d_%d" % (sl, c, tb)])
        n = 0
        for j in range(NT):
            for hf in range(2):
                b = 4 + (n % 2)
                n += 1
                psy = C.B[b]
                for c in range(gs):
                    p.op("pe", lambda e, gs=gs, c=c, j=j, hf=hf, psy=psy, hT=hT, wd=wd: e.matmul(psy, hT[:, c, j * 128:(j + 1) * 128], wd[:, c, hf * 512:(hf + 1) * 512], start=(c == 0), stop=(c == gs - 1)),
                         reads=["hT%d_%d_%d" % (sl, c, j // 4), "wd%d" % sl], writes=["B%d" % b])
                p.op("dve", lambda e, j=j, hf=hf, psy=psy: e.scalar_tensor_tensor(out=xs[:, j, hf * 512:(hf + 1) * 512], in0=psy, scalar=0.5, in1=xs[:, j, hf * 512:(hf + 1) * 512], op0=ALU.mult, op1=ALU.add),
                     reads=["B%d" % b, "xs%d" % j], writes=["xs%d" % j])


def _rmsT(C, src_keys, sq, nchunk, rows, width, inv_n, out_rstd, bank, bkey):
    p = C.p
    for i in range(nchunk):
        p.op("pe", lambda e, i=i: e.matmul(bank[0:rows, 0:width], C.ones[0:rows, 0:rows], sq[0:rows, i, 0:width], start=(i == 0), stop=(i == nchunk - 1)),
             reads=["ones"] + src_keys, writes=[bkey])
    p.op("act", lambda e: e.activation(out=out_rstd[0:rows, 0:width], in_=bank[0:rows, 0:width], func=AF.Ln, bias=C.epsb[0:rows, :], scale=inv_n),
         reads=[bkey, "epsb"], writes=["rstdT"])
    p.op("act", lambda e: e.activation(out=out_rstd[0:rows, 0:width], in_=out_rstd[0:rows, 0:width], func=AF.Exp, scale=-0.5),
         reads=["rstdT"], writes=["rstdT"])


def ph_even_in(C, li):
    p, A, d = C.p, C.A, C.d
    C.xnT = A.alloc([128, 8, T], BF16)
    ph_norm(C, d["mixg"], "mix")
    xnT = C.xnT
    win = A.alloc([128, 8, MIXIN], BF16)
    for k in range(8):
        _dma(C, "pool", win[:, k, :], d["win"][:, k, :], [], ["win"], "win")
    wrope = A.alloc([128, 8, 96], BF16)
    p.op("pool", lambda e: e.memset(wrope, 0.0), writes=["wrope"])
    _dma(C, "pool", wrope[:, :, 64:96], d["win"][:, :, 1152:1184], [], ["wrope"], "wrope")
    wq = A.alloc([128, 3, 768], BF16)
    _dma(C, "pool", wq, d["wq"], [], ["wq"], "wq")
    wkp = A.alloc([128, 2, 8, 96], BF16)
    p.op("pool", lambda e: e.memset(wkp, 0.0), writes=["wkp"])
    wv = A.alloc([128, 2, 8, 64], BF16)
    wkv4 = d["wkv"].rearrange("p i (h c) -> p i h c", c=128)
    for i in range(2):
        _dma(C, "pool", wkp[:, i, :, 0:64], wkv4[:, i, :, 0:64], [], ["wkp"], "wkp")
        _dma(C, "pool", wv[:, i, :, :], wkv4[:, i, :, 64:128], [], ["wv"], "wv")
    rm = A.alloc([96, 96], BF16)
    _dma(C, "pool", rm, d["rm"], [], ["rm"], "rm")
    gql = A.alloc([128, 3], F32)
    gkl = A.alloc([128, 2], F32)
    gqh = A.alloc([96, 1], F32)
    gkh = A.alloc([96, 1], F32)
    _dma(C, "sp", gql, d["gql"], [], ["gsm"], "gsm")
    _dma(C, "sp", gkl, d["gkl"], [], ["gsm"], "gsm")
    _dma(C, "sp", gqh, d["gqh"], [], ["gsm"], "gsm")
    _dma(C, "sp", gkh, d["gkh"], [], ["gsm"], "gsm")
    cosT = A.alloc([96, T], F32)
    sinT = A.alloc([96, T], F32)
    _dma(C, "sp", cosT, d["cosT"], [], ["cosT"], "cosT")
    _dma(C, "sp", sinT, d["sinT"], [], ["sinT"], "sinT")
    ust = [A.alloc([128, 512], F32) for _ in range(2)]
    lat = A.alloc([128, 3, 512], F32)
    sq = A.alloc([128, 3, 512], BF16)
    rstdT = A.alloc([128, 512], F32)
    latn = A.alloc([128, 3, 512], BF16)
    kvn = A.alloc([128, 2, 512], BF16)
    hq = [A.alloc([96, 512], BF16) for _ in range(2)]
    t1 = A.alloc([96, 512], F32)
    t2 = A.alloc([96, 512], F32)
    vst = [A.alloc([128, 8, 65], BF16) for _ in range(2)]
    for s in range(2):
        p.op("pool", lambda e, s=s: e.memset(vst[s], 1.0), writes=["vst%d" % s])
    nu = 0
    nh = 0
    for tb in range(4):
        tsl = slice(tb * 512, (tb + 1) * 512)
        for m in range(4):
            bk = C.B[nu % 2]
            bkey = "B%d" % (nu % 2)
            st = ust[nu % 2]
            skey = "ust%d" % (nu % 2)
            nu += 1
            for k in range(8):
                p.op("pe", lambda e, tsl=tsl, k=k, m=m, bk=bk: e.matmul(bk, win[:, k, m * 128:(m + 1) * 128], xnT[:, k, tsl], start=(k == 0), stop=(k == 7)),
                     reads=["win"] + xk(tb, k), writes=[bkey])
            p.op("act", lambda e, bk=bk, st=st: e.activation(out=st, in_=bk, func=AF.Copy), reads=[bkey], writes=[skey])
            _dma(C, "sp", d["uT_o"][m][:, tsl], st, [skey], ["uTo_%d_%d" % (m, tb)], skey)
            C.outkeys.append("uTo_%d_%d" % (m, tb))
        for (nm, m0, nch, inv_n, gl, outn) in (("q", 4, 3, 1.0 / QL, gql, latn), ("kv", 7, 2, 1.0 / KVL, gkl, kvn)):
            for i in range(nch):
                bk = C.B[2 + i]
                bkey = "B%d" % (2 + i)
                m = m0 + i
                for k in range(8):
                    p.op("pe", lambda e, tsl=tsl, k=k, m=m, bk=bk: e.matmul(bk, win[:, k, m * 128:(m + 1) * 128], xnT[:, k, tsl], start=(k == 0), stop=(k == 7)),
                         reads=["win"] + xk(tb, k), writes=[bkey])
                p.op("act", lambda e, i=i, bk=bk: e.activation(out=lat[:, i, :], in_=bk, func=AF.Copy), reads=[bkey], writes=["lat"])
                p.op("act", lambda e, i=i, bk=bk: e.activation(out=sq[:, i, :], in_=bk, func=AF.Square), reads=[bkey], writes=["sq"])
            _rmsT(C, ["sq"], sq, nch, 128, 512, inv_n, rstdT, C.B[5], "B5")
            for i in range(nch):
                p.op("dve", lambda e, i=i, gl=gl, outn=outn: e.scalar_tensor_tensor(out=outn[:, i, :], in0=lat[:, i, :], scalar=gl[:, i:i + 1], in1=rstdT, op0=ALU.mult, op1=ALU.mult),
                     reads=["lat", "rstdT", "gsm"], writes=[nm + "n"])
        for which in ("q", "k"):
            for h in range(H):
                b = nh % 2
                nh += 1
                bk = C.B[b]
                bkey = "B%d" % b
                hqt = hq[b]
                hkey = "hq%d" % b
                if which == "q":
                    for i in range(3):
                        p.op("pe", lambda e, i=i, h=h, bk=bk: e.matmul(bk[0:96, :], wq[:, i, h * 96:(h + 1) * 96], latn[:, i, :], start=(i == 0), stop=(i == 2)),
                             reads=["wq", "qn"], writes=[bkey])
                    gh = gqh
                else:
                    for k in range(8):
                        p.op("pe", lambda e, tsl=tsl, k=k, bk=bk: e.matmul(bk[0:96, :], wrope[:, k, :], xnT[:, k, tsl], start=(k == 0), stop=False),
                             reads=["wrope"] + xk(tb, k), writes=[bkey])
                    for i in range(2):
                        p.op("pe", lambda e, i=i, h=h, bk=bk: e.matmul(bk[0:96, :], wkp[:, i, h, :], kvn[:, i, :], start=False, stop=(i == 1)),
                             reads=["wkp", "kvn"], writes=[bkey])
                    gh = gkh
                p.op("act", lambda e, bk=bk: e.activation(out=sq[0:96, 0, :], in_=bk[0:96, :], func=AF.Square), reads=[bkey], writes=["sq"])
                _rmsT(C, ["sq"], sq, 1, 96, 512, 1.0 / DQK, rstdT, C.B[5], "B5")
                p.op("dve", lambda e, bk=bk, hqt=hqt, gh=gh: e.scalar_tensor_tensor(out=hqt, in0=bk[0:96, :], scalar=gh[:, 0:1], in1=rstdT[0:96, :], op0=ALU.mult, op1=ALU.mult),
                     reads=[bkey, "rstdT", "gsm"], writes=[hkey])
                rb = C.B[4]
                p.op("pe", lambda e, hqt=hqt, rb=rb: e.matmul(rb[0:96, :], rm, hqt, start=True, stop=True), reads=["rm", hkey], writes=["B4"])
                p.op("pool", lambda e, tsl=tsl, hqt=hqt: e.tensor_tensor(out=t1[64:96, :], in0=hqt[64:96, :], in1=cosT[64:96, tsl], op=ALU.mult),
                     reads=[hkey, "cosT"], writes=["t1"])
                p.op("dve", lambda e, tsl=tsl, rb=rb: e.tensor_tensor(out=t2[64:96, :], in0=rb[64:96, :], in1=sinT[64:96, tsl], op=ALU.mult),
                     reads=["B4", "sinT"], writes=["t2"])
                p.op("pool", lambda e, hqt=hqt: e.tensor_tensor(out=hqt[64:96, :], in0=t1[64:96, :], in1=t2[64:96, :], op=ALU.add),
                     reads=["t1", "t2"], writes=[hkey])
                if which == "q":
                    ok = "qTo_%d_%d" % (h, tb)
                    _dma(C, "sp", d["qT_o"][h][:, tsl], hqt, [hkey], [ok], hkey)
                    C.outkeys.append(ok)
                else:
                    for jj in range(4):
                        ok = "kTo_%d_%d_%d" % (h, tb, jj)
                        _dma(C, "sp", d["kT_o"][tb * 4 + jj][:, h, :], hqt[:, jj * 128:(jj + 1) * 128], [hkey], [ok], hkey)
                        C.outkeys.append(ok)
        for jj in range(4):
            j = tb * 4 + jj
            b = j % 2
            bk = C.B[2 + b]
            bkey = "B%d" % (2 + b)
            for i in range(2):
                p.op("pe", lambda e, i=i, jj=jj, bk=bk: e.matmul(bk, kvn[:, i, jj * 128:(jj + 1) * 128], wv[:, i, :, :], start=(i == 0), stop=(i == 1)),
                     reads=["kvn", "wv"], writes=[bkey])
            p.op("dve", lambda e, b=b, bk=bk: e.tensor_copy(out=vst[b][:, :, 0:64], in_=bk.rearrange("p (h c) -> p h c", c=64)),
                 reads=[bkey], writes=["vst%d" % b])
            ok = "vo_%d" % j
            _dma(C, "sp", d["v_o"][:, j, :, :], vst[b], ["vst%d" % b], [ok], "vst%d" % b)
            C.outkeys.append(ok)


def ph_even_out(C, li):
    p, A, d = C.p, C.A, C.d
    scale = DQK ** -0.5
    catT = A.alloc([128, 8, T], BF16)
    mark = A.top
    uextb = [A.alloc([128, NT, 144], F32) for _ in range(2)]
    wa = A.alloc([128, NT, 144], F32)
    wb = A.alloc([128, NT, 144], F32)
    pooled = A.alloc([128, 4, T], BF16)
    invc = A.alloc([128, 4, 16], F32)
    _dma(C, "sp", invc, d["invc"], [], ["invc"], "invc")
    wpool = A.alloc([128, 4, 128], BF16)
    _dma(C, "pool", wpool, d["wpool"], [], ["wpool"], "wpool")
    pscale = A.alloc([128, 4], F32)
    _dma(C, "sp", pscale, d["pscale"], [], ["pscale"], "pscale")
    tmp16 = A.alloc([128, 16], F32)
    for g in range(4):
        w = 2 ** (g + 1)
        src = uextb[g % 2]
        _dma(C, "sp", src, d["uext"][:, g, :, :], [], ["uext%d" % (g % 2)], "uext%d" % (g % 2))
        bufs = [wa, wb]
        sh = 1
        cur = src
        curkey = "uext%d" % (g % 2)
        nb = 0
        while sh < w:
            dst = bufs[nb % 2]
            dkey = "w%d" % (nb % 2)
            nb += 1
            p.op("dve", lambda e, cur=cur, dst=dst, sh=sh: e.tensor_tensor(out=dst[:, :, 2 * sh - 1:144], in0=cur[:, :, 2 * sh - 1:144], in1=cur[:, :, sh - 1:144 - sh], op=ALU.add),
                 reads=[curkey], writes=[dkey])
            if sh > 1:
                pass
            cur = dst
            curkey = dkey
            sh *= 2
        pv = pooled[:, g, :].rearrange("p (j t) -> p j t", t=128)
        p.op("dve", lambda e, cur=cur, src=src, pv=pv, w=w: e.scalar_tensor_tensor(out=pv, in0=cur[:, :, 16:144], scalar=1.0 / w, in1=src[:, :, 16:144], op0=ALU.mult, op1=ALU.subtract),
             reads=[curkey, "uext%d" % (g % 2)], writes=["pooled%d" % g])
        p.op("dve", lambda e, cur=cur, g=g: e.tensor_tensor(out=tmp16, in0=cur[:, 0, 16:32], in1=invc[:, g, :], op=ALU.mult),
             reads=[curkey, "invc"], writes=["tmp16"])
        p.op("dve", lambda e, src=src, g=g: e.tensor_tensor(out=pooled[:, g, 0:16], in0=tmp16, in1=src[:, 0, 16:32], op=ALU.subtract),
             reads=["tmp16", "uext%d" % (g % 2), "pooled%d" % g], writes=["pooled%d" % g])
    n = 0
    for g in range(4):
        for tb in range(4):
            b = n % 2
            n += 1
            bk = C.B[b]
            p.op("pe", lambda e, g=g, tb=tb, bk=bk: e.matmul(bk, wpool[:, g, :], pooled[:, g, tb * 512:(tb + 1) * 512], start=True, stop=True),
                 reads=["wpool", "pooled%d" % g], writes=["B%d" % b])
            p.op("act", lambda e, g=g, tb=tb, bk=bk: e.activation(out=catT[:, g, tb * 512:(tb + 1) * 512], in_=bk, func=AF.Copy, scale=pscale[:, g:g + 1]),
                 reads=["B%d" % b, "pscale"], writes=["catT"])
    p.barrier()
    A.top = mark
    qT = A.alloc([96, H, T], BF16)
    for h in range(H):
        _dma(C, "sp", qT[:, h, :], d["qT"][h], [], ["qT"], "qT%d" % (h % 4))
    mask = A.alloc([128, 8, 128], BF16)
    _dma(C, "pool", mask, d["mask"], [], ["mask"], "mask")
    krow = [A.alloc([96, 8, 4, 128], BF16) for _ in range(2)]
    vrow = [A.alloc([128, 8, 4, 65], BF16) for _ in range(2)]
    pT = [A.alloc([128, 512], BF16) for _ in range(3)]
    atok = A.alloc([128, NT, 512], BF16)
    rden = A.alloc([128, 4], F32)
    nrow = 0
    ns = 0
    for sb in range(4):
        for hg in range(2):
            for jr in range(4 * sb + 4):
                sl = nrow % 2
                nrow += 1
                kr, vr = krow[sl], vrow[sl]
                C.p.op("sp", lambda e, s, kr=kr, jr=jr, hg=hg: [e.dma_start(out=kr[:, r, :, :], in_=d["kall"][jr][:, r, hg * 4:(hg + 1) * 4, :]).then_inc(s, 16) for r in range(8)],
                       reads=[], writes=["krow%d" % sl], dma=8, slot="krow%d" % sl)
                C.p.op("sp", lambda e, s, vr=vr, jr=jr, hg=hg: [e.dma_start(out=vr[:, r, :, :], in_=d["vall"][jr][:, r, hg * 4:(hg + 1) * 4, :]).then_inc(s, 16) for r in range(8)],
                       reads=[], writes=["vrow%d" % sl], dma=8, slot="vrow%d" % sl)
                i0 = max(0, jr - 4 * sb)
                diag = jr >= 4 * sb
                ncols = (4 - i0) * 128
                q0 = (4 * sb + i0) * 128
                for r in range(8):
                    for hh in range(4):
                        h = hg * 4 + hh
                        b = 4 + (ns % 2)
                        pt = pT[ns % 3]
                        ptk = "pT%d" % (ns % 3)
                        ns += 1
                        bk = C.B[b]
                        bkey = "B%d" % b
                        if diag:
                            p.op("pe", lambda e, bk=bk, kr=kr, r=r, hh=hh, h=h, q0=q0: e.matmul(bk[:, 0:128], kr[:, r, hh, :], qT[:, h, q0:q0 + 128], start=True, stop=False),
                                 reads=["krow%d" % sl, "qT"], writes=[bkey])
                            p.op("pe", lambda e, bk=bk, r=r: e.matmul(bk[:, 0:128], C.ident, mask[:, r, :], start=False, stop=True),
                                 reads=["ident", "mask"], writes=[bkey])
                            if ncols > 128:
                                p.op("pe", lambda e, bk=bk, kr=kr, r=r, hh=hh, h=h, q0=q0, ncols=ncols: e.matmul(bk[:, 128:ncols], kr[:, r, hh, :], qT[:, h, q0 + 128:q0 + ncols], start=True, stop=True),
                                     reads=["krow%d" % sl, "qT"], writes=[bkey])
                        else:
                            p.op("pe", lambda e, bk=bk, kr=kr, r=r, hh=hh, h=h, q0=q0, ncols=ncols: e.matmul(bk[:, 0:ncols], kr[:, r, hh, :], qT[:, h, q0:q0 + ncols], start=True, stop=True),
                                 reads=["krow%d" % sl, "qT"], writes=[bkey])
                        p.op("act", lambda e, bk=bk, pt=pt, ncols=ncols: e.activation(out=pt[:, 0:ncols], in_=bk[:, 0:ncols], func=AF.Exp, scale=scale),
                             reads=[bkey], writes=[ptk])
                        for i in range(i0, 4):
                            first = (jr == 0 and r == 0 and hh == 0)
                            last = (jr == 4 * sb + i and r == 7)
                            p.op("pe", lambda e, i0=i0, i=i, pt=pt, vr=vr, r=r, hh=hh, first=first, last=last: e.matmul(C.B[i][:, hh * 65:(hh + 1) * 65], pt[:, (i - i0) * 128:(i - i0 + 1) * 128], vr[:, r, hh, :], start=first, stop=last, skip_group_check=True),
                                 reads=[ptk, "vrow%d" % sl], writes=["B%d" % i])
            for i in range(4):
                j = 4 * sb + i
                acc = C.B[i][:, 0:260].rearrange("p (h c) -> p h c", c=65)
                p.op("dve", lambda e, acc=acc: e.reciprocal(out=rden, in_=acc[:, :, 64]), reads=["B%d" % i], writes=["rden"])
                for hh in range(4):
                    h = hg * 4 + hh
                    p.op("dve", lambda e, acc=acc, hh=hh, h=h, j=j: e.tensor_scalar(out=atok[:, j, h * 64:(h + 1) * 64], in0=acc[:, hh, 0:64], scalar1=rden[:, hh:hh + 1], scalar2=None, op0=ALU.mult),
                         reads=["B%d" % i, "rden"], writes=["atok%d" % j])
    for j in range(NT):
        bt = C.BT[j % 2]
        for m in range(4):
            p.op("pe", lambda e, j=j, m=m, bt=bt: e.transpose(out=bt[:, m * 128:(m + 1) * 128], in_=atok[:, j, m * 128:(m + 1) * 128], identity=C.ident),
                 reads=["atok%d" % j, "ident"], writes=["BT%d" % (j % 2)])
        p.op("dve", lambda e, j=j, bt=bt: e.tensor_copy(out=catT[:, 4:8, j * 128:(j + 1) * 128], in_=bt[:, 0:512].rearrange("p (m t) -> p m t", t=128)),
             reads=["BT%d" % (j % 2)], writes=["catT"])
    p.barrier()
    A.top = mark
    _out_proj(C, catT, "catT", d["wout"])


def _out_proj(C, srcT, skey, w_d):
    p, A = C.p, C.A
    xs = C.xs
    wo = A.alloc([128, 8, 1024], BF16)
    for k in range(8):
        _dma(C, "pool", wo[:, k, :], w_d[:, k, :], [], ["wo"], "wo")
    n = 0
    for j in range(NT):
        for hf in range(2):
            b = n % 2
            n += 1
            bk = C.B[b]
            for m in range(8):
                p.op("pe", lambda e, j=j, hf=hf, m=m, bk=bk: e.matmul(bk, srcT[:, m, j * 128:(j + 1) * 128], wo[:, m, hf * 512:(hf + 1) * 512], start=(m == 0), stop=(m == 7)),
                     reads=[skey, "wo"], writes=["B%d" % b])
            p.op("dve", lambda e, j=j, hf=hf, bk=bk: e.tensor_tensor(out=xs[:, j, hf * 512:(hf + 1) * 512], in0=bk, in1=xs[:, j, hf * 512:(hf + 1) * 512], op=ALU.add),
                 reads=["B%d" % b, "xs%d" % j], writes=["xs%d" % j])


def ph_odd_in(C, li):
    p, A, d = C.p, C.A, C.d
    C.xnT = A.alloc([128, 8, T], BF16)
    ph_norm(C, d["mixg"], "mix")
    xnT = C.xnT
    wst = [A.alloc([128, 8, 384], BF16) for _ in range(2)]
    gst = [A.alloc([128, 512], F32) for _ in range(2)]
    cst = [A.alloc([128, 512], F32) for _ in range(2)]
    ust = [A.alloc([128, 512], F32) for _ in range(2)]
    n = 0
    for m in range(8):
        sl = m % 2
        w = wst[sl]
        for q in range(3):
            C.p.op("pool", lambda e, s, w=w, q=q, m=m: e.dma_start(out=w[:, :, q * 128:(q + 1) * 128], in_=d["wcin"][:, :, q * 1024 + m * 128:q * 1024 + (m + 1) * 128]).then_inc(s, 16),
                   reads=[], writes=["wst%d" % sl], dma=1, slot="wst%d" % sl)
        for tb in range(4):
            tsl = slice(tb * 512, (tb + 1) * 512)
            b = n % 2
            n += 1
            banks = [C.B[b], C.B[2 + b], C.B[4 + b]]
            bkeys = ["B%d" % b, "B%d" % (2 + b), "B%d" % (4 + b)]
            for q in range(3):
                for k in range(8):
                    p.op("pe", lambda e, tsl=tsl, q=q, k=k, w=w, bk=banks[q]: e.matmul(bk, w[:, k, q * 128:(q + 1) * 128], xnT[:, k, tsl], start=(k == 0), stop=(k == 7)),
                         reads=["wst%d" % sl] + xk(tb, k), writes=[bkeys[q]])
            p.op("act", lambda e, b=b, bk=banks[0]: e.activation(out=gst[b], in_=bk, func=AF.Copy), reads=[bkeys[0]], writes=["gst%d" % b])
            p.op("act", lambda e, b=b, bk=banks[1]: e.activation(out=cst[b], in_=bk, func=AF.Copy), reads=[bkeys[1]], writes=["cst%d" % b])
            p.op("dve", lambda e, b=b, bk=banks[2]: e.tensor_tensor(out=ust[b], in0=cst[b], in1=bk, op=ALU.mult), reads=["cst%d" % b, bkeys[2]], writes=["ust%d" % b])
            ok = "gbo_%d_%d" % (m, tb)
            _dma(C, "sp", d["gb_o"][m][:, tsl], gst[b], ["gst%d" % b], [ok], "gst%d" % b)
            C.outkeys.append(ok)
            ok = "uco_%d_%d" % (m, tb)
            _dma(C, "sp", d["uc_o"][m][:, tsl], ust[b], ["ust%d" % b], [ok], "ust%d" % b)
            C.outkeys.append(ok)


def ph_odd_out(C, li):
    p, A, d = C.p, C.A, C.d
    vT = A.alloc([128, 8, T], BF16)
    cw = A.alloc([128, 8, 3], F32)
    _dma(C, "sp", cw, d["cw"], [], ["cw"], "cw")
    ue = [A.alloc([128, NT, 130], F32) for _ in range(2)]
    gb = [A.alloc([128, T], F32) for _ in range(2)]
    y = [A.alloc([128, NT, 128], F32) for _ in range(2)]
    for m in range(8):
        sl = m % 2
        _dma(C, "sp", ue[sl], d["ucext"][:, m, :, :], [], ["ue%d" % sl], "ue%d" % sl)
        _dma(C, "sp", gb[sl], d["gbT"][m], [], ["gb%d" % sl], "gb%d" % sl)
        eng = "dve"
        u, yy, g = ue[sl], y[sl], gb[sl]
        p.op(eng, lambda e, u=u, yy=yy, m=m: e.tensor_scalar(out=yy, in0=u[:, :, 0:128], scalar1=cw[:, m, 0:1], scalar2=None, op0=ALU.mult),
             reads=["ue%d" % sl, "cw"], writes=["y%d" % sl])
        p.op(eng, lambda e, u=u, yy=yy, m=m: e.scalar_tensor_tensor(out=yy, in0=u[:, :, 1:129], scalar=cw[:, m, 1:2], in1=yy, op0=ALU.mult, op1=ALU.add),
             reads=["ue%d" % sl, "cw", "y%d" % sl], writes=["y%d" % sl])
        p.op(eng, lambda e, u=u, yy=yy, m=m: e.scalar_tensor_tensor(out=yy, in0=u[:, :, 2:130], scalar=cw[:, m, 2:3], in1=yy, op0=ALU.mult, op1=ALU.add),
             reads=["ue%d" % sl, "cw", "y%d" % sl], writes=["y%d" % sl])
        p.op(eng, lambda e, yy=yy, g=g, m=m: e.tensor_tensor(out=vT[:, m, :], in0=yy.rearrange("p j t -> p (j t)"), in1=g, op=ALU.mult),
             reads=["y%d" % sl, "gb%d" % sl], writes=["vT"])
    _out_proj(C, vT, "vT", d["wcout"])


def build_launch(spec):
    nc = bass.Bass("TRN2", target_bir_lowering=False)
    C = Ctx()
    C.nc = nc
    C.p = Prog(nc)
    C.d = {}
    C.outkeys = []
    for name, (shape, dt) in spec["inputs"].items():
        C.d[name] = nc.dram_tensor(name, list(shape), dt, kind="ExternalInput").ap()
    for name, (shape, dt) in spec["outputs"].items():
        C.d[name] = nc.dram_tensor(name, list(shape), dt, kind="ExternalOutput").ap()
    import contextlib
    with contextlib.ExitStack() as st:
        nbytes = 196 * 1024
        ar = st.enter_context(nc.sbuf_tensor("arena", [128, nbytes], U8))
        C.A = Arena(ar, nbytes)
        C.B = [st.enter_context(nc.psum_tensor("bank%d" % i, [128, 512], F32)) for i in range(6)]
        C.B = [b[:] for b in C.B]
        C.BT = [st.enter_context(nc.psum_tensor("bankT%d" % i, [128, 1024], BF16)) for i in range(2)]
        C.BT = [b[:] for b in C.BT]
        A = C.A
        C.xs = A.alloc([128, NT, D], F32)
        C.gT = A.alloc([128, 8], F32)
        C.ss = A.alloc([128, NT], F32)
        C.rstd = A.alloc([128, NT], F32)
        C.junk = A.alloc([128, D], BF16)
        C.xb = [A.alloc([128, D], BF16) for _ in range(2)]
        C.epsb = A.alloc([128, 1], F32)
        C.p.op("pool", lambda e: e.memset(C.epsb, EPS), writes=["epsb"])
        ph_consts(C)
        base = A.top
        for ph in spec["phases"]:
            C.p.barrier()
            A.top = base
            ph(C)
        C.p.op("sp", None, reads=list(C.outkeys))
        C.p.op("pool", None, reads=list(C.outkeys))
        C.p.emit()
    return nc


def alloc_ffn(C):
    A = C.A
    C.xnT = A.alloc([128, 8, T], BF16)
    C.wg = [A.alloc([128, 8, 384], BF16) for _ in range(2)]
    C.wu = [A.alloc([128, 8, 384], BF16) for _ in range(2)]
    C.wd = [A.alloc([128, 3, D], BF16) for _ in range(2)]
    C.hT = [A.alloc([128, 3, T], BF16) for _ in range(2)]
    C.sg = [A.alloc([128, 512], F32) for _ in range(2)]


def mk_ffn(which):
    def f(C):
        alloc_ffn(C)
        ph_ffn(C, C.d[which + "_g"], C.d[which + "_wg"], C.d[which + "_wu"], C.d[which + "_wd"], which)
    return f


FFN_IN = lambda w: {w + "_g": ((128, 8), F32), w + "_wg": ((128, 8, DFF), F32), w + "_wu": ((128, 8, DFF), F32), w + "_wd": ((128, NFC, D), F32)}
X_IO = ((128, NT, D), F32)
CONST_IN = {"ident": ((128, 128), F32), "ones": ((128, 128), F32)}
EVEN_IN_W = {"mixg": ((128, 8), F32), "win": ((128, 8, MIXIN), F32), "wq": ((128, 3, 768), F32), "wkv": ((128, 2, 1024), F32),
             "rm": ((96, 96), F32), "gql": ((128, 3), F32), "gkl": ((128, 2), F32), "gqh": ((96, 1), F32), "gkh": ((96, 1), F32),
             "cosT": ((96, T), F32), "sinT": ((96, T), F32)}
EVEN_IN_OUT = {"uT_o": ((4, 128, T), F32), "qT_o": ((H, 96, T), BF16), "kT_o": ((NT, 96, H, 128), BF16), "v_o": ((128, NT, H, 65), BF16)}
EVEN_OUT_IN = {"uext": ((128, 4, NT, 144), F32), "invc": ((128, 4, 16), F32), "wpool": ((128, 4, 128), F32), "pscale": ((128, 4), F32),
               "qT": ((H, 96, T), BF16), "mask": ((128, 8, 128), F32), "kall": ((NT, 96, 8, H, 128), BF16), "vall": ((NT, 128, 8, H, 65), BF16),
               "wout": ((128, 8, D), F32)}
ODD_IN_W = {"mixg": ((128, 8), F32), "wcin": ((128, 8, 3 * D), F32)}
ODD_IN_OUT = {"gb_o": ((8, 128, T), F32), "uc_o": ((8, 128, T), F32)}
ODD_OUT_IN = {"cw": ((128, 8, 3), F32), "ucext": ((128, 8, NT, 130), F32), "gbT": ((8, 128, T), F32), "wcout": ((128, 8, D), F32)}


def _load_x(C):
    ph_load_x(C, C.d["x"])


def _store_x(C):
    ph_store_x(C, C.d["x_o"], "xo")


def spec_A():
    return dict(inputs={"x": X_IO, **CONST_IN, **FFN_IN("f1"), **EVEN_IN_W},
                outputs={"x_o": X_IO, **EVEN_IN_OUT},
                phases=[_load_x, mk_ffn("f1"), _store_x, lambda C: ph_even_in(C, 0)])


def spec_B():
    return dict(inputs={"x": X_IO, **CONST_IN, **EVEN_OUT_IN, **FFN_IN("f2"), **FFN_IN("f1"), **ODD_IN_W},
                outputs={"x_o": X_IO, **ODD_IN_OUT},
                phases=[_load_x, lambda C: ph_even_out(C, 0), mk_ffn("f2"), mk_ffn("f1"), _store_x, lambda C: ph_odd_in(C, 1)])


def spec_C():
    return dict(inputs={"x": X_IO, **CONST_IN, **ODD_OUT_IN, **FFN_IN("f2"), **FFN_IN("f1"), **EVEN_IN_W},
                outputs={"x_o": X_IO, **EVEN_IN_OUT},
                phases=[_load_x, lambda C: ph_odd_out(C, 1), mk_ffn("f2"), mk_ffn("f1"), _store_x, lambda C: ph_even_in(C, 2)])


def spec_D():
    return dict(inputs={"x": X_IO, **CONST_IN, **ODD_OUT_IN, **FFN_IN("f2")},
                outputs={"x_o": X_IO},
                phases=[_load_x, lambda C: ph_odd_out(C, 3), mk_ffn("f2"), _store_x])


_NC_CACHE = {}


def _get_nc(name):
    if name not in _NC_CACHE:
        _NC_CACHE[name] = build_launch({"A": spec_A, "B": spec_B, "C": spec_C, "D": spec_D}[name]())
    return _NC_CACHE[name]


def _kmaj(w, p=128):
    K, N = w.shape
    return np.ascontiguousarray(w.reshape(K // p, p, N).transpose(1, 0, 2))


def _vec(g, p=128):
    return np.ascontiguousarray(g.reshape(-1, p).T)


def _ffn_inputs(prefix, inputs, which, layer):
    return {prefix + "_g": _vec(inputs[which + "_norm"][layer]),
            prefix + "_wg": _kmaj(inputs[which + "_w_gate"][layer]),
            prefix + "_wu": _kmaj(inputs[which + "_w_up"][layer]),
            prefix + "_wd": _kmaj(inputs[which + "_w_down"][layer])}


def _consts():
    return {"ident": np.eye(128, dtype=np.float32), "ones": np.ones((128, 128), np.float32)}


def _rope_tables(c):
    pos = (np.arange(NT)[:, None] * NC + c) * 128 + np.arange(128)[None, :]
    pos = pos.reshape(-1).astype(np.float32)
    inv = (np.float32(10000.0) ** (-np.arange(0, 32, 2, dtype=np.float32) / np.float32(32))).astype(np.float32)
    ang = (pos[:, None] * inv[None, :]).astype(np.float32)
    cos = np.cos(ang).astype(np.float32).T
    sin = np.sin(ang).astype(np.float32).T
    cosT = np.zeros((96, T), np.float32)
    sinT = np.zeros((96, T), np.float32)
    cosT[64:80] = cos
    cosT[80:96] = cos
    sinT[64:80] = sin
    sinT[80:96] = sin
    return cosT, sinT


def _rm():
    rm = np.zeros((96, 96), np.float32)
    for i in range(16):
        rm[80 + i, 64 + i] = -1.0
        rm[64 + i, 80 + i] = 1.0
    return rm


def _even_in_inputs(inputs, i, c):
    cosT, sinT = _rope_tables(c)
    return {"mixg": _vec(inputs["mix_norm"][2 * i]), "win": _kmaj(inputs["a_w_in"][i]), "wq": _kmaj(inputs["a_w_q_up"][i]),
            "wkv": _kmaj(inputs["a_w_kv_up"][i]), "rm": _rm(), "gql": _vec(inputs["a_q_a_norm"][i]), "gkl": _vec(inputs["a_kv_a_norm"][i]),
            "gqh": np.ascontiguousarray(inputs["a_q_head_norm"][i].reshape(96, 1)), "gkh": np.ascontiguousarray(inputs["a_k_head_norm"][i].reshape(96, 1)),
            "cosT": cosT, "sinT": sinT}


def _mask(c):
    m = np.zeros((128, 8, 128), np.float32)
    for r in range(8):
        if r > c:
            m[:, r, :] = -30000.0
        elif r == c:
            m[:, r, :] = np.where(np.arange(128)[:, None] > np.arange(128)[None, :], -30000.0, 0.0)
    return m


def _even_out_inputs(inputs, i, c, res):
    uT = np.stack([r["uT_o"] for r in res])
    uT = uT.reshape(NC, 4, 128, NT, 128)
    glob = uT.transpose(1, 2, 3, 0, 4).reshape(4, 128, NT * NC, 128)
    ext = np.zeros((128, 4, NT, 144), np.float32)
    for j in range(NT):
        t = 8 * j + c
        ext[:, :, j, 16:144] = glob[:, :, t, :].transpose(1, 0, 2)
        if t > 0:
            ext[:, :, j, 0:16] = glob[:, :, t - 1, 112:128].transpose(1, 0, 2)
    invc = np.zeros((128, 4, 16), np.float32)
    for g in range(4):
        w = 2 ** (g + 1)
        if c == 0:
            invc[:, g, :] = 1.0 / np.minimum(np.arange(16) + 1, w).astype(np.float32)
        else:
            invc[:, g, :] = 1.0 / w
    kall = np.stack([r["kT_o"] for r in res])
    kall = np.ascontiguousarray(kall.transpose(1, 2, 0, 3, 4))
    vall = np.stack([r["v_o"] for r in res])
    vall = np.ascontiguousarray(vall.transpose(2, 1, 0, 3, 4))
    return {"uext": ext, "invc": invc, "wpool": np.ascontiguousarray(inputs["a_w_pool"][i].transpose(1, 0, 2)),
            "pscale": _vec(inputs["a_pool_scale"][i]), "qT": res[c]["qT_o"], "mask": _mask(c), "kall": kall, "vall": vall,
            "wout": _kmaj(inputs["a_w_out"][i])}, (kall, vall)


def _odd_in_inputs(inputs, i):
    return {"mixg": _vec(inputs["mix_norm"][2 * i + 1]), "wcin": _kmaj(inputs["c_w_in"][i])}


def _odd_out_inputs(inputs, i, c, res):
    uc = np.stack([r["uc_o"] for r in res]).reshape(NC, 8, 128, NT, 128)
    glob = uc.transpose(1, 2, 3, 0, 4).reshape(8, 128, NT * NC, 128)
    ext = np.zeros((128, 8, NT, 130), np.float32)
    for j in range(NT):
        t = 8 * j + c
        ext[:, :, j, 2:130] = glob[:, :, t, :].transpose(1, 0, 2)
        if t > 0:
            ext[:, :, j, 0:2] = glob[:, :, t - 1, 126:128].transpose(1, 0, 2)
    cw = np.ascontiguousarray(inputs["c_conv_w"][i].reshape(3, 8, 128).transpose(2, 1, 0))
    return {"cw": cw, "ucext": ext, "gbT": res[c]["gb_o"], "wcout": _kmaj(inputs["c_w_out"][i])}


def _run(name, in_maps):
    nc = _get_nc(name)
    res = run_bass_kernel_spmd(nc, in_maps, core_ids=list(range(NC)))
    return res.results


def kernel(**inputs):
    inputs = {k: np.asarray(v) for k, v in inputs.items()}
    x = inputs["x"][0].reshape(NT, NC, 128, D)
    xc = [np.ascontiguousarray(x[:, c].transpose(1, 0, 2)) for c in range(NC)]
    f1 = _ffn_inputs("f1", inputs, "ffn1", 0)
    res = _run("A", [{"x": xc[c], **_consts(), **f1, **_even_in_inputs(inputs, 0, c)} for c in range(NC)])
    f2 = _ffn_inputs("f2", inputs, "ffn2", 0)
    f1 = _ffn_inputs("f1", inputs, "ffn1", 1)
    oi = _odd_in_inputs(inputs, 0)
    maps = []
    for c in range(NC):
        eo, _ = _even_out_inputs(inputs, 0, c, res)
        maps.append({"x": res[c]["x_o"], **_consts(), **eo, **f2, **f1, **oi})
    res = _run("B", maps)
    f2 = _ffn_inputs("f2", inputs, "ffn2", 1)
    f1 = _ffn_inputs("f1", inputs, "ffn1", 2)
    maps = [{"x": res[c]["x_o"], **_consts(), **_odd_out_inputs(inputs, 0, c, res), **f2, **f1, **_even_in_inputs(inputs, 1, c)} for c in range(NC)]
    res = _run("C", maps)
    f2 = _ffn_inputs("f2", inputs, "ffn2", 2)
    f1 = _ffn_inputs("f1", inputs, "ffn1", 3)
    oi = _odd_in_inputs(inputs, 1)
    maps = []
    for c in range(NC):
        eo, _ = _even_out_inputs(inputs, 1, c, res)
        maps.append({"x": res[c]["x_o"], **_consts(), **eo, **f2, **f1, **oi})
    res = _run("B", maps)
    f2 = _ffn_inputs("f2", inputs, "ffn2", 3)
    maps = [{"x": res[c]["x_o"], **_consts(), **_odd_out_inputs(inputs, 1, c, res), **f2} for c in range(NC)]
    res = _run("D", maps)
    out = np.zeros((NT, NC, 128, D), np.float32)
    for c in range(NC):
        out[:, c] = res[c]["x_o"].transpose(1, 0, 2)
    return out.reshape(1, S, D)
```

```python
import numpy as np
import ml_dtypes
import concourse.bass as bass
import concourse.mybir as mybir
from concourse.bass_utils import run_bass_kernel_spmd

F32 = mybir.dt.float32
BF16 = mybir.dt.bfloat16
U8 = mybir.dt.uint8
AF = mybir.ActivationFunctionType
ALU = mybir.AluOpType

NC = 8
S = 16384
D = 1024
T = S // NC
NT = T // 128
DFF = 2816
NFC = DFF // 128
EPS = 1e-6
H = 8
DQK = 96
DV = 64
QL = 384
KVL = 256
MIXIN = 1184
ENGS = ("pe", "act", "dve", "pool", "sp")
FF_GROUPS = [(0, 3), (3, 6), (6, 9), (9, 12), (12, 15), (15, 18), (18, 20), (20, 22)]


class _Op:
    __slots__ = ("idx", "eng", "fn", "deps", "dma", "slot", "ms", "tok")

    def __init__(self, idx, eng, fn, deps, dma, slot):
        self.idx, self.eng, self.fn, self.deps, self.dma, self.slot = idx, eng, fn, deps, dma, slot
        self.ms = False
        self.tok = None


class Prog:
    def __init__(self, nc):
        self.nc = nc
        self.ops = []
        self.last_writer = {}
        self.readers = {}
        self.since_barrier = []

    def op(self, eng, fn, reads=(), writes=(), dma=0, slot=None):
        bank_reads = [k for k in reads if k[0] == "B"]
        if bank_reads:
            reads = [k for k in reads if k[0] != "B"]
            writes = list(writes) + bank_reads
        deps = set()
        for k in reads:
            w = self.last_writer.get(k)
            if w is not None:
                deps.add(w)
        for k in writes:
            w = self.last_writer.get(k)
            if w is not None:
                deps.add(w)
            deps.update(self.readers.get(k, ()))
        best = {}
        red = set()
        for dd in deps:
            q = self.ops[dd]
            if q.dma:
                red.add(dd)
            elif dd > best.get(q.eng, -1):
                best[q.eng] = dd
        red.update(best.values())
        deps = red
        idx = len(self.ops)
        o = _Op(idx, eng, fn, deps, dma, slot)
        self.ops.append(o)
        self.since_barrier.append(idx)
        for k in writes:
            self.last_writer[k] = idx
            self.readers[k] = []
        for k in reads:
            self.readers.setdefault(k, []).append(idx)
        return idx

    def barrier(self):
        prev = self.since_barrier
        self.since_barrier = []
        last = {}
        dmas = set()
        for i in prev:
            o = self.ops[i]
            if o.fn is None:
                continue
            if o.dma:
                dmas.add(i)
            else:
                last[o.eng] = i
        for e in ENGS:
            deps = set(dmas)
            for e2, i in last.items():
                if e2 != e:
                    deps.add(i)
            idx = len(self.ops)
            self.ops.append(_Op(idx, e, None, deps, 0, None))
        self.last_writer = {}
        self.readers = {}

    def emit(self):
        nc = self.nc
        ops = self.ops
        for o in ops:
            for d in o.deps:
                q = ops[d]
                if q.dma or q.fn is None:
                    continue
                if q.eng == "pe" and o.eng == "pe" and not o.dma and o.fn is not None:
                    continue
                q.ms = True
        cnt = {e: 0 for e in ENGS}
        slots = {}
        for o in ops:
            if o.fn is None:
                continue
            if o.dma:
                c = slots.get(o.slot, 0) + 16 * o.dma
                slots[o.slot] = c
                o.tok = ("dma:" + o.slot, c)
            elif o.ms:
                cnt[o.eng] += 1
                o.tok = ("eng:" + o.eng, cnt[o.eng])
        names = ["eng:" + e for e in ENGS if cnt[e] > 0] + ["dma:" + s for s in slots]
        sems = {n: nc.alloc_semaphore(name="s%d" % i) for i, n in enumerate(names)}
        by_eng = {e: [o for o in ops if o.eng == e] for e in ENGS}

        def run(en, eng):
            seen = {}
            for o in by_eng[en]:
                need = {}
                for d in o.deps:
                    q = ops[d]
                    if q.tok is None:
                        continue
                    if q.eng == "pe" and en == "pe" and not q.dma and not o.dma:
                        continue
                    sn, c = q.tok
                    if c > need.get(sn, 0):
                        need[sn] = c
                for sn, c in need.items():
                    if seen.get(sn, 0) >= c:
                        continue
                    eng.wait_ge(sems[sn], c)
                    seen[sn] = c
                if o.fn is None:
                    continue
                if o.dma:
                    o.fn(eng, sems[o.tok[0]])
                else:
                    ins = o.fn(eng)
                    if o.ms:
                        ins.then_inc(sems[o.tok[0]], 1)

        with nc.Block() as block:
            @block.tensor
            def _(e):
                run("pe", e)

            @block.scalar
            def _(e):
                run("act", e)

            @block.vector
            def _(e):
                run("dve", e)

            @block.gpsimd
            def _(e):
                run("pool", e)

            @block.sync
            def _(e):
                run("sp", e)


class Arena:
    def __init__(self, ar, nbytes):
        self.ar = ar
        self.n = nbytes
        self.top = 0

    def alloc(self, shape, dt, parts=128):
        esz = 2 if dt == BF16 else 4
        free = int(np.prod(shape[1:]))
        nb = free * esz
        off = (self.top + 63) // 64 * 64
        assert off + nb <= self.n, ("SBUF arena overflow", off, nb, self.n)
        self.top = off + nb
        v = self.ar[0:shape[0], off:off + nb].bitcast(dt)
        if len(shape) == 3:
            v = v.rearrange("p (a b) -> p a b", a=shape[1])
        elif len(shape) == 4:
            v = v.rearrange("p (a b c) -> p a b c", a=shape[1], b=shape[2])
        return v


class Ctx:
    pass


def _dma(C, eng, out, in_, reads, writes, slot):
    C.p.op(eng, lambda e, s: e.dma_start(out=out, in_=in_).then_inc(s, 16), reads=reads, writes=writes, dma=1, slot=slot)


def wload(C, dst, src, wkey, eng="pool"):
    shape = tuple(dst.shape)
    n = int(np.prod(shape[1:]))
    assert n <= 1536, shape
    i = C.stg_n % len(C.stg)
    C.stg_n += 1
    st = C.stg[i][0:shape[0], 0:n]
    if len(shape) == 3:
        st = st.rearrange("p (a b) -> p a b", a=shape[1])
    _dma(C, "sp", st, src, [], ["stg%d" % i], "stg%d" % i)
    C.p.op(eng, lambda e: e.tensor_copy(out=dst, in_=st), reads=["stg%d" % i], writes=[wkey])


def xk(tb, k):
    return ["xnT%d_%d" % (4 * tb + i, k) for i in range(4)]


def ph_load_x(C, x_d):
    for j in range(NT):
        _dma(C, "sp", C.xs[:, j, :], x_d[:, j, :], [], ["xs%d" % j], "xs%d" % j)


def ph_store_x(C, x_d, keyname):
    for j in range(NT):
        _dma(C, "sp", x_d[:, j, :], C.xs[:, j, :], ["xs%d" % j], [keyname + "%d" % j], "xo%d" % (j % 4))
    C.outkeys += [keyname + "%d" % j for j in range(NT)]


def ph_consts(C):
    p = C.p
    A = C.A
    C.ident = A.alloc([128, 128], BF16)
    C.ones = A.alloc([128, 128], BF16)
    _dma(C, "pool", C.ident, C.d["ident"], [], ["ident"], "c_ident")
    _dma(C, "pool", C.ones, C.d["ones"], [], ["ones"], "c_ones")


def ph_norm(C, g_d, tag):
    p = C.p
    xs, xnT = C.xs, C.xnT
    gT = C.gT
    _dma(C, "sp", gT, g_d, [], ["gT"], "gT")
    ss, rstd, junk = C.ss, C.rstd, C.junk
    for j in range(NT):
        p.op("act", lambda e, j=j: e.activation(out=junk, in_=xs[:, j, :], func=AF.Square, accum_out=ss[:, j:j + 1]),
             reads=["xs%d" % j], writes=["junk", "ss"])
    p.op("act", lambda e: e.activation(out=rstd, in_=ss, func=AF.Ln, bias=C.epsb, scale=1.0 / D), reads=["ss", "epsb"], writes=["rstd"])
    p.op("act", lambda e: e.activation(out=rstd, in_=rstd, func=AF.Exp, scale=-0.5), reads=["rstd"], writes=["rstd"])
    for j in range(NT):
        xb = C.xb[j % 2]
        bt = C.BT[j % 2]
        p.op("dve", lambda e, j=j, xb=xb: e.tensor_scalar(out=xb, in0=xs[:, j, :], scalar1=rstd[:, j:j + 1], scalar2=None, op0=ALU.mult),
             reads=["xs%d" % j, "rstd"], writes=["xb%d" % (j % 2)])
        for k in range(8):
            p.op("pe", lambda e, k=k, xb=xb, bt=bt: e.transpose(out=bt[:, k * 128:(k + 1) * 128], in_=xb[:, k * 128:(k + 1) * 128], identity=C.ident),
                 reads=["xb%d" % (j % 2), "ident"], writes=["BT%d" % (j % 2)])
        for k in range(8):
            if j % 2 == 0:
                p.op("act", lambda e, k=k, j=j, bt=bt: e.activation(out=xnT[:, k, j * 128:(j + 1) * 128], in_=bt[:, k * 128:(k + 1) * 128],
                                                                func=AF.Copy, scale=gT[:, k:k + 1]),
                     reads=["BT%d" % (j % 2), "gT"], writes=["xnT%d_%d" % (j, k)])
            else:
                p.op("dve", lambda e, k=k, j=j, bt=bt: e.tensor_scalar(out=xnT[:, k, j * 128:(j + 1) * 128], in0=bt[:, k * 128:(k + 1) * 128],
                                                                   scalar1=gT[:, k:k + 1], scalar2=None, op0=ALU.mult),
                     reads=["BT%d" % (j % 2), "gT"], writes=["xnT%d_%d" % (j, k)])


def ph_ffn(C, g_d, wg_d, wu_d, wd_d, tag):
    p = C.p
    ph_norm(C, g_d, tag)
    xs, xnT = C.xs, C.xnT
    NG = len(FF_GROUPS)

    def loads_gu(gi):
        c0, c1 = FF_GROUPS[gi]
        gs = c1 - c0
        sl = gi % 2
        wg, wu = C.wg[sl], C.wu[sl]
        for kh in range(2):
            wload(C, wg[:, kh * 4:kh * 4 + 4, 0:gs * 128], wg_d[:, kh * 4:kh * 4 + 4, c0 * 128:c1 * 128], "wg%d" % sl)
        for kh in range(2):
            wload(C, wu[:, kh * 4:kh * 4 + 4, 0:gs * 128], wu_d[:, kh * 4:kh * 4 + 4, c0 * 128:c1 * 128], "wu%d" % sl)

    def loads_d(gi):
        c0, c1 = FF_GROUPS[gi]
        sl = gi % 2
        wd = C.wd[sl]
        for c in range(c1 - c0):
            wload(C, wd[:, c, :], wd_d[:, c0 + c, :], "wd%d" % sl)

    cnt = {"gu": 0, "dn": 0}

    def gu_units(gi):
        c0, c1 = FF_GROUPS[gi]
        gs = c1 - c0
        sl = gi % 2
        wg, wu, hT = C.wg[sl], C.wu[sl], C.hT[sl]
        units = []
        for c in range(gs):
            for tb in range(4):
                b = cnt["gu"] % 2
                cnt["gu"] += 1
                psg, psu = C.B[b], C.B[2 + b]
                u = []
                for k in range(8):
                    u.append(("pe", lambda e, k=k, c=c, tb=tb, psg=psg, wg=wg: e.matmul(psg, wg[:, k, c * 128:(c + 1) * 128], xnT[:, k, tb * 512:(tb + 1) * 512], start=(k == 0), stop=(k == 7)),
                              ["wg%d" % sl] + xk(tb, k), ["B%d" % b]))
                for k in range(8):
                    u.append(("pe", lambda e, k=k, c=c, tb=tb, psu=psu, wu=wu: e.matmul(psu, wu[:, k, c * 128:(c + 1) * 128], xnT[:, k, tb * 512:(tb + 1) * 512], start=(k == 0), stop=(k == 7)),
                              ["wu%d" % sl] + xk(tb, k), ["B%d" % (2 + b)]))
                sg = C.sg[b]
                u.append(("act", lambda e, sg=sg, psg=psg: e.activation(out=sg, in_=psg, func=AF.Silu), ["B%d" % b], ["sg%d" % b]))
                u.append(("dve", lambda e, sg=sg, psu=psu, hT=hT, c=c, tb=tb: e.tensor_tensor(out=hT[:, c, tb * 512:(tb + 1) * 512], in0=sg, in1=psu, op=ALU.mult),
                          ["sg%d" % b, "B%d" % (2 + b)], ["hT%d_%d_%d" % (sl, c, tb)]))
                units.append(u)
        return units

    def dn_units(gi):
        c0, c1 = FF_GROUPS[gi]
        gs = c1 - c0
        sl = gi % 2
        wd, hT = C.wd[sl], C.hT[sl]
        units = []
        for j in range(NT):
            for hf in range(2):
                b = 4 + (cnt["dn"] % 2)
                cnt["dn"] += 1
                psy = C.B[b]
                u = []
                for c in range(gs):
                    u.append(("pe", lambda e, gs=gs, c=c, j=j, hf=hf, psy=psy, hT=hT, wd=wd: e.matmul(psy, hT[:, c, j * 128:(j + 1) * 128], wd[:, c, hf * 512:(hf + 1) * 512], start=(c == 0), stop=(c == gs - 1)),
                              ["hT%d_%d_%d" % (sl, c, j // 4), "wd%d" % sl], ["B%d" % b]))
                u.append(("dve", lambda e, j=j, hf=hf, psy=psy: e.scalar_tensor_tensor(out=xs[:, j, hf * 512:(hf + 1) * 512], in0=psy, scalar=0.5, in1=xs[:, j, hf * 512:(hf + 1) * 512], op0=ALU.mult, op1=ALU.add),
                          ["B%d" % b, "xs%d" % j], ["xs%d" % j]))
                units.append(u)
        return units

    def emit(u):
        for (eng, fn, rd, wr) in u:
            p.op(eng, fn, reads=rd, writes=wr)

    loads_gu(0)
    loads_gu(1)
    loads_d(0)
    loads_d(1)
    for u in gu_units(0):
        emit(u)
    for gi in range(NG):
        if gi + 2 < NG:
            loads_gu(gi + 2)
        dn = dn_units(gi)
        gu = gu_units(gi + 1) if gi + 1 < NG else []
        nd, ng = len(dn), len(gu)
        di = 0
        for ui in range(ng):
            emit(gu[ui])
            tgt = (ui + 1) * nd // ng
            while di < tgt:
                emit(dn[di])
                di += 1
        while di < nd:
            emit(dn[di])
            di += 1
        if gi + 2 < NG:
            loads_d(gi + 2)


def _rmsT(C, src_keys, sq, nchunk, rows, width, inv_n, out_rstd, bank, bkey):
    p = C.p
    for i in range(nchunk):
        p.op("pe", lambda e, i=i: e.matmul(bank[0:rows, 0:width], C.ones[0:rows, 0:rows], sq[0:rows, i, 0:width], start=(i == 0), stop=(i == nchunk - 1)),
             reads=["ones"] + src_keys, writes=[bkey])
    p.op("act", lambda e: e.activation(out=out_rstd[0:rows, 0:width], in_=bank[0:rows, 0:width], func=AF.Ln, bias=C.epsb[0:rows, :], scale=inv_n),
         reads=[bkey, "epsb"], writes=["rstdT"])
    p.op("act", lambda e: e.activation(out=out_rstd[0:rows, 0:width], in_=out_rstd[0:rows, 0:width], func=AF.Exp, scale=-0.5),
         reads=["rstdT"], writes=["rstdT"])


def ph_even_in(C, li):
    p, A, d = C.p, C.A, C.d
    C.xnT = A.alloc([128, 8, T], BF16)
    ph_norm(C, d["mixg"], "mix")
    xnT = C.xnT
    win = A.alloc([128, 8, MIXIN], BF16)
    for k in range(8):
        wload(C, win[:, k, :], d["win"][:, k, :], "win")
    wrope = A.alloc([128, 8, 96], BF16)
    p.op("pool", lambda e: e.memset(wrope, 0.0), writes=["wrope"])
    _dma(C, "pool", wrope[:, :, 64:96], d["win"][:, :, 1152:1184], [], ["wrope"], "wrope")
    wq = A.alloc([128, 3, 768], BF16)
    for i in range(3):
        wload(C, wq[:, i, :], d["wq"][:, i, :], "wq")
    wkp = A.alloc([128, 2, 8, 96], BF16)
    p.op("pool", lambda e: e.memset(wkp, 0.0), writes=["wkp"])
    wv = A.alloc([128, 2, 8, 64], BF16)
    wkv4 = d["wkv"].rearrange("p i (h c) -> p i h c", c=128)
    for i in range(2):
        _dma(C, "pool", wkp[:, i, :, 0:64], wkv4[:, i, :, 0:64], [], ["wkp"], "wkp")
        _dma(C, "pool", wv[:, i, :, :], wkv4[:, i, :, 64:128], [], ["wv"], "wv")
    rm = A.alloc([96, 96], BF16)
    _dma(C, "pool", rm, d["rm"], [], ["rm"], "rm")
    gql = A.alloc([128, 3], F32)
    gkl = A.alloc([128, 2], F32)
    gqh = A.alloc([96, 1], F32)
    gkh = A.alloc([96, 1], F32)
    _dma(C, "sp", gql, d["gql"], [], ["gsm"], "gsm")
    _dma(C, "sp", gkl, d["gkl"], [], ["gsm"], "gsm")
    _dma(C, "sp", gqh, d["gqh"], [], ["gsm"], "gsm")
    _dma(C, "sp", gkh, d["gkh"], [], ["gsm"], "gsm")
    cosT = A.alloc([96, T], F32)
    sinT = A.alloc([96, T], F32)
    _dma(C, "sp", cosT, d["cosT"], [], ["cosT"], "cosT")
    _dma(C, "sp", sinT, d["sinT"], [], ["sinT"], "sinT")
    ust = [A.alloc([128, 512], F32) for _ in range(2)]
    lat = A.alloc([128, 3, 512], F32)
    sq = A.alloc([128, 3, 512], BF16)
    rstdT = A.alloc([128, 512], F32)
    latn = A.alloc([128, 3, 512], BF16)
    kvn = A.alloc([128, 2, 512], BF16)
    hq = [A.alloc([96, 512], BF16) for _ in range(2)]
    t1 = A.alloc([96, 512], F32)
    t2 = A.alloc([96, 512], F32)
    vst = [A.alloc([128, 8, 65], BF16) for _ in range(2)]
    for s in range(2):
        p.op("pool", lambda e, s=s: e.memset(vst[s], 1.0), writes=["vst%d" % s])
    nu = 0
    nh = 0
    for tb in range(4):
        tsl = slice(tb * 512, (tb + 1) * 512)
        for m in range(4):
            bk = C.B[nu % 2]
            bkey = "B%d" % (nu % 2)
            st = ust[nu % 2]
            skey = "ust%d" % (nu % 2)
            nu += 1
            for k in range(8):
                p.op("pe", lambda e, tsl=tsl, k=k, m=m, bk=bk: e.matmul(bk, win[:, k, m * 128:(m + 1) * 128], xnT[:, k, tsl], start=(k == 0), stop=(k == 7)),
                     reads=["win"] + xk(tb, k), writes=[bkey])
            p.op("act", lambda e, bk=bk, st=st: e.activation(out=st, in_=bk, func=AF.Copy), reads=[bkey], writes=[skey])
            _dma(C, "sp", d["uT_o"][m][:, tsl], st, [skey], ["uTo_%d_%d" % (m, tb)], skey)
            C.outkeys.append("uTo_%d_%d" % (m, tb))
        for (nm, m0, nch, inv_n, gl, outn) in (("q", 4, 3, 1.0 / QL, gql, latn), ("kv", 7, 2, 1.0 / KVL, gkl, kvn)):
            for i in range(nch):
                bk = C.B[2 + i]
                bkey = "B%d" % (2 + i)
                m = m0 + i
                for k in range(8):
                    p.op("pe", lambda e, tsl=tsl, k=k, m=m, bk=bk: e.matmul(bk, win[:, k, m * 128:(m + 1) * 128], xnT[:, k, tsl], start=(k == 0), stop=(k == 7)),
                         reads=["win"] + xk(tb, k), writes=[bkey])
                p.op("act", lambda e, i=i, bk=bk: e.activation(out=lat[:, i, :], in_=bk, func=AF.Copy), reads=[bkey], writes=["lat"])
                p.op("act", lambda e, i=i, bk=bk: e.activation(out=sq[:, i, :], in_=bk, func=AF.Square), reads=[bkey], writes=["sq"])
            _rmsT(C, ["sq"], sq, nch, 128, 512, inv_n, rstdT, C.B[5], "B5")
            for i in range(nch):
                p.op("dve", lambda e, i=i, gl=gl, outn=outn: e.scalar_tensor_tensor(out=outn[:, i, :], in0=lat[:, i, :], scalar=gl[:, i:i + 1], in1=rstdT, op0=ALU.mult, op1=ALU.mult),
                     reads=["lat", "rstdT", "gsm"], writes=[nm + "n"])
        for which in ("q", "k"):
            for h in range(H):
                b = nh % 2
                nh += 1
                bk = C.B[b]
                bkey = "B%d" % b
                hqt = hq[b]
                hkey = "hq%d" % b
                if which == "q":
                    for i in range(3):
                        p.op("pe", lambda e, i=i, h=h, bk=bk: e.matmul(bk[0:96, :], wq[:, i, h * 96:(h + 1) * 96], latn[:, i, :], start=(i == 0), stop=(i == 2)),
                             reads=["wq", "qn"], writes=[bkey])
                    gh = gqh
                else:
                    for k in range(8):
                        p.op("pe", lambda e, tsl=tsl, k=k, bk=bk: e.matmul(bk[0:96, :], wrope[:, k, :], xnT[:, k, tsl], start=(k == 0), stop=False),
                             reads=["wrope"] + xk(tb, k), writes=[bkey])
                    for i in range(2):
                        p.op("pe", lambda e, i=i, h=h, bk=bk: e.matmul(bk[0:96, :], wkp[:, i, h, :], kvn[:, i, :], start=False, stop=(i == 1)),
                             reads=["wkp", "kvn"], writes=[bkey])
                    gh = gkh
                p.op("act", lambda e, bk=bk: e.activation(out=sq[0:96, 0, :], in_=bk[0:96, :], func=AF.Square), reads=[bkey], writes=["sq"])
                _rmsT(C, ["sq"], sq, 1, 96, 512, 1.0 / DQK, rstdT, C.B[5], "B5")
                p.op("dve", lambda e, bk=bk, hqt=hqt, gh=gh: e.scalar_tensor_tensor(out=hqt, in0=bk[0:96, :], scalar=gh[:, 0:1], in1=rstdT[0:96, :], op0=ALU.mult, op1=ALU.mult),
                     reads=[bkey, "rstdT", "gsm"], writes=[hkey])
                rb = C.B[4]
                p.op("pe", lambda e, hqt=hqt, rb=rb: e.matmul(rb[0:96, :], rm, hqt, start=True, stop=True), reads=["rm", hkey], writes=["B4"])
                p.op("pool", lambda e, tsl=tsl, hqt=hqt: e.tensor_tensor(out=t1[64:96, :], in0=hqt[64:96, :], in1=cosT[64:96, tsl], op=ALU.mult),
                     reads=[hkey, "cosT"], writes=["t1"])
                p.op("dve", lambda e, tsl=tsl, rb=rb: e.tensor_tensor(out=t2[64:96, :], in0=rb[64:96, :], in1=sinT[64:96, tsl], op=ALU.mult),
                     reads=["B4", "sinT"], writes=["t2"])
                p.op("pool", lambda e, hqt=hqt: e.tensor_tensor(out=hqt[64:96, :], in0=t1[64:96, :], in1=t2[64:96, :], op=ALU.add),
                     reads=["t1", "t2"], writes=[hkey])
                if which == "q":
                    ok = "qTo_%d_%d" % (h, tb)
                    _dma(C, "sp", d["qT_o"][h][:, tsl], hqt, [hkey], [ok], hkey)
                    C.outkeys.append(ok)
                else:
                    for jj in range(4):
                        ok = "kTo_%d_%d_%d" % (h, tb, jj)
                        _dma(C, "sp", d["kT_o"][tb * 4 + jj][:, h, :], hqt[:, jj * 128:(jj + 1) * 128], [hkey], [ok], hkey)
                        C.outkeys.append(ok)
        for jj in range(4):
            j = tb * 4 + jj
            b = j % 2
            bk = C.B[2 + b]
            bkey = "B%d" % (2 + b)
            for i in range(2):
                p.op("pe", lambda e, i=i, jj=jj, bk=bk: e.matmul(bk, kvn[:, i, jj * 128:(jj + 1) * 128], wv[:, i, :, :], start=(i == 0), stop=(i == 1)),
                     reads=["kvn", "wv"], writes=[bkey])
            p.op("dve", lambda e, b=b, bk=bk: e.tensor_copy(out=vst[b][:, :, 0:64], in_=bk.rearrange("p (h c) -> p h c", c=64)),
                 reads=[bkey], writes=["vst%d" % b])
            ok = "vo_%d" % j
            _dma(C, "sp", d["v_o"][:, j, :, :], vst[b], ["vst%d" % b], [ok], "vst%d" % b)
            C.outkeys.append(ok)


def ph_even_out(C, li):
    p, A, d = C.p, C.A, C.d
    scale = DQK ** -0.5
    catT = A.alloc([128, 8, T], BF16)
    mark = A.top
    uextb = [A.alloc([128, NT, 144], F32) for _ in range(2)]
    wa = A.alloc([128, NT, 144], F32)
    wb = A.alloc([128, NT, 144], F32)
    pooled = A.alloc([128, 4, T], BF16)
    invc = A.alloc([128, 4, 16], F32)
    _dma(C, "sp", invc, d["invc"], [], ["invc"], "invc")
    wpool = A.alloc([128, 4, 128], BF16)
    _dma(C, "pool", wpool, d["wpool"], [], ["wpool"], "wpool")
    pscale = A.alloc([128, 4], F32)
    _dma(C, "sp", pscale, d["pscale"], [], ["pscale"], "pscale")
    tmp16 = A.alloc([128, 16], F32)
    for g in range(4):
        w = 2 ** (g + 1)
        src = uextb[g % 2]
        _dma(C, "sp", src, d["uext"][:, g, :, :], [], ["uext%d" % (g % 2)], "uext%d" % (g % 2))
        bufs = [wa, wb]
        sh = 1
        cur = src
        curkey = "uext%d" % (g % 2)
        nb = 0
        while sh < w:
            dst = bufs[nb % 2]
            dkey = "w%d" % (nb % 2)
            nb += 1
            p.op("dve", lambda e, cur=cur, dst=dst, sh=sh: e.tensor_tensor(out=dst[:, :, 2 * sh - 1:144], in0=cur[:, :, 2 * sh - 1:144], in1=cur[:, :, sh - 1:144 - sh], op=ALU.add),
                 reads=[curkey], writes=[dkey])
            if sh > 1:
                pass
            cur = dst
            curkey = dkey
            sh *= 2
        pv = pooled[:, g, :].rearrange("p (j t) -> p j t", t=128)
        p.op("dve", lambda e, cur=cur, src=src, pv=pv, w=w: e.scalar_tensor_tensor(out=pv, in0=cur[:, :, 16:144], scalar=1.0 / w, in1=src[:, :, 16:144], op0=ALU.mult, op1=ALU.subtract),
             reads=[curkey, "uext%d" % (g % 2)], writes=["pooled%d" % g])
        p.op("dve", lambda e, cur=cur, g=g: e.tensor_tensor(out=tmp16, in0=cur[:, 0, 16:32], in1=invc[:, g, :], op=ALU.mult),
             reads=[curkey, "invc"], writes=["tmp16"])
        p.op("dve", lambda e, src=src, g=g: e.tensor_tensor(out=pooled[:, g, 0:16], in0=tmp16, in1=src[:, 0, 16:32], op=ALU.subtract),
             reads=["tmp16", "uext%d" % (g % 2), "pooled%d" % g], writes=["pooled%d" % g])
    n = 0
    for g in range(4):
        for tb in range(4):
            b = n % 2
            n += 1
            bk = C.B[b]
            p.op("pe", lambda e, g=g, tb=tb, bk=bk: e.matmul(bk, wpool[:, g, :], pooled[:, g, tb * 512:(tb + 1) * 512], start=True, stop=True),
                 reads=["wpool", "pooled%d" % g], writes=["B%d" % b])
            p.op("act", lambda e, g=g, tb=tb, bk=bk: e.activation(out=catT[:, g, tb * 512:(tb + 1) * 512], in_=bk, func=AF.Copy, scale=pscale[:, g:g + 1]),
                 reads=["B%d" % b, "pscale"], writes=["catT"])
    p.barrier()
    A.top = mark
    qT = A.alloc([96, H, T], BF16)
    for h in range(H):
        _dma(C, "sp", qT[:, h, :], d["qT"][h], [], ["qT"], "qT%d" % (h % 4))
    mask = A.alloc([128, 8, 128], BF16)
    _dma(C, "pool", mask, d["mask"], [], ["mask"], "mask")
    krow = [A.alloc([96, 8, 4, 128], BF16) for _ in range(2)]
    vrow = [A.alloc([128, 8, 4, 65], BF16) for _ in range(2)]
    pT = [A.alloc([128, 512], BF16) for _ in range(3)]
    atok = A.alloc([128, NT, 512], BF16)
    rden = A.alloc([128, 4], F32)
    nrow = 0
    ns = 0
    LAG = 1
    for sb in range(4):
        for hg in range(2):
            steps = []
            for jr in range(4 * sb + 4):
                sl = nrow % 2
                nrow += 1
                kr, vr = krow[sl], vrow[sl]
                pre = [("sp", lambda e, s, kr=kr, jr=jr, hg=hg: [e.dma_start(out=kr[:, r, :, :], in_=d["kall"][jr][:, r, hg * 4:(hg + 1) * 4, :]).then_inc(s, 16) for r in range(8)],
                        [], ["krow%d" % sl], 8, "krow%d" % sl),
                       ("sp", lambda e, s, vr=vr, jr=jr, hg=hg: [e.dma_start(out=vr[:, r, :, :], in_=d["vall"][jr][:, r, hg * 4:(hg + 1) * 4, :]).then_inc(s, 16) for r in range(8)],
                        [], ["vrow%d" % sl], 8, "vrow%d" % sl)]
                i0 = max(0, jr - 4 * sb)
                diag = jr >= 4 * sb
                ncols = (4 - i0) * 128
                q0 = (4 * sb + i0) * 128
                for r in range(8):
                    for hh in range(4):
                        h = hg * 4 + hh
                        b = 4 + (ns % 2)
                        pt = pT[ns % 3]
                        ptk = "pT%d" % (ns % 3)
                        ns += 1
                        bk = C.B[b]
                        bkey = "B%d" % b
                        S_ops = list(pre)
                        pre = []
                        kk = "krow%d" % sl
                        if diag:
                            S_ops.append(("pe", lambda e, bk=bk, kr=kr, r=r, hh=hh, h=h, q0=q0: e.matmul(bk[:, 0:128], kr[:, r, hh, :], qT[:, h, q0:q0 + 128], start=True, stop=False),
                                          [kk, "qT"], [bkey], 0, None))
                            S_ops.append(("pe", lambda e, bk=bk, r=r: e.matmul(bk[:, 0:128], C.ident, mask[:, r, :], start=False, stop=True),
                                          ["ident", "mask"], [bkey], 0, None))
                            if ncols > 128:
                                S_ops.append(("pe", lambda e, bk=bk, kr=kr, r=r, hh=hh, h=h, q0=q0, ncols=ncols: e.matmul(bk[:, 128:ncols], kr[:, r, hh, :], qT[:, h, q0 + 128:q0 + ncols], start=True, stop=True),
                                              [kk, "qT"], [bkey], 0, None))
                        else:
                            S_ops.append(("pe", lambda e, bk=bk, kr=kr, r=r, hh=hh, h=h, q0=q0, ncols=ncols: e.matmul(bk[:, 0:ncols], kr[:, r, hh, :], qT[:, h, q0:q0 + ncols], start=True, stop=True),
                                          [kk, "qT"], [bkey], 0, None))
                        S_ops.append(("act", lambda e, bk=bk, pt=pt, ncols=ncols: e.activation(out=pt[:, 0:ncols], in_=bk[:, 0:ncols], func=AF.Exp, scale=scale),
                                      [bkey], [ptk], 0, None))
                        PV_ops = []
                        for i in range(i0, 4):
                            first = (jr == 0 and r == 0 and hh == 0)
                            last = (jr == 4 * sb + i and r == 7)
                            PV_ops.append(("pe", lambda e, i0=i0, i=i, pt=pt, vr=vr, r=r, hh=hh, first=first, last=last: e.matmul(C.B[i][:, hh * 65:(hh + 1) * 65], pt[:, (i - i0) * 128:(i - i0 + 1) * 128], vr[:, r, hh, :], start=first, stop=last, skip_group_check=True),
                                           [ptk, "vrow%d" % sl], ["B%d" % i], 0, None))
                        steps.append((S_ops, PV_ops))
            for n in range(len(steps) + LAG):
                if n < len(steps):
                    for (eng, fn, rd, wr, dma, slot) in steps[n][0]:
                        p.op(eng, fn, reads=rd, writes=wr, dma=dma, slot=slot)
                if n - LAG >= 0:
                    for (eng, fn, rd, wr, dma, slot) in steps[n - LAG][1]:
                        p.op(eng, fn, reads=rd, writes=wr, dma=dma, slot=slot)
            for i in range(4):
                j = 4 * sb + i
                acc = C.B[i][:, 0:260].rearrange("p (h c) -> p h c", c=65)
                p.op("dve", lambda e, acc=acc: e.reciprocal(out=rden, in_=acc[:, :, 64]), reads=["B%d" % i], writes=["rden"])
                for hh in range(4):
                    h = hg * 4 + hh
                    p.op("dve", lambda e, acc=acc, hh=hh, h=h, j=j: e.tensor_scalar(out=atok[:, j, h * 64:(h + 1) * 64], in0=acc[:, hh, 0:64], scalar1=rden[:, hh:hh + 1], scalar2=None, op0=ALU.mult),
                         reads=["B%d" % i, "rden"], writes=["atok%d" % j])
    for j in range(NT):
        bt = C.BT[j % 2]
        for m in range(4):
            p.op("pe", lambda e, j=j, m=m, bt=bt: e.transpose(out=bt[:, m * 128:(m + 1) * 128], in_=atok[:, j, m * 128:(m + 1) * 128], identity=C.ident),
                 reads=["atok%d" % j, "ident"], writes=["BT%d" % (j % 2)])
        p.op("dve", lambda e, j=j, bt=bt: e.tensor_copy(out=catT[:, 4:8, j * 128:(j + 1) * 128], in_=bt[:, 0:512].rearrange("p (m t) -> p m t", t=128)),
             reads=["BT%d" % (j % 2)], writes=["catT"])
    p.barrier()
    A.top = mark
    _out_proj(C, catT, "catT", d["wout"])


def _out_proj(C, srcT, skey, w_d):
    p, A = C.p, C.A
    xs = C.xs
    wo = A.alloc([128, 8, 1024], BF16)
    for k in range(8):
        wload(C, wo[:, k, :], w_d[:, k, :], "wo")
    n = 0
    for j in range(NT):
        for hf in range(2):
            b = n % 2
            n += 1
            bk = C.B[b]
            for m in range(8):
                p.op("pe", lambda e, j=j, hf=hf, m=m, bk=bk: e.matmul(bk, srcT[:, m, j * 128:(j + 1) * 128], wo[:, m, hf * 512:(hf + 1) * 512], start=(m == 0), stop=(m == 7)),
                     reads=[skey, "wo"], writes=["B%d" % b])
            p.op("dve", lambda e, j=j, hf=hf, bk=bk: e.tensor_tensor(out=xs[:, j, hf * 512:(hf + 1) * 512], in0=bk, in1=xs[:, j, hf * 512:(hf + 1) * 512], op=ALU.add),
                 reads=["B%d" % b, "xs%d" % j], writes=["xs%d" % j])


def ph_odd_in(C, li):
    p, A, d = C.p, C.A, C.d
    C.xnT = A.alloc([128, 8, T], BF16)
    ph_norm(C, d["mixg"], "mix")
    xnT = C.xnT
    wst = [A.alloc([128, 8, 384], BF16) for _ in range(2)]
    gst = [A.alloc([128, 512], F32) for _ in range(2)]
    cst = [A.alloc([128, 512], F32) for _ in range(2)]
    ust = [A.alloc([128, 512], F32) for _ in range(2)]
    n = 0
    for m in range(8):
        sl = m % 2
        w = wst[sl]
        for q in range(3):
            wload(C, w[:, :, q * 128:(q + 1) * 128], d["wcin"][:, :, q * 1024 + m * 128:q * 1024 + (m + 1) * 128], "wst%d" % sl)
        for tb in range(4):
            tsl = slice(tb * 512, (tb + 1) * 512)
            b = n % 2
            n += 1
            banks = [C.B[b], C.B[2 + b], C.B[4 + b]]
            bkeys = ["B%d" % b, "B%d" % (2 + b), "B%d" % (4 + b)]
            for q in range(3):
                for k in range(8):
                    p.op("pe", lambda e, tsl=tsl, q=q, k=k, w=w, bk=banks[q]: e.matmul(bk, w[:, k, q * 128:(q + 1) * 128], xnT[:, k, tsl], start=(k == 0), stop=(k == 7)),
                         reads=["wst%d" % sl] + xk(tb, k), writes=[bkeys[q]])
            p.op("act", lambda e, b=b, bk=banks[0]: e.activation(out=gst[b], in_=bk, func=AF.Copy), reads=[bkeys[0]], writes=["gst%d" % b])
            p.op("act", lambda e, b=b, bk=banks[1]: e.activation(out=cst[b], in_=bk, func=AF.Copy), reads=[bkeys[1]], writes=["cst%d" % b])
            p.op("dve", lambda e, b=b, bk=banks[2]: e.tensor_tensor(out=ust[b], in0=cst[b], in1=bk, op=ALU.mult), reads=["cst%d" % b, bkeys[2]], writes=["ust%d" % b])
            ok = "gbo_%d_%d" % (m, tb)
            _dma(C, "sp", d["gb_o"][m][:, tsl], gst[b], ["gst%d" % b], [ok], "gst%d" % b)
            C.outkeys.append(ok)
            ok = "uco_%d_%d" % (m, tb)
            _dma(C, "sp", d["uc_o"][m][:, tsl], ust[b], ["ust%d" % b], [ok], "ust%d" % b)
            C.outkeys.append(ok)


def ph_odd_out(C, li):
    p, A, d = C.p, C.A, C.d
    vT = A.alloc([128, 8, T], BF16)
    cw = A.alloc([128, 8, 3], F32)
    _dma(C, "sp", cw, d["cw"], [], ["cw"], "cw")
    ue = [A.alloc([128, NT, 130], F32) for _ in range(2)]
    gb = [A.alloc([128, T], F32) for _ in range(2)]
    y = [A.alloc([128, NT, 128], F32) for _ in range(2)]
    for m in range(8):
        sl = m % 2
        _dma(C, "sp", ue[sl], d["ucext"][:, m, :, :], [], ["ue%d" % sl], "ue%d" % sl)
        _dma(C, "sp", gb[sl], d["gbT"][m], [], ["gb%d" % sl], "gb%d" % sl)
        eng = "dve"
        u, yy, g = ue[sl], y[sl], gb[sl]
        p.op(eng, lambda e, u=u, yy=yy, m=m: e.tensor_scalar(out=yy, in0=u[:, :, 0:128], scalar1=cw[:, m, 0:1], scalar2=None, op0=ALU.mult),
             reads=["ue%d" % sl, "cw"], writes=["y%d" % sl])
        p.op(eng, lambda e, u=u, yy=yy, m=m: e.scalar_tensor_tensor(out=yy, in0=u[:, :, 1:129], scalar=cw[:, m, 1:2], in1=yy, op0=ALU.mult, op1=ALU.add),
             reads=["ue%d" % sl, "cw", "y%d" % sl], writes=["y%d" % sl])
        p.op(eng, lambda e, u=u, yy=yy, m=m: e.scalar_tensor_tensor(out=yy, in0=u[:, :, 2:130], scalar=cw[:, m, 2:3], in1=yy, op0=ALU.mult, op1=ALU.add),
             reads=["ue%d" % sl, "cw", "y%d" % sl], writes=["y%d" % sl])
        p.op(eng, lambda e, yy=yy, g=g, m=m: e.tensor_tensor(out=vT[:, m, :], in0=yy.rearrange("p j t -> p (j t)"), in1=g, op=ALU.mult),
             reads=["y%d" % sl, "gb%d" % sl], writes=["vT"])
    _out_proj(C, vT, "vT", d["wcout"])


def build_launch(spec):
    nc = bass.Bass("TRN2", target_bir_lowering=False)
    C = Ctx()
    C.nc = nc
    C.p = Prog(nc)
    C.d = {}
    C.outkeys = []
    for name, (shape, dt) in spec["inputs"].items():
        C.d[name] = nc.dram_tensor(name, list(shape), dt, kind="ExternalInput").ap()
    for name, (shape, dt) in spec["outputs"].items():
        C.d[name] = nc.dram_tensor(name, list(shape), dt, kind="ExternalOutput").ap()
    import contextlib
    with contextlib.ExitStack() as st:
        nbytes = 204 * 1024
        ar = st.enter_context(nc.sbuf_tensor("arena", [128, nbytes], U8))
        C.A = Arena(ar, nbytes)
        C.B = [st.enter_context(nc.psum_tensor("bank%d" % i, [128, 512], F32)) for i in range(6)]
        C.B = [b[:] for b in C.B]
        C.BT = [st.enter_context(nc.psum_tensor("bankT%d" % i, [128, 1024], BF16)) for i in range(2)]
        C.BT = [b[:] for b in C.BT]
        A = C.A
        C.xs = A.alloc([128, NT, D], F32)
        C.gT = A.alloc([128, 8], F32)
        C.ss = A.alloc([128, NT], F32)
        C.rstd = A.alloc([128, NT], F32)
        C.junk = A.alloc([128, D], BF16)
        C.xb = [A.alloc([128, D], BF16) for _ in range(2)]
        C.epsb = A.alloc([128, 1], F32)
        C.stg = [A.alloc([128, 1536], F32) for _ in range(3)]
        C.stg_n = 0
        C.p.op("pool", lambda e: e.memset(C.epsb, EPS), writes=["epsb"])
        ph_consts(C)
        base = A.top
        for ph in spec["phases"]:
            C.p.barrier()
            A.top = base
            ph(C)
        C.p.op("sp", None, reads=list(C.outkeys))
        C.p.op("pool", None, reads=list(C.outkeys))
        C.p.emit()
    return nc


def alloc_ffn(C):
    A = C.A
    C.xnT = A.alloc([128, 8, T], BF16)
    C.wg = [A.alloc([128, 8, 384], BF16) for _ in range(2)]
    C.wu = [A.alloc([128, 8, 384], BF16) for _ in range(2)]
    C.wd = [A.alloc([128, 3, D], BF16) for _ in range(2)]
    C.hT = [A.alloc([128, 3, T], BF16) for _ in range(2)]
    C.sg = [A.alloc([128, 512], F32) for _ in range(2)]


def mk_ffn(which):
    def f(C):
        alloc_ffn(C)
        ph_ffn(C, C.d[which + "_g"], C.d[which + "_wg"], C.d[which + "_wu"], C.d[which + "_wd"], which)
    return f


FFN_IN = lambda w: {w + "_g": ((128, 8), F32), w + "_wg": ((128, 8, DFF), F32), w + "_wu": ((128, 8, DFF), F32), w + "_wd": ((128, NFC, D), F32)}
X_IO = ((128, NT, D), F32)
CONST_IN = {"ident": ((128, 128), F32), "ones": ((128, 128), F32)}
EVEN_IN_W = {"mixg": ((128, 8), F32), "win": ((128, 8, MIXIN), F32), "wq": ((128, 3, 768), F32), "wkv": ((128, 2, 1024), F32),
             "rm": ((96, 96), F32), "gql": ((128, 3), F32), "gkl": ((128, 2), F32), "gqh": ((96, 1), F32), "gkh": ((96, 1), F32),
             "cosT": ((96, T), F32), "sinT": ((96, T), F32)}
EVEN_IN_OUT = {"uT_o": ((4, 128, T), F32), "qT_o": ((H, 96, T), BF16), "kT_o": ((NT, 96, H, 128), BF16), "v_o": ((128, NT, H, 65), BF16)}
EVEN_OUT_IN = {"uext": ((128, 4, NT, 144), F32), "invc": ((128, 4, 16), F32), "wpool": ((128, 4, 128), F32), "pscale": ((128, 4), F32),
               "qT": ((H, 96, T), BF16), "mask": ((128, 8, 128), F32), "kall": ((NT, 96, 8, H, 128), BF16), "vall": ((NT, 128, 8, H, 65), BF16),
               "wout": ((128, 8, D), F32)}
ODD_IN_W = {"mixg": ((128, 8), F32), "wcin": ((128, 8, 3 * D), F32)}
ODD_IN_OUT = {"gb_o": ((8, 128, T), F32), "uc_o": ((8, 128, T), F32)}
ODD_OUT_IN = {"cw": ((128, 8, 3), F32), "ucext": ((128, 8, NT, 130), F32), "gbT": ((8, 128, T), F32), "wcout": ((128, 8, D), F32)}


def _load_x(C):
    ph_load_x(C, C.d["x"])


def _store_x(C):
    ph_store_x(C, C.d["x_o"], "xo")


def spec_A():
    return dict(inputs={"x": X_IO, **CONST_IN, **FFN_IN("f1"), **EVEN_IN_W},
                outputs={"x_o": X_IO, **EVEN_IN_OUT},
                phases=[_load_x, mk_ffn("f1"), _store_x, lambda C: ph_even_in(C, 0)])


def spec_B():
    return dict(inputs={"x": X_IO, **CONST_IN, **EVEN_OUT_IN, **FFN_IN("f2"), **FFN_IN("f1"), **ODD_IN_W},
                outputs={"x_o": X_IO, **ODD_IN_OUT},
                phases=[_load_x, lambda C: ph_even_out(C, 0), mk_ffn("f2"), mk_ffn("f1"), _store_x, lambda C: ph_odd_in(C, 1)])


def spec_C():
    return dict(inputs={"x": X_IO, **CONST_IN, **ODD_OUT_IN, **FFN_IN("f2"), **FFN_IN("f1"), **EVEN_IN_W},
                outputs={"x_o": X_IO, **EVEN_IN_OUT},
                phases=[_load_x, lambda C: ph_odd_out(C, 1), mk_ffn("f2"), mk_ffn("f1"), _store_x, lambda C: ph_even_in(C, 2)])


def spec_D():
    return dict(inputs={"x": X_IO, **CONST_IN, **ODD_OUT_IN, **FFN_IN("f2")},
                outputs={"x_o": X_IO},
                phases=[_load_x, lambda C: ph_odd_out(C, 3), mk_ffn("f2"), _store_x])


_NC_CACHE = {}


def _get_nc(name):
    if name not in _NC_CACHE:
        _NC_CACHE[name] = build_launch({"A": spec_A, "B": spec_B, "C": spec_C, "D": spec_D}[name]())
    return _NC_CACHE[name]


def _kmaj(w, p=128):
    K, N = w.shape
    return np.ascontiguousarray(w.reshape(K // p, p, N).transpose(1, 0, 2))


def _vec(g, p=128):
    return np.ascontiguousarray(g.reshape(-1, p).T)


def _ffn_inputs(prefix, inputs, which, layer):
    return {prefix + "_g": _vec(inputs[which + "_norm"][layer]),
            prefix + "_wg": _kmaj(inputs[which + "_w_gate"][layer]),
            prefix + "_wu": _kmaj(inputs[which + "_w_up"][layer]),
            prefix + "_wd": _kmaj(inputs[which + "_w_down"][layer])}


def _consts():
    return {"ident": np.eye(128, dtype=np.float32), "ones": np.ones((128, 128), np.float32)}


def _rope_tables(c):
    pos = (np.arange(NT)[:, None] * NC + c) * 128 + np.arange(128)[None, :]
    pos = pos.reshape(-1).astype(np.float32)
    inv = (np.float32(10000.0) ** (-np.arange(0, 32, 2, dtype=np.float32) / np.float32(32))).astype(np.float32)
    ang = (pos[:, None] * inv[None, :]).astype(np.float32)
    cos = np.cos(ang).astype(np.float32).T
    sin = np.sin(ang).astype(np.float32).T
    cosT = np.zeros((96, T), np.float32)
    sinT = np.zeros((96, T), np.float32)
    cosT[64:80] = cos
    cosT[80:96] = cos
    sinT[64:80] = sin
    sinT[80:96] = sin
    return cosT, sinT


def _rm():
    rm = np.zeros((96, 96), np.float32)
    for i in range(16):
        rm[80 + i, 64 + i] = -1.0
        rm[64 + i, 80 + i] = 1.0
    return rm


def _even_in_inputs(inputs, i, c):
    cosT, sinT = _rope_tables(c)
    return {"mixg": _vec(inputs["mix_norm"][2 * i]), "win": _kmaj(inputs["a_w_in"][i]), "wq": _kmaj(inputs["a_w_q_up"][i]),
            "wkv": _kmaj(inputs["a_w_kv_up"][i]), "rm": _rm(), "gql": _vec(inputs["a_q_a_norm"][i]), "gkl": _vec(inputs["a_kv_a_norm"][i]),
            "gqh": np.ascontiguousarray(inputs["a_q_head_norm"][i].reshape(96, 1)), "gkh": np.ascontiguousarray(inputs["a_k_head_norm"][i].reshape(96, 1)),
            "cosT": cosT, "sinT": sinT}


def _mask(c):
    m = np.zeros((128, 8, 128), np.float32)
    for r in range(8):
        if r > c:
            m[:, r, :] = -30000.0
        elif r == c:
            m[:, r, :] = np.where(np.arange(128)[:, None] > np.arange(128)[None, :], -30000.0, 0.0)
    return m


def _even_out_inputs(inputs, i, c, res):
    uT = np.stack([r["uT_o"] for r in res])
    uT = uT.reshape(NC, 4, 128, NT, 128)
    glob = uT.transpose(1, 2, 3, 0, 4).reshape(4, 128, NT * NC, 128)
    ext = np.zeros((128, 4, NT, 144), np.float32)
    for j in range(NT):
        t = 8 * j + c
        ext[:, :, j, 16:144] = glob[:, :, t, :].transpose(1, 0, 2)
        if t > 0:
            ext[:, :, j, 0:16] = glob[:, :, t - 1, 112:128].transpose(1, 0, 2)
    invc = np.zeros((128, 4, 16), np.float32)
    for g in range(4):
        w = 2 ** (g + 1)
        if c == 0:
            invc[:, g, :] = 1.0 / np.minimum(np.arange(16) + 1, w).astype(np.float32)
        else:
            invc[:, g, :] = 1.0 / w
    kall = np.stack([r["kT_o"] for r in res])
    kall = np.ascontiguousarray(kall.transpose(1, 2, 0, 3, 4))
    vall = np.stack([r["v_o"] for r in res])
    vall = np.ascontiguousarray(vall.transpose(2, 1, 0, 3, 4))
    return {"uext": ext, "invc": invc, "wpool": np.ascontiguousarray(inputs["a_w_pool"][i].transpose(1, 0, 2)),
            "pscale": _vec(inputs["a_pool_scale"][i]), "qT": res[c]["qT_o"], "mask": _mask(c), "kall": kall, "vall": vall,
            "wout": _kmaj(inputs["a_w_out"][i])}, (kall, vall)


def _odd_in_inputs(inputs, i):
    return {"mixg": _vec(inputs["mix_norm"][2 * i + 1]), "wcin": _kmaj(inputs["c_w_in"][i])}


def _odd_out_inputs(inputs, i, c, res):
    uc = np.stack([r["uc_o"] for r in res]).reshape(NC, 8, 128, NT, 128)
    glob = uc.transpose(1, 2, 3, 0, 4).reshape(8, 128, NT * NC, 128)
    ext = np.zeros((128, 8, NT, 130), np.float32)
    for j in range(NT):
        t = 8 * j + c
        ext[:, :, j, 2:130] = glob[:, :, t, :].transpose(1, 0, 2)
        if t > 0:
            ext[:, :, j, 0:2] = glob[:, :, t - 1, 126:128].transpose(1, 0, 2)
    cw = np.ascontiguousarray(inputs["c_conv_w"][i].reshape(3, 8, 128).transpose(2, 1, 0))
    return {"cw": cw, "ucext": ext, "gbT": res[c]["gb_o"], "wcout": _kmaj(inputs["c_w_out"][i])}


def _run(name, in_maps):
    nc = _get_nc(name)
    res = run_bass_kernel_spmd(nc, in_maps, core_ids=list(range(NC)))
    return res.results


def kernel(**inputs):
    inputs = {k: np.asarray(v) for k, v in inputs.items()}
    x = inputs["x"][0].reshape(NT, NC, 128, D)
    xc = [np.ascontiguousarray(x[:, c].transpose(1, 0, 2)) for c in range(NC)]
    f1 = _ffn_inputs("f1", inputs, "ffn1", 0)
    res = _run("A", [{"x": xc[c], **_consts(), **f1, **_even_in_inputs(inputs, 0, c)} for c in range(NC)])
    f2 = _ffn_inputs("f2", inputs, "ffn2", 0)
    f1 = _ffn_inputs("f1", inputs, "ffn1", 1)
    oi = _odd_in_inputs(inputs, 0)
    maps = []
    for c in range(NC):
        eo, _ = _even_out_inputs(inputs, 0, c, res)
        maps.append({"x": res[c]["x_o"], **_consts(), **eo, **f2, **f1, **oi})
    res = _run("B", maps)
    f2 = _ffn_inputs("f2", inputs, "ffn2", 1)
    f1 = _ffn_inputs("f1", inputs, "ffn1", 2)
    maps = [{"x": res[c]["x_o"], **_consts(), **_odd_out_inputs(inputs, 0, c, res), **f2, **f1, **_even_in_inputs(inputs, 1, c)} for c in range(NC)]
    res = _run("C", maps)
    f2 = _ffn_inputs("f2", inputs, "ffn2", 2)
    f1 = _ffn_inputs("f1", inputs, "ffn1", 3)
    oi = _odd_in_inputs(inputs, 1)
    maps = []
    for c in range(NC):
        eo, _ = _even_out_inputs(inputs, 1, c, res)
        maps.append({"x": res[c]["x_o"], **_consts(), **eo, **f2, **f1, **oi})
    res = _run("B", maps)
    f2 = _ffn_inputs("f2", inputs, "ffn2", 3)
    maps = [{"x": res[c]["x_o"], **_consts(), **_odd_out_inputs(inputs, 1, c, res), **f2} for c in range(NC)]
    res = _run("D", maps)
    out = np.zeros((NT, NC, 128, D), np.float32)
    for c in range(NC):
        out[:, c] = res[c]["x_o"].transpose(1, 0, 2)
    return out.reshape(1, S, D)
```

```python
import numpy as np
import ml_dtypes
import concourse.bass as bass
import concourse.mybir as mybir
from concourse.bass_utils import run_bass_kernel_spmd

F32 = mybir.dt.float32
BF16 = mybir.dt.bfloat16
U8 = mybir.dt.uint8
AF = mybir.ActivationFunctionType
ALU = mybir.AluOpType

NC = 8
S = 16384
D = 1024
T = S // NC
NT = T // 128
DFF = 2816
NFC = DFF // 128
EPS = 1e-6
H = 8
DQK = 96
DV = 64
QL = 384
KVL = 256
MIXIN = 1184
ENGS = ("pe", "act", "dve", "pool", "sp")
FF_GROUPS = [(0, 3), (3, 6), (6, 9), (9, 12), (12, 15), (15, 18), (18, 20), (20, 22)]


class _Op:
    __slots__ = ("idx", "eng", "fn", "deps", "dma", "slot", "ms", "tok")

    def __init__(self, idx, eng, fn, deps, dma, slot):
        self.idx, self.eng, self.fn, self.deps, self.dma, self.slot = idx, eng, fn, deps, dma, slot
        self.ms = False
        self.tok = None


class Prog:
    def __init__(self, nc):
        self.nc = nc
        self.ops = []
        self.last_writer = {}
        self.readers = {}
        self.since_barrier = []

    def op(self, eng, fn, reads=(), writes=(), dma=0, slot=None):
        bank_reads = [k for k in reads if k[0] == "B"]
        if bank_reads:
            reads = [k for k in reads if k[0] != "B"]
            writes = list(writes) + bank_reads
        deps = set()
        for k in reads:
            w = self.last_writer.get(k)
            if w is not None:
                deps.add(w)
        for k in writes:
            w = self.last_writer.get(k)
            if w is not None:
                deps.add(w)
            deps.update(self.readers.get(k, ()))
        best = {}
        red = set()
        for dd in deps:
            q = self.ops[dd]
            if q.dma:
                red.add(dd)
            elif dd > best.get(q.eng, -1):
                best[q.eng] = dd
        red.update(best.values())
        deps = red
        idx = len(self.ops)
        o = _Op(idx, eng, fn, deps, dma, slot)
        self.ops.append(o)
        self.since_barrier.append(idx)
        for k in writes:
            self.last_writer[k] = idx
            self.readers[k] = []
        for k in reads:
            self.readers.setdefault(k, []).append(idx)
        return idx

    def barrier(self):
        prev = self.since_barrier
        self.since_barrier = []
        last = {}
        dmas = set()
        for i in prev:
            o = self.ops[i]
            if o.fn is None:
                continue
            if o.dma:
                dmas.add(i)
            else:
                last[o.eng] = i
        for e in ENGS:
            deps = set(dmas)
            for e2, i in last.items():
                if e2 != e:
                    deps.add(i)
            idx = len(self.ops)
            self.ops.append(_Op(idx, e, None, deps, 0, None))
        self.last_writer = {}
        self.readers = {}

    def emit(self):
        nc = self.nc
        ops = self.ops
        for o in ops:
            for d in o.deps:
                q = ops[d]
                if q.dma or q.fn is None:
                    continue
                if q.eng == "pe" and o.eng == "pe" and not o.dma and o.fn is not None:
                    continue
                q.ms = True
        cnt = {e: 0 for e in ENGS}
        slots = {}
        for o in ops:
            if o.fn is None:
                continue
            if o.dma:
                c = slots.get(o.slot, 0) + 16 * o.dma
                slots[o.slot] = c
                o.tok = ("dma:" + o.slot, c)
            elif o.ms:
                cnt[o.eng] += 1
                o.tok = ("eng:" + o.eng, cnt[o.eng])
        names = ["eng:" + e for e in ENGS if cnt[e] > 0] + ["dma:" + s for s in slots]
        sems = {n: nc.alloc_semaphore(name="s%d" % i) for i, n in enumerate(names)}
        by_eng = {e: [o for o in ops if o.eng == e] for e in ENGS}

        def run(en, eng):
            seen = {}
            for o in by_eng[en]:
                need = {}
                for d in o.deps:
                    q = ops[d]
                    if q.tok is None:
                        continue
                    if q.eng == "pe" and en == "pe" and not q.dma and not o.dma:
                        continue
                    sn, c = q.tok
                    if c > need.get(sn, 0):
                        need[sn] = c
                for sn, c in need.items():
                    if seen.get(sn, 0) >= c:
                        continue
                    eng.wait_ge(sems[sn], c)
                    seen[sn] = c
                if o.fn is None:
                    continue
                if o.dma:
                    o.fn(eng, sems[o.tok[0]])
                else:
                    ins = o.fn(eng)
                    if o.ms:
                        ins.then_inc(sems[o.tok[0]], 1)

        with nc.Block() as block:
            @block.tensor
            def _(e):
                run("pe", e)

            @block.scalar
            def _(e):
                run("act", e)

            @block.vector
            def _(e):
                run("dve", e)

            @block.gpsimd
            def _(e):
                run("pool", e)

            @block.sync
            def _(e):
                run("sp", e)


class Arena:
    def __init__(self, ar, nbytes):
        self.ar = ar
        self.n = nbytes
        self.top = 0

    def alloc(self, shape, dt, parts=128):
        esz = 2 if dt == BF16 else 4
        free = int(np.prod(shape[1:]))
        nb = free * esz
        off = (self.top + 63) // 64 * 64
        assert off + nb <= self.n, ("SBUF arena overflow", off, nb, self.n)
        self.top = off + nb
        v = self.ar[0:shape[0], off:off + nb].bitcast(dt)
        if len(shape) == 3:
            v = v.rearrange("p (a b) -> p a b", a=shape[1])
        elif len(shape) == 4:
            v = v.rearrange("p (a b c) -> p a b c", a=shape[1], b=shape[2])
        return v


class Ctx:
    pass


def _dma(C, eng, out, in_, reads, writes, slot):
    C.p.op(eng, lambda e, s: e.dma_start(out=out, in_=in_).then_inc(s, 16), reads=reads, writes=writes, dma=1, slot=slot)


def wload(C, dst, src, wkey, eng="pool"):
    shape = tuple(dst.shape)
    n = int(np.prod(shape[1:]))
    assert n <= 1536, shape
    i = C.stg_n % len(C.stg)
    C.stg_n += 1
    st = C.stg[i][0:shape[0], 0:n]
    if len(shape) == 3:
        st = st.rearrange("p (a b) -> p a b", a=shape[1])
    _dma(C, "sp", st, src, [], ["stg%d" % i], "stg%d" % i)
    C.p.op(eng, lambda e: e.tensor_copy(out=dst, in_=st), reads=["stg%d" % i], writes=[wkey])


def xk(tb, k):
    return ["xnT%d_%d" % (4 * tb + i, k) for i in range(4)]


def ph_load_x(C, x_d):
    for j in range(NT):
        _dma(C, "sp", C.xs[:, j, :], x_d[:, j, :], [], ["xs%d" % j], "xs%d" % j)


def ph_store_x(C, x_d, keyname):
    for j in range(NT):
        _dma(C, "sp", x_d[:, j, :], C.xs[:, j, :], ["xs%d" % j], [keyname + "%d" % j], "xo%d" % (j % 4))
    C.outkeys += [keyname + "%d" % j for j in range(NT)]


def ph_consts(C):
    p = C.p
    A = C.A
    C.ident = A.alloc([128, 128], BF16)
    C.ones = A.alloc([128, 128], BF16)
    _dma(C, "pool", C.ident, C.d["ident"], [], ["ident"], "c_ident")
    _dma(C, "pool", C.ones, C.d["ones"], [], ["ones"], "c_ones")


def ph_norm(C, g_d, tag):
    p = C.p
    xs, xnT = C.xs, C.xnT
    gT = C.gT
    _dma(C, "sp", gT, g_d, [], ["gT"], "gT")
    ss, rstd, junk = C.ss, C.rstd, C.junk
    for j in range(NT):
        p.op("act", lambda e, j=j: e.activation(out=junk, in_=xs[:, j, :], func=AF.Square, accum_out=ss[:, j:j + 1]),
             reads=["xs%d" % j], writes=["junk", "ss"])
    p.op("act", lambda e: e.activation(out=rstd, in_=ss, func=AF.Ln, bias=C.epsb, scale=1.0 / D), reads=["ss", "epsb"], writes=["rstd"])
    p.op("act", lambda e: e.activation(out=rstd, in_=rstd, func=AF.Exp, scale=-0.5), reads=["rstd"], writes=["rstd"])
    for j in range(NT):
        xb = C.xb[j % 2]
        bt = C.BT[j % 2]
        p.op("dve", lambda e, j=j, xb=xb: e.tensor_scalar(out=xb, in0=xs[:, j, :], scalar1=rstd[:, j:j + 1], scalar2=None, op0=ALU.mult),
             reads=["xs%d" % j, "rstd"], writes=["xb%d" % (j % 2)])
        for k in range(8):
            p.op("pe", lambda e, k=k, xb=xb, bt=bt: e.transpose(out=bt[:, k * 128:(k + 1) * 128], in_=xb[:, k * 128:(k + 1) * 128], identity=C.ident),
                 reads=["xb%d" % (j % 2), "ident"], writes=["BT%d" % (j % 2)])
        for k in range(8):
            if j % 2 == 0:
                p.op("act", lambda e, k=k, j=j, bt=bt: e.activation(out=xnT[:, k, j * 128:(j + 1) * 128], in_=bt[:, k * 128:(k + 1) * 128],
                                                                func=AF.Copy, scale=gT[:, k:k + 1]),
                     reads=["BT%d" % (j % 2), "gT"], writes=["xnT%d_%d" % (j, k)])
            else:
                p.op("dve", lambda e, k=k, j=j, bt=bt: e.tensor_scalar(out=xnT[:, k, j * 128:(j + 1) * 128], in0=bt[:, k * 128:(k + 1) * 128],
                                                                   scalar1=gT[:, k:k + 1], scalar2=None, op0=ALU.mult),
                     reads=["BT%d" % (j % 2), "gT"], writes=["xnT%d_%d" % (j, k)])


def ph_ffn(C, g_d, wg_d, wu_d, wd_d, tag):
    p = C.p
    ph_norm(C, g_d, tag)
    xs, xnT = C.xs, C.xnT
    NG = len(FF_GROUPS)

    def loads_gu(gi):
        c0, c1 = FF_GROUPS[gi]
        gs = c1 - c0
        sl = gi % 2
        wg, wu = C.wg[sl], C.wu[sl]
        for kh in range(2):
            wload(C, wg[:, kh * 4:kh * 4 + 4, 0:gs * 128], wg_d[:, kh * 4:kh * 4 + 4, c0 * 128:c1 * 128], "wg%d" % sl)
        for kh in range(2):
            wload(C, wu[:, kh * 4:kh * 4 + 4, 0:gs * 128], wu_d[:, kh * 4:kh * 4 + 4, c0 * 128:c1 * 128], "wu%d" % sl)

    def loads_d(gi):
        c0, c1 = FF_GROUPS[gi]
        sl = gi % 2
        wd = C.wd[sl]
        for c in range(c1 - c0):
            wload(C, wd[:, c, :], wd_d[:, c0 + c, :], "wd%d" % sl)

    cnt = {"gu": 0, "dn": 0}

    def gu_units(gi):
        c0, c1 = FF_GROUPS[gi]
        gs = c1 - c0
        sl = gi % 2
        wg, wu, hT = C.wg[sl], C.wu[sl], C.hT[sl]
        units = []
        for c in range(gs):
            for tb in range(4):
                b = cnt["gu"] % 2
                cnt["gu"] += 1
                psg, psu = C.B[b], C.B[2 + b]
                u = []
                for k in range(8):
                    u.append(("pe", lambda e, k=k, c=c, tb=tb, psg=psg, wg=wg: e.matmul(psg, wg[:, k, c * 128:(c + 1) * 128], xnT[:, k, tb * 512:(tb + 1) * 512], start=(k == 0), stop=(k == 7)),
                              ["wg%d" % sl] + xk(tb, k), ["B%d" % b]))
                for k in range(8):
                    u.append(("pe", lambda e, k=k, c=c, tb=tb, psu=psu, wu=wu: e.matmul(psu, wu[:, k, c * 128:(c + 1) * 128], xnT[:, k, tb * 512:(tb + 1) * 512], start=(k == 0), stop=(k == 7)),
                              ["wu%d" % sl] + xk(tb, k), ["B%d" % (2 + b)]))
                sg = C.sg[b]
                u.append(("act", lambda e, sg=sg, psg=psg: e.activation(out=sg, in_=psg, func=AF.Silu), ["B%d" % b], ["sg%d" % b]))
                u.append(("dve", lambda e, sg=sg, psu=psu, hT=hT, c=c, tb=tb: e.tensor_tensor(out=hT[:, c, tb * 512:(tb + 1) * 512], in0=sg, in1=psu, op=ALU.mult),
                          ["sg%d" % b, "B%d" % (2 + b)], ["hT%d_%d_%d" % (sl, c, tb)]))
                units.append(u)
        return units

    def dn_units(gi):
        c0, c1 = FF_GROUPS[gi]
        gs = c1 - c0
        sl = gi % 2
        wd, hT = C.wd[sl], C.hT[sl]
        units = []
        for j in range(NT):
            for hf in range(2):
                b = 4 + (cnt["dn"] % 2)
                cnt["dn"] += 1
                psy = C.B[b]
                u = []
                for c in range(gs):
                    u.append(("pe", lambda e, gs=gs, c=c, j=j, hf=hf, psy=psy, hT=hT, wd=wd: e.matmul(psy, hT[:, c, j * 128:(j + 1) * 128], wd[:, c, hf * 512:(hf + 1) * 512], start=(c == 0), stop=(c == gs - 1)),
                              ["hT%d_%d_%d" % (sl, c, j // 4), "wd%d" % sl], ["B%d" % b]))
                u.append(("dve", lambda e, j=j, hf=hf, psy=psy: e.scalar_tensor_tensor(out=xs[:, j, hf * 512:(hf + 1) * 512], in0=psy, scalar=0.5, in1=xs[:, j, hf * 512:(hf + 1) * 512], op0=ALU.mult, op1=ALU.add),
                          ["B%d" % b, "xs%d" % j], ["xs%d" % j]))
                units.append(u)
        return units

    def emit(u):
        for (eng, fn, rd, wr) in u:
            p.op(eng, fn, reads=rd, writes=wr)

    loads_gu(0)
    loads_gu(1)
    loads_d(0)
    loads_d(1)
    for u in gu_units(0):
        emit(u)
    for gi in range(NG):
        if gi + 2 < NG:
            loads_gu(gi + 2)
        dn = dn_units(gi)
        gu = gu_units(gi + 1) if gi + 1 < NG else []
        nd, ng = len(dn), len(gu)
        di = 0
        for ui in range(ng):
            emit(gu[ui])
            tgt = (ui + 1) * nd // ng
            while di < tgt:
                emit(dn[di])
                di += 1
        while di < nd:
            emit(dn[di])
            di += 1
        if gi + 2 < NG:
            loads_d(gi + 2)


def _rmsT(C, src_keys, sq, nchunk, rows, width, inv_n, out_rstd, bank, bkey):
    p = C.p
    for i in range(nchunk):
        p.op("pe", lambda e, i=i: e.matmul(bank[0:rows, 0:width], C.ones[0:rows, 0:rows], sq[0:rows, i, 0:width], start=(i == 0), stop=(i == nchunk - 1)),
             reads=["ones"] + src_keys, writes=[bkey])
    p.op("act", lambda e: e.activation(out=out_rstd[0:rows, 0:width], in_=bank[0:rows, 0:width], func=AF.Ln, bias=C.epsb[0:rows, :], scale=inv_n),
         reads=[bkey, "epsb"], writes=["rstdT"])
    p.op("act", lambda e: e.activation(out=out_rstd[0:rows, 0:width], in_=out_rstd[0:rows, 0:width], func=AF.Exp, scale=-0.5),
         reads=["rstdT"], writes=["rstdT"])


def ph_even_in(C, li):
    p, A, d = C.p, C.A, C.d
    C.xnT = A.alloc([128, 8, T], BF16)
    ph_norm(C, d["mixg"], "mix")
    xnT = C.xnT
    win = A.alloc([128, 8, MIXIN], BF16)
    for k in range(8):
        wload(C, win[:, k, :], d["win"][:, k, :], "win")
    wrope = A.alloc([128, 8, 96], BF16)
    p.op("pool", lambda e: e.memset(wrope, 0.0), writes=["wrope"])
    _dma(C, "pool", wrope[:, :, 64:96], d["win"][:, :, 1152:1184], [], ["wrope"], "wrope")
    wq = A.alloc([128, 3, 768], BF16)
    for i in range(3):
        wload(C, wq[:, i, :], d["wq"][:, i, :], "wq")
    wkp = A.alloc([128, 2, 8, 96], BF16)
    p.op("pool", lambda e: e.memset(wkp, 0.0), writes=["wkp"])
    wv = A.alloc([128, 2, 8, 64], BF16)
    wkv4 = d["wkv"].rearrange("p i (h c) -> p i h c", c=128)
    for i in range(2):
        _dma(C, "pool", wkp[:, i, :, 0:64], wkv4[:, i, :, 0:64], [], ["wkp"], "wkp")
        _dma(C, "pool", wv[:, i, :, :], wkv4[:, i, :, 64:128], [], ["wv"], "wv")
    rm = A.alloc([96, 96], BF16)
    _dma(C, "pool", rm, d["rm"], [], ["rm"], "rm")
    gql = A.alloc([128, 3], F32)
    gkl = A.alloc([128, 2], F32)
    gqh = A.alloc([96, 1], F32)
    gkh = A.alloc([96, 1], F32)
    _dma(C, "sp", gql, d["gql"], [], ["gsm"], "gsm")
    _dma(C, "sp", gkl, d["gkl"], [], ["gsm"], "gsm")
    _dma(C, "sp", gqh, d["gqh"], [], ["gsm"], "gsm")
    _dma(C, "sp", gkh, d["gkh"], [], ["gsm"], "gsm")
    cosb = [A.alloc([96, 512], F32) for _ in range(2)]
    sinb = [A.alloc([96, 512], F32) for _ in range(2)]
    ust = [A.alloc([128, 512], F32) for _ in range(2)]
    lat = A.alloc([128, 3, 512], F32)
    sq = A.alloc([128, 3, 512], BF16)
    rstdT = A.alloc([128, 512], F32)
    latn = A.alloc([128, 3, 512], BF16)
    kvn = A.alloc([128, 2, 512], BF16)
    hq = [A.alloc([96, 512], BF16) for _ in range(4)]
    t1 = [A.alloc([96, 512], F32) for _ in range(2)]
    t2 = [A.alloc([96, 512], F32) for _ in range(2)]
    sqh = [A.alloc([96, 512], BF16) for _ in range(2)]
    rsh = [A.alloc([96, 512], F32) for _ in range(2)]
    vst = [A.alloc([128, 8, 65], BF16) for _ in range(2)]
    for s in range(2):
        p.op("pool", lambda e, s=s: e.memset(vst[s], 1.0), writes=["vst%d" % s])
    nu = 0
    nh = 0
    for tb in range(4):
        tsl = slice(tb * 512, (tb + 1) * 512)
        cosT, sinT = cosb[tb % 2], sinb[tb % 2]
        ckey, skey2 = "cos%d" % (tb % 2), "sin%d" % (tb % 2)
        _dma(C, "sp", cosT[64:96, :], d["cosT"][64:96, tsl], [], [ckey], ckey)
        _dma(C, "sp", sinT[64:96, :], d["sinT"][64:96, tsl], [], [skey2], skey2)
        for m in range(4):
            bk = C.B[nu % 2]
            bkey = "B%d" % (nu % 2)
            st = ust[nu % 2]
            skey = "ust%d" % (nu % 2)
            nu += 1
            for k in range(8):
                p.op("pe", lambda e, tsl=tsl, k=k, m=m, bk=bk: e.matmul(bk, win[:, k, m * 128:(m + 1) * 128], xnT[:, k, tsl], start=(k == 0), stop=(k == 7)),
                     reads=["win"] + xk(tb, k), writes=[bkey])
            p.op("act", lambda e, bk=bk, st=st: e.activation(out=st, in_=bk, func=AF.Copy), reads=[bkey], writes=[skey])
            _dma(C, "sp", d["uT_o"][m][:, tsl], st, [skey], ["uTo_%d_%d" % (m, tb)], skey)
            C.outkeys.append("uTo_%d_%d" % (m, tb))
        for (nm, m0, nch, inv_n, gl, outn) in (("q", 4, 3, 1.0 / QL, gql, latn), ("kv", 7, 2, 1.0 / KVL, gkl, kvn)):
            for i in range(nch):
                bk = C.B[2 + i]
                bkey = "B%d" % (2 + i)
                m = m0 + i
                for k in range(8):
                    p.op("pe", lambda e, tsl=tsl, k=k, m=m, bk=bk: e.matmul(bk, win[:, k, m * 128:(m + 1) * 128], xnT[:, k, tsl], start=(k == 0), stop=(k == 7)),
                         reads=["win"] + xk(tb, k), writes=[bkey])
                p.op("act", lambda e, i=i, bk=bk: e.activation(out=lat[:, i, :], in_=bk, func=AF.Copy), reads=[bkey], writes=["lat"])
                p.op("act", lambda e, i=i, bk=bk: e.activation(out=sq[:, i, :], in_=bk, func=AF.Square), reads=[bkey], writes=["sq"])
            _rmsT(C, ["sq"], sq, nch, 128, 512, inv_n, rstdT, C.B[5], "B5")
            for i in range(nch):
                p.op("dve", lambda e, i=i, gl=gl, outn=outn: e.scalar_tensor_tensor(out=outn[:, i, :], in0=lat[:, i, :], scalar=gl[:, i:i + 1], in1=rstdT, op0=ALU.mult, op1=ALU.mult),
                     reads=["lat", "rstdT", "gsm"], writes=[nm + "n"])
        def head_stages(which, h, b, hqt, hkey):
            bk = C.B[b]
            bkey = "B%d" % b
            ssb, sskey = (C.B[5], "B5") if b == 0 else (C.B[3], "B3")
            rb, rkey = (C.B[4], "B4") if b == 0 else (C.B[2], "B2")
            sqb, rsb, t1b, t2b = sqh[b], rsh[b], t1[b], t2[b]
            st = [[] for _ in range(9)]
            if which == "q":
                for i in range(3):
                    st[0].append(("pe", lambda e, i=i, h=h, bk=bk: e.matmul(bk[0:96, :], wq[:, i, h * 96:(h + 1) * 96], latn[:, i, :], start=(i == 0), stop=(i == 2)),
                                  ["wq", "qn"], [bkey]))
                gh = gqh
            else:
                for k in range(8):
                    st[0].append(("pe", lambda e, tsl=tsl, k=k, bk=bk: e.matmul(bk[0:96, :], wrope[:, k, :], xnT[:, k, tsl], start=(k == 0), stop=False),
                                  ["wrope"] + xk(tb, k), [bkey]))
                for i in range(2):
                    st[0].append(("pe", lambda e, i=i, h=h, bk=bk: e.matmul(bk[0:96, :], wkp[:, i, h, :], kvn[:, i, :], start=False, stop=(i == 1)),
                                  ["wkp", "kvn"], [bkey]))
                gh = gkh
            st[1].append(("act", lambda e, bk=bk, sqb=sqb: e.activation(out=sqb, in_=bk[0:96, :], func=AF.Square), [bkey], ["sqh%d" % b]))
            st[2].append(("pe", lambda e, ssb=ssb, sqb=sqb: e.matmul(ssb[0:96, :], C.ones[0:96, 0:96], sqb, start=True, stop=True), ["ones", "sqh%d" % b], [sskey]))
            st[3].append(("act", lambda e, ssb=ssb, rsb=rsb: e.activation(out=rsb, in_=ssb[0:96, :], func=AF.Ln, bias=C.epsb[0:96, :], scale=1.0 / DQK), [sskey, "epsb"], ["rsh%d" % b]))
            st[3].append(("act", lambda e, rsb=rsb: e.activation(out=rsb, in_=rsb, func=AF.Exp, scale=-0.5), ["rsh%d" % b], ["rsh%d" % b]))
            st[4].append(("dve", lambda e, bk=bk, hqt=hqt, gh=gh, rsb=rsb: e.scalar_tensor_tensor(out=hqt, in0=bk[0:96, :], scalar=gh[:, 0:1], in1=rsb, op0=ALU.mult, op1=ALU.mult),
                          [bkey, "rsh%d" % b, "gsm"], [hkey]))
            st[5].append(("pe", lambda e, hqt=hqt, rb=rb: e.matmul(rb[0:96, :], rm, hqt, start=True, stop=True), ["rm", hkey], [rkey]))
            st[6].append(("pool", lambda e, cosT=cosT, hqt=hqt, t1b=t1b: e.tensor_tensor(out=t1b[64:96, :], in0=hqt[64:96, :], in1=cosT[64:96, :], op=ALU.mult),
                          [hkey, ckey], ["t1_%d" % b]))
            st[6].append(("dve", lambda e, sinT=sinT, rb=rb, t2b=t2b: e.tensor_tensor(out=t2b[64:96, :], in0=rb[64:96, :], in1=sinT[64:96, :], op=ALU.mult),
                          [rkey, skey2], ["t2_%d" % b]))
            st[7].append(("pool", lambda e, hqt=hqt, t1b=t1b, t2b=t2b: e.tensor_tensor(out=hqt[64:96, :], in0=t1b[64:96, :], in1=t2b[64:96, :], op=ALU.add),
                          ["t1_%d" % b, "t2_%d" % b], [hkey]))
            return st

        for which in ("q", "k"):
            for hp in range(H // 2):
                pp = nh % 2
                nh += 1
                sts = []
                for b in range(2):
                    h = hp * 2 + b
                    hqt = hq[pp * 2 + b]
                    hkey = "hq%d" % (pp * 2 + b)
                    sts.append((h, hqt, hkey, head_stages(which, h, b, hqt, hkey)))
                for si in range(9):
                    for (h, hqt, hkey, st) in sts:
                        for (eng, fn, rd, wr) in st[si]:
                            p.op(eng, fn, reads=rd, writes=wr)
                for (h, hqt, hkey, st) in sts:
                    if which == "q":
                        ok = "qTo_%d_%d" % (h, tb)
                        _dma(C, "sp", d["qT_o"][h][:, tsl], hqt, [hkey], [ok], hkey)
                        C.outkeys.append(ok)
                    else:
                        for jj in range(4):
                            ok = "kTo_%d_%d_%d" % (h, tb, jj)
                            _dma(C, "sp", d["kT_o"][tb * 4 + jj][:, h, :], hqt[:, jj * 128:(jj + 1) * 128], [hkey], [ok], hkey)
                            C.outkeys.append(ok)
        for jj in range(4):
            j = tb * 4 + jj
            b = j % 2
            bk = C.B[2 + b]
            bkey = "B%d" % (2 + b)
            for i in range(2):
                p.op("pe", lambda e, i=i, jj=jj, bk=bk: e.matmul(bk, kvn[:, i, jj * 128:(jj + 1) * 128], wv[:, i, :, :], start=(i == 0), stop=(i == 1)),
                     reads=["kvn", "wv"], writes=[bkey])
            p.op("dve", lambda e, b=b, bk=bk: e.tensor_copy(out=vst[b][:, :, 0:64], in_=bk.rearrange("p (h c) -> p h c", c=64)),
                 reads=[bkey], writes=["vst%d" % b])
            ok = "vo_%d" % j
            _dma(C, "sp", d["v_o"][:, j, :, :], vst[b], ["vst%d" % b], [ok], "vst%d" % b)
            C.outkeys.append(ok)


def ph_even_out(C, li):
    p, A, d = C.p, C.A, C.d
    scale = DQK ** -0.5
    catT = A.alloc([128, 8, T], BF16)
    mark = A.top
    uextb = [A.alloc([128, NT, 144], F32) for _ in range(2)]
    wa = A.alloc([128, NT, 144], F32)
    wb = A.alloc([128, NT, 144], F32)
    pooled = A.alloc([128, 4, T], BF16)
    invc = A.alloc([128, 4, 16], F32)
    _dma(C, "sp", invc, d["invc"], [], ["invc"], "invc")
    wpool = A.alloc([128, 4, 128], BF16)
    _dma(C, "pool", wpool, d["wpool"], [], ["wpool"], "wpool")
    pscale = A.alloc([128, 4], F32)
    _dma(C, "sp", pscale, d["pscale"], [], ["pscale"], "pscale")
    tmp16 = A.alloc([128, 16], F32)
    for g in range(4):
        w = 2 ** (g + 1)
        src = uextb[g % 2]
        _dma(C, "sp", src, d["uext"][:, g, :, :], [], ["uext%d" % (g % 2)], "uext%d" % (g % 2))
        bufs = [wa, wb]
        sh = 1
        cur = src
        curkey = "uext%d" % (g % 2)
        nb = 0
        while sh < w:
            dst = bufs[nb % 2]
            dkey = "w%d" % (nb % 2)
            nb += 1
            p.op("dve", lambda e, cur=cur, dst=dst, sh=sh: e.tensor_tensor(out=dst[:, :, 2 * sh - 1:144], in0=cur[:, :, 2 * sh - 1:144], in1=cur[:, :, sh - 1:144 - sh], op=ALU.add),
                 reads=[curkey], writes=[dkey])
            if sh > 1:
                pass
            cur = dst
            curkey = dkey
            sh *= 2
        pv = pooled[:, g, :].rearrange("p (j t) -> p j t", t=128)
        p.op("dve", lambda e, cur=cur, src=src, pv=pv, w=w: e.scalar_tensor_tensor(out=pv, in0=cur[:, :, 16:144], scalar=1.0 / w, in1=src[:, :, 16:144], op0=ALU.mult, op1=ALU.subtract),
             reads=[curkey, "uext%d" % (g % 2)], writes=["pooled%d" % g])
        p.op("dve", lambda e, cur=cur, g=g: e.tensor_tensor(out=tmp16, in0=cur[:, 0, 16:32], in1=invc[:, g, :], op=ALU.mult),
             reads=[curkey, "invc"], writes=["tmp16"])
        p.op("dve", lambda e, src=src, g=g: e.tensor_tensor(out=pooled[:, g, 0:16], in0=tmp16, in1=src[:, 0, 16:32], op=ALU.subtract),
             reads=["tmp16", "uext%d" % (g % 2), "pooled%d" % g], writes=["pooled%d" % g])
    n = 0
    for g in range(4):
        for tb in range(4):
            b = n % 2
            n += 1
            bk = C.B[b]
            p.op("pe", lambda e, g=g, tb=tb, bk=bk: e.matmul(bk, wpool[:, g, :], pooled[:, g, tb * 512:(tb + 1) * 512], start=True, stop=True),
                 reads=["wpool", "pooled%d" % g], writes=["B%d" % b])
            p.op("act", lambda e, g=g, tb=tb, bk=bk: e.activation(out=catT[:, g, tb * 512:(tb + 1) * 512], in_=bk, func=AF.Copy, scale=pscale[:, g:g + 1]),
                 reads=["B%d" % b, "pscale"], writes=["catT"])
    p.barrier()
    A.top = mark
    qT = A.alloc([96, H, T], BF16)
    for h in range(H):
        _dma(C, "sp", qT[:, h, :], d["qT"][h], [], ["qT"], "qT%d" % (h % 4))
    mask = A.alloc([128, 8, 128], BF16)
    _dma(C, "pool", mask, d["mask"], [], ["mask"], "mask")
    krow = [A.alloc([96, 8, 4, 128], BF16) for _ in range(2)]
    vrow = [A.alloc([128, 8, 4, 65], BF16) for _ in range(2)]
    atok = A.alloc([128, NT, 512], BF16)
    rden = A.alloc([128, 4], F32)
    nrow = 0
    ns = 0
    LAG = 1
    pT = [A.alloc([128, 2, 512], BF16) for _ in range(3)]
    SD = [(C.SA.rearrange("p (b c) -> p b c", b=2), ["B4", "B5"]),
          (C.SB.rearrange("p (b c) -> p b c", b=2), ["BT0", "BT1"])]
    for sb in range(4):
        for hg in range(2):
            steps = []
            for jr in range(4 * sb + 4):
                sl = nrow % 2
                nrow += 1
                kr, vr = krow[sl], vrow[sl]
                pre = [("sp", lambda e, s, kr=kr, jr=jr, hg=hg: [e.dma_start(out=kr[:, r, :, :], in_=d["kall"][jr][:, r, hg * 4:(hg + 1) * 4, :]).then_inc(s, 16) for r in range(8)],
                        [], ["krow%d" % sl], 8, "krow%d" % sl),
                       ("sp", lambda e, s, vr=vr, jr=jr, hg=hg: [e.dma_start(out=vr[:, r, :, :], in_=d["vall"][jr][:, r, hg * 4:(hg + 1) * 4, :]).then_inc(s, 16) for r in range(8)],
                        [], ["vrow%d" % sl], 8, "vrow%d" % sl)]
                i0 = max(0, jr - 4 * sb)
                diag = jr >= 4 * sb
                ncols = (4 - i0) * 128
                q0 = (4 * sb + i0) * 128
                kk = "krow%d" % sl
                for r in range(8):
                    for hp in range(2):
                        sd, sdkeys = SD[ns % 2]
                        pt = pT[ns % 3]
                        ptk = "pT%d" % (ns % 3)
                        ns += 1
                        S_ops = list(pre)
                        pre = []
                        for half in range(2):
                            hh = hp * 2 + half
                            h = hg * 4 + hh
                            bk = sd[:, half, :]
                            bkey = sdkeys[half]
                            if diag:
                                S_ops.append(("pe", lambda e, bk=bk, kr=kr, r=r, hh=hh, h=h, q0=q0: e.matmul(bk[:, 0:128], kr[:, r, hh, :], qT[:, h, q0:q0 + 128], start=True, stop=False),
                                              [kk, "qT"], [bkey], 0, None))
                                S_ops.append(("pe", lambda e, bk=bk, r=r: e.matmul(bk[:, 0:128], C.ident, mask[:, r, :], start=False, stop=True),
                                              ["ident", "mask"], [bkey], 0, None))
                                if ncols > 128:
                                    S_ops.append(("pe", lambda e, bk=bk, kr=kr, r=r, hh=hh, h=h, q0=q0, ncols=ncols: e.matmul(bk[:, 128:ncols], kr[:, r, hh, :], qT[:, h, q0 + 128:q0 + ncols], start=True, stop=True),
                                                  [kk, "qT"], [bkey], 0, None))
                            else:
                                S_ops.append(("pe", lambda e, bk=bk, kr=kr, r=r, hh=hh, h=h, q0=q0, ncols=ncols: e.matmul(bk[:, 0:ncols], kr[:, r, hh, :], qT[:, h, q0:q0 + ncols], start=True, stop=True),
                                              [kk, "qT"], [bkey], 0, None))
                        S_ops.append(("act", lambda e, sd=sd, pt=pt, ncols=ncols: e.activation(out=pt[:, :, 0:ncols], in_=sd[:, :, 0:ncols], func=AF.Exp, scale=scale),
                                      list(sdkeys), [ptk], 0, None))
                        PV_ops = []
                        for half in range(2):
                            hh = hp * 2 + half
                            for i in range(i0, 4):
                                first = (jr == 0 and r == 0 and hh == 0)
                                last = (jr == 4 * sb + i and r == 7)
                                PV_ops.append(("pe", lambda e, i0=i0, i=i, pt=pt, vr=vr, r=r, hh=hh, half=half, first=first, last=last: e.matmul(C.B[i][:, hh * 65:(hh + 1) * 65], pt[:, half, (i - i0) * 128:(i - i0 + 1) * 128], vr[:, r, hh, :], start=first, stop=last, skip_group_check=True),
                                               [ptk, "vrow%d" % sl], ["B%d" % i], 0, None))
                        steps.append((S_ops, PV_ops))
            for n in range(len(steps) + LAG):
                if n < len(steps):
                    for (eng, fn, rd, wr, dma, slot) in steps[n][0]:
                        p.op(eng, fn, reads=rd, writes=wr, dma=dma, slot=slot)
                if n - LAG >= 0:
                    for (eng, fn, rd, wr, dma, slot) in steps[n - LAG][1]:
                        p.op(eng, fn, reads=rd, writes=wr, dma=dma, slot=slot)
            for i in range(4):
                j = 4 * sb + i
                acc = C.B[i][:, 0:260].rearrange("p (h c) -> p h c", c=65)
                p.op("dve", lambda e, acc=acc: e.reciprocal(out=rden, in_=acc[:, :, 64]), reads=["B%d" % i], writes=["rden"])
                for hh in range(4):
                    h = hg * 4 + hh
                    p.op("dve", lambda e, acc=acc, hh=hh, h=h, j=j: e.tensor_scalar(out=atok[:, j, h * 64:(h + 1) * 64], in0=acc[:, hh, 0:64], scalar1=rden[:, hh:hh + 1], scalar2=None, op0=ALU.mult),
                         reads=["B%d" % i, "rden"], writes=["atok%d" % j])
    for j in range(NT):
        bt = C.BT[j % 2]
        for m in range(4):
            p.op("pe", lambda e, j=j, m=m, bt=bt: e.transpose(out=bt[:, m * 128:(m + 1) * 128], in_=atok[:, j, m * 128:(m + 1) * 128], identity=C.ident),
                 reads=["atok%d" % j, "ident"], writes=["BT%d" % (j % 2)])
        p.op("dve", lambda e, j=j, bt=bt: e.tensor_copy(out=catT[:, 4:8, j * 128:(j + 1) * 128], in_=bt[:, 0:512].rearrange("p (m t) -> p m t", t=128)),
             reads=["BT%d" % (j % 2)], writes=["catT"])
    p.barrier()
    A.top = mark
    _out_proj(C, catT, "catT", d["wout"])


def _out_proj(C, srcT, skey, w_d):
    p, A = C.p, C.A
    xs = C.xs
    wo = A.alloc([128, 8, 1024], BF16)
    for k in range(8):
        wload(C, wo[:, k, :], w_d[:, k, :], "wo")
    n = 0
    for j in range(NT):
        for hf in range(2):
            b = n % 2
            n += 1
            bk = C.B[b]
            for m in range(8):
                p.op("pe", lambda e, j=j, hf=hf, m=m, bk=bk: e.matmul(bk, srcT[:, m, j * 128:(j + 1) * 128], wo[:, m, hf * 512:(hf + 1) * 512], start=(m == 0), stop=(m == 7)),
                     reads=[skey, "wo"], writes=["B%d" % b])
            p.op("dve", lambda e, j=j, hf=hf, bk=bk: e.tensor_tensor(out=xs[:, j, hf * 512:(hf + 1) * 512], in0=bk, in1=xs[:, j, hf * 512:(hf + 1) * 512], op=ALU.add),
                 reads=["B%d" % b, "xs%d" % j], writes=["xs%d" % j])


def ph_odd_in(C, li):
    p, A, d = C.p, C.A, C.d
    C.xnT = A.alloc([128, 8, T], BF16)
    ph_norm(C, d["mixg"], "mix")
    xnT = C.xnT
    wst = [A.alloc([128, 8, 384], BF16) for _ in range(2)]
    gst = [A.alloc([128, 512], F32) for _ in range(2)]
    cst = [A.alloc([128, 512], F32) for _ in range(2)]
    ust = [A.alloc([128, 512], F32) for _ in range(2)]
    n = 0
    for m in range(8):
        sl = m % 2
        w = wst[sl]
        for q in range(3):
            wload(C, w[:, :, q * 128:(q + 1) * 128], d["wcin"][:, :, q * 1024 + m * 128:q * 1024 + (m + 1) * 128], "wst%d" % sl)
        for tb in range(4):
            tsl = slice(tb * 512, (tb + 1) * 512)
            b = n % 2
            n += 1
            banks = [C.B[b], C.B[2 + b], C.B[4 + b]]
            bkeys = ["B%d" % b, "B%d" % (2 + b), "B%d" % (4 + b)]
            for q in range(3):
                for k in range(8):
                    p.op("pe", lambda e, tsl=tsl, q=q, k=k, w=w, bk=banks[q]: e.matmul(bk, w[:, k, q * 128:(q + 1) * 128], xnT[:, k, tsl], start=(k == 0), stop=(k == 7)),
                         reads=["wst%d" % sl] + xk(tb, k), writes=[bkeys[q]])
            p.op("act", lambda e, b=b, bk=banks[0]: e.activation(out=gst[b], in_=bk, func=AF.Copy), reads=[bkeys[0]], writes=["gst%d" % b])
            p.op("act", lambda e, b=b, bk=banks[1]: e.activation(out=cst[b], in_=bk, func=AF.Copy), reads=[bkeys[1]], writes=["cst%d" % b])
            p.op("dve", lambda e, b=b, bk=banks[2]: e.tensor_tensor(out=ust[b], in0=cst[b], in1=bk, op=ALU.mult), reads=["cst%d" % b, bkeys[2]], writes=["ust%d" % b])
            ok = "gbo_%d_%d" % (m, tb)
            _dma(C, "sp", d["gb_o"][m][:, tsl], gst[b], ["gst%d" % b], [ok], "gst%d" % b)
            C.outkeys.append(ok)
            ok = "uco_%d_%d" % (m, tb)
            _dma(C, "sp", d["uc_o"][m][:, tsl], ust[b], ["ust%d" % b], [ok], "ust%d" % b)
            C.outkeys.append(ok)


def ph_odd_out(C, li):
    p, A, d = C.p, C.A, C.d
    vT = A.alloc([128, 8, T], BF16)
    cw = A.alloc([128, 8, 3], F32)
    _dma(C, "sp", cw, d["cw"], [], ["cw"], "cw")
    ue = [A.alloc([128, NT, 130], F32) for _ in range(2)]
    gb = [A.alloc([128, T], F32) for _ in range(2)]
    y = [A.alloc([128, NT, 128], F32) for _ in range(2)]
    for m in range(8):
        sl = m % 2
        _dma(C, "sp", ue[sl], d["ucext"][:, m, :, :], [], ["ue%d" % sl], "ue%d" % sl)
        _dma(C, "sp", gb[sl], d["gbT"][m], [], ["gb%d" % sl], "gb%d" % sl)
        eng = "dve"
        u, yy, g = ue[sl], y[sl], gb[sl]
        p.op(eng, lambda e, u=u, yy=yy, m=m: e.tensor_scalar(out=yy, in0=u[:, :, 0:128], scalar1=cw[:, m, 0:1], scalar2=None, op0=ALU.mult),
             reads=["ue%d" % sl, "cw"], writes=["y%d" % sl])
        p.op(eng, lambda e, u=u, yy=yy, m=m: e.scalar_tensor_tensor(out=yy, in0=u[:, :, 1:129], scalar=cw[:, m, 1:2], in1=yy, op0=ALU.mult, op1=ALU.add),
             reads=["ue%d" % sl, "cw", "y%d" % sl], writes=["y%d" % sl])
        p.op(eng, lambda e, u=u, yy=yy, m=m: e.scalar_tensor_tensor(out=yy, in0=u[:, :, 2:130], scalar=cw[:, m, 2:3], in1=yy, op0=ALU.mult, op1=ALU.add),
             reads=["ue%d" % sl, "cw", "y%d" % sl], writes=["y%d" % sl])
        p.op(eng, lambda e, yy=yy, g=g, m=m: e.tensor_tensor(out=vT[:, m, :], in0=yy.rearrange("p j t -> p (j t)"), in1=g, op=ALU.mult),
             reads=["y%d" % sl, "gb%d" % sl], writes=["vT"])
    _out_proj(C, vT, "vT", d["wcout"])


def build_launch(spec):
    nc = bass.Bass("TRN2", target_bir_lowering=False)
    C = Ctx()
    C.nc = nc
    C.p = Prog(nc)
    C.d = {}
    C.outkeys = []
    for name, (shape, dt) in spec["inputs"].items():
        C.d[name] = nc.dram_tensor(name, list(shape), dt, kind="ExternalInput").ap()
    for name, (shape, dt) in spec["outputs"].items():
        C.d[name] = nc.dram_tensor(name, list(shape), dt, kind="ExternalOutput").ap()
    import contextlib
    with contextlib.ExitStack() as st:
        nbytes = 204 * 1024
        ar = st.enter_context(nc.sbuf_tensor("arena", [128, nbytes], U8))
        C.A = Arena(ar, nbytes)
        C.B = [st.enter_context(nc.psum_tensor("bank%d" % i, [128, 512], F32))[:] for i in range(4)]
        C.SA = st.enter_context(nc.psum_tensor("bankSA", [128, 1024], F32))[:]
        C.SB = st.enter_context(nc.psum_tensor("bankSB", [128, 1024], F32))[:]
        C.B += [C.SA[:, 0:512], C.SA[:, 512:1024]]
        C.BT = [C.SB[:, 0:512].bitcast(BF16), C.SB[:, 512:1024].bitcast(BF16)]
        A = C.A
        C.xs = A.alloc([128, NT, D], F32)
        C.gT = A.alloc([128, 8], F32)
        C.ss = A.alloc([128, NT], F32)
        C.rstd = A.alloc([128, NT], F32)
        C.junk = A.alloc([128, D], BF16)
        C.xb = [A.alloc([128, D], BF16) for _ in range(2)]
        C.epsb = A.alloc([128, 1], F32)
        C.stg = [A.alloc([128, 1536], F32) for _ in range(3)]
        C.stg_n = 0
        C.p.op("pool", lambda e: e.memset(C.epsb, EPS), writes=["epsb"])
        ph_consts(C)
        base = A.top
        for ph in spec["phases"]:
            C.p.barrier()
            A.top = base
            ph(C)
        C.p.op("sp", None, reads=list(C.outkeys))
        C.p.op("pool", None, reads=list(C.outkeys))
        C.p.emit()
    return nc


def alloc_ffn(C):
    A = C.A
    C.xnT = A.alloc([128, 8, T], BF16)
    C.wg = [A.alloc([128, 8, 384], BF16) for _ in range(2)]
    C.wu = [A.alloc([128, 8, 384], BF16) for _ in range(2)]
    C.wd = [A.alloc([128, 3, D], BF16) for _ in range(2)]
    C.hT = [A.alloc([128, 3, T], BF16) for _ in range(2)]
    C.sg = [A.alloc([128, 512], F32) for _ in range(2)]


def mk_ffn(which):
    def f(C):
        alloc_ffn(C)
        ph_ffn(C, C.d[which + "_g"], C.d[which + "_wg"], C.d[which + "_wu"], C.d[which + "_wd"], which)
    return f


FFN_IN = lambda w: {w + "_g": ((128, 8), F32), w + "_wg": ((128, 8, DFF), F32), w + "_wu": ((128, 8, DFF), F32), w + "_wd": ((128, NFC, D), F32)}
X_IO = ((128, NT, D), F32)
CONST_IN = {"ident": ((128, 128), F32), "ones": ((128, 128), F32)}
EVEN_IN_W = {"mixg": ((128, 8), F32), "win": ((128, 8, MIXIN), F32), "wq": ((128, 3, 768), F32), "wkv": ((128, 2, 1024), F32),
             "rm": ((96, 96), F32), "gql": ((128, 3), F32), "gkl": ((128, 2), F32), "gqh": ((96, 1), F32), "gkh": ((96, 1), F32),
             "cosT": ((96, T), F32), "sinT": ((96, T), F32)}
EVEN_IN_OUT = {"uT_o": ((4, 128, T), F32), "qT_o": ((H, 96, T), BF16), "kT_o": ((NT, 96, H, 128), BF16), "v_o": ((128, NT, H, 65), BF16)}
EVEN_OUT_IN = {"uext": ((128, 4, NT, 144), F32), "invc": ((128, 4, 16), F32), "wpool": ((128, 4, 128), F32), "pscale": ((128, 4), F32),
               "qT": ((H, 96, T), BF16), "mask": ((128, 8, 128), F32), "kall": ((NT, 96, 8, H, 128), BF16), "vall": ((NT, 128, 8, H, 65), BF16),
               "wout": ((128, 8, D), F32)}
ODD_IN_W = {"mixg": ((128, 8), F32), "wcin": ((128, 8, 3 * D), F32)}
ODD_IN_OUT = {"gb_o": ((8, 128, T), F32), "uc_o": ((8, 128, T), F32)}
ODD_OUT_IN = {"cw": ((128, 8, 3), F32), "ucext": ((128, 8, NT, 130), F32), "gbT": ((8, 128, T), F32), "wcout": ((128, 8, D), F32)}


def _load_x(C):
    ph_load_x(C, C.d["x"])


def _store_x(C):
    ph_store_x(C, C.d["x_o"], "xo")


def spec_A():
    return dict(inputs={"x": X_IO, **CONST_IN, **FFN_IN("f1"), **EVEN_IN_W},
                outputs={"x_o": X_IO, **EVEN_IN_OUT},
                phases=[_load_x, mk_ffn("f1"), _store_x, lambda C: ph_even_in(C, 0)])


def spec_B():
    return dict(inputs={"x": X_IO, **CONST_IN, **EVEN_OUT_IN, **FFN_IN("f2"), **FFN_IN("f1"), **ODD_IN_W},
                outputs={"x_o": X_IO, **ODD_IN_OUT},
                phases=[_load_x, lambda C: ph_even_out(C, 0), mk_ffn("f2"), mk_ffn("f1"), _store_x, lambda C: ph_odd_in(C, 1)])


def spec_C():
    return dict(inputs={"x": X_IO, **CONST_IN, **ODD_OUT_IN, **FFN_IN("f2"), **FFN_IN("f1"), **EVEN_IN_W},
                outputs={"x_o": X_IO, **EVEN_IN_OUT},
                phases=[_load_x, lambda C: ph_odd_out(C, 1), mk_ffn("f2"), mk_ffn("f1"), _store_x, lambda C: ph_even_in(C, 2)])


def spec_D():
    return dict(inputs={"x": X_IO, **CONST_IN, **ODD_OUT_IN, **FFN_IN("f2")},
                outputs={"x_o": X_IO},
                phases=[_load_x, lambda C: ph_odd_out(C, 3), mk_ffn("f2"), _store_x])


_NC_CACHE = {}


def _get_nc(name):
    if name not in _NC_CACHE:
        _NC_CACHE[name] = build_launch({"A": spec_A, "B": spec_B, "C": spec_C, "D": spec_D}[name]())
    return _NC_CACHE[name]


def _kmaj(w, p=128):
    K, N = w.shape
    return np.ascontiguousarray(w.reshape(K // p, p, N).transpose(1, 0, 2))


def _vec(g, p=128):
    return np.ascontiguousarray(g.reshape(-1, p).T)


def _ffn_inputs(prefix, inputs, which, layer):
    return {prefix + "_g": _vec(inputs[which + "_norm"][layer]),
            prefix + "_wg": _kmaj(inputs[which + "_w_gate"][layer]),
            prefix + "_wu": _kmaj(inputs[which + "_w_up"][layer]),
            prefix + "_wd": _kmaj(inputs[which + "_w_down"][layer])}


def _consts():
    return {"ident": np.eye(128, dtype=np.float32), "ones": np.ones((128, 128), np.float32)}


def _rope_tables(c):
    pos = (np.arange(NT)[:, None] * NC + c) * 128 + np.arange(128)[None, :]
    pos = pos.reshape(-1).astype(np.float32)
    inv = (np.float32(10000.0) ** (-np.arange(0, 32, 2, dtype=np.float32) / np.float32(32))).astype(np.float32)
    ang = (pos[:, None] * inv[None, :]).astype(np.float32)
    cos = np.cos(ang).astype(np.float32).T
    sin = np.sin(ang).astype(np.float32).T
    cosT = np.zeros((96, T), np.float32)
    sinT = np.zeros((96, T), np.float32)
    cosT[64:80] = cos
    cosT[80:96] = cos
    sinT[64:80] = sin
    sinT[80:96] = sin
    return cosT, sinT


def _rm():
    rm = np.zeros((96, 96), np.float32)
    for i in range(16):
        rm[80 + i, 64 + i] = -1.0
        rm[64 + i, 80 + i] = 1.0
    return rm


def _even_in_inputs(inputs, i, c):
    cosT, sinT = _rope_tables(c)
    return {"mixg": _vec(inputs["mix_norm"][2 * i]), "win": _kmaj(inputs["a_w_in"][i]), "wq": _kmaj(inputs["a_w_q_up"][i]),
            "wkv": _kmaj(inputs["a_w_kv_up"][i]), "rm": _rm(), "gql": _vec(inputs["a_q_a_norm"][i]), "gkl": _vec(inputs["a_kv_a_norm"][i]),
            "gqh": np.ascontiguousarray(inputs["a_q_head_norm"][i].reshape(96, 1)), "gkh": np.ascontiguousarray(inputs["a_k_head_norm"][i].reshape(96, 1)),
            "cosT": cosT, "sinT": sinT}


def _mask(c):
    m = np.zeros((128, 8, 128), np.float32)
    for r in range(8):
        if r > c:
            m[:, r, :] = -30000.0
        elif r == c:
            m[:, r, :] = np.where(np.arange(128)[:, None] > np.arange(128)[None, :], -30000.0, 0.0)
    return m


def _even_out_inputs(inputs, i, c, res):
    uT = np.stack([r["uT_o"] for r in res])
    uT = uT.reshape(NC, 4, 128, NT, 128)
    glob = uT.transpose(1, 2, 3, 0, 4).reshape(4, 128, NT * NC, 128)
    ext = np.zeros((128, 4, NT, 144), np.float32)
    for j in range(NT):
        t = 8 * j + c
        ext[:, :, j, 16:144] = glob[:, :, t, :].transpose(1, 0, 2)
        if t > 0:
            ext[:, :, j, 0:16] = glob[:, :, t - 1, 112:128].transpose(1, 0, 2)
    invc = np.zeros((128, 4, 16), np.float32)
    for g in range(4):
        w = 2 ** (g + 1)
        if c == 0:
            invc[:, g, :] = 1.0 / np.minimum(np.arange(16) + 1, w).astype(np.float32)
        else:
            invc[:, g, :] = 1.0 / w
    kall = np.stack([r["kT_o"] for r in res])
    kall = np.ascontiguousarray(kall.transpose(1, 2, 0, 3, 4))
    vall = np.stack([r["v_o"] for r in res])
    vall = np.ascontiguousarray(vall.transpose(2, 1, 0, 3, 4))
    return {"uext": ext, "invc": invc, "wpool": np.ascontiguousarray(inputs["a_w_pool"][i].transpose(1, 0, 2)),
            "pscale": _vec(inputs["a_pool_scale"][i]), "qT": res[c]["qT_o"], "mask": _mask(c), "kall": kall, "vall": vall,
            "wout": _kmaj(inputs["a_w_out"][i])}, (kall, vall)


def _odd_in_inputs(inputs, i):
    return {"mixg": _vec(inputs["mix_norm"][2 * i + 1]), "wcin": _kmaj(inputs["c_w_in"][i])}


def _odd_out_inputs(inputs, i, c, res):
    uc = np.stack([r["uc_o"] for r in res]).reshape(NC, 8, 128, NT, 128)
    glob = uc.transpose(1, 2, 3, 0, 4).reshape(8, 128, NT * NC, 128)
    ext = np.zeros((128, 8, NT, 130), np.float32)
    for j in range(NT):
        t = 8 * j + c
        ext[:, :, j, 2:130] = glob[:, :, t, :].transpose(1, 0, 2)
        if t > 0:
            ext[:, :, j, 0:2] = glob[:, :, t - 1, 126:128].transpose(1, 0, 2)
    cw = np.ascontiguousarray(inputs["c_conv_w"][i].reshape(3, 8, 128).transpose(2, 1, 0))
    return {"cw": cw, "ucext": ext, "gbT": res[c]["gb_o"], "wcout": _kmaj(inputs["c_w_out"][i])}


def _run(name, in_maps):
    nc = _get_nc(name)
    res = run_bass_kernel_spmd(nc, in_maps, core_ids=list(range(NC)))
    return res.results


def kernel(**inputs):
    inputs = {k: np.asarray(v) for k, v in inputs.items()}
    x = inputs["x"][0].reshape(NT, NC, 128, D)
    xc = [np.ascontiguousarray(x[:, c].transpose(1, 0, 2)) for c in range(NC)]
    f1 = _ffn_inputs("f1", inputs, "ffn1", 0)
    res = _run("A", [{"x": xc[c], **_consts(), **f1, **_even_in_inputs(inputs, 0, c)} for c in range(NC)])
    f2 = _ffn_inputs("f2", inputs, "ffn2", 0)
    f1 = _ffn_inputs("f1", inputs, "ffn1", 1)
    oi = _odd_in_inputs(inputs, 0)
    maps = []
    for c in range(NC):
        eo, _ = _even_out_inputs(inputs, 0, c, res)
        maps.append({"x": res[c]["x_o"], **_consts(), **eo, **f2, **f1, **oi})
    res = _run("B", maps)
    f2 = _ffn_inputs("f2", inputs, "ffn2", 1)
    f1 = _ffn_inputs("f1", inputs, "ffn1", 2)
    maps = [{"x": res[c]["x_o"], **_consts(), **_odd_out_inputs(inputs, 0, c, res), **f2, **f1, **_even_in_inputs(inputs, 1, c)} for c in range(NC)]
    res = _run("C", maps)
    f2 = _ffn_inputs("f2", inputs, "ffn2", 2)
    f1 = _ffn_inputs("f1", inputs, "ffn1", 3)
    oi = _odd_in_inputs(inputs, 1)
    maps = []
    for c in range(NC):
        eo, _ = _even_out_inputs(inputs, 1, c, res)
        maps.append({"x": res[c]["x_o"], **_consts(), **eo, **f2, **f1, **oi})
    res = _run("B", maps)
    f2 = _ffn_inputs("f2", inputs, "ffn2", 3)
    maps = [{"x": res[c]["x_o"], **_consts(), **_odd_out_inputs(inputs, 1, c, res), **f2} for c in range(NC)]
    res = _run("D", maps)
    out = np.zeros((NT, NC, 128, D), np.float32)
    for c in range(NC):
        out[:, c] = res[c]["x_o"].transpose(1, 0, 2)
    return out.reshape(1, S, D)
```

```python
import numpy as np
import ml_dtypes
import concourse.bass as bass
import concourse.mybir as mybir
from concourse.bass_utils import run_bass_kernel_spmd

F32 = mybir.dt.float32
BF16 = mybir.dt.bfloat16
U8 = mybir.dt.uint8
AF = mybir.ActivationFunctionType
ALU = mybir.AluOpType

NC = 8
S = 16384
D = 1024
T = S // NC
NT = T // 128
DFF = 2816
NFC = DFF // 128
EPS = 1e-6
H = 8
DQK = 96
DV = 64
QL = 384
KVL = 256
MIXIN = 1184
ENGS = ("pe", "act", "dve", "pool", "sp")
FF_GROUPS = [(0, 3), (3, 6), (6, 9), (9, 12), (12, 15), (15, 18), (18, 20), (20, 22)]


class _Op:
    __slots__ = ("idx", "eng", "fn", "deps", "dma", "slot", "ms", "tok", "persist")

    def __init__(self, idx, eng, fn, deps, dma, slot):
        self.idx, self.eng, self.fn, self.deps, self.dma, self.slot = idx, eng, fn, deps, dma, slot
        self.ms = False
        self.tok = None
        self.persist = False


class Prog:
    def __init__(self, nc):
        self.nc = nc
        self.ops = []
        self.last_writer = {}
        self.readers = {}
        self.since_barrier = []

    def op(self, eng, fn, reads=(), writes=(), dma=0, slot=None):
        bank_reads = [k for k in reads if k[0] == "B"]
        if bank_reads:
            reads = [k for k in reads if k[0] != "B"]
            writes = list(writes) + bank_reads
        deps = set()
        for k in reads:
            w = self.last_writer.get(k)
            if w is not None:
                deps.add(w)
        for k in writes:
            w = self.last_writer.get(k)
            if w is not None:
                deps.add(w)
            deps.update(self.readers.get(k, ()))
        best = {}
        red = set()
        for dd in deps:
            q = self.ops[dd]
            if q.dma:
                red.add(dd)
            elif dd > best.get(q.eng, -1):
                best[q.eng] = dd
        red.update(best.values())
        deps = red
        idx = len(self.ops)
        o = _Op(idx, eng, fn, deps, dma, slot)
        allk = list(reads) + list(writes)
        o.persist = bool(dma) and len(allk) > 0 and all(k.startswith(self.PERSIST) for k in allk)
        self.ops.append(o)
        self.since_barrier.append(idx)
        for k in writes:
            self.last_writer[k] = idx
            self.readers[k] = []
        for k in reads:
            self.readers.setdefault(k, []).append(idx)
        return idx

    PERSIST = ("xs", "xo", "uTo", "qTo", "kTo", "vo_", "gbo", "uco")

    def barrier(self):
        prev = self.since_barrier
        self.since_barrier = []
        last = {}
        dmas = set()
        for i in prev:
            o = self.ops[i]
            if o.fn is None:
                continue
            if o.dma:
                if not o.persist:
                    dmas.add(i)
                else:
                    self.since_barrier.append(i)
            else:
                last[o.eng] = i
        for e in ENGS:
            deps = set(dmas)
            for e2, i in last.items():
                if e2 != e:
                    deps.add(i)
            idx = len(self.ops)
            self.ops.append(_Op(idx, e, None, deps, 0, None))
        self.last_writer = {k: v for k, v in self.last_writer.items() if k.startswith(self.PERSIST)}
        self.readers = {k: v for k, v in self.readers.items() if k.startswith(self.PERSIST)}

    def emit(self):
        nc = self.nc
        ops = self.ops
        for o in ops:
            for d in o.deps:
                q = ops[d]
                if q.dma or q.fn is None:
                    continue
                if q.eng == "pe" and o.eng == "pe" and not o.dma and o.fn is not None:
                    continue
                q.ms = True
        cnt = {e: 0 for e in ENGS}
        slots = {}
        for o in ops:
            if o.fn is None:
                continue
            if o.dma:
                c = slots.get(o.slot, 0) + 16 * o.dma
                slots[o.slot] = c
                o.tok = ("dma:" + o.slot, c)
            elif o.ms:
                cnt[o.eng] += 1
                o.tok = ("eng:" + o.eng, cnt[o.eng])
        names = ["eng:" + e for e in ENGS if cnt[e] > 0] + ["dma:" + s for s in slots]
        sems = {n: nc.alloc_semaphore(name="s%d" % i) for i, n in enumerate(names)}
        by_eng = {e: [o for o in ops if o.eng == e] for e in ENGS}

        def run(en, eng):
            seen = {}
            for o in by_eng[en]:
                need = {}
                for d in o.deps:
                    q = ops[d]
                    if q.tok is None:
                        continue
                    if q.eng == "pe" and en == "pe" and not q.dma and not o.dma:
                        continue
                    sn, c = q.tok
                    if c > need.get(sn, 0):
                        need[sn] = c
                for sn, c in need.items():
                    if seen.get(sn, 0) >= c:
                        continue
                    eng.wait_ge(sems[sn], c)
                    seen[sn] = c
                if o.fn is None:
                    continue
                if o.dma:
                    o.fn(eng, sems[o.tok[0]])
                else:
                    ins = o.fn(eng)
                    if o.ms:
                        ins.then_inc(sems[o.tok[0]], 1)

        with nc.Block() as block:
            @block.tensor
            def _(e):
                run("pe", e)

            @block.scalar
            def _(e):
                run("act", e)

            @block.vector
            def _(e):
                run("dve", e)

            @block.gpsimd
            def _(e):
                run("pool", e)

            @block.sync
            def _(e):
                run("sp", e)


class Arena:
    def __init__(self, ar, nbytes):
        self.ar = ar
        self.n = nbytes
        self.top = 0

    def alloc(self, shape, dt, parts=128):
        esz = 2 if dt == BF16 else 4
        free = int(np.prod(shape[1:]))
        nb = free * esz
        off = (self.top + 63) // 64 * 64
        assert off + nb <= self.n, ("SBUF arena overflow", off, nb, self.n)
        self.top = off + nb
        v = self.ar[0:shape[0], off:off + nb].bitcast(dt)
        if len(shape) == 3:
            v = v.rearrange("p (a b) -> p a b", a=shape[1])
        elif len(shape) == 4:
            v = v.rearrange("p (a b c) -> p a b c", a=shape[1], b=shape[2])
        return v


class Ctx:
    pass


def _dma(C, eng, out, in_, reads, writes, slot):
    C.p.op(eng, lambda e, s: e.dma_start(out=out, in_=in_).then_inc(s, 16), reads=reads, writes=writes, dma=1, slot=slot)


def wload(C, dst, src, wkey, eng="pool"):
    shape = tuple(dst.shape)
    n = int(np.prod(shape[1:]))
    assert n <= 1536, shape
    i = C.stg_n % len(C.stg)
    C.stg_n += 1
    st = C.stg[i][0:shape[0], 0:n]
    if len(shape) == 3:
        st = st.rearrange("p (a b) -> p a b", a=shape[1])
    _dma(C, "sp", st, src, [], ["stg%d" % i], "stg%d" % i)
    C.p.op(eng, lambda e: e.tensor_copy(out=dst, in_=st), reads=["stg%d" % i], writes=[wkey])


def xk(tb, k):
    return ["xnT%d_%d" % (4 * tb + i, k) for i in range(4)]


def ph_load_x(C, x_d):
    for j in range(NT):
        _dma(C, "sp", C.xs[:, j, :], x_d[:, j, :], [], ["xs%d" % j], "xs%d" % j)


def ph_store_x(C, x_d, keyname):
    for j in range(NT):
        _dma(C, "sp", x_d[:, j, :], C.xs[:, j, :], ["xs%d" % j], [keyname + "%d" % j], "xo%d" % (j % 4))
    C.outkeys += [keyname + "%d" % j for j in range(NT)]


def ph_consts(C):
    p = C.p
    A = C.A
    C.ident = A.alloc([128, 128], BF16)
    C.ones = A.alloc([128, 128], BF16)
    _dma(C, "pool", C.ident, C.d["ident"], [], ["ident"], "c_ident")
    _dma(C, "pool", C.ones, C.d["ones"], [], ["ones"], "c_ones")


def ph_norm(C, g_d, tag):
    p = C.p
    xs, xnT = C.xs, C.xnT
    gT = C.gT
    _dma(C, "sp", gT, g_d, [], ["gT"], "gT")
    ss, rstd, junk = C.ss, C.rstd, C.junk
    for j in range(NT):
        p.op("act", lambda e, j=j: e.activation(out=junk, in_=xs[:, j, :], func=AF.Square, accum_out=ss[:, j:j + 1]),
             reads=["xs%d" % j], writes=["junk", "ss"])
    p.op("act", lambda e: e.activation(out=rstd, in_=ss, func=AF.Ln, bias=C.epsb, scale=1.0 / D), reads=["ss", "epsb"], writes=["rstd"])
    p.op("act", lambda e: e.activation(out=rstd, in_=rstd, func=AF.Exp, scale=-0.5), reads=["rstd"], writes=["rstd"])
    for j in range(NT):
        xb = C.xb[j % 2]
        bt = C.BT[j % 2]
        p.op("dve", lambda e, j=j, xb=xb: e.tensor_scalar(out=xb, in0=xs[:, j, :], scalar1=rstd[:, j:j + 1], scalar2=None, op0=ALU.mult),
             reads=["xs%d" % j, "rstd"], writes=["xb%d" % (j % 2)])
        for k in range(8):
            p.op("pe", lambda e, k=k, xb=xb, bt=bt: e.transpose(out=bt[:, k * 128:(k + 1) * 128], in_=xb[:, k * 128:(k + 1) * 128], identity=C.ident),
                 reads=["xb%d" % (j % 2), "ident"], writes=["BT%d" % (j % 2)])
        for k in range(8):
            if j % 2 == 0:
                p.op("act", lambda e, k=k, j=j, bt=bt: e.activation(out=xnT[:, k, j * 128:(j + 1) * 128], in_=bt[:, k * 128:(k + 1) * 128],
                                                                func=AF.Copy, scale=gT[:, k:k + 1]),
                     reads=["BT%d" % (j % 2), "gT"], writes=["xnT%d_%d" % (j, k)])
            else:
                p.op("dve", lambda e, k=k, j=j, bt=bt: e.tensor_scalar(out=xnT[:, k, j * 128:(j + 1) * 128], in0=bt[:, k * 128:(k + 1) * 128],
                                                                   scalar1=gT[:, k:k + 1], scalar2=None, op0=ALU.mult),
                     reads=["BT%d" % (j % 2), "gT"], writes=["xnT%d_%d" % (j, k)])


def ph_ffn(C, g_d, wg_d, wu_d, wd_d, tag):
    p = C.p
    ph_norm(C, g_d, tag)
    xs, xnT = C.xs, C.xnT
    NG = len(FF_GROUPS)

    def loads_gu(gi):
        c0, c1 = FF_GROUPS[gi]
        gs = c1 - c0
        sl = gi % 2
        wg, wu = C.wg[sl], C.wu[sl]
        for kh in range(2):
            wload(C, wg[:, kh * 4:kh * 4 + 4, 0:gs * 128], wg_d[:, kh * 4:kh * 4 + 4, c0 * 128:c1 * 128], "wg%d" % sl)
        for kh in range(2):
            wload(C, wu[:, kh * 4:kh * 4 + 4, 0:gs * 128], wu_d[:, kh * 4:kh * 4 + 4, c0 * 128:c1 * 128], "wu%d" % sl)

    def loads_d(gi):
        c0, c1 = FF_GROUPS[gi]
        sl = gi % 2
        wd = C.wd[sl]
        for c in range(c1 - c0):
            wload(C, wd[:, c, :], wd_d[:, c0 + c, :], "wd%d" % sl)

    cnt = {"gu": 0, "dn": 0}

    def gu_units(gi):
        c0, c1 = FF_GROUPS[gi]
        gs = c1 - c0
        sl = gi % 2
        wg, wu, hT = C.wg[sl], C.wu[sl], C.hT[sl]
        units = []
        for tb in range(4):
            for c in range(gs):
                b = cnt["gu"] % 2
                cnt["gu"] += 1
                psg, psu = C.B[b], C.B[2 + b]
                u = []
                for k in range(8):
                    u.append(("pe", lambda e, k=k, c=c, tb=tb, psg=psg, wg=wg: e.matmul(psg, wg[:, k, c * 128:(c + 1) * 128], xnT[:, k, tb * 512:(tb + 1) * 512], start=(k == 0), stop=(k == 7)),
                              ["wg%d" % sl] + xk(tb, k), ["B%d" % b]))
                for k in range(8):
                    u.append(("pe", lambda e, k=k, c=c, tb=tb, psu=psu, wu=wu: e.matmul(psu, wu[:, k, c * 128:(c + 1) * 128], xnT[:, k, tb * 512:(tb + 1) * 512], start=(k == 0), stop=(k == 7)),
                              ["wu%d" % sl] + xk(tb, k), ["B%d" % (2 + b)]))
                sg = C.sg[b]
                u.append(("act", lambda e, sg=sg, psg=psg: e.activation(out=sg, in_=psg, func=AF.Silu), ["B%d" % b], ["sg%d" % b]))
                u.append(("dve", lambda e, sg=sg, psu=psu, hT=hT, c=c, tb=tb: e.tensor_tensor(out=hT[:, c, tb * 512:(tb + 1) * 512], in0=sg, in1=psu, op=ALU.mult),
                          ["sg%d" % b, "B%d" % (2 + b)], ["hT%d_%d_%d" % (sl, c, tb)]))
                units.append(u)
        return units

    def dn_units(gi):
        c0, c1 = FF_GROUPS[gi]
        gs = c1 - c0
        sl = gi % 2
        wd, hT = C.wd[sl], C.hT[sl]
        units = []
        for j in range(NT):
            for hf in range(2):
                b = 4 + (cnt["dn"] % 2)
                cnt["dn"] += 1
                psy = C.B[b]
                u = []
                for c in range(gs):
                    u.append(("pe", lambda e, gs=gs, c=c, j=j, hf=hf, psy=psy, hT=hT, wd=wd: e.matmul(psy, hT[:, c, j * 128:(j + 1) * 128], wd[:, c, hf * 512:(hf + 1) * 512], start=(c == 0), stop=(c == gs - 1)),
                              ["hT%d_%d_%d" % (sl, c, j // 4), "wd%d" % sl], ["B%d" % b]))
                u.append(("dve", lambda e, j=j, hf=hf, psy=psy: e.scalar_tensor_tensor(out=xs[:, j, hf * 512:(hf + 1) * 512], in0=psy, scalar=0.5, in1=xs[:, j, hf * 512:(hf + 1) * 512], op0=ALU.mult, op1=ALU.add),
                          ["B%d" % b, "xs%d" % j], ["xs%d" % j]))
                units.append(u)
        return units

    def emit(u):
        for (eng, fn, rd, wr) in u:
            p.op(eng, fn, reads=rd, writes=wr)

    loads_gu(0)
    loads_gu(1)
    loads_d(0)
    loads_d(1)
    for u in gu_units(0):
        emit(u)
    for gi in range(NG):
        if gi + 2 < NG:
            loads_gu(gi + 2)
        dn = dn_units(gi)
        gu = gu_units(gi + 1) if gi + 1 < NG else []
        nd, ng = len(dn), len(gu)
        di = 0
        for ui in range(ng):
            emit(gu[ui])
            tgt = (ui + 1) * nd // ng
            while di < tgt:
                emit(dn[di])
                di += 1
        while di < nd:
            emit(dn[di])
            di += 1
        if gi + 2 < NG:
            loads_d(gi + 2)


def _rmsT(C, src_keys, sq, nchunk, rows, width, inv_n, out_rstd, bank, bkey):
    p = C.p
    for i in range(nchunk):
        p.op("pe", lambda e, i=i: e.matmul(bank[0:rows, 0:width], C.ones[0:rows, 0:rows], sq[0:rows, i, 0:width], start=(i == 0), stop=(i == nchunk - 1)),
             reads=["ones"] + src_keys, writes=[bkey])
    p.op("act", lambda e: e.activation(out=out_rstd[0:rows, 0:width], in_=bank[0:rows, 0:width], func=AF.Ln, bias=C.epsb[0:rows, :], scale=inv_n),
         reads=[bkey, "epsb"], writes=["rstdT"])
    p.op("act", lambda e: e.activation(out=out_rstd[0:rows, 0:width], in_=out_rstd[0:rows, 0:width], func=AF.Exp, scale=-0.5),
         reads=["rstdT"], writes=["rstdT"])


def ph_even_in(C, li):
    p, A, d = C.p, C.A, C.d
    C.xnT = A.alloc([128, 8, T], BF16)
    ph_norm(C, d["mixg"], "mix")
    xnT = C.xnT
    win = A.alloc([128, 8, MIXIN], BF16)
    for k in range(8):
        wload(C, win[:, k, :], d["win"][:, k, :], "win")
    wrope = A.alloc([128, 8, 96], BF16)
    p.op("pool", lambda e: e.memset(wrope, 0.0), writes=["wrope"])
    _dma(C, "pool", wrope[:, :, 64:96], d["win"][:, :, 1152:1184], [], ["wrope"], "wrope")
    wq = A.alloc([128, 3, 768], BF16)
    for i in range(3):
        wload(C, wq[:, i, :], d["wq"][:, i, :], "wq")
    wkp = A.alloc([128, 2, 8, 96], BF16)
    p.op("pool", lambda e: e.memset(wkp, 0.0), writes=["wkp"])
    wv = A.alloc([128, 2, 8, 64], BF16)
    wkv4 = d["wkv"].rearrange("p i (h c) -> p i h c", c=128)
    for i in range(2):
        _dma(C, "pool", wkp[:, i, :, 0:64], wkv4[:, i, :, 0:64], [], ["wkp"], "wkp")
        _dma(C, "pool", wv[:, i, :, :], wkv4[:, i, :, 64:128], [], ["wv"], "wv")
    rm = A.alloc([96, 96], BF16)
    _dma(C, "pool", rm, d["rm"], [], ["rm"], "rm")
    gql = A.alloc([128, 3], F32)
    gkl = A.alloc([128, 2], F32)
    gqh = A.alloc([96, 1], F32)
    gkh = A.alloc([96, 1], F32)
    _dma(C, "sp", gql, d["gql"], [], ["gsm"], "gsm")
    _dma(C, "sp", gkl, d["gkl"], [], ["gsm"], "gsm")
    _dma(C, "sp", gqh, d["gqh"], [], ["gsm"], "gsm")
    _dma(C, "sp", gkh, d["gkh"], [], ["gsm"], "gsm")
    cosb = [A.alloc([96, 512], F32) for _ in range(2)]
    sinb = [A.alloc([96, 512], F32) for _ in range(2)]
    ust = [A.alloc([128, 512], F32) for _ in range(2)]
    lat = A.alloc([128, 3, 512], F32)
    sq = A.alloc([128, 3, 512], BF16)
    rstdT = A.alloc([128, 512], F32)
    latn = A.alloc([128, 3, 512], BF16)
    kvn = A.alloc([128, 2, 512], BF16)
    hq = [A.alloc([96, 512], BF16) for _ in range(4)]
    t1 = [A.alloc([96, 512], F32) for _ in range(2)]
    t2 = [A.alloc([96, 512], F32) for _ in range(2)]
    sqh = [A.alloc([96, 512], BF16) for _ in range(2)]
    rsh = [A.alloc([96, 512], F32) for _ in range(2)]
    vst = [A.alloc([128, 8, 65], BF16) for _ in range(2)]
    for s in range(2):
        p.op("pool", lambda e, s=s: e.memset(vst[s], 1.0), writes=["vst%d" % s])
    nu = 0
    nh = 0
    for tb in range(4):
        tsl = slice(tb * 512, (tb + 1) * 512)
        cosT, sinT = cosb[tb % 2], sinb[tb % 2]
        ckey, skey2 = "cos%d" % (tb % 2), "sin%d" % (tb % 2)
        _dma(C, "sp", cosT[64:96, :], d["cosT"][64:96, tsl], [], [ckey], ckey)
        _dma(C, "sp", sinT[64:96, :], d["sinT"][64:96, tsl], [], [skey2], skey2)
        for m in range(4):
            bk = C.B[nu % 2]
            bkey = "B%d" % (nu % 2)
            st = ust[nu % 2]
            skey = "ust%d" % (nu % 2)
            nu += 1
            for k in range(8):
                p.op("pe", lambda e, tsl=tsl, k=k, m=m, bk=bk: e.matmul(bk, win[:, k, m * 128:(m + 1) * 128], xnT[:, k, tsl], start=(k == 0), stop=(k == 7)),
                     reads=["win"] + xk(tb, k), writes=[bkey])
            p.op("act", lambda e, bk=bk, st=st: e.activation(out=st, in_=bk, func=AF.Copy), reads=[bkey], writes=[skey])
            _dma(C, "sp", d["uT_o"][m][:, tsl], st, [skey], ["uTo_%d_%d" % (m, tb)], skey)
            C.outkeys.append("uTo_%d_%d" % (m, tb))
        for (nm, m0, nch, inv_n, gl, outn) in (("q", 4, 3, 1.0 / QL, gql, latn), ("kv", 7, 2, 1.0 / KVL, gkl, kvn)):
            for i in range(nch):
                bk = C.B[2 + i]
                bkey = "B%d" % (2 + i)
                m = m0 + i
                for k in range(8):
                    p.op("pe", lambda e, tsl=tsl, k=k, m=m, bk=bk: e.matmul(bk, win[:, k, m * 128:(m + 1) * 128], xnT[:, k, tsl], start=(k == 0), stop=(k == 7)),
                         reads=["win"] + xk(tb, k), writes=[bkey])
                p.op("act", lambda e, i=i, bk=bk: e.activation(out=lat[:, i, :], in_=bk, func=AF.Copy), reads=[bkey], writes=["lat"])
                p.op("act", lambda e, i=i, bk=bk: e.activation(out=sq[:, i, :], in_=bk, func=AF.Square), reads=[bkey], writes=["sq"])
            _rmsT(C, ["sq"], sq, nch, 128, 512, inv_n, rstdT, C.B[5], "B5")
            for i in range(nch):
                p.op("dve", lambda e, i=i, gl=gl, outn=outn: e.scalar_tensor_tensor(out=outn[:, i, :], in0=lat[:, i, :], scalar=gl[:, i:i + 1], in1=rstdT, op0=ALU.mult, op1=ALU.mult),
                     reads=["lat", "rstdT", "gsm"], writes=[nm + "n"])
        def head_stages(which, h, b, hqt, hkey):
            bk = C.B[b]
            bkey = "B%d" % b
            ssb, sskey = (C.B[5], "B5") if b == 0 else (C.B[3], "B3")
            rb, rkey = (C.B[4], "B4") if b == 0 else (C.B[2], "B2")
            sqb, rsb, t1b, t2b = sqh[b], rsh[b], t1[b], t2[b]
            st = [[] for _ in range(9)]
            if which == "q":
                for i in range(3):
                    st[0].append(("pe", lambda e, i=i, h=h, bk=bk: e.matmul(bk[0:96, :], wq[:, i, h * 96:(h + 1) * 96], latn[:, i, :], start=(i == 0), stop=(i == 2)),
                                  ["wq", "qn"], [bkey]))
                gh = gqh
            else:
                for k in range(8):
                    st[0].append(("pe", lambda e, tsl=tsl, k=k, bk=bk: e.matmul(bk[0:96, :], wrope[:, k, :], xnT[:, k, tsl], start=(k == 0), stop=False),
                                  ["wrope"] + xk(tb, k), [bkey]))
                for i in range(2):
                    st[0].append(("pe", lambda e, i=i, h=h, bk=bk: e.matmul(bk[0:96, :], wkp[:, i, h, :], kvn[:, i, :], start=False, stop=(i == 1)),
                                  ["wkp", "kvn"], [bkey]))
                gh = gkh
            st[1].append(("act", lambda e, bk=bk, sqb=sqb: e.activation(out=sqb, in_=bk[0:96, :], func=AF.Square), [bkey], ["sqh%d" % b]))
            st[2].append(("pe", lambda e, ssb=ssb, sqb=sqb: e.matmul(ssb[0:96, :], C.ones[0:96, 0:96], sqb, start=True, stop=True), ["ones", "sqh%d" % b], [sskey]))
            st[3].append(("act", lambda e, ssb=ssb, rsb=rsb: e.activation(out=rsb, in_=ssb[0:96, :], func=AF.Ln, bias=C.epsb[0:96, :], scale=1.0 / DQK), [sskey, "epsb"], ["rsh%d" % b]))
            st[3].append(("act", lambda e, rsb=rsb: e.activation(out=rsb, in_=rsb, func=AF.Exp, scale=-0.5), ["rsh%d" % b], ["rsh%d" % b]))
            st[4].append(("dve", lambda e, bk=bk, hqt=hqt, gh=gh, rsb=rsb: e.scalar_tensor_tensor(out=hqt, in0=bk[0:96, :], scalar=gh[:, 0:1], in1=rsb, op0=ALU.mult, op1=ALU.mult),
                          [bkey, "rsh%d" % b, "gsm"], [hkey]))
            st[5].append(("pe", lambda e, hqt=hqt, rb=rb: e.matmul(rb[0:96, :], rm, hqt, start=True, stop=True), ["rm", hkey], [rkey]))
            st[6].append(("pool", lambda e, cosT=cosT, hqt=hqt, t1b=t1b: e.tensor_tensor(out=t1b[64:96, :], in0=hqt[64:96, :], in1=cosT[64:96, :], op=ALU.mult),
                          [hkey, ckey], ["t1_%d" % b]))
            st[6].append(("dve", lambda e, sinT=sinT, rb=rb, t2b=t2b: e.tensor_tensor(out=t2b[64:96, :], in0=rb[64:96, :], in1=sinT[64:96, :], op=ALU.mult),
                          [rkey, skey2], ["t2_%d" % b]))
            st[7].append(("dve", lambda e, hqt=hqt, t1b=t1b, t2b=t2b: e.tensor_tensor(out=hqt[64:96, :], in0=t1b[64:96, :], in1=t2b[64:96, :], op=ALU.add),
                          ["t1_%d" % b, "t2_%d" % b], [hkey]))
            return st

        for which in ("q", "k"):
            for hp in range(H // 2):
                pp = nh % 2
                nh += 1
                sts = []
                for b in range(2):
                    h = hp * 2 + b
                    hqt = hq[pp * 2 + b]
                    hkey = "hq%d" % (pp * 2 + b)
                    sts.append((h, hqt, hkey, head_stages(which, h, b, hqt, hkey)))
                for si in range(9):
                    for (h, hqt, hkey, st) in sts:
                        for (eng, fn, rd, wr) in st[si]:
                            p.op(eng, fn, reads=rd, writes=wr)
                for (h, hqt, hkey, st) in sts:
                    if which == "q":
                        ok = "qTo_%d_%d" % (h, tb)
                        _dma(C, "sp", d["qT_o"][h][:, tsl], hqt, [hkey], [ok], hkey)
                        C.outkeys.append(ok)
                    else:
                        for jj in range(4):
                            ok = "kTo_%d_%d_%d" % (h, tb, jj)
                            _dma(C, "sp", d["kT_o"][tb * 4 + jj][:, h, :], hqt[:, jj * 128:(jj + 1) * 128], [hkey], [ok], hkey)
                            C.outkeys.append(ok)
        for jj in range(4):
            j = tb * 4 + jj
            b = j % 2
            bk = C.B[2 + b]
            bkey = "B%d" % (2 + b)
            for i in range(2):
                p.op("pe", lambda e, i=i, jj=jj, bk=bk: e.matmul(bk, kvn[:, i, jj * 128:(jj + 1) * 128], wv[:, i, :, :], start=(i == 0), stop=(i == 1)),
                     reads=["kvn", "wv"], writes=[bkey])
            p.op("dve", lambda e, b=b, bk=bk: e.tensor_copy(out=vst[b][:, :, 0:64], in_=bk.rearrange("p (h c) -> p h c", c=64)),
                 reads=[bkey], writes=["vst%d" % b])
            ok = "vo_%d" % j
            _dma(C, "sp", d["v_o"][:, j, :, :], vst[b], ["vst%d" % b], [ok], "vst%d" % b)
            C.outkeys.append(ok)


def ph_even_out(C, li):
    p, A, d = C.p, C.A, C.d
    scale = DQK ** -0.5
    catT = A.alloc([128, 8, T], BF16)
    mark = A.top
    uextb = [A.alloc([128, NT, 144], F32) for _ in range(2)]
    wa = A.alloc([128, NT, 144], F32)
    wb = A.alloc([128, NT, 144], F32)
    pooled = A.alloc([128, 4, T], BF16)
    invc = A.alloc([128, 4, 16], F32)
    _dma(C, "sp", invc, d["invc"], [], ["invc"], "invc")
    wpool = A.alloc([128, 4, 128], BF16)
    _dma(C, "pool", wpool, d["wpool"], [], ["wpool"], "wpool")
    pscale = A.alloc([128, 4], F32)
    _dma(C, "sp", pscale, d["pscale"], [], ["pscale"], "pscale")
    tmp16 = A.alloc([128, 16], F32)
    for g in range(4):
        w = 2 ** (g + 1)
        src = uextb[g % 2]
        _dma(C, "sp", src, d["uext"][:, g, :, :], [], ["uext%d" % (g % 2)], "uext%d" % (g % 2))
        bufs = [wa, wb]
        sh = 1
        cur = src
        curkey = "uext%d" % (g % 2)
        nb = 0
        while sh < w:
            dst = bufs[nb % 2]
            dkey = "w%d" % (nb % 2)
            nb += 1
            p.op("dve", lambda e, cur=cur, dst=dst, sh=sh: e.tensor_tensor(out=dst[:, :, 2 * sh - 1:144], in0=cur[:, :, 2 * sh - 1:144], in1=cur[:, :, sh - 1:144 - sh], op=ALU.add),
                 reads=[curkey], writes=[dkey])
            if sh > 1:
                pass
            cur = dst
            curkey = dkey
            sh *= 2
        pv = pooled[:, g, :].rearrange("p (j t) -> p j t", t=128)
        p.op("dve", lambda e, cur=cur, src=src, pv=pv, w=w: e.scalar_tensor_tensor(out=pv, in0=cur[:, :, 16:144], scalar=1.0 / w, in1=src[:, :, 16:144], op0=ALU.mult, op1=ALU.subtract),
             reads=[curkey, "uext%d" % (g % 2)], writes=["pooled%d" % g])
        p.op("dve", lambda e, cur=cur, g=g: e.tensor_tensor(out=tmp16, in0=cur[:, 0, 16:32], in1=invc[:, g, :], op=ALU.mult),
             reads=[curkey, "invc"], writes=["tmp16"])
        p.op("dve", lambda e, src=src, g=g: e.tensor_tensor(out=pooled[:, g, 0:16], in0=tmp16, in1=src[:, 0, 16:32], op=ALU.subtract),
             reads=["tmp16", "uext%d" % (g % 2), "pooled%d" % g], writes=["pooled%d" % g])
    n = 0
    for g in range(4):
        for tb in range(4):
            b = n % 2
            n += 1
            bk = C.B[b]
            p.op("pe", lambda e, g=g, tb=tb, bk=bk: e.matmul(bk, wpool[:, g, :], pooled[:, g, tb * 512:(tb + 1) * 512], start=True, stop=True),
                 reads=["wpool", "pooled%d" % g], writes=["B%d" % b])
            p.op("act", lambda e, g=g, tb=tb, bk=bk: e.activation(out=catT[:, g, tb * 512:(tb + 1) * 512], in_=bk, func=AF.Copy, scale=pscale[:, g:g + 1]),
                 reads=["B%d" % b, "pscale"], writes=["catT"])
    p.barrier()
    A.top = mark
    qT = A.alloc([96, H, T], BF16)
    for h in range(H):
        _dma(C, "sp", qT[:, h, :], d["qT"][h], [], ["qT"], "qT%d" % (h % 4))
    mask = A.alloc([128, 8, 128], BF16)
    _dma(C, "pool", mask, d["mask"], [], ["mask"], "mask")
    krow = [A.alloc([96, 8, 4, 128], BF16) for _ in range(2)]
    vrow = [A.alloc([128, 8, 4, 65], BF16) for _ in range(2)]
    atok = A.alloc([128, NT, 512], BF16)
    rden = A.alloc([128, 4], F32)
    nrow = 0
    ns = 0
    LAG = 1
    pT = [A.alloc([128, 2, 512], BF16) for _ in range(3)]
    SD = [(C.SA.rearrange("p (b c) -> p b c", b=2), ["B4", "B5"]),
          (C.SB.rearrange("p (b c) -> p b c", b=2), ["BT0", "BT1"])]
    for sb in range(4):
        for hg in range(2):
            steps = []
            for jr in range(4 * sb + 4):
                sl = nrow % 2
                nrow += 1
                kr, vr = krow[sl], vrow[sl]
                pre = [("sp", lambda e, s, kr=kr, jr=jr, hg=hg: [e.dma_start(out=kr[:, r, :, :], in_=d["kall"][jr][:, r, hg * 4:(hg + 1) * 4, :]).then_inc(s, 16) for r in range(8)],
                        [], ["krow%d" % sl], 8, "krow%d" % sl),
                       ("sp", lambda e, s, vr=vr, jr=jr, hg=hg: [e.dma_start(out=vr[:, r, :, :], in_=d["vall"][jr][:, r, hg * 4:(hg + 1) * 4, :]).then_inc(s, 16) for r in range(8)],
                        [], ["vrow%d" % sl], 8, "vrow%d" % sl)]
                i0 = max(0, jr - 4 * sb)
                diag = jr >= 4 * sb
                ncols = (4 - i0) * 128
                q0 = (4 * sb + i0) * 128
                kk = "krow%d" % sl
                for r in range(8):
                    for hp in range(2):
                        sd, sdkeys = SD[ns % 2]
                        pt = pT[ns % 3]
                        ptk = "pT%d" % (ns % 3)
                        ns += 1
                        S_ops = list(pre)
                        pre = []
                        for half in range(2):
                            hh = hp * 2 + half
                            h = hg * 4 + hh
                            bk = sd[:, half, :]
                            bkey = sdkeys[half]
                            if diag:
                                S_ops.append(("pe", lambda e, bk=bk, kr=kr, r=r, hh=hh, h=h, q0=q0: e.matmul(bk[:, 0:128], kr[:, r, hh, :], qT[:, h, q0:q0 + 128], start=True, stop=False),
                                              [kk, "qT"], [bkey], 0, None))
                                S_ops.append(("pe", lambda e, bk=bk, r=r: e.matmul(bk[:, 0:128], C.ident, mask[:, r, :], start=False, stop=True),
                                              ["ident", "mask"], [bkey], 0, None))
                                if ncols > 128:
                                    S_ops.append(("pe", lambda e, bk=bk, kr=kr, r=r, hh=hh, h=h, q0=q0, ncols=ncols: e.matmul(bk[:, 128:ncols], kr[:, r, hh, :], qT[:, h, q0 + 128:q0 + ncols], start=True, stop=True),
                                                  [kk, "qT"], [bkey], 0, None))
                            else:
                                S_ops.append(("pe", lambda e, bk=bk, kr=kr, r=r, hh=hh, h=h, q0=q0, ncols=ncols: e.matmul(bk[:, 0:ncols], kr[:, r, hh, :], qT[:, h, q0:q0 + ncols], start=True, stop=True),
                                              [kk, "qT"], [bkey], 0, None))
                        S_ops.append(("act", lambda e, sd=sd, pt=pt, ncols=ncols: e.activation(out=pt[:, :, 0:ncols], in_=sd[:, :, 0:ncols], func=AF.Exp, scale=scale),
                                      list(sdkeys), [ptk], 0, None))
                        PV_ops = []
                        for half in range(2):
                            hh = hp * 2 + half
                            for i in range(i0, 4):
                                first = (jr == 0 and r == 0 and hh == 0)
                                last = (jr == 4 * sb + i and r == 7)
                                PV_ops.append(("pe", lambda e, i0=i0, i=i, pt=pt, vr=vr, r=r, hh=hh, half=half, first=first, last=last: e.matmul(C.B[i][:, hh * 65:(hh + 1) * 65], pt[:, half, (i - i0) * 128:(i - i0 + 1) * 128], vr[:, r, hh, :], start=first, stop=last, skip_group_check=True),
                                               [ptk, "vrow%d" % sl], ["B%d" % i], 0, None))
                        steps.append((S_ops, PV_ops))
            for n in range(len(steps) + LAG):
                if n < len(steps):
                    for (eng, fn, rd, wr, dma, slot) in steps[n][0]:
                        p.op(eng, fn, reads=rd, writes=wr, dma=dma, slot=slot)
                if n - LAG >= 0:
                    for (eng, fn, rd, wr, dma, slot) in steps[n - LAG][1]:
                        p.op(eng, fn, reads=rd, writes=wr, dma=dma, slot=slot)
            for i in range(4):
                j = 4 * sb + i
                acc = C.B[i][:, 0:260].rearrange("p (h c) -> p h c", c=65)
                p.op("dve", lambda e, acc=acc: e.reciprocal(out=rden, in_=acc[:, :, 64]), reads=["B%d" % i], writes=["rden"])
                for hh in range(4):
                    h = hg * 4 + hh
                    p.op("dve", lambda e, acc=acc, hh=hh, h=h, j=j: e.tensor_scalar(out=atok[:, j, h * 64:(h + 1) * 64], in0=acc[:, hh, 0:64], scalar1=rden[:, hh:hh + 1], scalar2=None, op0=ALU.mult),
                         reads=["B%d" % i, "rden"], writes=["atok%d" % j])
    for j in range(NT):
        bt = C.BT[j % 2]
        for m in range(4):
            p.op("pe", lambda e, j=j, m=m, bt=bt: e.transpose(out=bt[:, m * 128:(m + 1) * 128], in_=atok[:, j, m * 128:(m + 1) * 128], identity=C.ident),
                 reads=["atok%d" % j, "ident"], writes=["BT%d" % (j % 2)])
        p.op("dve", lambda e, j=j, bt=bt: e.tensor_copy(out=catT[:, 4:8, j * 128:(j + 1) * 128], in_=bt[:, 0:512].rearrange("p (m t) -> p m t", t=128)),
             reads=["BT%d" % (j % 2)], writes=["catT"])
    p.barrier()
    A.top = mark
    _out_proj(C, catT, "catT", d["wout"])


def _out_proj(C, srcT, skey, w_d):
    p, A = C.p, C.A
    xs = C.xs
    wo = A.alloc([128, 8, 1024], BF16)
    for k in range(8):
        wload(C, wo[:, k, :], w_d[:, k, :], "wo")
    n = 0
    for j in range(NT):
        for hf in range(2):
            b = n % 2
            n += 1
            bk = C.B[b]
            for m in range(8):
                p.op("pe", lambda e, j=j, hf=hf, m=m, bk=bk: e.matmul(bk, srcT[:, m, j * 128:(j + 1) * 128], wo[:, m, hf * 512:(hf + 1) * 512], start=(m == 0), stop=(m == 7)),
                     reads=[skey, "wo"], writes=["B%d" % b])
            p.op("dve", lambda e, j=j, hf=hf, bk=bk: e.tensor_tensor(out=xs[:, j, hf * 512:(hf + 1) * 512], in0=bk, in1=xs[:, j, hf * 512:(hf + 1) * 512], op=ALU.add),
                 reads=["B%d" % b, "xs%d" % j], writes=["xs%d" % j])


def ph_odd_in(C, li):
    p, A, d = C.p, C.A, C.d
    C.xnT = A.alloc([128, 8, T], BF16)
    ph_norm(C, d["mixg"], "mix")
    xnT = C.xnT
    wst = [A.alloc([128, 8, 384], BF16) for _ in range(2)]
    gst = [A.alloc([128, 512], F32) for _ in range(2)]
    cst = [A.alloc([128, 512], F32) for _ in range(2)]
    ust = [A.alloc([128, 512], F32) for _ in range(2)]
    n = 0
    for m in range(8):
        sl = m % 2
        w = wst[sl]
        for q in range(3):
            wload(C, w[:, :, q * 128:(q + 1) * 128], d["wcin"][:, :, q * 1024 + m * 128:q * 1024 + (m + 1) * 128], "wst%d" % sl)
        for tb in range(4):
            tsl = slice(tb * 512, (tb + 1) * 512)
            b = n % 2
            n += 1
            banks = [C.B[b], C.B[2 + b], C.B[4 + b]]
            bkeys = ["B%d" % b, "B%d" % (2 + b), "B%d" % (4 + b)]
            for q in range(3):
                for k in range(8):
                    p.op("pe", lambda e, tsl=tsl, q=q, k=k, w=w, bk=banks[q]: e.matmul(bk, w[:, k, q * 128:(q + 1) * 128], xnT[:, k, tsl], start=(k == 0), stop=(k == 7)),
                         reads=["wst%d" % sl] + xk(tb, k), writes=[bkeys[q]])
            p.op("act", lambda e, b=b, bk=banks[0]: e.activation(out=gst[b], in_=bk, func=AF.Copy), reads=[bkeys[0]], writes=["gst%d" % b])
            p.op("act", lambda e, b=b, bk=banks[1]: e.activation(out=cst[b], in_=bk, func=AF.Copy), reads=[bkeys[1]], writes=["cst%d" % b])
            p.op("dve", lambda e, b=b, bk=banks[2]: e.tensor_tensor(out=ust[b], in0=cst[b], in1=bk, op=ALU.mult), reads=["cst%d" % b, bkeys[2]], writes=["ust%d" % b])
            ok = "gbo_%d_%d" % (m, tb)
            _dma(C, "sp", d["gb_o"][m][:, tsl], gst[b], ["gst%d" % b], [ok], "gst%d" % b)
            C.outkeys.append(ok)
            ok = "uco_%d_%d" % (m, tb)
            _dma(C, "sp", d["uc_o"][m][:, tsl], ust[b], ["ust%d" % b], [ok], "ust%d" % b)
            C.outkeys.append(ok)


def ph_odd_out(C, li):
    p, A, d = C.p, C.A, C.d
    vT = A.alloc([128, 8, T], BF16)
    cw = A.alloc([128, 8, 3], F32)
    _dma(C, "sp", cw, d["cw"], [], ["cw"], "cw")
    ue = [A.alloc([128, NT, 130], F32) for _ in range(2)]
    gb = [A.alloc([128, T], F32) for _ in range(2)]
    y = [A.alloc([128, NT, 128], F32) for _ in range(2)]
    for m in range(8):
        sl = m % 2
        _dma(C, "sp", ue[sl], d["ucext"][:, m, :, :], [], ["ue%d" % sl], "ue%d" % sl)
        _dma(C, "sp", gb[sl], d["gbT"][m], [], ["gb%d" % sl], "gb%d" % sl)
        eng = "dve"
        u, yy, g = ue[sl], y[sl], gb[sl]
        p.op(eng, lambda e, u=u, yy=yy, m=m: e.tensor_scalar(out=yy, in0=u[:, :, 0:128], scalar1=cw[:, m, 0:1], scalar2=None, op0=ALU.mult),
             reads=["ue%d" % sl, "cw"], writes=["y%d" % sl])
        p.op(eng, lambda e, u=u, yy=yy, m=m: e.scalar_tensor_tensor(out=yy, in0=u[:, :, 1:129], scalar=cw[:, m, 1:2], in1=yy, op0=ALU.mult, op1=ALU.add),
             reads=["ue%d" % sl, "cw", "y%d" % sl], writes=["y%d" % sl])
        p.op(eng, lambda e, u=u, yy=yy, m=m: e.scalar_tensor_tensor(out=yy, in0=u[:, :, 2:130], scalar=cw[:, m, 2:3], in1=yy, op0=ALU.mult, op1=ALU.add),
             reads=["ue%d" % sl, "cw", "y%d" % sl], writes=["y%d" % sl])
        p.op(eng, lambda e, yy=yy, g=g, m=m: e.tensor_tensor(out=vT[:, m, :], in0=yy.rearrange("p j t -> p (j t)"), in1=g, op=ALU.mult),
             reads=["y%d" % sl, "gb%d" % sl], writes=["vT"])
    _out_proj(C, vT, "vT", d["wcout"])


def build_launch(spec):
    nc = bass.Bass("TRN2", target_bir_lowering=False)
    C = Ctx()
    C.nc = nc
    C.p = Prog(nc)
    C.d = {}
    C.outkeys = []
    for name, (shape, dt) in spec["inputs"].items():
        C.d[name] = nc.dram_tensor(name, list(shape), dt, kind="ExternalInput").ap()
    for name, (shape, dt) in spec["outputs"].items():
        C.d[name] = nc.dram_tensor(name, list(shape), dt, kind="ExternalOutput").ap()
    import contextlib
    with contextlib.ExitStack() as st:
        nbytes = 204 * 1024
        ar = st.enter_context(nc.sbuf_tensor("arena", [128, nbytes], U8))
        C.A = Arena(ar, nbytes)
        C.B = [st.enter_context(nc.psum_tensor("bank%d" % i, [128, 512], F32))[:] for i in range(4)]
        C.SA = st.enter_context(nc.psum_tensor("bankSA", [128, 1024], F32))[:]
        C.SB = st.enter_context(nc.psum_tensor("bankSB", [128, 1024], F32))[:]
        C.B += [C.SA[:, 0:512], C.SA[:, 512:1024]]
        C.BT = [C.SB[:, 0:512].bitcast(BF16), C.SB[:, 512:1024].bitcast(BF16)]
        A = C.A
        C.xs = A.alloc([128, NT, D], F32)
        C.gT = A.alloc([128, 8], F32)
        C.ss = A.alloc([128, NT], F32)
        C.rstd = A.alloc([128, NT], F32)
        C.junk = A.alloc([128, D], BF16)
        C.xb = [A.alloc([128, D], BF16) for _ in range(2)]
        C.epsb = A.alloc([128, 1], F32)
        C.stg = [A.alloc([128, 1536], F32) for _ in range(3)]
        C.stg_n = 0
        C.p.op("pool", lambda e: e.memset(C.epsb, EPS), writes=["epsb"])
        ph_consts(C)
        base = A.top
        for ph in spec["phases"]:
            if ph in (_load_x, _store_x):
                ph(C)
                continue
            C.p.barrier()
            A.top = base
            ph(C)
        C.p.op("sp", None, reads=list(C.outkeys))
        C.p.op("pool", None, reads=list(C.outkeys))
        C.p.emit()
    return nc


def alloc_ffn(C):
    A = C.A
    C.xnT = A.alloc([128, 8, T], BF16)
    C.wg = [A.alloc([128, 8, 384], BF16) for _ in range(2)]
    C.wu = [A.alloc([128, 8, 384], BF16) for _ in range(2)]
    C.wd = [A.alloc([128, 3, D], BF16) for _ in range(2)]
    C.hT = [A.alloc([128, 3, T], BF16) for _ in range(2)]
    C.sg = [A.alloc([128, 512], F32) for _ in range(2)]


def mk_ffn(which):
    def f(C):
        alloc_ffn(C)
        ph_ffn(C, C.d[which + "_g"], C.d[which + "_wg"], C.d[which + "_wu"], C.d[which + "_wd"], which)
    return f


FFN_IN = lambda w: {w + "_g": ((128, 8), F32), w + "_wg": ((128, 8, DFF), F32), w + "_wu": ((128, 8, DFF), F32), w + "_wd": ((128, NFC, D), F32)}
X_IO = ((128, NT, D), F32)
CONST_IN = {"ident": ((128, 128), F32), "ones": ((128, 128), F32)}
EVEN_IN_W = {"mixg": ((128, 8), F32), "win": ((128, 8, MIXIN), F32), "wq": ((128, 3, 768), F32), "wkv": ((128, 2, 1024), F32),
             "rm": ((96, 96), F32), "gql": ((128, 3), F32), "gkl": ((128, 2), F32), "gqh": ((96, 1), F32), "gkh": ((96, 1), F32),
             "cosT": ((96, T), F32), "sinT": ((96, T), F32)}
EVEN_IN_OUT = {"uT_o": ((4, 128, T), F32), "qT_o": ((H, 96, T), BF16), "kT_o": ((NT, 96, H, 128), BF16), "v_o": ((128, NT, H, 65), BF16)}
EVEN_OUT_IN = {"uext": ((128, 4, NT, 144), F32), "invc": ((128, 4, 16), F32), "wpool": ((128, 4, 128), F32), "pscale": ((128, 4), F32),
               "qT": ((H, 96, T), BF16), "mask": ((128, 8, 128), F32), "kall": ((NT, 96, 8, H, 128), BF16), "vall": ((NT, 128, 8, H, 65), BF16),
               "wout": ((128, 8, D), F32)}
ODD_IN_W = {"mixg": ((128, 8), F32), "wcin": ((128, 8, 3 * D), F32)}
ODD_IN_OUT = {"gb_o": ((8, 128, T), F32), "uc_o": ((8, 128, T), F32)}
ODD_OUT_IN = {"cw": ((128, 8, 3), F32), "ucext": ((128, 8, NT, 130), F32), "gbT": ((8, 128, T), F32), "wcout": ((128, 8, D), F32)}


def _load_x(C):
    ph_load_x(C, C.d["x"])


def _store_x(C):
    ph_store_x(C, C.d["x_o"], "xo")


def spec_A():
    return dict(inputs={"x": X_IO, **CONST_IN, **FFN_IN("f1"), **EVEN_IN_W},
                outputs={"x_o": X_IO, **EVEN_IN_OUT},
                phases=[_load_x, mk_ffn("f1"), _store_x, lambda C: ph_even_in(C, 0)])


def spec_B():
    return dict(inputs={"x": X_IO, **CONST_IN, **EVEN_OUT_IN, **FFN_IN("f2"), **FFN_IN("f1"), **ODD_IN_W},
                outputs={"x_o": X_IO, **ODD_IN_OUT},
                phases=[_load_x, lambda C: ph_even_out(C, 0), mk_ffn("f2"), mk_ffn("f1"), _store_x, lambda C: ph_odd_in(C, 1)])


def spec_C():
    return dict(inputs={"x": X_IO, **CONST_IN, **ODD_OUT_IN, **FFN_IN("f2"), **FFN_IN("f1"), **EVEN_IN_W},
                outputs={"x_o": X_IO, **EVEN_IN_OUT},
                phases=[_load_x, lambda C: ph_odd_out(C, 1), mk_ffn("f2"), mk_ffn("f1"), _store_x, lambda C: ph_even_in(C, 2)])


def spec_D():
    return dict(inputs={"x": X_IO, **CONST_IN, **ODD_OUT_IN, **FFN_IN("f2")},
                outputs={"x_o": X_IO},
                phases=[_load_x, lambda C: ph_odd_out(C, 3), mk_ffn("f2"), _store_x])


_NC_CACHE = {}


def _get_nc(name):
    if name not in _NC_CACHE:
        _NC_CACHE[name] = build_launch({"A": spec_A, "B": spec_B, "C": spec_C, "D": spec_D}[name]())
    return _NC_CACHE[name]


def _kmaj(w, p=128):
    K, N = w.shape
    return np.ascontiguousarray(w.reshape(K // p, p, N).transpose(1, 0, 2))


def _vec(g, p=128):
    return np.ascontiguousarray(g.reshape(-1, p).T)


def _ffn_inputs(prefix, inputs, which, layer):
    return {prefix + "_g": _vec(inputs[which + "_norm"][layer]),
            prefix + "_wg": _kmaj(inputs[which + "_w_gate"][layer]),
            prefix + "_wu": _kmaj(inputs[which + "_w_up"][layer]),
            prefix + "_wd": _kmaj(inputs[which + "_w_down"][layer])}


def _consts():
    return {"ident": np.eye(128, dtype=np.float32), "ones": np.ones((128, 128), np.float32)}


def _rope_tables(c):
    pos = (np.arange(NT)[:, None] * NC + c) * 128 + np.arange(128)[None, :]
    pos = pos.reshape(-1).astype(np.float32)
    inv = (np.float32(10000.0) ** (-np.arange(0, 32, 2, dtype=np.float32) / np.float32(32))).astype(np.float32)
    ang = (pos[:, None] * inv[None, :]).astype(np.float32)
    cos = np.cos(ang).astype(np.float32).T
    sin = np.sin(ang).astype(np.float32).T
    cosT = np.zeros((96, T), np.float32)
    sinT = np.zeros((96, T), np.float32)
    cosT[64:80] = cos
    cosT[80:96] = cos
    sinT[64:80] = sin
    sinT[80:96] = sin
    return cosT, sinT


def _rm():
    rm = np.zeros((96, 96), np.float32)
    for i in range(16):
        rm[80 + i, 64 + i] = -1.0
        rm[64 + i, 80 + i] = 1.0
    return rm


def _even_in_inputs(inputs, i, c):
    cosT, sinT = _rope_tables(c)
    return {"mixg": _vec(inputs["mix_norm"][2 * i]), "win": _kmaj(inputs["a_w_in"][i]), "wq": _kmaj(inputs["a_w_q_up"][i]),
            "wkv": _kmaj(inputs["a_w_kv_up"][i]), "rm": _rm(), "gql": _vec(inputs["a_q_a_norm"][i]), "gkl": _vec(inputs["a_kv_a_norm"][i]),
            "gqh": np.ascontiguousarray(inputs["a_q_head_norm"][i].reshape(96, 1)), "gkh": np.ascontiguousarray(inputs["a_k_head_norm"][i].reshape(96, 1)),
            "cosT": cosT, "sinT": sinT}


def _mask(c):
    m = np.zeros((128, 8, 128), np.float32)
    for r in range(8):
        if r > c:
            m[:, r, :] = -30000.0
        elif r == c:
            m[:, r, :] = np.where(np.arange(128)[:, None] > np.arange(128)[None, :], -30000.0, 0.0)
    return m


def _even_out_inputs(inputs, i, c, res):
    uT = np.stack([r["uT_o"] for r in res])
    uT = uT.reshape(NC, 4, 128, NT, 128)
    glob = uT.transpose(1, 2, 3, 0, 4).reshape(4, 128, NT * NC, 128)
    ext = np.zeros((128, 4, NT, 144), np.float32)
    for j in range(NT):
        t = 8 * j + c
        ext[:, :, j, 16:144] = glob[:, :, t, :].transpose(1, 0, 2)
        if t > 0:
            ext[:, :, j, 0:16] = glob[:, :, t - 1, 112:128].transpose(1, 0, 2)
    invc = np.zeros((128, 4, 16), np.float32)
    for g in range(4):
        w = 2 ** (g + 1)
        if c == 0:
            invc[:, g, :] = 1.0 / np.minimum(np.arange(16) + 1, w).astype(np.float32)
        else:
            invc[:, g, :] = 1.0 / w
    kall = np.stack([r["kT_o"] for r in res])
    kall = np.ascontiguousarray(kall.transpose(1, 2, 0, 3, 4))
    vall = np.stack([r["v_o"] for r in res])
    vall = np.ascontiguousarray(vall.transpose(2, 1, 0, 3, 4))
    return {"uext": ext, "invc": invc, "wpool": np.ascontiguousarray(inputs["a_w_pool"][i].transpose(1, 0, 2)),
            "pscale": _vec(inputs["a_pool_scale"][i]), "qT": res[c]["qT_o"], "mask": _mask(c), "kall": kall, "vall": vall,
            "wout": _kmaj(inputs["a_w_out"][i])}, (kall, vall)


def _odd_in_inputs(inputs, i):
    return {"mixg": _vec(inputs["mix_norm"][2 * i + 1]), "wcin": _kmaj(inputs["c_w_in"][i])}


def _odd_out_inputs(inputs, i, c, res):
    uc = np.stack([r["uc_o"] for r in res]).reshape(NC, 8, 128, NT, 128)
    glob = uc.transpose(1, 2, 3, 0, 4).reshape(8, 128, NT * NC, 128)
    ext = np.zeros((128, 8, NT, 130), np.float32)
    for j in range(NT):
        t = 8 * j + c
        ext[:, :, j, 2:130] = glob[:, :, t, :].transpose(1, 0, 2)
        if t > 0:
            ext[:, :, j, 0:2] = glob[:, :, t - 1, 126:128].transpose(1, 0, 2)
    cw = np.ascontiguousarray(inputs["c_conv_w"][i].reshape(3, 8, 128).transpose(2, 1, 0))
    return {"cw": cw, "ucext": ext, "gbT": res[c]["gb_o"], "wcout": _kmaj(inputs["c_w_out"][i])}


def _run(name, in_maps):
    nc = _get_nc(name)
    res = run_bass_kernel_spmd(nc, in_maps, core_ids=list(range(NC)))
    return res.results


def kernel(**inputs):
    inputs = {k: np.asarray(v) for k, v in inputs.items()}
    x = inputs["x"][0].reshape(NT, NC, 128, D)
    xc = [np.ascontiguousarray(x[:, c].transpose(1, 0, 2)) for c in range(NC)]
    f1 = _ffn_inputs("f1", inputs, "ffn1", 0)
    res = _run("A", [{"x": xc[c], **_consts(), **f1, **_even_in_inputs(inputs, 0, c)} for c in range(NC)])
    f2 = _ffn_inputs("f2", inputs, "ffn2", 0)
    f1 = _ffn_inputs("f1", inputs, "ffn1", 1)
    oi = _odd_in_inputs(inputs, 0)
    maps = []
    for c in range(NC):
        eo, _ = _even_out_inputs(inputs, 0, c, res)
        maps.append({"x": res[c]["x_o"], **_consts(), **eo, **f2, **f1, **oi})
    res = _run("B", maps)
    f2 = _ffn_inputs("f2", inputs, "ffn2", 1)
    f1 = _ffn_inputs("f1", inputs, "ffn1", 2)
    maps = [{"x": res[c]["x_o"], **_consts(), **_odd_out_inputs(inputs, 0, c, res), **f2, **f1, **_even_in_inputs(inputs, 1, c)} for c in range(NC)]
    res = _run("C", maps)
    f2 = _ffn_inputs("f2", inputs, "ffn2", 2)
    f1 = _ffn_inputs("f1", inputs, "ffn1", 3)
    oi = _odd_in_inputs(inputs, 1)
    maps = []
    for c in range(NC):
        eo, _ = _even_out_inputs(inputs, 1, c, res)
        maps.append({"x": res[c]["x_o"], **_consts(), **eo, **f2, **f1, **oi})
    res = _run("B", maps)
    f2 = _ffn_inputs("f2", inputs, "ffn2", 3)
    maps = [{"x": res[c]["x_o"], **_consts(), **_odd_out_inputs(inputs, 1, c, res), **f2} for c in range(NC)]
    res = _run("D", maps)
    out = np.zeros((NT, NC, 128, D), np.float32)
    for c in range(NC):
        out[:, c] = res[c]["x_o"].transpose(1, 0, 2)
    return out.reshape(1, S, D)
```

```python
import numpy as np
import ml_dtypes
import concourse.bass as bass
import concourse.mybir as mybir
from concourse.bass_utils import run_bass_kernel_spmd

F32 = mybir.dt.float32
BF16 = mybir.dt.bfloat16
U8 = mybir.dt.uint8
AF = mybir.ActivationFunctionType
ALU = mybir.AluOpType

NC = 8
S = 16384
D = 1024
T = S // NC
NT = T // 128
DFF = 2816
NFC = DFF // 128
EPS = 1e-6
H = 8
DQK = 96
DV = 64
QL = 384
KVL = 256
MIXIN = 1184
ENGS = ("pe", "act", "dve", "pool", "sp")
FF_GROUPS = [(0, 3), (3, 6), (6, 9), (9, 12), (12, 15), (15, 18), (18, 20), (20, 22)]


class _Op:
    __slots__ = ("idx", "eng", "fn", "deps", "dma", "slot", "ms", "tok", "persist")

    def __init__(self, idx, eng, fn, deps, dma, slot):
        self.idx, self.eng, self.fn, self.deps, self.dma, self.slot = idx, eng, fn, deps, dma, slot
        self.ms = False
        self.tok = None
        self.persist = False


class Prog:
    def __init__(self, nc):
        self.nc = nc
        self.ops = []
        self.last_writer = {}
        self.readers = {}
        self.since_barrier = []

    def op(self, eng, fn, reads=(), writes=(), dma=0, slot=None):
        bank_reads = [k for k in reads if k[0] == "B"]
        if bank_reads:
            reads = [k for k in reads if k[0] != "B"]
            writes = list(writes) + bank_reads
        deps = set()
        for k in reads:
            w = self.last_writer.get(k)
            if w is not None:
                deps.add(w)
        for k in writes:
            w = self.last_writer.get(k)
            if w is not None:
                deps.add(w)
            deps.update(self.readers.get(k, ()))
        best = {}
        red = set()
        for dd in deps:
            q = self.ops[dd]
            if q.dma:
                red.add(dd)
            elif dd > best.get(q.eng, -1):
                best[q.eng] = dd
        red.update(best.values())
        deps = red
        idx = len(self.ops)
        o = _Op(idx, eng, fn, deps, dma, slot)
        allk = list(reads) + list(writes)
        o.persist = bool(dma) and len(allk) > 0 and all(k.startswith(self.PERSIST) for k in allk)
        self.ops.append(o)
        self.since_barrier.append(idx)
        for k in writes:
            self.last_writer[k] = idx
            self.readers[k] = []
        for k in reads:
            self.readers.setdefault(k, []).append(idx)
        return idx

    PERSIST = ("xs", "xo", "uTo", "qTo", "kTo", "vo_", "gbo", "uco")

    def barrier(self):
        prev = self.since_barrier
        self.since_barrier = []
        last = {}
        dmas = set()
        for i in prev:
            o = self.ops[i]
            if o.fn is None:
                continue
            if o.dma:
                if not o.persist:
                    dmas.add(i)
                else:
                    self.since_barrier.append(i)
            else:
                last[o.eng] = i
        for e in ENGS:
            deps = set(dmas)
            for e2, i in last.items():
                if e2 != e:
                    deps.add(i)
            idx = len(self.ops)
            self.ops.append(_Op(idx, e, None, deps, 0, None))
        self.last_writer = {k: v for k, v in self.last_writer.items() if k.startswith(self.PERSIST)}
        self.readers = {k: v for k, v in self.readers.items() if k.startswith(self.PERSIST)}

    def emit(self):
        nc = self.nc
        ops = self.ops
        for o in ops:
            for d in o.deps:
                q = ops[d]
                if q.dma or q.fn is None:
                    continue
                if q.eng == "pe" and o.eng == "pe" and not o.dma and o.fn is not None:
                    continue
                q.ms = True
        cnt = {e: 0 for e in ENGS}
        slots = {}
        for o in ops:
            if o.fn is None:
                continue
            if o.dma:
                c = slots.get(o.slot, 0) + 16 * o.dma
                slots[o.slot] = c
                o.tok = ("dma:" + o.slot, c)
            elif o.ms:
                cnt[o.eng] += 1
                o.tok = ("eng:" + o.eng, cnt[o.eng])
        names = ["eng:" + e for e in ENGS if cnt[e] > 0] + ["dma:" + s for s in slots]
        sems = {n: nc.alloc_semaphore(name="s%d" % i) for i, n in enumerate(names)}
        by_eng = {e: [o for o in ops if o.eng == e] for e in ENGS}

        def run(en, eng):
            seen = {}
            for o in by_eng[en]:
                need = {}
                for d in o.deps:
                    q = ops[d]
                    if q.tok is None:
                        continue
                    if q.eng == "pe" and en == "pe" and not q.dma and not o.dma:
                        continue
                    sn, c = q.tok
                    if c > need.get(sn, 0):
                        need[sn] = c
                for sn, c in need.items():
                    if seen.get(sn, 0) >= c:
                        continue
                    eng.wait_ge(sems[sn], c)
                    seen[sn] = c
                if o.fn is None:
                    continue
                if o.dma:
                    o.fn(eng, sems[o.tok[0]])
                else:
                    ins = o.fn(eng)
                    if o.ms:
                        ins.then_inc(sems[o.tok[0]], 1)

        with nc.Block() as block:
            @block.tensor
            def _(e):
                run("pe", e)

            @block.scalar
            def _(e):
                run("act", e)

            @block.vector
            def _(e):
                run("dve", e)

            @block.gpsimd
            def _(e):
                run("pool", e)

            @block.sync
            def _(e):
                run("sp", e)


class Arena:
    def __init__(self, ar, nbytes):
        self.ar = ar
        self.n = nbytes
        self.top = 0

    def alloc(self, shape, dt, parts=128):
        esz = 2 if dt == BF16 else 4
        free = int(np.prod(shape[1:]))
        nb = free * esz
        off = (self.top + 63) // 64 * 64
        assert off + nb <= self.n, ("SBUF arena overflow", off, nb, self.n)
        self.top = off + nb
        v = self.ar[0:shape[0], off:off + nb].bitcast(dt)
        if len(shape) == 3:
            v = v.rearrange("p (a b) -> p a b", a=shape[1])
        elif len(shape) == 4:
            v = v.rearrange("p (a b c) -> p a b c", a=shape[1], b=shape[2])
        return v


class Ctx:
    pass


def _dma(C, eng, out, in_, reads, writes, slot):
    C.p.op(eng, lambda e, s: e.dma_start(out=out, in_=in_).then_inc(s, 16), reads=reads, writes=writes, dma=1, slot=slot)


def wload(C, dst, src, wkey, eng="act"):
    shape = tuple(dst.shape)
    n = int(np.prod(shape[1:]))
    assert n <= 1536, shape
    i = C.stg_n % len(C.stg)
    C.stg_n += 1
    st = C.stg[i][0:shape[0], 0:n]
    if len(shape) == 3:
        st = st.rearrange("p (a b) -> p a b", a=shape[1])
    _dma(C, "sp", st, src, [], ["stg%d" % i], "stg%d" % i)
    if eng == "act":
        C.p.op(eng, lambda e: e.activation(out=dst, in_=st, func=AF.Copy), reads=["stg%d" % i], writes=[wkey])
    else:
        C.p.op(eng, lambda e: e.tensor_copy(out=dst, in_=st), reads=["stg%d" % i], writes=[wkey])


def xk(tb, k):
    return ["xnT%d_%d" % (4 * tb + i, k) for i in range(4)]


def ph_load_x(C, x_d):
    for j in range(NT):
        _dma(C, "sp", C.xs[:, j, :], x_d[:, j, :], [], ["xs%d" % j], "xs%d" % j)


def ph_store_x(C, x_d, keyname):
    for j in range(NT):
        _dma(C, "sp", x_d[:, j, :], C.xs[:, j, :], ["xs%d" % j], [keyname + "%d" % j], "xo%d" % (j % 4))
    C.outkeys += [keyname + "%d" % j for j in range(NT)]


def ph_consts(C):
    p = C.p
    A = C.A
    C.ident = A.alloc([128, 128], BF16)
    C.ones = A.alloc([128, 128], BF16)
    _dma(C, "pool", C.ident, C.d["ident"], [], ["ident"], "c_ident")
    _dma(C, "pool", C.ones, C.d["ones"], [], ["ones"], "c_ones")


def ph_norm(C, g_d, tag):
    p = C.p
    xs, xnT = C.xs, C.xnT
    gT = C.gT
    gfull = C.gfull
    _dma(C, "sp", gT, g_d, [], ["gT"], "gT")
    for k in range(8):
        p.op("pool", lambda e, k=k: e.tensor_scalar(out=gfull[:, k, :], in0=C.onesf, scalar1=gT[:, k:k + 1], scalar2=None, op0=ALU.mult),
             reads=["gT", "onesf"], writes=["gfull"])
    ss, rstd = C.ss, C.rstd
    for j in range(NT):
        col = (j // 2) + (8 if j % 2 else 0)
        jb = C.xb[j % 2]
        if j % 2 == 0:
            p.op("act", lambda e, j=j, col=col, jb=jb: e.activation(out=jb, in_=xs[:, j, :], func=AF.Square, accum_out=ss[:, col:col + 1]),
                 reads=["xs%d" % j], writes=["xb0", "ssA"])
        else:
            p.op("dve", lambda e, j=j, col=col, jb=jb: e.scalar_tensor_tensor(out=jb, in0=xs[:, j, :], scalar=1.0, in1=xs[:, j, :], op0=ALU.mult, op1=ALU.mult, accum_out=ss[:, col:col + 1]),
                 reads=["xs%d" % j], writes=["xb1", "ssD"])
    p.op("act", lambda e: e.activation(out=rstd, in_=ss, func=AF.Ln, bias=C.epsb, scale=1.0 / D), reads=["ssA", "ssD", "epsb"], writes=["rstd"])
    p.op("act", lambda e: e.activation(out=rstd, in_=rstd, func=AF.Exp, scale=-0.5), reads=["rstd"], writes=["rstd"])
    for j in range(NT):
        col = (j // 2) + (8 if j % 2 else 0)
        xb = C.xb[j % 2]
        bt = C.BT[j % 2]
        p.op("act", lambda e, j=j, col=col, xb=xb: e.activation(out=xb, in_=xs[:, j, :], func=AF.Copy, scale=rstd[:, col:col + 1]),
             reads=["xs%d" % j, "rstd"], writes=["xb%d" % (j % 2)])
        for k in range(8):
            p.op("pe", lambda e, k=k, xb=xb, bt=bt: e.transpose(out=bt[:, k * 128:(k + 1) * 128], in_=xb[:, k * 128:(k + 1) * 128], identity=C.ident),
                 reads=["xb%d" % (j % 2), "ident"], writes=["BT%d" % (j % 2)])
        p.op("dve", lambda e, j=j, bt=bt: e.tensor_tensor(out=xnT[:, :, j * 128:(j + 1) * 128], in0=bt.rearrange("p (k t) -> p k t", t=128), in1=gfull, op=ALU.mult),
             reads=["BT%d" % (j % 2), "gfull"], writes=["xnT%d_%d" % (j, k) for k in range(8)])


def ph_ffn(C, g_d, wg_d, wu_d, wd_d, tag):
    p = C.p
    ph_norm(C, g_d, tag)
    xs, xnT = C.xs, C.xnT
    NG = len(FF_GROUPS)

    def loads_gu(gi):
        c0, c1 = FF_GROUPS[gi]
        gs = c1 - c0
        sl = gi % 2
        wg, wu = C.wg[sl], C.wu[sl]
        out = []
        for kh in range(2):
            out.append(lambda kh=kh: wload(C, wg[:, kh * 4:kh * 4 + 4, 0:gs * 128], wg_d[:, kh * 4:kh * 4 + 4, c0 * 128:c1 * 128], "wg%d" % sl))
        for kh in range(2):
            out.append(lambda kh=kh: wload(C, wu[:, kh * 4:kh * 4 + 4, 0:gs * 128], wu_d[:, kh * 4:kh * 4 + 4, c0 * 128:c1 * 128], "wu%d" % sl))
        return out

    def loads_d(gi):
        c0, c1 = FF_GROUPS[gi]
        sl = gi % 2
        wd = C.wd[sl]
        return [lambda c=c: wload(C, wd[:, c, :], wd_d[:, c0 + c, :], "wd%d" % sl) for c in range(c1 - c0)]

    cnt = {"gu": 0, "dn": 0}

    def gu_units(gi):
        c0, c1 = FF_GROUPS[gi]
        gs = c1 - c0
        sl = gi % 2
        wg, wu, hT = C.wg[sl], C.wu[sl], C.hT[sl]
        units = []
        for tb in range(4):
            for c in range(gs):
                b = cnt["gu"] % 2
                cnt["gu"] += 1
                psg, psu = C.B[b], C.B[2 + b]
                u = []
                for k in range(8):
                    u.append(("pe", lambda e, k=k, c=c, tb=tb, psg=psg, wg=wg: e.matmul(psg, wg[:, k, c * 128:(c + 1) * 128], xnT[:, k, tb * 512:(tb + 1) * 512], start=(k == 0), stop=(k == 7)),
                              ["wg%d" % sl] + xk(tb, k), ["B%d" % b]))
                for k in range(8):
                    u.append(("pe", lambda e, k=k, c=c, tb=tb, psu=psu, wu=wu: e.matmul(psu, wu[:, k, c * 128:(c + 1) * 128], xnT[:, k, tb * 512:(tb + 1) * 512], start=(k == 0), stop=(k == 7)),
                              ["wu%d" % sl] + xk(tb, k), ["B%d" % (2 + b)]))
                sg = C.sg[b]
                u.append(("act", lambda e, sg=sg, psg=psg: e.activation(out=sg, in_=psg, func=AF.Silu), ["B%d" % b], ["sg%d" % b]))
                u.append(("dve", lambda e, sg=sg, psu=psu, hT=hT, c=c, tb=tb: e.tensor_tensor(out=hT[:, c, tb * 512:(tb + 1) * 512], in0=sg, in1=psu, op=ALU.mult),
                          ["sg%d" % b, "B%d" % (2 + b)], ["hT%d_%d_%d" % (sl, c, tb)]))
                units.append(u)
        return units

    def dn_units(gi):
        c0, c1 = FF_GROUPS[gi]
        gs = c1 - c0
        sl = gi % 2
        wd, hT = C.wd[sl], C.hT[sl]
        units = []
        for j in range(NT):
            for hf in range(2):
                b = 4 + (cnt["dn"] % 2)
                cnt["dn"] += 1
                psy = C.B[b]
                u = []
                for c in range(gs):
                    u.append(("pe", lambda e, gs=gs, c=c, j=j, hf=hf, psy=psy, hT=hT, wd=wd: e.matmul(psy, hT[:, c, j * 128:(j + 1) * 128], wd[:, c, hf * 512:(hf + 1) * 512], start=(c == 0), stop=(c == gs - 1)),
                              ["hT%d_%d_%d" % (sl, c, j // 4), "wd%d" % sl], ["B%d" % b]))
                u.append(("dve", lambda e, j=j, hf=hf, psy=psy: e.scalar_tensor_tensor(out=xs[:, j, hf * 512:(hf + 1) * 512], in0=psy, scalar=0.5, in1=xs[:, j, hf * 512:(hf + 1) * 512], op0=ALU.mult, op1=ALU.add),
                          ["B%d" % b, "xs%d" % j], ["xs%d" % j]))
                units.append(u)
        return units

    def emit(u):
        for (eng, fn, rd, wr) in u:
            p.op(eng, fn, reads=rd, writes=wr)

    for f in loads_gu(0) + loads_gu(1) + loads_d(0):
        f()
    for u in gu_units(0):
        emit(u)
    for gi in range(NG):
        pend = []
        if gi + 2 < NG:
            pend += loads_gu(gi + 2)
        if gi + 1 < NG:
            pend += loads_d(gi + 1)
        dn = dn_units(gi)
        gu = gu_units(gi + 1) if gi + 1 < NG else []
        nd, ng = len(dn), len(gu)
        di = 0
        pi = 0
        for ui in range(ng):
            emit(gu[ui])
            tgt = (ui + 1) * nd // ng
            while di < tgt:
                emit(dn[di])
                di += 1
            ptgt = (ui + 1) * len(pend) // ng
            while pi < ptgt:
                pend[pi]()
                pi += 1
        while di < nd:
            emit(dn[di])
            di += 1
        while pi < len(pend):
            pend[pi]()
            pi += 1


def _rmsT(C, src_keys, sq, nchunk, rows, width, inv_n, out_rstd, bank, bkey):
    p = C.p
    for i in range(nchunk):
        p.op("pe", lambda e, i=i: e.matmul(bank[0:rows, 0:width], C.ones[0:rows, 0:rows], sq[0:rows, i, 0:width], start=(i == 0), stop=(i == nchunk - 1)),
             reads=["ones"] + src_keys, writes=[bkey])
    p.op("act", lambda e: e.activation(out=out_rstd[0:rows, 0:width], in_=bank[0:rows, 0:width], func=AF.Ln, bias=C.epsb[0:rows, :], scale=inv_n),
         reads=[bkey, "epsb"], writes=["rstdT"])
    p.op("act", lambda e: e.activation(out=out_rstd[0:rows, 0:width], in_=out_rstd[0:rows, 0:width], func=AF.Exp, scale=-0.5),
         reads=["rstdT"], writes=["rstdT"])


def ph_even_in(C, li):
    p, A, d = C.p, C.A, C.d
    C.xnT = A.alloc([128, 8, T], BF16)
    ph_norm(C, d["mixg"], "mix")
    xnT = C.xnT
    win = A.alloc([128, 8, MIXIN], BF16)
    for k in range(8):
        wload(C, win[:, k, :], d["win"][:, k, :], "win")
    wrope = A.alloc([128, 8, 96], BF16)
    p.op("pool", lambda e: e.memset(wrope, 0.0), writes=["wrope"])
    _dma(C, "pool", wrope[:, :, 64:96], d["win"][:, :, 1152:1184], [], ["wrope"], "wrope")
    wq = A.alloc([128, 3, 768], BF16)
    for i in range(3):
        wload(C, wq[:, i, :], d["wq"][:, i, :], "wq")
    wkp = A.alloc([128, 2, 8, 96], BF16)
    p.op("pool", lambda e: e.memset(wkp, 0.0), writes=["wkp"])
    wv = A.alloc([128, 2, 8, 64], BF16)
    wkv4 = d["wkv"].rearrange("p i (h c) -> p i h c", c=128)
    for i in range(2):
        _dma(C, "pool", wkp[:, i, :, 0:64], wkv4[:, i, :, 0:64], [], ["wkp"], "wkp")
        _dma(C, "pool", wv[:, i, :, :], wkv4[:, i, :, 64:128], [], ["wv"], "wv")
    rm = A.alloc([96, 96], BF16)
    _dma(C, "pool", rm, d["rm"], [], ["rm"], "rm")
    gql = A.alloc([128, 3], F32)
    gkl = A.alloc([128, 2], F32)
    gqh = A.alloc([96, 1], F32)
    gkh = A.alloc([96, 1], F32)
    _dma(C, "sp", gql, d["gql"], [], ["gsm"], "gsm")
    _dma(C, "sp", gkl, d["gkl"], [], ["gsm"], "gsm")
    _dma(C, "sp", gqh, d["gqh"], [], ["gsm"], "gsm")
    _dma(C, "sp", gkh, d["gkh"], [], ["gsm"], "gsm")
    cosb = [A.alloc([96, 512], F32) for _ in range(2)]
    sinb = [A.alloc([96, 512], F32) for _ in range(2)]
    ust = [A.alloc([128, 512], F32) for _ in range(2)]
    lat = A.alloc([128, 3, 512], F32)
    sq = A.alloc([128, 3, 512], BF16)
    rstdT = A.alloc([128, 512], F32)
    latn = A.alloc([128, 3, 512], BF16)
    kvn = A.alloc([128, 2, 512], BF16)
    hq = [A.alloc([96, 512], BF16) for _ in range(4)]
    t1 = [A.alloc([96, 512], F32) for _ in range(2)]
    t2 = [A.alloc([96, 512], F32) for _ in range(2)]
    sqh = [A.alloc([96, 512], BF16) for _ in range(2)]
    rsh = [A.alloc([96, 512], F32) for _ in range(2)]
    vst = [A.alloc([128, 8, 65], BF16) for _ in range(2)]
    for s in range(2):
        p.op("pool", lambda e, s=s: e.memset(vst[s], 1.0), writes=["vst%d" % s])
    nu = 0
    nh = 0
    for tb in range(4):
        tsl = slice(tb * 512, (tb + 1) * 512)
        cosT, sinT = cosb[tb % 2], sinb[tb % 2]
        ckey, skey2 = "cos%d" % (tb % 2), "sin%d" % (tb % 2)
        _dma(C, "sp", cosT[64:96, :], d["cosT"][64:96, tsl], [], [ckey], ckey)
        _dma(C, "sp", sinT[64:96, :], d["sinT"][64:96, tsl], [], [skey2], skey2)
        for m in range(4):
            bk = C.B[nu % 2]
            bkey = "B%d" % (nu % 2)
            st = ust[nu % 2]
            skey = "ust%d" % (nu % 2)
            nu += 1
            for k in range(8):
                p.op("pe", lambda e, tsl=tsl, k=k, m=m, bk=bk: e.matmul(bk, win[:, k, m * 128:(m + 1) * 128], xnT[:, k, tsl], start=(k == 0), stop=(k == 7)),
                     reads=["win"] + xk(tb, k), writes=[bkey])
            p.op("act", lambda e, bk=bk, st=st: e.activation(out=st, in_=bk, func=AF.Copy), reads=[bkey], writes=[skey])
            _dma(C, "sp", d["uT_o"][m][:, tsl], st, [skey], ["uTo_%d_%d" % (m, tb)], skey)
            C.outkeys.append("uTo_%d_%d" % (m, tb))
        for (nm, m0, nch, inv_n, gl, outn) in (("q", 4, 3, 1.0 / QL, gql, latn), ("kv", 7, 2, 1.0 / KVL, gkl, kvn)):
            for i in range(nch):
                bk = C.B[2 + i]
                bkey = "B%d" % (2 + i)
                m = m0 + i
                for k in range(8):
                    p.op("pe", lambda e, tsl=tsl, k=k, m=m, bk=bk: e.matmul(bk, win[:, k, m * 128:(m + 1) * 128], xnT[:, k, tsl], start=(k == 0), stop=(k == 7)),
                         reads=["win"] + xk(tb, k), writes=[bkey])
                p.op("act", lambda e, i=i, bk=bk: e.activation(out=lat[:, i, :], in_=bk, func=AF.Copy), reads=[bkey], writes=["lat"])
                p.op("act", lambda e, i=i, bk=bk: e.activation(out=sq[:, i, :], in_=bk, func=AF.Square), reads=[bkey], writes=["sq"])
            _rmsT(C, ["sq"], sq, nch, 128, 512, inv_n, rstdT, C.B[5], "B5")
            for i in range(nch):
                p.op("dve", lambda e, i=i, gl=gl, outn=outn: e.scalar_tensor_tensor(out=outn[:, i, :], in0=lat[:, i, :], scalar=gl[:, i:i + 1], in1=rstdT, op0=ALU.mult, op1=ALU.mult),
                     reads=["lat", "rstdT", "gsm"], writes=[nm + "n"])
        def head_stages(which, h, b, hqt, hkey):
            bk = C.B[b]
            bkey = "B%d" % b
            ssb, sskey = (C.B[5], "B5") if b == 0 else (C.B[3], "B3")
            rb, rkey = (C.B[4], "B4") if b == 0 else (C.B[2], "B2")
            sqb, rsb, t1b, t2b = sqh[b], rsh[b], t1[b], t2[b]
            st = [[] for _ in range(9)]
            if which == "q":
                for i in range(3):
                    st[0].append(("pe", lambda e, i=i, h=h, bk=bk: e.matmul(bk[0:96, :], wq[:, i, h * 96:(h + 1) * 96], latn[:, i, :], start=(i == 0), stop=(i == 2)),
                                  ["wq", "qn"], [bkey]))
                gh = gqh
            else:
                for k in range(8):
                    st[0].append(("pe", lambda e, tsl=tsl, k=k, bk=bk: e.matmul(bk[0:96, :], wrope[:, k, :], xnT[:, k, tsl], start=(k == 0), stop=False),
                                  ["wrope"] + xk(tb, k), [bkey]))
                for i in range(2):
                    st[0].append(("pe", lambda e, i=i, h=h, bk=bk: e.matmul(bk[0:96, :], wkp[:, i, h, :], kvn[:, i, :], start=False, stop=(i == 1)),
                                  ["wkp", "kvn"], [bkey]))
                gh = gkh
            st[1].append(("act", lambda e, bk=bk, sqb=sqb: e.activation(out=sqb, in_=bk[0:96, :], func=AF.Square), [bkey], ["sqh%d" % b]))
            st[2].append(("pe", lambda e, ssb=ssb, sqb=sqb: e.matmul(ssb[0:96, :], C.ones[0:96, 0:96], sqb, start=True, stop=True), ["ones", "sqh%d" % b], [sskey]))
            st[3].append(("act", lambda e, ssb=ssb, rsb=rsb: e.activation(out=rsb, in_=ssb[0:96, :], func=AF.Ln, bias=C.epsb[0:96, :], scale=1.0 / DQK), [sskey, "epsb"], ["rsh%d" % b]))
            st[3].append(("act", lambda e, rsb=rsb: e.activation(out=rsb, in_=rsb, func=AF.Exp, scale=-0.5), ["rsh%d" % b], ["rsh%d" % b]))
            st[4].append(("dve", lambda e, bk=bk, hqt=hqt, gh=gh, rsb=rsb: e.scalar_tensor_tensor(out=hqt, in0=bk[0:96, :], scalar=gh[:, 0:1], in1=rsb, op0=ALU.mult, op1=ALU.mult),
                          [bkey, "rsh%d" % b, "gsm"], [hkey]))
            st[5].append(("pe", lambda e, hqt=hqt, rb=rb: e.matmul(rb[0:96, :], rm, hqt, start=True, stop=True), ["rm", hkey], [rkey]))
            st[6].append(("pool", lambda e, cosT=cosT, hqt=hqt, t1b=t1b: e.tensor_tensor(out=t1b[64:96, :], in0=hqt[64:96, :], in1=cosT[64:96, :], op=ALU.mult),
                          [hkey, ckey], ["t1_%d" % b]))
            st[6].append(("dve", lambda e, sinT=sinT, rb=rb, t2b=t2b: e.tensor_tensor(out=t2b[64:96, :], in0=rb[64:96, :], in1=sinT[64:96, :], op=ALU.mult),
                          [rkey, skey2], ["t2_%d" % b]))
            st[7].append(("dve", lambda e, hqt=hqt, t1b=t1b, t2b=t2b: e.tensor_tensor(out=hqt[64:96, :], in0=t1b[64:96, :], in1=t2b[64:96, :], op=ALU.add),
                          ["t1_%d" % b, "t2_%d" % b], [hkey]))
            return st

        for which in ("q", "k"):
            for hp in range(H // 2):
                pp = nh % 2
                nh += 1
                sts = []
                for b in range(2):
                    h = hp * 2 + b
                    hqt = hq[pp * 2 + b]
                    hkey = "hq%d" % (pp * 2 + b)
                    sts.append((h, hqt, hkey, head_stages(which, h, b, hqt, hkey)))
                for si in range(9):
                    for (h, hqt, hkey, st) in sts:
                        for (eng, fn, rd, wr) in st[si]:
                            p.op(eng, fn, reads=rd, writes=wr)
                for (h, hqt, hkey, st) in sts:
                    if which == "q":
                        ok = "qTo_%d_%d" % (h, tb)
                        _dma(C, "sp", d["qT_o"][h][:, tsl], hqt, [hkey], [ok], hkey)
                        C.outkeys.append(ok)
                    else:
                        for jj in range(4):
                            ok = "kTo_%d_%d_%d" % (h, tb, jj)
                            _dma(C, "sp", d["kT_o"][tb * 4 + jj][:, h, :], hqt[:, jj * 128:(jj + 1) * 128], [hkey], [ok], hkey)
                            C.outkeys.append(ok)
        for jj in range(4):
            j = tb * 4 + jj
            b = j % 2
            bk = C.B[2 + b]
            bkey = "B%d" % (2 + b)
            for i in range(2):
                p.op("pe", lambda e, i=i, jj=jj, bk=bk: e.matmul(bk, kvn[:, i, jj * 128:(jj + 1) * 128], wv[:, i, :, :], start=(i == 0), stop=(i == 1)),
                     reads=["kvn", "wv"], writes=[bkey])
            p.op("dve", lambda e, b=b, bk=bk: e.tensor_copy(out=vst[b][:, :, 0:64], in_=bk.rearrange("p (h c) -> p h c", c=64)),
                 reads=[bkey], writes=["vst%d" % b])
            ok = "vo_%d" % j
            _dma(C, "sp", d["v_o"][:, j, :, :], vst[b], ["vst%d" % b], [ok], "vst%d" % b)
            C.outkeys.append(ok)


def ph_even_out(C, li):
    p, A, d = C.p, C.A, C.d
    scale = DQK ** -0.5
    catT = A.alloc([128, 8, T], BF16)
    mark = A.top
    uextb = [A.alloc([128, NT, 144], F32) for _ in range(2)]
    wa = A.alloc([128, NT, 144], F32)
    wb = A.alloc([128, NT, 144], F32)
    pooled = A.alloc([128, 4, T], BF16)
    invc = A.alloc([128, 4, 16], F32)
    _dma(C, "sp", invc, d["invc"], [], ["invc"], "invc")
    wpool = A.alloc([128, 4, 128], BF16)
    _dma(C, "pool", wpool, d["wpool"], [], ["wpool"], "wpool")
    pscale = A.alloc([128, 4], F32)
    _dma(C, "sp", pscale, d["pscale"], [], ["pscale"], "pscale")
    tmp16 = A.alloc([128, 16], F32)
    for g in range(4):
        w = 2 ** (g + 1)
        src = uextb[g % 2]
        _dma(C, "sp", src, d["uext"][:, g, :, :], [], ["uext%d" % (g % 2)], "uext%d" % (g % 2))
        bufs = [wa, wb]
        sh = 1
        cur = src
        curkey = "uext%d" % (g % 2)
        nb = 0
        while sh < w:
            dst = bufs[nb % 2]
            dkey = "w%d" % (nb % 2)
            nb += 1
            p.op("dve", lambda e, cur=cur, dst=dst, sh=sh: e.tensor_tensor(out=dst[:, :, 2 * sh - 1:144], in0=cur[:, :, 2 * sh - 1:144], in1=cur[:, :, sh - 1:144 - sh], op=ALU.add),
                 reads=[curkey], writes=[dkey])
            if sh > 1:
                pass
            cur = dst
            curkey = dkey
            sh *= 2
        pv = pooled[:, g, :].rearrange("p (j t) -> p j t", t=128)
        p.op("dve", lambda e, cur=cur, src=src, pv=pv, w=w: e.scalar_tensor_tensor(out=pv, in0=cur[:, :, 16:144], scalar=1.0 / w, in1=src[:, :, 16:144], op0=ALU.mult, op1=ALU.subtract),
             reads=[curkey, "uext%d" % (g % 2)], writes=["pooled%d" % g])
        p.op("dve", lambda e, cur=cur, g=g: e.tensor_tensor(out=tmp16, in0=cur[:, 0, 16:32], in1=invc[:, g, :], op=ALU.mult),
             reads=[curkey, "invc"], writes=["tmp16"])
        p.op("dve", lambda e, src=src, g=g: e.tensor_tensor(out=pooled[:, g, 0:16], in0=tmp16, in1=src[:, 0, 16:32], op=ALU.subtract),
             reads=["tmp16", "uext%d" % (g % 2), "pooled%d" % g], writes=["pooled%d" % g])
    n = 0
    for g in range(4):
        for tb in range(4):
            b = n % 2
            n += 1
            bk = C.B[b]
            p.op("pe", lambda e, g=g, tb=tb, bk=bk: e.matmul(bk, wpool[:, g, :], pooled[:, g, tb * 512:(tb + 1) * 512], start=True, stop=True),
                 reads=["wpool", "pooled%d" % g], writes=["B%d" % b])
            p.op("act", lambda e, g=g, tb=tb, bk=bk: e.activation(out=catT[:, g, tb * 512:(tb + 1) * 512], in_=bk, func=AF.Copy, scale=pscale[:, g:g + 1]),
                 reads=["B%d" % b, "pscale"], writes=["catT"])
    p.barrier()
    A.top = mark
    qT = A.alloc([96, H, T], BF16)
    for h in range(H):
        _dma(C, "sp", qT[:, h, :], d["qT"][h], [], ["qT"], "qT%d" % (h % 4))
    mask = A.alloc([128, 8, 128], BF16)
    _dma(C, "pool", mask, d["mask"], [], ["mask"], "mask")
    krow = [A.alloc([96, 8, 4, 128], BF16) for _ in range(2)]
    vrow = [A.alloc([128, 8, 4, 65], BF16) for _ in range(2)]
    atok = A.alloc([128, NT, 512], BF16)
    rden = A.alloc([128, 4], F32)
    nrow = 0
    ns = 0
    LAG = 1
    pT = [A.alloc([128, 2, 512], BF16) for _ in range(3)]
    SD = [(C.SA.rearrange("p (b c) -> p b c", b=2), ["B4", "B5"]),
          (C.SB.rearrange("p (b c) -> p b c", b=2), ["BT0", "BT1"])]
    for sb in range(4):
        for hg in range(2):
            steps = []
            for jr in range(4 * sb + 4):
                sl = nrow % 2
                nrow += 1
                kr, vr = krow[sl], vrow[sl]
                pre = [("sp", lambda e, s, kr=kr, jr=jr, hg=hg: [e.dma_start(out=kr[:, r, :, :], in_=d["kall"][jr][:, r, hg * 4:(hg + 1) * 4, :]).then_inc(s, 16) for r in range(8)],
                        [], ["krow%d" % sl], 8, "krow%d" % sl),
                       ("sp", lambda e, s, vr=vr, jr=jr, hg=hg: [e.dma_start(out=vr[:, r, :, :], in_=d["vall"][jr][:, r, hg * 4:(hg + 1) * 4, :]).then_inc(s, 16) for r in range(8)],
                        [], ["vrow%d" % sl], 8, "vrow%d" % sl)]
                i0 = max(0, jr - 4 * sb)
                diag = jr >= 4 * sb
                ncols = (4 - i0) * 128
                q0 = (4 * sb + i0) * 128
                kk = "krow%d" % sl
                for r in range(8):
                    for hp in range(2):
                        sd, sdkeys = SD[ns % 2]
                        pt = pT[ns % 3]
                        ptk = "pT%d" % (ns % 3)
                        ns += 1
                        S_ops = list(pre)
                        pre = []
                        for half in range(2):
                            hh = hp * 2 + half
                            h = hg * 4 + hh
                            bk = sd[:, half, :]
                            bkey = sdkeys[half]
                            if diag:
                                S_ops.append(("pe", lambda e, bk=bk, kr=kr, r=r, hh=hh, h=h, q0=q0: e.matmul(bk[:, 0:128], kr[:, r, hh, :], qT[:, h, q0:q0 + 128], start=True, stop=False),
                                              [kk, "qT"], [bkey], 0, None))
                                S_ops.append(("pe", lambda e, bk=bk, r=r: e.matmul(bk[:, 0:128], C.ident, mask[:, r, :], start=False, stop=True),
                                              ["ident", "mask"], [bkey], 0, None))
                                if ncols > 128:
                                    S_ops.append(("pe", lambda e, bk=bk, kr=kr, r=r, hh=hh, h=h, q0=q0, ncols=ncols: e.matmul(bk[:, 128:ncols], kr[:, r, hh, :], qT[:, h, q0 + 128:q0 + ncols], start=True, stop=True),
                                                  [kk, "qT"], [bkey], 0, None))
                            else:
                                S_ops.append(("pe", lambda e, bk=bk, kr=kr, r=r, hh=hh, h=h, q0=q0, ncols=ncols: e.matmul(bk[:, 0:ncols], kr[:, r, hh, :], qT[:, h, q0:q0 + ncols], start=True, stop=True),
                                              [kk, "qT"], [bkey], 0, None))
                        S_ops.append(("act", lambda e, sd=sd, pt=pt, ncols=ncols: e.activation(out=pt[:, :, 0:ncols], in_=sd[:, :, 0:ncols], func=AF.Exp, scale=scale),
                                      list(sdkeys), [ptk], 0, None))
                        PV_ops = []
                        for half in range(2):
                            hh = hp * 2 + half
                            for i in range(i0, 4):
                                first = (jr == 0 and r == 0 and hh == 0)
                                last = (jr == 4 * sb + i and r == 7)
                                PV_ops.append(("pe", lambda e, i0=i0, i=i, pt=pt, vr=vr, r=r, hh=hh, half=half, first=first, last=last: e.matmul(C.B[i][:, hh * 65:(hh + 1) * 65], pt[:, half, (i - i0) * 128:(i - i0 + 1) * 128], vr[:, r, hh, :], start=first, stop=last, skip_group_check=True),
                                               [ptk, "vrow%d" % sl], ["B%d" % i], 0, None))
                        steps.append((S_ops, PV_ops))
            for n in range(len(steps) + LAG):
                if n < len(steps):
                    for (eng, fn, rd, wr, dma, slot) in steps[n][0]:
                        p.op(eng, fn, reads=rd, writes=wr, dma=dma, slot=slot)
                if n - LAG >= 0:
                    for (eng, fn, rd, wr, dma, slot) in steps[n - LAG][1]:
                        p.op(eng, fn, reads=rd, writes=wr, dma=dma, slot=slot)
            for i in range(4):
                j = 4 * sb + i
                acc = C.B[i][:, 0:260].rearrange("p (h c) -> p h c", c=65)
                p.op("dve", lambda e, acc=acc: e.reciprocal(out=rden, in_=acc[:, :, 64]), reads=["B%d" % i], writes=["rden"])
                for hh in range(4):
                    h = hg * 4 + hh
                    p.op("dve", lambda e, acc=acc, hh=hh, h=h, j=j: e.tensor_scalar(out=atok[:, j, h * 64:(h + 1) * 64], in0=acc[:, hh, 0:64], scalar1=rden[:, hh:hh + 1], scalar2=None, op0=ALU.mult),
                         reads=["B%d" % i, "rden"], writes=["atok%d" % j])
    for j in range(NT):
        bt = C.BT[j % 2]
        for m in range(4):
            p.op("pe", lambda e, j=j, m=m, bt=bt: e.transpose(out=bt[:, m * 128:(m + 1) * 128], in_=atok[:, j, m * 128:(m + 1) * 128], identity=C.ident),
                 reads=["atok%d" % j, "ident"], writes=["BT%d" % (j % 2)])
        p.op("dve", lambda e, j=j, bt=bt: e.tensor_copy(out=catT[:, 4:8, j * 128:(j + 1) * 128], in_=bt[:, 0:512].rearrange("p (m t) -> p m t", t=128)),
             reads=["BT%d" % (j % 2)], writes=["catT"])
    p.barrier()
    A.top = mark
    _out_proj(C, catT, "catT", d["wout"])


def _out_proj(C, srcT, skey, w_d):
    p, A = C.p, C.A
    xs = C.xs
    wo = A.alloc([128, 8, 1024], BF16)
    for k in range(8):
        wload(C, wo[:, k, :], w_d[:, k, :], "wo")
    n = 0
    for j in range(NT):
        for hf in range(2):
            b = n % 2
            n += 1
            bk = C.B[b]
            for m in range(8):
                p.op("pe", lambda e, j=j, hf=hf, m=m, bk=bk: e.matmul(bk, srcT[:, m, j * 128:(j + 1) * 128], wo[:, m, hf * 512:(hf + 1) * 512], start=(m == 0), stop=(m == 7)),
                     reads=[skey, "wo"], writes=["B%d" % b])
            p.op("dve", lambda e, j=j, hf=hf, bk=bk: e.tensor_tensor(out=xs[:, j, hf * 512:(hf + 1) * 512], in0=bk, in1=xs[:, j, hf * 512:(hf + 1) * 512], op=ALU.add),
                 reads=["B%d" % b, "xs%d" % j], writes=["xs%d" % j])


def ph_odd_in(C, li):
    p, A, d = C.p, C.A, C.d
    C.xnT = A.alloc([128, 8, T], BF16)
    ph_norm(C, d["mixg"], "mix")
    xnT = C.xnT
    wst = [A.alloc([128, 8, 384], BF16) for _ in range(2)]
    gst = [A.alloc([128, 512], F32) for _ in range(2)]
    cst = [A.alloc([128, 512], F32) for _ in range(2)]
    ust = [A.alloc([128, 512], F32) for _ in range(2)]
    n = 0
    for m in range(8):
        sl = m % 2
        w = wst[sl]
        for q in range(3):
            wload(C, w[:, :, q * 128:(q + 1) * 128], d["wcin"][:, :, q * 1024 + m * 128:q * 1024 + (m + 1) * 128], "wst%d" % sl)
        for tb in range(4):
            tsl = slice(tb * 512, (tb + 1) * 512)
            b = n % 2
            n += 1
            banks = [C.B[b], C.B[2 + b], C.B[4 + b]]
            bkeys = ["B%d" % b, "B%d" % (2 + b), "B%d" % (4 + b)]
            for q in range(3):
                for k in range(8):
                    p.op("pe", lambda e, tsl=tsl, q=q, k=k, w=w, bk=banks[q]: e.matmul(bk, w[:, k, q * 128:(q + 1) * 128], xnT[:, k, tsl], start=(k == 0), stop=(k == 7)),
                         reads=["wst%d" % sl] + xk(tb, k), writes=[bkeys[q]])
            p.op("act", lambda e, b=b, bk=banks[0]: e.activation(out=gst[b], in_=bk, func=AF.Copy), reads=[bkeys[0]], writes=["gst%d" % b])
            p.op("act", lambda e, b=b, bk=banks[1]: e.activation(out=cst[b], in_=bk, func=AF.Copy), reads=[bkeys[1]], writes=["cst%d" % b])
            p.op("dve", lambda e, b=b, bk=banks[2]: e.tensor_tensor(out=ust[b], in0=cst[b], in1=bk, op=ALU.mult), reads=["cst%d" % b, bkeys[2]], writes=["ust%d" % b])
            ok = "gbo_%d_%d" % (m, tb)
            _dma(C, "sp", d["gb_o"][m][:, tsl], gst[b], ["gst%d" % b], [ok], "gst%d" % b)
            C.outkeys.append(ok)
            ok = "uco_%d_%d" % (m, tb)
            _dma(C, "sp", d["uc_o"][m][:, tsl], ust[b], ["ust%d" % b], [ok], "ust%d" % b)
            C.outkeys.append(ok)


def ph_odd_out(C, li):
    p, A, d = C.p, C.A, C.d
    vT = A.alloc([128, 8, T], BF16)
    cw = A.alloc([128, 8, 3], F32)
    _dma(C, "sp", cw, d["cw"], [], ["cw"], "cw")
    ue = [A.alloc([128, NT, 130], F32) for _ in range(2)]
    gb = [A.alloc([128, T], F32) for _ in range(2)]
    y = [A.alloc([128, NT, 128], F32) for _ in range(2)]
    for m in range(8):
        sl = m % 2
        _dma(C, "sp", ue[sl], d["ucext"][:, m, :, :], [], ["ue%d" % sl], "ue%d" % sl)
        _dma(C, "sp", gb[sl], d["gbT"][m], [], ["gb%d" % sl], "gb%d" % sl)
        eng = "dve"
        u, yy, g = ue[sl], y[sl], gb[sl]
        p.op(eng, lambda e, u=u, yy=yy, m=m: e.tensor_scalar(out=yy, in0=u[:, :, 0:128], scalar1=cw[:, m, 0:1], scalar2=None, op0=ALU.mult),
             reads=["ue%d" % sl, "cw"], writes=["y%d" % sl])
        p.op(eng, lambda e, u=u, yy=yy, m=m: e.scalar_tensor_tensor(out=yy, in0=u[:, :, 1:129], scalar=cw[:, m, 1:2], in1=yy, op0=ALU.mult, op1=ALU.add),
             reads=["ue%d" % sl, "cw", "y%d" % sl], writes=["y%d" % sl])
        p.op(eng, lambda e, u=u, yy=yy, m=m: e.scalar_tensor_tensor(out=yy, in0=u[:, :, 2:130], scalar=cw[:, m, 2:3], in1=yy, op0=ALU.mult, op1=ALU.add),
             reads=["ue%d" % sl, "cw", "y%d" % sl], writes=["y%d" % sl])
        p.op(eng, lambda e, yy=yy, g=g, m=m: e.tensor_tensor(out=vT[:, m, :], in0=yy.rearrange("p j t -> p (j t)"), in1=g, op=ALU.mult),
             reads=["y%d" % sl, "gb%d" % sl], writes=["vT"])
    _out_proj(C, vT, "vT", d["wcout"])


def build_launch(spec):
    nc = bass.Bass("TRN2", target_bir_lowering=False)
    C = Ctx()
    C.nc = nc
    C.p = Prog(nc)
    C.d = {}
    C.outkeys = []
    for name, (shape, dt) in spec["inputs"].items():
        C.d[name] = nc.dram_tensor(name, list(shape), dt, kind="ExternalInput").ap()
    for name, (shape, dt) in spec["outputs"].items():
        C.d[name] = nc.dram_tensor(name, list(shape), dt, kind="ExternalOutput").ap()
    import contextlib
    with contextlib.ExitStack() as st:
        nbytes = 204 * 1024
        ar = st.enter_context(nc.sbuf_tensor("arena", [128, nbytes], U8))
        C.A = Arena(ar, nbytes)
        C.B = [st.enter_context(nc.psum_tensor("bank%d" % i, [128, 512], F32))[:] for i in range(4)]
        C.SA = st.enter_context(nc.psum_tensor("bankSA", [128, 1024], F32))[:]
        C.SB = st.enter_context(nc.psum_tensor("bankSB", [128, 1024], F32))[:]
        C.B += [C.SA[:, 0:512], C.SA[:, 512:1024]]
        C.BT = [C.SB[:, 0:512].bitcast(BF16), C.SB[:, 512:1024].bitcast(BF16)]
        A = C.A
        C.xs = A.alloc([128, NT, D], F32)
        C.gT = A.alloc([128, 8], F32)
        C.ss = A.alloc([128, NT], F32)
        C.rstd = A.alloc([128, NT], F32)
        C.gfull = A.alloc([128, 8, 128], F32)
        C.onesf = A.alloc([128, 128], F32)
        C.p.op("pool", lambda e: e.memset(C.onesf, 1.0), writes=["onesf"])
        C.xb = [A.alloc([128, D], BF16) for _ in range(2)]
        C.epsb = A.alloc([128, 1], F32)
        C.stg = [A.alloc([128, 1536], F32) for _ in range(3)]
        C.stg_n = 0
        C.p.op("pool", lambda e: e.memset(C.epsb, EPS), writes=["epsb"])
        ph_consts(C)
        base = A.top
        for ph in spec["phases"]:
            if ph in (_load_x, _store_x):
                ph(C)
                continue
            C.p.barrier()
            A.top = base
            ph(C)
        C.p.op("sp", None, reads=list(C.outkeys))
        C.p.op("pool", None, reads=list(C.outkeys))
        C.p.emit()
    return nc


def alloc_ffn(C):
    A = C.A
    C.xnT = A.alloc([128, 8, T], BF16)
    C.wg = [A.alloc([128, 8, 384], BF16) for _ in range(2)]
    C.wu = [A.alloc([128, 8, 384], BF16) for _ in range(2)]
    C.wd = [A.alloc([128, 3, D], BF16) for _ in range(2)]
    C.hT = [A.alloc([128, 3, T], BF16) for _ in range(2)]
    C.sg = [A.alloc([128, 512], F32) for _ in range(2)]


def mk_ffn(which):
    def f(C):
        alloc_ffn(C)
        ph_ffn(C, C.d[which + "_g"], C.d[which + "_wg"], C.d[which + "_wu"], C.d[which + "_wd"], which)
    return f


FFN_IN = lambda w: {w + "_g": ((128, 8), F32), w + "_wg": ((128, 8, DFF), F32), w + "_wu": ((128, 8, DFF), F32), w + "_wd": ((128, NFC, D), F32)}
X_IO = ((128, NT, D), F32)
CONST_IN = {"ident": ((128, 128), F32), "ones": ((128, 128), F32)}
EVEN_IN_W = {"mixg": ((128, 8), F32), "win": ((128, 8, MIXIN), F32), "wq": ((128, 3, 768), F32), "wkv": ((128, 2, 1024), F32),
             "rm": ((96, 96), F32), "gql": ((128, 3), F32), "gkl": ((128, 2), F32), "gqh": ((96, 1), F32), "gkh": ((96, 1), F32),
             "cosT": ((96, T), F32), "sinT": ((96, T), F32)}
EVEN_IN_OUT = {"uT_o": ((4, 128, T), F32), "qT_o": ((H, 96, T), BF16), "kT_o": ((NT, 96, H, 128), BF16), "v_o": ((128, NT, H, 65), BF16)}
EVEN_OUT_IN = {"uext": ((128, 4, NT, 144), F32), "invc": ((128, 4, 16), F32), "wpool": ((128, 4, 128), F32), "pscale": ((128, 4), F32),
               "qT": ((H, 96, T), BF16), "mask": ((128, 8, 128), F32), "kall": ((NT, 96, 8, H, 128), BF16), "vall": ((NT, 128, 8, H, 65), BF16),
               "wout": ((128, 8, D), F32)}
ODD_IN_W = {"mixg": ((128, 8), F32), "wcin": ((128, 8, 3 * D), F32)}
ODD_IN_OUT = {"gb_o": ((8, 128, T), F32), "uc_o": ((8, 128, T), F32)}
ODD_OUT_IN = {"cw": ((128, 8, 3), F32), "ucext": ((128, 8, NT, 130), F32), "gbT": ((8, 128, T), F32), "wcout": ((128, 8, D), F32)}


def _load_x(C):
    ph_load_x(C, C.d["x"])


def _store_x(C):
    ph_store_x(C, C.d["x_o"], "xo")


def spec_A():
    return dict(inputs={"x": X_IO, **CONST_IN, **FFN_IN("f1"), **EVEN_IN_W},
                outputs={"x_o": X_IO, **EVEN_IN_OUT},
                phases=[_load_x, mk_ffn("f1"), _store_x, lambda C: ph_even_in(C, 0)])


def spec_B():
    return dict(inputs={"x": X_IO, **CONST_IN, **EVEN_OUT_IN, **FFN_IN("f2"), **FFN_IN("f1"), **ODD_IN_W},
                outputs={"x_o": X_IO, **ODD_IN_OUT},
                phases=[_load_x, lambda C: ph_even_out(C, 0), mk_ffn("f2"), mk_ffn("f1"), _store_x, lambda C: ph_odd_in(C, 1)])


def spec_C():
    return dict(inputs={"x": X_IO, **CONST_IN, **ODD_OUT_IN, **FFN_IN("f2"), **FFN_IN("f1"), **EVEN_IN_W},
                outputs={"x_o": X_IO, **EVEN_IN_OUT},
                phases=[_load_x, lambda C: ph_odd_out(C, 1), mk_ffn("f2"), mk_ffn("f1"), _store_x, lambda C: ph_even_in(C, 2)])


def spec_D():
    return dict(inputs={"x": X_IO, **CONST_IN, **ODD_OUT_IN, **FFN_IN("f2")},
                outputs={"x_o": X_IO},
                phases=[_load_x, lambda C: ph_odd_out(C, 3), mk_ffn("f2"), _store_x])


_NC_CACHE = {}


def _get_nc(name):
    if name not in _NC_CACHE:
        _NC_CACHE[name] = build_launch({"A": spec_A, "B": spec_B, "C": spec_C, "D": spec_D}[name]())
    return _NC_CACHE[name]


def _kmaj(w, p=128):
    K, N = w.shape
    return np.ascontiguousarray(w.reshape(K // p, p, N).transpose(1, 0, 2))


def _vec(g, p=128):
    return np.ascontiguousarray(g.reshape(-1, p).T)


def _ffn_inputs(prefix, inputs, which, layer):
    return {prefix + "_g": _vec(inputs[which + "_norm"][layer]),
            prefix + "_wg": _kmaj(inputs[which + "_w_gate"][layer]),
            prefix + "_wu": _kmaj(inputs[which + "_w_up"][layer]),
            prefix + "_wd": _kmaj(inputs[which + "_w_down"][layer])}


def _consts():
    return {"ident": np.eye(128, dtype=np.float32), "ones": np.ones((128, 128), np.float32)}


def _rope_tables(c):
    pos = (np.arange(NT)[:, None] * NC + c) * 128 + np.arange(128)[None, :]
    pos = pos.reshape(-1).astype(np.float32)
    inv = (np.float32(10000.0) ** (-np.arange(0, 32, 2, dtype=np.float32) / np.float32(32))).astype(np.float32)
    ang = (pos[:, None] * inv[None, :]).astype(np.float32)
    cos = np.cos(ang).astype(np.float32).T
    sin = np.sin(ang).astype(np.float32).T
    cosT = np.zeros((96, T), np.float32)
    sinT = np.zeros((96, T), np.float32)
    cosT[64:80] = cos
    cosT[80:96] = cos
    sinT[64:80] = sin
    sinT[80:96] = sin
    return cosT, sinT


def _rm():
    rm = np.zeros((96, 96), np.float32)
    for i in range(16):
        rm[80 + i, 64 + i] = -1.0
        rm[64 + i, 80 + i] = 1.0
    return rm


def _even_in_inputs(inputs, i, c):
    cosT, sinT = _rope_tables(c)
    return {"mixg": _vec(inputs["mix_norm"][2 * i]), "win": _kmaj(inputs["a_w_in"][i]), "wq": _kmaj(inputs["a_w_q_up"][i]),
            "wkv": _kmaj(inputs["a_w_kv_up"][i]), "rm": _rm(), "gql": _vec(inputs["a_q_a_norm"][i]), "gkl": _vec(inputs["a_kv_a_norm"][i]),
            "gqh": np.ascontiguousarray(inputs["a_q_head_norm"][i].reshape(96, 1)), "gkh": np.ascontiguousarray(inputs["a_k_head_norm"][i].reshape(96, 1)),
            "cosT": cosT, "sinT": sinT}


def _mask(c):
    m = np.zeros((128, 8, 128), np.float32)
    for r in range(8):
        if r > c:
            m[:, r, :] = -30000.0
        elif r == c:
            m[:, r, :] = np.where(np.arange(128)[:, None] > np.arange(128)[None, :], -30000.0, 0.0)
    return m


def _even_out_inputs(inputs, i, c, res):
    uT = np.stack([r["uT_o"] for r in res])
    uT = uT.reshape(NC, 4, 128, NT, 128)
    glob = uT.transpose(1, 2, 3, 0, 4).reshape(4, 128, NT * NC, 128)
    ext = np.zeros((128, 4, NT, 144), np.float32)
    for j in range(NT):
        t = 8 * j + c
        ext[:, :, j, 16:144] = glob[:, :, t, :].transpose(1, 0, 2)
        if t > 0:
            ext[:, :, j, 0:16] = glob[:, :, t - 1, 112:128].transpose(1, 0, 2)
    invc = np.zeros((128, 4, 16), np.float32)
    for g in range(4):
        w = 2 ** (g + 1)
        if c == 0:
            invc[:, g, :] = 1.0 / np.minimum(np.arange(16) + 1, w).astype(np.float32)
        else:
            invc[:, g, :] = 1.0 / w
    kall = np.stack([r["kT_o"] for r in res])
    kall = np.ascontiguousarray(kall.transpose(1, 2, 0, 3, 4))
    vall = np.stack([r["v_o"] for r in res])
    vall = np.ascontiguousarray(vall.transpose(2, 1, 0, 3, 4))
    return {"uext": ext, "invc": invc, "wpool": np.ascontiguousarray(inputs["a_w_pool"][i].transpose(1, 0, 2)),
            "pscale": _vec(inputs["a_pool_scale"][i]), "qT": res[c]["qT_o"], "mask": _mask(c), "kall": kall, "vall": vall,
            "wout": _kmaj(inputs["a_w_out"][i])}, (kall, vall)


def _odd_in_inputs(inputs, i):
    return {"mixg": _vec(inputs["mix_norm"][2 * i + 1]), "wcin": _kmaj(inputs["c_w_in"][i])}


def _odd_out_inputs(inputs, i, c, res):
    uc = np.stack([r["uc_o"] for r in res]).reshape(NC, 8, 128, NT, 128)
    glob = uc.transpose(1, 2, 3, 0, 4).reshape(8, 128, NT * NC, 128)
    ext = np.zeros((128, 8, NT, 130), np.float32)
    for j in range(NT):
        t = 8 * j + c
        ext[:, :, j, 2:130] = glob[:, :, t, :].transpose(1, 0, 2)
        if t > 0:
            ext[:, :, j, 0:2] = glob[:, :, t - 1, 126:128].transpose(1, 0, 2)
    cw = np.ascontiguousarray(inputs["c_conv_w"][i].reshape(3, 8, 128).transpose(2, 1, 0))
    return {"cw": cw, "ucext": ext, "gbT": res[c]["gb_o"], "wcout": _kmaj(inputs["c_w_out"][i])}


def _run(name, in_maps):
    nc = _get_nc(name)
    res = run_bass_kernel_spmd(nc, in_maps, core_ids=list(range(NC)))
    return res.results


def kernel(**inputs):
    inputs = {k: np.asarray(v) for k, v in inputs.items()}
    x = inputs["x"][0].reshape(NT, NC, 128, D)
    xc = [np.ascontiguousarray(x[:, c].transpose(1, 0, 2)) for c in range(NC)]
    f1 = _ffn_inputs("f1", inputs, "ffn1", 0)
    res = _run("A", [{"x": xc[c], **_consts(), **f1, **_even_in_inputs(inputs, 0, c)} for c in range(NC)])
    f2 = _ffn_inputs("f2", inputs, "ffn2", 0)
    f1 = _ffn_inputs("f1", inputs, "ffn1", 1)
    oi = _odd_in_inputs(inputs, 0)
    maps = []
    for c in range(NC):
        eo, _ = _even_out_inputs(inputs, 0, c, res)
        maps.append({"x": res[c]["x_o"], **_consts(), **eo, **f2, **f1, **oi})
    res = _run("B", maps)
    f2 = _ffn_inputs("f2", inputs, "ffn2", 1)
    f1 = _ffn_inputs("f1", inputs, "ffn1", 2)
    maps = [{"x": res[c]["x_o"], **_consts(), **_odd_out_inputs(inputs, 0, c, res), **f2, **f1, **_even_in_inputs(inputs, 1, c)} for c in range(NC)]
    res = _run("C", maps)
    f2 = _ffn_inputs("f2", inputs, "ffn2", 2)
    f1 = _ffn_inputs("f1", inputs, "ffn1", 3)
    oi = _odd_in_inputs(inputs, 1)
    maps = []
    for c in range(NC):
        eo, _ = _even_out_inputs(inputs, 1, c, res)
        maps.append({"x": res[c]["x_o"], **_consts(), **eo, **f2, **f1, **oi})
    res = _run("B", maps)
    f2 = _ffn_inputs("f2", inputs, "ffn2", 3)
    maps = [{"x": res[c]["x_o"], **_consts(), **_odd_out_inputs(inputs, 1, c, res), **f2} for c in range(NC)]
    res = _run("D", maps)
    out = np.zeros((NT, NC, 128, D), np.float32)
    for c in range(NC):
        out[:, c] = res[c]["x_o"].transpose(1, 0, 2)
    return out.reshape(1, S, D)
```

```python
import numpy as np
import ml_dtypes
import concourse.bass as bass
import concourse.mybir as mybir
from concourse.bass_utils import run_bass_kernel_spmd

F32 = mybir.dt.float32
BF16 = mybir.dt.bfloat16
U8 = mybir.dt.uint8
AF = mybir.ActivationFunctionType
ALU = mybir.AluOpType

NC = 8
S = 16384
D = 1024
T = S // NC
NT = T // 128
DFF = 2816
NFC = DFF // 128
EPS = 1e-6
H = 8
DQK = 96
DV = 64
QL = 384
KVL = 256
MIXIN = 1184
ENGS = ("pe", "act", "dve", "pool", "sp")
FF_GROUPS = [(0, 3), (3, 6), (6, 9), (9, 12), (12, 15), (15, 18), (18, 20), (20, 22)]


class _Op:
    __slots__ = ("idx", "eng", "fn", "deps", "dma", "slot", "ms", "tok", "persist")

    def __init__(self, idx, eng, fn, deps, dma, slot):
        self.idx, self.eng, self.fn, self.deps, self.dma, self.slot = idx, eng, fn, deps, dma, slot
        self.ms = False
        self.tok = None
        self.persist = False


class Prog:
    def __init__(self, nc):
        self.nc = nc
        self.ops = []
        self.last_writer = {}
        self.readers = {}
        self.since_barrier = []

    def op(self, eng, fn, reads=(), writes=(), dma=0, slot=None):
        bank_reads = [k for k in reads if k[0] == "B"]
        if bank_reads:
            reads = [k for k in reads if k[0] != "B"]
            writes = list(writes) + bank_reads
        deps = set()
        for k in reads:
            w = self.last_writer.get(k)
            if w is not None:
                deps.add(w)
        for k in writes:
            w = self.last_writer.get(k)
            if w is not None:
                deps.add(w)
            deps.update(self.readers.get(k, ()))
        best = {}
        red = set()
        for dd in deps:
            q = self.ops[dd]
            if q.dma:
                red.add(dd)
            elif dd > best.get(q.eng, -1):
                best[q.eng] = dd
        red.update(best.values())
        deps = red
        idx = len(self.ops)
        o = _Op(idx, eng, fn, deps, dma, slot)
        allk = list(reads) + list(writes)
        o.persist = bool(dma) and len(allk) > 0 and all(k.startswith(self.PERSIST) for k in allk)
        self.ops.append(o)
        self.since_barrier.append(idx)
        for k in writes:
            self.last_writer[k] = idx
            self.readers[k] = []
        for k in reads:
            self.readers.setdefault(k, []).append(idx)
        return idx

    PERSIST = ("xs", "xo", "uTo", "qTo", "kTo", "vo_", "gbo", "uco")

    def barrier(self):
        prev = self.since_barrier
        self.since_barrier = []
        last = {}
        dmas = set()
        for i in prev:
            o = self.ops[i]
            if o.fn is None:
                continue
            if o.dma:
                if not o.persist:
                    dmas.add(i)
                else:
                    self.since_barrier.append(i)
            else:
                last[o.eng] = i
        for e in ENGS:
            deps = set(dmas)
            for e2, i in last.items():
                if e2 != e:
                    deps.add(i)
            idx = len(self.ops)
            self.ops.append(_Op(idx, e, None, deps, 0, None))
        self.last_writer = {k: v for k, v in self.last_writer.items() if k.startswith(self.PERSIST)}
        self.readers = {k: v for k, v in self.readers.items() if k.startswith(self.PERSIST)}

    def emit(self):
        nc = self.nc
        ops = self.ops
        for o in ops:
            for d in o.deps:
                q = ops[d]
                if q.dma or q.fn is None:
                    continue
                if q.eng == "pe" and o.eng == "pe" and not o.dma and o.fn is not None:
                    continue
                q.ms = True
        cnt = {e: 0 for e in ENGS}
        slots = {}
        for o in ops:
            if o.fn is None:
                continue
            if o.dma:
                c = slots.get(o.slot, 0) + 16 * o.dma
                slots[o.slot] = c
                o.tok = ("dma:" + o.slot, c)
            elif o.ms:
                cnt[o.eng] += 1
                o.tok = ("eng:" + o.eng, cnt[o.eng])
        names = ["eng:" + e for e in ENGS if cnt[e] > 0] + ["dma:" + s for s in slots]
        sems = {n: nc.alloc_semaphore(name="s%d" % i) for i, n in enumerate(names)}
        by_eng = {e: [o for o in ops if o.eng == e] for e in ENGS}

        def run(en, eng):
            seen = {}
            for o in by_eng[en]:
                need = {}
                for d in o.deps:
                    q = ops[d]
                    if q.tok is None:
                        continue
                    if q.eng == "pe" and en == "pe" and not q.dma and not o.dma:
                        continue
                    sn, c = q.tok
                    if c > need.get(sn, 0):
                        need[sn] = c
                for sn, c in need.items():
                    if seen.get(sn, 0) >= c:
                        continue
                    eng.wait_ge(sems[sn], c)
                    seen[sn] = c
                if o.fn is None:
                    continue
                if o.dma:
                    o.fn(eng, sems[o.tok[0]])
                else:
                    ins = o.fn(eng)
                    if o.ms:
                        ins.then_inc(sems[o.tok[0]], 1)

        with nc.Block() as block:
            @block.tensor
            def _(e):
                run("pe", e)

            @block.scalar
            def _(e):
                run("act", e)

            @block.vector
            def _(e):
                run("dve", e)

            @block.gpsimd
            def _(e):
                run("pool", e)

            @block.sync
            def _(e):
                run("sp", e)


class Arena:
    def __init__(self, ar, nbytes):
        self.ar = ar
        self.n = nbytes
        self.top = 0

    def alloc(self, shape, dt, parts=128):
        esz = 2 if dt == BF16 else 4
        free = int(np.prod(shape[1:]))
        nb = free * esz
        off = (self.top + 63) // 64 * 64
        assert off + nb <= self.n, ("SBUF arena overflow", off, nb, self.n)
        self.top = off + nb
        v = self.ar[0:shape[0], off:off + nb].bitcast(dt)
        if len(shape) == 3:
            v = v.rearrange("p (a b) -> p a b", a=shape[1])
        elif len(shape) == 4:
            v = v.rearrange("p (a b c) -> p a b c", a=shape[1], b=shape[2])
        return v


class Ctx:
    pass


def _dma(C, eng, out, in_, reads, writes, slot):
    C.p.op(eng, lambda e, s: e.dma_start(out=out, in_=in_).then_inc(s, 16), reads=reads, writes=writes, dma=1, slot=slot)


def wload(C, dst, src, wkey, eng="act"):
    shape = tuple(dst.shape)
    n = int(np.prod(shape[1:]))
    assert n <= 1536, shape
    i = C.stg_n % len(C.stg)
    C.stg_n += 1
    st = C.stg[i][0:shape[0], 0:n]
    if len(shape) == 3:
        st = st.rearrange("p (a b) -> p a b", a=shape[1])
    _dma(C, "sp", st, src, [], ["stg%d" % i], "stg%d" % i)
    if eng == "act":
        C.p.op(eng, lambda e: e.activation(out=dst, in_=st, func=AF.Copy), reads=["stg%d" % i], writes=[wkey])
    else:
        C.p.op(eng, lambda e: e.tensor_copy(out=dst, in_=st), reads=["stg%d" % i], writes=[wkey])


def xk(tb, k):
    return ["xnT%d_%d" % (4 * tb + i, k) for i in range(4)]


def ph_load_x(C, x_d):
    for j in range(NT):
        _dma(C, "sp", C.xs[:, j, :], x_d[:, j, :], [], ["xs%d" % j], "xs%d" % j)


def ph_store_x(C, x_d, keyname):
    for j in range(NT):
        _dma(C, "sp", x_d[:, j, :], C.xs[:, j, :], ["xs%d" % j], [keyname + "%d" % j], "xo%d" % (j % 4))
    C.outkeys += [keyname + "%d" % j for j in range(NT)]


def ph_consts(C):
    p = C.p
    A = C.A
    C.ident = A.alloc([128, 128], BF16)
    C.ones = A.alloc([128, 128], BF16)
    _dma(C, "pool", C.ident, C.d["ident"], [], ["ident"], "c_ident")
    _dma(C, "pool", C.ones, C.d["ones"], [], ["ones"], "c_ones")


def ph_norm(C, g_d, tag):
    p = C.p
    xs, xnT = C.xs, C.xnT
    gT = C.gT
    gfull = C.gfull
    _dma(C, "sp", gT, g_d, [], ["gT"], "gT")
    for k in range(8):
        p.op("pool", lambda e, k=k: e.tensor_scalar(out=gfull[:, k, :], in0=C.onesf, scalar1=gT[:, k:k + 1], scalar2=None, op0=ALU.mult),
             reads=["gT", "onesf"], writes=["gfull"])
    ss, rstd = C.ss, C.rstd
    for j in range(NT):
        col = (j // 2) + (8 if j % 2 else 0)
        jb = C.xb[j % 2]
        if j % 2 == 0:
            p.op("act", lambda e, j=j, col=col, jb=jb: e.activation(out=jb, in_=xs[:, j, :], func=AF.Square, accum_out=ss[:, col:col + 1]),
                 reads=["xs%d" % j], writes=["xb0", "ssA"])
        else:
            p.op("dve", lambda e, j=j, col=col, jb=jb: e.scalar_tensor_tensor(out=jb, in0=xs[:, j, :], scalar=1.0, in1=xs[:, j, :], op0=ALU.mult, op1=ALU.mult, accum_out=ss[:, col:col + 1]),
                 reads=["xs%d" % j], writes=["xb1", "ssD"])
    p.op("act", lambda e: e.activation(out=rstd, in_=ss, func=AF.Ln, bias=C.epsb, scale=1.0 / D), reads=["ssA", "ssD", "epsb"], writes=["rstd"])
    p.op("act", lambda e: e.activation(out=rstd, in_=rstd, func=AF.Exp, scale=-0.5), reads=["rstd"], writes=["rstd"])
    for j in range(NT):
        col = (j // 2) + (8 if j % 2 else 0)
        xb = C.xb[j % 2]
        bt = C.BT[j % 2]
        p.op("act", lambda e, j=j, col=col, xb=xb: e.activation(out=xb, in_=xs[:, j, :], func=AF.Copy, scale=rstd[:, col:col + 1]),
             reads=["xs%d" % j, "rstd"], writes=["xb%d" % (j % 2)])
        for k in range(8):
            p.op("pe", lambda e, k=k, xb=xb, bt=bt: e.transpose(out=bt[:, k * 128:(k + 1) * 128], in_=xb[:, k * 128:(k + 1) * 128], identity=C.ident),
                 reads=["xb%d" % (j % 2), "ident"], writes=["BT%d" % (j % 2)])
        p.op("dve", lambda e, j=j, bt=bt: e.tensor_tensor(out=xnT[:, :, j * 128:(j + 1) * 128], in0=bt.rearrange("p (k t) -> p k t", t=128), in1=gfull, op=ALU.mult),
             reads=["BT%d" % (j % 2), "gfull"], writes=["xnT%d_%d" % (j, k) for k in range(8)])


def ph_ffn(C, g_d, wg_d, wu_d, wd_d, tag):
    p = C.p
    ph_norm(C, g_d, tag)
    xs, xnT = C.xs, C.xnT
    NG = len(FF_GROUPS)

    def loads_gu(gi):
        c0, c1 = FF_GROUPS[gi]
        gs = c1 - c0
        sl = gi % 2
        wg, wu = C.wg[sl], C.wu[sl]
        out = []
        for kh in range(2):
            out.append(lambda kh=kh: wload(C, wg[:, kh * 4:kh * 4 + 4, 0:gs * 128], wg_d[:, kh * 4:kh * 4 + 4, c0 * 128:c1 * 128], "wg%d" % sl))
        for kh in range(2):
            out.append(lambda kh=kh: wload(C, wu[:, kh * 4:kh * 4 + 4, 0:gs * 128], wu_d[:, kh * 4:kh * 4 + 4, c0 * 128:c1 * 128], "wu%d" % sl))
        return out

    def loads_d(gi):
        c0, c1 = FF_GROUPS[gi]
        sl = gi % 2
        wd = C.wd[sl]
        return [lambda c=c: wload(C, wd[:, c, :], wd_d[:, c0 + c, :], "wd%d" % sl) for c in range(c1 - c0)]

    cnt = {"gu": 0, "dn": 0}
    PSY = [(C.B[4], "B4"), (C.B[5], "B5"), (C.SB[:, 0:512], "BT0"), (C.SB[:, 512:1024], "BT1")]

    def gu_units(gi):
        c0, c1 = FF_GROUPS[gi]
        gs = c1 - c0
        sl = gi % 2
        wg, wu, hT = C.wg[sl], C.wu[sl], C.hT[sl]
        units = []
        for tb in range(4):
            for c in range(gs):
                b = cnt["gu"] % 2
                cnt["gu"] += 1
                psg, psu = C.B[b], C.B[2 + b]
                u = []
                for k in range(8):
                    u.append(("pe", lambda e, k=k, c=c, tb=tb, psg=psg, wg=wg: e.matmul(psg, wg[:, k, c * 128:(c + 1) * 128], xnT[:, k, tb * 512:(tb + 1) * 512], start=(k == 0), stop=(k == 7)),
                              ["wg%d" % sl] + xk(tb, k), ["B%d" % b]))
                for k in range(8):
                    u.append(("pe", lambda e, k=k, c=c, tb=tb, psu=psu, wu=wu: e.matmul(psu, wu[:, k, c * 128:(c + 1) * 128], xnT[:, k, tb * 512:(tb + 1) * 512], start=(k == 0), stop=(k == 7)),
                              ["wu%d" % sl] + xk(tb, k), ["B%d" % (2 + b)]))
                sg = C.sg[b]
                u.append(("act", lambda e, sg=sg, psg=psg: e.activation(out=sg, in_=psg, func=AF.Silu), ["B%d" % b], ["sg%d" % b]))
                u.append(("dve", lambda e, sg=sg, psu=psu, hT=hT, c=c, tb=tb: e.tensor_tensor(out=hT[:, c, tb * 512:(tb + 1) * 512], in0=sg, in1=psu, op=ALU.mult),
                          ["sg%d" % b, "B%d" % (2 + b)], ["hT%d_%d_%d" % (sl, c, tb)]))
                units.append(u)
        return units

    def dn_units(gi):
        c0, c1 = FF_GROUPS[gi]
        gs = c1 - c0
        sl = gi % 2
        wd, hT = C.wd[sl], C.hT[sl]
        units = []
        for j in range(NT):
            for hf in range(2):
                psy, pkey = PSY[cnt["dn"] % 4]
                cnt["dn"] += 1
                u = []
                for c in range(gs):
                    u.append(("pe", lambda e, gs=gs, c=c, j=j, hf=hf, psy=psy, hT=hT, wd=wd: e.matmul(psy, hT[:, c, j * 128:(j + 1) * 128], wd[:, c, hf * 512:(hf + 1) * 512], start=(c == 0), stop=(c == gs - 1)),
                              ["hT%d_%d_%d" % (sl, c, j // 4), "wd%d" % sl], [pkey]))
                u.append(("dve", lambda e, j=j, hf=hf, psy=psy: e.scalar_tensor_tensor(out=xs[:, j, hf * 512:(hf + 1) * 512], in0=psy, scalar=0.5, in1=xs[:, j, hf * 512:(hf + 1) * 512], op0=ALU.mult, op1=ALU.add),
                          [pkey, "xs%d" % j], ["xs%d" % j]))
                units.append(u)
        return units

    def emit(u):
        for (eng, fn, rd, wr) in u:
            p.op(eng, fn, reads=rd, writes=wr)

    for f in loads_gu(0) + loads_gu(1) + loads_d(0):
        f()
    for u in gu_units(0):
        emit(u)
    for gi in range(NG):
        pend = []
        if gi + 2 < NG:
            pend += loads_gu(gi + 2)
        if gi + 1 < NG:
            pend += loads_d(gi + 1)
        dn = dn_units(gi)
        gu = gu_units(gi + 1) if gi + 1 < NG else []
        nd, ng = len(dn), len(gu)
        di = 0
        pi = 0
        for ui in range(ng):
            emit(gu[ui])
            tgt = (ui + 1) * nd // ng
            while di < tgt:
                emit(dn[di])
                di += 1
            ptgt = (ui + 1) * len(pend) // ng
            while pi < ptgt:
                pend[pi]()
                pi += 1
        while di < nd:
            emit(dn[di])
            di += 1
        while pi < len(pend):
            pend[pi]()
            pi += 1


def _rmsT(C, src_keys, sq, nchunk, rows, width, inv_n, out_rstd, bank, bkey):
    p = C.p
    for i in range(nchunk):
        p.op("pe", lambda e, i=i: e.matmul(bank[0:rows, 0:width], C.ones[0:rows, 0:rows], sq[0:rows, i, 0:width], start=(i == 0), stop=(i == nchunk - 1)),
             reads=["ones"] + src_keys, writes=[bkey])
    p.op("act", lambda e: e.activation(out=out_rstd[0:rows, 0:width], in_=bank[0:rows, 0:width], func=AF.Ln, bias=C.epsb[0:rows, :], scale=inv_n),
         reads=[bkey, "epsb"], writes=["rstdT"])
    p.op("act", lambda e: e.activation(out=out_rstd[0:rows, 0:width], in_=out_rstd[0:rows, 0:width], func=AF.Exp, scale=-0.5),
         reads=["rstdT"], writes=["rstdT"])


def ph_even_in(C, li):
    p, A, d = C.p, C.A, C.d
    C.xnT = A.alloc([128, 8, T], BF16)
    ph_norm(C, d["mixg"], "mix")
    xnT = C.xnT
    win = A.alloc([128, 8, MIXIN], BF16)
    for k in range(8):
        wload(C, win[:, k, :], d["win"][:, k, :], "win")
    wrope = A.alloc([128, 8, 96], BF16)
    p.op("pool", lambda e: e.memset(wrope, 0.0), writes=["wrope"])
    _dma(C, "pool", wrope[:, :, 64:96], d["win"][:, :, 1152:1184], [], ["wrope"], "wrope")
    wq = A.alloc([128, 3, 768], BF16)
    for i in range(3):
        wload(C, wq[:, i, :], d["wq"][:, i, :], "wq")
    wkp = A.alloc([128, 2, 8, 96], BF16)
    p.op("pool", lambda e: e.memset(wkp, 0.0), writes=["wkp"])
    wv = A.alloc([128, 2, 8, 64], BF16)
    wkv4 = d["wkv"].rearrange("p i (h c) -> p i h c", c=128)
    for i in range(2):
        _dma(C, "pool", wkp[:, i, :, 0:64], wkv4[:, i, :, 0:64], [], ["wkp"], "wkp")
        _dma(C, "pool", wv[:, i, :, :], wkv4[:, i, :, 64:128], [], ["wv"], "wv")
    rm = A.alloc([96, 96], BF16)
    _dma(C, "pool", rm, d["rm"], [], ["rm"], "rm")
    gql = A.alloc([128, 3], F32)
    gkl = A.alloc([128, 2], F32)
    gqh = A.alloc([96, 1], F32)
    gkh = A.alloc([96, 1], F32)
    _dma(C, "sp", gql, d["gql"], [], ["gsm"], "gsm")
    _dma(C, "sp", gkl, d["gkl"], [], ["gsm"], "gsm")
    _dma(C, "sp", gqh, d["gqh"], [], ["gsm"], "gsm")
    _dma(C, "sp", gkh, d["gkh"], [], ["gsm"], "gsm")
    cosb = [A.alloc([96, 512], F32) for _ in range(2)]
    sinb = [A.alloc([96, 512], F32) for _ in range(2)]
    ust = [A.alloc([128, 512], F32) for _ in range(2)]
    lat = A.alloc([128, 3, 512], F32)
    sq = A.alloc([128, 3, 512], BF16)
    rstdT = A.alloc([128, 512], F32)
    latn = A.alloc([128, 3, 512], BF16)
    kvn = A.alloc([128, 2, 512], BF16)
    hq = [A.alloc([96, 512], BF16) for _ in range(4)]
    t1 = [A.alloc([96, 512], F32) for _ in range(2)]
    t2 = [A.alloc([96, 512], F32) for _ in range(2)]
    sqh = [A.alloc([96, 512], BF16) for _ in range(2)]
    rsh = [A.alloc([96, 512], F32) for _ in range(2)]
    vst = [A.alloc([128, 8, 65], BF16) for _ in range(2)]
    for s in range(2):
        p.op("pool", lambda e, s=s: e.memset(vst[s], 1.0), writes=["vst%d" % s])
    nu = 0
    nh = 0
    for tb in range(4):
        tsl = slice(tb * 512, (tb + 1) * 512)
        cosT, sinT = cosb[tb % 2], sinb[tb % 2]
        ckey, skey2 = "cos%d" % (tb % 2), "sin%d" % (tb % 2)
        _dma(C, "sp", cosT[64:96, :], d["cosT"][64:96, tsl], [], [ckey], ckey)
        _dma(C, "sp", sinT[64:96, :], d["sinT"][64:96, tsl], [], [skey2], skey2)
        for m in range(4):
            bk = C.B[nu % 2]
            bkey = "B%d" % (nu % 2)
            st = ust[nu % 2]
            skey = "ust%d" % (nu % 2)
            nu += 1
            for k in range(8):
                p.op("pe", lambda e, tsl=tsl, k=k, m=m, bk=bk: e.matmul(bk, win[:, k, m * 128:(m + 1) * 128], xnT[:, k, tsl], start=(k == 0), stop=(k == 7)),
                     reads=["win"] + xk(tb, k), writes=[bkey])
            p.op("act", lambda e, bk=bk, st=st: e.activation(out=st, in_=bk, func=AF.Copy), reads=[bkey], writes=[skey])
            _dma(C, "sp", d["uT_o"][m][:, tsl], st, [skey], ["uTo_%d_%d" % (m, tb)], skey)
            C.outkeys.append("uTo_%d_%d" % (m, tb))
        for (nm, m0, nch, inv_n, gl, outn) in (("q", 4, 3, 1.0 / QL, gql, latn), ("kv", 7, 2, 1.0 / KVL, gkl, kvn)):
            for i in range(nch):
                bk = C.B[2 + i]
                bkey = "B%d" % (2 + i)
                m = m0 + i
                for k in range(8):
                    p.op("pe", lambda e, tsl=tsl, k=k, m=m, bk=bk: e.matmul(bk, win[:, k, m * 128:(m + 1) * 128], xnT[:, k, tsl], start=(k == 0), stop=(k == 7)),
                         reads=["win"] + xk(tb, k), writes=[bkey])
                p.op("act", lambda e, i=i, bk=bk: e.activation(out=lat[:, i, :], in_=bk, func=AF.Copy), reads=[bkey], writes=["lat"])
                p.op("act", lambda e, i=i, bk=bk: e.activation(out=sq[:, i, :], in_=bk, func=AF.Square), reads=[bkey], writes=["sq"])
            _rmsT(C, ["sq"], sq, nch, 128, 512, inv_n, rstdT, C.B[5], "B5")
            for i in range(nch):
                p.op("dve", lambda e, i=i, gl=gl, outn=outn: e.scalar_tensor_tensor(out=outn[:, i, :], in0=lat[:, i, :], scalar=gl[:, i:i + 1], in1=rstdT, op0=ALU.mult, op1=ALU.mult),
                     reads=["lat", "rstdT", "gsm"], writes=[nm + "n"])
        def head_stages(which, h, b, hqt, hkey):
            bk = C.B[b]
            bkey = "B%d" % b
            ssb, sskey = (C.B[5], "B5") if b == 0 else (C.B[3], "B3")
            rb, rkey = (C.B[4], "B4") if b == 0 else (C.B[2], "B2")
            sqb, rsb, t1b, t2b = sqh[b], rsh[b], t1[b], t2[b]
            st = [[] for _ in range(9)]
            if which == "q":
                for i in range(3):
                    st[0].append(("pe", lambda e, i=i, h=h, bk=bk: e.matmul(bk[0:96, :], wq[:, i, h * 96:(h + 1) * 96], latn[:, i, :], start=(i == 0), stop=(i == 2)),
                                  ["wq", "qn"], [bkey]))
                gh = gqh
            else:
                for k in range(8):
                    st[0].append(("pe", lambda e, tsl=tsl, k=k, bk=bk: e.matmul(bk[0:96, :], wrope[:, k, :], xnT[:, k, tsl], start=(k == 0), stop=False),
                                  ["wrope"] + xk(tb, k), [bkey]))
                for i in range(2):
                    st[0].append(("pe", lambda e, i=i, h=h, bk=bk: e.matmul(bk[0:96, :], wkp[:, i, h, :], kvn[:, i, :], start=False, stop=(i == 1)),
                                  ["wkp", "kvn"], [bkey]))
                gh = gkh
            st[1].append(("act", lambda e, bk=bk, sqb=sqb: e.activation(out=sqb, in_=bk[0:96, :], func=AF.Square), [bkey], ["sqh%d" % b]))
            st[2].append(("pe", lambda e, ssb=ssb, sqb=sqb: e.matmul(ssb[0:96, :], C.ones[0:96, 0:96], sqb, start=True, stop=True), ["ones", "sqh%d" % b], [sskey]))
            st[3].append(("act", lambda e, ssb=ssb, rsb=rsb: e.activation(out=rsb, in_=ssb[0:96, :], func=AF.Ln, bias=C.epsb[0:96, :], scale=1.0 / DQK), [sskey, "epsb"], ["rsh%d" % b]))
            st[3].append(("act", lambda e, rsb=rsb: e.activation(out=rsb, in_=rsb, func=AF.Exp, scale=-0.5), ["rsh%d" % b], ["rsh%d" % b]))
            st[4].append(("dve", lambda e, bk=bk, hqt=hqt, gh=gh, rsb=rsb: e.scalar_tensor_tensor(out=hqt, in0=bk[0:96, :], scalar=gh[:, 0:1], in1=rsb, op0=ALU.mult, op1=ALU.mult),
                          [bkey, "rsh%d" % b, "gsm"], [hkey]))
            st[5].append(("pe", lambda e, hqt=hqt, rb=rb: e.matmul(rb[0:96, :], rm, hqt, start=True, stop=True), ["rm", hkey], [rkey]))
            st[6].append(("pool", lambda e, cosT=cosT, hqt=hqt, t1b=t1b: e.tensor_tensor(out=t1b[64:96, :], in0=hqt[64:96, :], in1=cosT[64:96, :], op=ALU.mult),
                          [hkey, ckey], ["t1_%d" % b]))
            st[6].append(("dve", lambda e, sinT=sinT, rb=rb, t2b=t2b: e.tensor_tensor(out=t2b[64:96, :], in0=rb[64:96, :], in1=sinT[64:96, :], op=ALU.mult),
                          [rkey, skey2], ["t2_%d" % b]))
            st[7].append(("dve", lambda e, hqt=hqt, t1b=t1b, t2b=t2b: e.tensor_tensor(out=hqt[64:96, :], in0=t1b[64:96, :], in1=t2b[64:96, :], op=ALU.add),
                          ["t1_%d" % b, "t2_%d" % b], [hkey]))
            return st

        for which in ("q", "k"):
            for hp in range(H // 2):
                pp = nh % 2
                nh += 1
                sts = []
                for b in range(2):
                    h = hp * 2 + b
                    hqt = hq[pp * 2 + b]
                    hkey = "hq%d" % (pp * 2 + b)
                    sts.append((h, hqt, hkey, head_stages(which, h, b, hqt, hkey)))
                for si in range(9):
                    for (h, hqt, hkey, st) in sts:
                        for (eng, fn, rd, wr) in st[si]:
                            p.op(eng, fn, reads=rd, writes=wr)
                for (h, hqt, hkey, st) in sts:
                    if which == "q":
                        ok = "qTo_%d_%d" % (h, tb)
                        _dma(C, "sp", d["qT_o"][h][:, tsl], hqt, [hkey], [ok], hkey)
                        C.outkeys.append(ok)
                    else:
                        for jj in range(4):
                            ok = "kTo_%d_%d_%d" % (h, tb, jj)
                            _dma(C, "sp", d["kT_o"][tb * 4 + jj][:, h, :], hqt[:, jj * 128:(jj + 1) * 128], [hkey], [ok], hkey)
                            C.outkeys.append(ok)
        for jj in range(4):
            j = tb * 4 + jj
            b = j % 2
            bk = C.B[2 + b]
            bkey = "B%d" % (2 + b)
            for i in range(2):
                p.op("pe", lambda e, i=i, jj=jj, bk=bk: e.matmul(bk, kvn[:, i, jj * 128:(jj + 1) * 128], wv[:, i, :, :], start=(i == 0), stop=(i == 1)),
                     reads=["kvn", "wv"], writes=[bkey])
            p.op("dve", lambda e, b=b, bk=bk: e.tensor_copy(out=vst[b][:, :, 0:64], in_=bk.rearrange("p (h c) -> p h c", c=64)),
                 reads=[bkey], writes=["vst%d" % b])
            ok = "vo_%d" % j
            _dma(C, "sp", d["v_o"][:, j, :, :], vst[b], ["vst%d" % b], [ok], "vst%d" % b)
            C.outkeys.append(ok)


def ph_even_out(C, li):
    p, A, d = C.p, C.A, C.d
    scale = DQK ** -0.5
    catT = A.alloc([128, 8, T], BF16)
    mark = A.top
    uextb = [A.alloc([128, NT, 144], F32) for _ in range(2)]
    wa = A.alloc([128, NT, 144], F32)
    wb = A.alloc([128, NT, 144], F32)
    pooled = A.alloc([128, 4, T], BF16)
    invc = A.alloc([128, 4, 16], F32)
    _dma(C, "sp", invc, d["invc"], [], ["invc"], "invc")
    wpool = A.alloc([128, 4, 128], BF16)
    _dma(C, "pool", wpool, d["wpool"], [], ["wpool"], "wpool")
    pscale = A.alloc([128, 4], F32)
    _dma(C, "sp", pscale, d["pscale"], [], ["pscale"], "pscale")
    tmp16 = A.alloc([128, 16], F32)
    for g in range(4):
        w = 2 ** (g + 1)
        src = uextb[g % 2]
        _dma(C, "sp", src, d["uext"][:, g, :, :], [], ["uext%d" % (g % 2)], "uext%d" % (g % 2))
        bufs = [wa, wb]
        sh = 1
        cur = src
        curkey = "uext%d" % (g % 2)
        nb = 0
        while sh < w:
            dst = bufs[nb % 2]
            dkey = "w%d" % (nb % 2)
            nb += 1
            p.op("dve", lambda e, cur=cur, dst=dst, sh=sh: e.tensor_tensor(out=dst[:, :, 2 * sh - 1:144], in0=cur[:, :, 2 * sh - 1:144], in1=cur[:, :, sh - 1:144 - sh], op=ALU.add),
                 reads=[curkey], writes=[dkey])
            if sh > 1:
                pass
            cur = dst
            curkey = dkey
            sh *= 2
        pv = pooled[:, g, :].rearrange("p (j t) -> p j t", t=128)
        p.op("dve", lambda e, cur=cur, src=src, pv=pv, w=w: e.scalar_tensor_tensor(out=pv, in0=cur[:, :, 16:144], scalar=1.0 / w, in1=src[:, :, 16:144], op0=ALU.mult, op1=ALU.subtract),
             reads=[curkey, "uext%d" % (g % 2)], writes=["pooled%d" % g])
        p.op("dve", lambda e, cur=cur, g=g: e.tensor_tensor(out=tmp16, in0=cur[:, 0, 16:32], in1=invc[:, g, :], op=ALU.mult),
             reads=[curkey, "invc"], writes=["tmp16"])
        p.op("dve", lambda e, src=src, g=g: e.tensor_tensor(out=pooled[:, g, 0:16], in0=tmp16, in1=src[:, 0, 16:32], op=ALU.subtract),
             reads=["tmp16", "uext%d" % (g % 2), "pooled%d" % g], writes=["pooled%d" % g])
    n = 0
    for g in range(4):
        for tb in range(4):
            b = n % 2
            n += 1
            bk = C.B[b]
            p.op("pe", lambda e, g=g, tb=tb, bk=bk: e.matmul(bk, wpool[:, g, :], pooled[:, g, tb * 512:(tb + 1) * 512], start=True, stop=True),
                 reads=["wpool", "pooled%d" % g], writes=["B%d" % b])
            p.op("act", lambda e, g=g, tb=tb, bk=bk: e.activation(out=catT[:, g, tb * 512:(tb + 1) * 512], in_=bk, func=AF.Copy, scale=pscale[:, g:g + 1]),
                 reads=["B%d" % b, "pscale"], writes=["catT"])
    p.barrier()
    A.top = mark
    qT = A.alloc([96, H, T], BF16)
    for h in range(H):
        _dma(C, "sp", qT[:, h, :], d["qT"][h], [], ["qT"], "qT%d" % (h % 4))
    mask = A.alloc([128, 8, 128], BF16)
    _dma(C, "pool", mask, d["mask"], [], ["mask"], "mask")
    krow = [A.alloc([96, 8, 4, 128], BF16) for _ in range(2)]
    vrow = [A.alloc([128, 8, 4, 65], BF16) for _ in range(2)]
    atok = A.alloc([128, NT, 512], BF16)
    rden = A.alloc([128, 4], F32)
    nrow = 0
    ns = 0
    LAG = 1
    pT = [A.alloc([128, 2, 512], BF16) for _ in range(3)]
    SD = [(C.SA.rearrange("p (b c) -> p b c", b=2), ["B4", "B5"]),
          (C.SB.rearrange("p (b c) -> p b c", b=2), ["BT0", "BT1"])]
    for sb in range(4):
        for hg in range(2):
            steps = []
            for jr in range(4 * sb + 4):
                sl = nrow % 2
                nrow += 1
                kr, vr = krow[sl], vrow[sl]
                pre = [("sp", lambda e, s, kr=kr, jr=jr, hg=hg: [e.dma_start(out=kr[:, r, :, :], in_=d["kall"][jr][:, r, hg * 4:(hg + 1) * 4, :]).then_inc(s, 16) for r in range(8)],
                        [], ["krow%d" % sl], 8, "krow%d" % sl),
                       ("sp", lambda e, s, vr=vr, jr=jr, hg=hg: [e.dma_start(out=vr[:, r, :, :], in_=d["vall"][jr][:, r, hg * 4:(hg + 1) * 4, :]).then_inc(s, 16) for r in range(8)],
                        [], ["vrow%d" % sl], 8, "vrow%d" % sl)]
                i0 = max(0, jr - 4 * sb)
                diag = jr >= 4 * sb
                ncols = (4 - i0) * 128
                q0 = (4 * sb + i0) * 128
                kk = "krow%d" % sl
                for r in range(8):
                    for hp in range(2):
                        sd, sdkeys = SD[ns % 2]
                        pt = pT[ns % 3]
                        ptk = "pT%d" % (ns % 3)
                        ns += 1
                        S_ops = list(pre)
                        pre = []
                        for half in range(2):
                            hh = hp * 2 + half
                            h = hg * 4 + hh
                            bk = sd[:, half, :]
                            bkey = sdkeys[half]
                            if diag:
                                S_ops.append(("pe", lambda e, bk=bk, kr=kr, r=r, hh=hh, h=h, q0=q0: e.matmul(bk[:, 0:128], kr[:, r, hh, :], qT[:, h, q0:q0 + 128], start=True, stop=False),
                                              [kk, "qT"], [bkey], 0, None))
                                S_ops.append(("pe", lambda e, bk=bk, r=r: e.matmul(bk[:, 0:128], C.ident, mask[:, r, :], start=False, stop=True),
                                              ["ident", "mask"], [bkey], 0, None))
                                if ncols > 128:
                                    S_ops.append(("pe", lambda e, bk=bk, kr=kr, r=r, hh=hh, h=h, q0=q0, ncols=ncols: e.matmul(bk[:, 128:ncols], kr[:, r, hh, :], qT[:, h, q0 + 128:q0 + ncols], start=True, stop=True),
                                                  [kk, "qT"], [bkey], 0, None))
                            else:
                                S_ops.append(("pe", lambda e, bk=bk, kr=kr, r=r, hh=hh, h=h, q0=q0, ncols=ncols: e.matmul(bk[:, 0:ncols], kr[:, r, hh, :], qT[:, h, q0:q0 + ncols], start=True, stop=True),
                                              [kk, "qT"], [bkey], 0, None))
                        S_ops.append(("act", lambda e, sd=sd, pt=pt, ncols=ncols: e.activation(out=pt[:, :, 0:ncols], in_=sd[:, :, 0:ncols], func=AF.Exp, scale=scale),
                                      list(sdkeys), [ptk], 0, None))
                        PV_ops = []
                        for half in range(2):
                            hh = hp * 2 + half
                            for i in range(i0, 4):
                                first = (jr == 0 and r == 0 and hh == 0)
                                last = (jr == 4 * sb + i and r == 7)
                                PV_ops.append(("pe", lambda e, i0=i0, i=i, pt=pt, vr=vr, r=r, hh=hh, half=half, first=first, last=last: e.matmul(C.B[i][:, hh * 65:(hh + 1) * 65], pt[:, half, (i - i0) * 128:(i - i0 + 1) * 128], vr[:, r, hh, :], start=first, stop=last, skip_group_check=True),
                                               [ptk, "vrow%d" % sl], ["B%d" % i], 0, None))
                        steps.append((S_ops, PV_ops))
            for n in range(len(steps) + LAG):
                if n < len(steps):
                    for (eng, fn, rd, wr, dma, slot) in steps[n][0]:
                        p.op(eng, fn, reads=rd, writes=wr, dma=dma, slot=slot)
                if n - LAG >= 0:
                    for (eng, fn, rd, wr, dma, slot) in steps[n - LAG][1]:
                        p.op(eng, fn, reads=rd, writes=wr, dma=dma, slot=slot)
            for i in range(4):
                j = 4 * sb + i
                acc = C.B[i][:, 0:260].rearrange("p (h c) -> p h c", c=65)
                p.op("dve", lambda e, acc=acc: e.reciprocal(out=rden, in_=acc[:, :, 64]), reads=["B%d" % i], writes=["rden"])
                for hh in range(4):
                    h = hg * 4 + hh
                    p.op("dve", lambda e, acc=acc, hh=hh, h=h, j=j: e.tensor_scalar(out=atok[:, j, h * 64:(h + 1) * 64], in0=acc[:, hh, 0:64], scalar1=rden[:, hh:hh + 1], scalar2=None, op0=ALU.mult),
                         reads=["B%d" % i, "rden"], writes=["atok%d" % j])
    for j in range(NT):
        bt = C.BT[j % 2]
        for m in range(4):
            p.op("pe", lambda e, j=j, m=m, bt=bt: e.transpose(out=bt[:, m * 128:(m + 1) * 128], in_=atok[:, j, m * 128:(m + 1) * 128], identity=C.ident),
                 reads=["atok%d" % j, "ident"], writes=["BT%d" % (j % 2)])
        p.op("dve", lambda e, j=j, bt=bt: e.tensor_copy(out=catT[:, 4:8, j * 128:(j + 1) * 128], in_=bt[:, 0:512].rearrange("p (m t) -> p m t", t=128)),
             reads=["BT%d" % (j % 2)], writes=["catT"])
    p.barrier()
    A.top = mark
    _out_proj(C, catT, "catT", d["wout"])


def _out_proj(C, srcT, skey, w_d):
    p, A = C.p, C.A
    xs = C.xs
    wo = A.alloc([128, 8, 1024], BF16)
    for k in range(8):
        wload(C, wo[:, k, :], w_d[:, k, :], "wo")
    n = 0
    for j in range(NT):
        for hf in range(2):
            b = n % 2
            n += 1
            bk = C.B[b]
            for m in range(8):
                p.op("pe", lambda e, j=j, hf=hf, m=m, bk=bk: e.matmul(bk, srcT[:, m, j * 128:(j + 1) * 128], wo[:, m, hf * 512:(hf + 1) * 512], start=(m == 0), stop=(m == 7)),
                     reads=[skey, "wo"], writes=["B%d" % b])
            p.op("dve", lambda e, j=j, hf=hf, bk=bk: e.tensor_tensor(out=xs[:, j, hf * 512:(hf + 1) * 512], in0=bk, in1=xs[:, j, hf * 512:(hf + 1) * 512], op=ALU.add),
                 reads=["B%d" % b, "xs%d" % j], writes=["xs%d" % j])


def ph_odd_in(C, li):
    p, A, d = C.p, C.A, C.d
    C.xnT = A.alloc([128, 8, T], BF16)
    ph_norm(C, d["mixg"], "mix")
    xnT = C.xnT
    wst = [A.alloc([128, 8, 384], BF16) for _ in range(2)]
    gst = [A.alloc([128, 512], F32) for _ in range(2)]
    cst = [A.alloc([128, 512], F32) for _ in range(2)]
    ust = [A.alloc([128, 512], F32) for _ in range(2)]
    n = 0
    for m in range(8):
        sl = m % 2
        w = wst[sl]
        for q in range(3):
            wload(C, w[:, :, q * 128:(q + 1) * 128], d["wcin"][:, :, q * 1024 + m * 128:q * 1024 + (m + 1) * 128], "wst%d" % sl)
        for tb in range(4):
            tsl = slice(tb * 512, (tb + 1) * 512)
            b = n % 2
            n += 1
            banks = [C.B[b], C.B[2 + b], C.B[4 + b]]
            bkeys = ["B%d" % b, "B%d" % (2 + b), "B%d" % (4 + b)]
            for q in range(3):
                for k in range(8):
                    p.op("pe", lambda e, tsl=tsl, q=q, k=k, w=w, bk=banks[q]: e.matmul(bk, w[:, k, q * 128:(q + 1) * 128], xnT[:, k, tsl], start=(k == 0), stop=(k == 7)),
                         reads=["wst%d" % sl] + xk(tb, k), writes=[bkeys[q]])
            p.op("act", lambda e, b=b, bk=banks[0]: e.activation(out=gst[b], in_=bk, func=AF.Copy), reads=[bkeys[0]], writes=["gst%d" % b])
            p.op("act", lambda e, b=b, bk=banks[1]: e.activation(out=cst[b], in_=bk, func=AF.Copy), reads=[bkeys[1]], writes=["cst%d" % b])
            p.op("dve", lambda e, b=b, bk=banks[2]: e.tensor_tensor(out=ust[b], in0=cst[b], in1=bk, op=ALU.mult), reads=["cst%d" % b, bkeys[2]], writes=["ust%d" % b])
            ok = "gbo_%d_%d" % (m, tb)
            _dma(C, "sp", d["gb_o"][m][:, tsl], gst[b], ["gst%d" % b], [ok], "gst%d" % b)
            C.outkeys.append(ok)
            ok = "uco_%d_%d" % (m, tb)
            _dma(C, "sp", d["uc_o"][m][:, tsl], ust[b], ["ust%d" % b], [ok], "ust%d" % b)
            C.outkeys.append(ok)


def ph_odd_out(C, li):
    p, A, d = C.p, C.A, C.d
    vT = A.alloc([128, 8, T], BF16)
    cw = A.alloc([128, 8, 3], F32)
    _dma(C, "sp", cw, d["cw"], [], ["cw"], "cw")
    ue = [A.alloc([128, NT, 130], F32) for _ in range(2)]
    gb = [A.alloc([128, T], F32) for _ in range(2)]
    y = [A.alloc([128, NT, 128], F32) for _ in range(2)]
    for m in range(8):
        sl = m % 2
        _dma(C, "sp", ue[sl], d["ucext"][:, m, :, :], [], ["ue%d" % sl], "ue%d" % sl)
        _dma(C, "sp", gb[sl], d["gbT"][m], [], ["gb%d" % sl], "gb%d" % sl)
        eng = "dve"
        u, yy, g = ue[sl], y[sl], gb[sl]
        p.op(eng, lambda e, u=u, yy=yy, m=m: e.tensor_scalar(out=yy, in0=u[:, :, 0:128], scalar1=cw[:, m, 0:1], scalar2=None, op0=ALU.mult),
             reads=["ue%d" % sl, "cw"], writes=["y%d" % sl])
        p.op(eng, lambda e, u=u, yy=yy, m=m: e.scalar_tensor_tensor(out=yy, in0=u[:, :, 1:129], scalar=cw[:, m, 1:2], in1=yy, op0=ALU.mult, op1=ALU.add),
             reads=["ue%d" % sl, "cw", "y%d" % sl], writes=["y%d" % sl])
        p.op(eng, lambda e, u=u, yy=yy, m=m: e.scalar_tensor_tensor(out=yy, in0=u[:, :, 2:130], scalar=cw[:, m, 2:3], in1=yy, op0=ALU.mult, op1=ALU.add),
             reads=["ue%d" % sl, "cw", "y%d" % sl], writes=["y%d" % sl])
        p.op(eng, lambda e, yy=yy, g=g, m=m: e.tensor_tensor(out=vT[:, m, :], in0=yy.rearrange("p j t -> p (j t)"), in1=g, op=ALU.mult),
             reads=["y%d" % sl, "gb%d" % sl], writes=["vT"])
    _out_proj(C, vT, "vT", d["wcout"])


def build_launch(spec):
    nc = bass.Bass("TRN2", target_bir_lowering=False)
    C = Ctx()
    C.nc = nc
    C.p = Prog(nc)
    C.d = {}
    C.outkeys = []
    for name, (shape, dt) in spec["inputs"].items():
        C.d[name] = nc.dram_tensor(name, list(shape), dt, kind="ExternalInput").ap()
    for name, (shape, dt) in spec["outputs"].items():
        C.d[name] = nc.dram_tensor(name, list(shape), dt, kind="ExternalOutput").ap()
    import contextlib
    with contextlib.ExitStack() as st:
        nbytes = 204 * 1024
        ar = st.enter_context(nc.sbuf_tensor("arena", [128, nbytes], U8))
        C.A = Arena(ar, nbytes)
        C.B = [st.enter_context(nc.psum_tensor("bank%d" % i, [128, 512], F32))[:] for i in range(4)]
        C.SA = st.enter_context(nc.psum_tensor("bankSA", [128, 1024], F32))[:]
        C.SB = st.enter_context(nc.psum_tensor("bankSB", [128, 1024], F32))[:]
        C.B += [C.SA[:, 0:512], C.SA[:, 512:1024]]
        C.BT = [C.SB[:, 0:512].bitcast(BF16), C.SB[:, 512:1024].bitcast(BF16)]
        A = C.A
        C.xs = A.alloc([128, NT, D], F32)
        C.gT = A.alloc([128, 8], F32)
        C.ss = A.alloc([128, NT], F32)
        C.rstd = A.alloc([128, NT], F32)
        C.gfull = A.alloc([128, 8, 128], F32)
        C.onesf = A.alloc([128, 128], F32)
        C.p.op("pool", lambda e: e.memset(C.onesf, 1.0), writes=["onesf"])
        C.xb = [A.alloc([128, D], BF16) for _ in range(2)]
        C.epsb = A.alloc([128, 1], F32)
        C.stg = [A.alloc([128, 1536], F32) for _ in range(3)]
        C.stg_n = 0
        C.p.op("pool", lambda e: e.memset(C.epsb, EPS), writes=["epsb"])
        ph_consts(C)
        base = A.top
        for ph in spec["phases"]:
            if ph in (_load_x, _store_x):
                ph(C)
                continue
            C.p.barrier()
            A.top = base
            ph(C)
        C.p.op("sp", None, reads=list(C.outkeys))
        C.p.op("pool", None, reads=list(C.outkeys))
        C.p.emit()
    return nc


def alloc_ffn(C):
    A = C.A
    C.xnT = A.alloc([128, 8, T], BF16)
    C.wg = [A.alloc([128, 8, 384], BF16) for _ in range(2)]
    C.wu = [A.alloc([128, 8, 384], BF16) for _ in range(2)]
    C.wd = [A.alloc([128, 3, D], BF16) for _ in range(2)]
    C.hT = [A.alloc([128, 3, T], BF16) for _ in range(2)]
    C.sg = [A.alloc([128, 512], F32) for _ in range(2)]


def mk_ffn(which):
    def f(C):
        alloc_ffn(C)
        ph_ffn(C, C.d[which + "_g"], C.d[which + "_wg"], C.d[which + "_wu"], C.d[which + "_wd"], which)
    return f


FFN_IN = lambda w: {w + "_g": ((128, 8), F32), w + "_wg": ((128, 8, DFF), F32), w + "_wu": ((128, 8, DFF), F32), w + "_wd": ((128, NFC, D), F32)}
X_IO = ((128, NT, D), F32)
CONST_IN = {"ident": ((128, 128), F32), "ones": ((128, 128), F32)}
EVEN_IN_W = {"mixg": ((128, 8), F32), "win": ((128, 8, MIXIN), F32), "wq": ((128, 3, 768), F32), "wkv": ((128, 2, 1024), F32),
             "rm": ((96, 96), F32), "gql": ((128, 3), F32), "gkl": ((128, 2), F32), "gqh": ((96, 1), F32), "gkh": ((96, 1), F32),
             "cosT": ((96, T), F32), "sinT": ((96, T), F32)}
EVEN_IN_OUT = {"uT_o": ((4, 128, T), F32), "qT_o": ((H, 96, T), BF16), "kT_o": ((NT, 96, H, 128), BF16), "v_o": ((128, NT, H, 65), BF16)}
EVEN_OUT_IN = {"uext": ((128, 4, NT, 144), F32), "invc": ((128, 4, 16), F32), "wpool": ((128, 4, 128), F32), "pscale": ((128, 4), F32),
               "qT": ((H, 96, T), BF16), "mask": ((128, 8, 128), F32), "kall": ((NT, 96, 8, H, 128), BF16), "vall": ((NT, 128, 8, H, 65), BF16),
               "wout": ((128, 8, D), F32)}
ODD_IN_W = {"mixg": ((128, 8), F32), "wcin": ((128, 8, 3 * D), F32)}
ODD_IN_OUT = {"gb_o": ((8, 128, T), F32), "uc_o": ((8, 128, T), F32)}
ODD_OUT_IN = {"cw": ((128, 8, 3), F32), "ucext": ((128, 8, NT, 130), F32), "gbT": ((8, 128, T), F32), "wcout": ((128, 8, D), F32)}


def _load_x(C):
    ph_load_x(C, C.d["x"])


def _store_x(C):
    ph_store_x(C, C.d["x_o"], "xo")


def spec_A():
    return dict(inputs={"x": X_IO, **CONST_IN, **FFN_IN("f1"), **EVEN_IN_W},
                outputs={"x_o": X_IO, **EVEN_IN_OUT},
                phases=[_load_x, mk_ffn("f1"), _store_x, lambda C: ph_even_in(C, 0)])


def spec_B():
    return dict(inputs={"x": X_IO, **CONST_IN, **EVEN_OUT_IN, **FFN_IN("f2"), **FFN_IN("f1"), **ODD_IN_W},
                outputs={"x_o": X_IO, **ODD_IN_OUT},
                phases=[_load_x, lambda C: ph_even_out(C, 0), mk_ffn("f2"), mk_ffn("f1"), _store_x, lambda C: ph_odd_in(C, 1)])


def spec_C():
    return dict(inputs={"x": X_IO, **CONST_IN, **ODD_OUT_IN, **FFN_IN("f2"), **FFN_IN("f1"), **EVEN_IN_W},
                outputs={"x_o": X_IO, **EVEN_IN_OUT},
                phases=[_load_x, lambda C: ph_odd_out(C, 1), mk_ffn("f2"), mk_ffn("f1"), _store_x, lambda C: ph_even_in(C, 2)])


def spec_D():
    return dict(inputs={"x": X_IO, **CONST_IN, **ODD_OUT_IN, **FFN_IN("f2")},
                outputs={"x_o": X_IO},
                phases=[_load_x, lambda C: ph_odd_out(C, 3), mk_ffn("f2"), _store_x])


_NC_CACHE = {}


def _get_nc(name):
    if name not in _NC_CACHE:
        _NC_CACHE[name] = build_launch({"A": spec_A, "B": spec_B, "C": spec_C, "D": spec_D}[name]())
    return _NC_CACHE[name]


def _kmaj(w, p=128):
    K, N = w.shape
    return np.ascontiguousarray(w.reshape(K // p, p, N).transpose(1, 0, 2))


def _vec(g, p=128):
    return np.ascontiguousarray(g.reshape(-1, p).T)


def _ffn_inputs(prefix, inputs, which, layer):
    return {prefix + "_g": _vec(inputs[which + "_norm"][layer]),
            prefix + "_wg": _kmaj(inputs[which + "_w_gate"][layer]),
            prefix + "_wu": _kmaj(inputs[which + "_w_up"][layer]),
            prefix + "_wd": _kmaj(inputs[which + "_w_down"][layer])}


def _consts():
    return {"ident": np.eye(128, dtype=np.float32), "ones": np.ones((128, 128), np.float32)}


def _rope_tables(c):
    pos = (np.arange(NT)[:, None] * NC + c) * 128 + np.arange(128)[None, :]
    pos = pos.reshape(-1).astype(np.float32)
    inv = (np.float32(10000.0) ** (-np.arange(0, 32, 2, dtype=np.float32) / np.float32(32))).astype(np.float32)
    ang = (pos[:, None] * inv[None, :]).astype(np.float32)
    cos = np.cos(ang).astype(np.float32).T
    sin = np.sin(ang).astype(np.float32).T
    cosT = np.zeros((96, T), np.float32)
    sinT = np.zeros((96, T), np.float32)
    cosT[64:80] = cos
    cosT[80:96] = cos
    sinT[64:80] = sin
    sinT[80:96] = sin
    return cosT, sinT


def _rm():
    rm = np.zeros((96, 96), np.float32)
    for i in range(16):
        rm[80 + i, 64 + i] = -1.0
        rm[64 + i, 80 + i] = 1.0
    return rm


def _even_in_inputs(inputs, i, c):
    cosT, sinT = _rope_tables(c)
    return {"mixg": _vec(inputs["mix_norm"][2 * i]), "win": _kmaj(inputs["a_w_in"][i]), "wq": _kmaj(inputs["a_w_q_up"][i]),
            "wkv": _kmaj(inputs["a_w_kv_up"][i]), "rm": _rm(), "gql": _vec(inputs["a_q_a_norm"][i]), "gkl": _vec(inputs["a_kv_a_norm"][i]),
            "gqh": np.ascontiguousarray(inputs["a_q_head_norm"][i].reshape(96, 1)), "gkh": np.ascontiguousarray(inputs["a_k_head_norm"][i].reshape(96, 1)),
            "cosT": cosT, "sinT": sinT}


def _mask(c):
    m = np.zeros((128, 8, 128), np.float32)
    for r in range(8):
        if r > c:
            m[:, r, :] = -30000.0
        elif r == c:
            m[:, r, :] = np.where(np.arange(128)[:, None] > np.arange(128)[None, :], -30000.0, 0.0)
    return m


def _even_out_inputs(inputs, i, c, res):
    uT = np.stack([r["uT_o"] for r in res])
    uT = uT.reshape(NC, 4, 128, NT, 128)
    glob = uT.transpose(1, 2, 3, 0, 4).reshape(4, 128, NT * NC, 128)
    ext = np.zeros((128, 4, NT, 144), np.float32)
    for j in range(NT):
        t = 8 * j + c
        ext[:, :, j, 16:144] = glob[:, :, t, :].transpose(1, 0, 2)
        if t > 0:
            ext[:, :, j, 0:16] = glob[:, :, t - 1, 112:128].transpose(1, 0, 2)
    invc = np.zeros((128, 4, 16), np.float32)
    for g in range(4):
        w = 2 ** (g + 1)
        if c == 0:
            invc[:, g, :] = 1.0 / np.minimum(np.arange(16) + 1, w).astype(np.float32)
        else:
            invc[:, g, :] = 1.0 / w
    kall = np.stack([r["kT_o"] for r in res])
    kall = np.ascontiguousarray(kall.transpose(1, 2, 0, 3, 4))
    vall = np.stack([r["v_o"] for r in res])
    vall = np.ascontiguousarray(vall.transpose(2, 1, 0, 3, 4))
    return {"uext": ext, "invc": invc, "wpool": np.ascontiguousarray(inputs["a_w_pool"][i].transpose(1, 0, 2)),
            "pscale": _vec(inputs["a_pool_scale"][i]), "qT": res[c]["qT_o"], "mask": _mask(c), "kall": kall, "vall": vall,
            "wout": _kmaj(inputs["a_w_out"][i])}, (kall, vall)


def _odd_in_inputs(inputs, i):
    return {"mixg": _vec(inputs["mix_norm"][2 * i + 1]), "wcin": _kmaj(inputs["c_w_in"][i])}


def _odd_out_inputs(inputs, i, c, res):
    uc = np.stack([r["uc_o"] for r in res]).reshape(NC, 8, 128, NT, 128)
    glob = uc.transpose(1, 2, 3, 0, 4).reshape(8, 128, NT * NC, 128)
    ext = np.zeros((128, 8, NT, 130), np.float32)
    for j in range(NT):
        t = 8 * j + c
        ext[:, :, j, 2:130] = glob[:, :, t, :].transpose(1, 0, 2)
        if t > 0:
            ext[:, :, j, 0:2] = glob[:, :, t - 1, 126:128].transpose(1, 0, 2)
    cw = np.ascontiguousarray(inputs["c_conv_w"][i].reshape(3, 8, 128).transpose(2, 1, 0))
    return {"cw": cw, "ucext": ext, "gbT": res[c]["gb_o"], "wcout": _kmaj(inputs["c_w_out"][i])}


def _run(name, in_maps):
    nc = _get_nc(name)
    res = run_bass_kernel_spmd(nc, in_maps, core_ids=list(range(NC)))
    return res.results


def kernel(**inputs):
    inputs = {k: np.asarray(v) for k, v in inputs.items()}
    x = inputs["x"][0].reshape(NT, NC, 128, D)
    xc = [np.ascontiguousarray(x[:, c].transpose(1, 0, 2)) for c in range(NC)]
    f1 = _ffn_inputs("f1", inputs, "ffn1", 0)
    res = _run("A", [{"x": xc[c], **_consts(), **f1, **_even_in_inputs(inputs, 0, c)} for c in range(NC)])
    f2 = _ffn_inputs("f2", inputs, "ffn2", 0)
    f1 = _ffn_inputs("f1", inputs, "ffn1", 1)
    oi = _odd_in_inputs(inputs, 0)
    maps = []
    for c in range(NC):
        eo, _ = _even_out_inputs(inputs, 0, c, res)
        maps.append({"x": res[c]["x_o"], **_consts(), **eo, **f2, **f1, **oi})
    res = _run("B", maps)
    f2 = _ffn_inputs("f2", inputs, "ffn2", 1)
    f1 = _ffn_inputs("f1", inputs, "ffn1", 2)
    maps = [{"x": res[c]["x_o"], **_consts(), **_odd_out_inputs(inputs, 0, c, res), **f2, **f1, **_even_in_inputs(inputs, 1, c)} for c in range(NC)]
    res = _run("C", maps)
    f2 = _ffn_inputs("f2", inputs, "ffn2", 2)
    f1 = _ffn_inputs("f1", inputs, "ffn1", 3)
    oi = _odd_in_inputs(inputs, 1)
    maps = []
    for c in range(NC):
        eo, _ = _even_out_inputs(inputs, 1, c, res)
        maps.append({"x": res[c]["x_o"], **_consts(), **eo, **f2, **f1, **oi})
    res = _run("B", maps)
    f2 = _ffn_inputs("f2", inputs, "ffn2", 3)
    maps = [{"x": res[c]["x_o"], **_consts(), **_odd_out_inputs(inputs, 1, c, res), **f2} for c in range(NC)]
    res = _run("D", maps)
    out = np.zeros((NT, NC, 128, D), np.float32)
    for c in range(NC):
        out[:, c] = res[c]["x_o"].transpose(1, 0, 2)
    return out.reshape(1, S, D)
```

```python
import numpy as np
import ml_dtypes
import concourse.bass as bass
import concourse.mybir as mybir
from concourse.bass_utils import run_bass_kernel_spmd

F32 = mybir.dt.float32
BF16 = mybir.dt.bfloat16
U8 = mybir.dt.uint8
AF = mybir.ActivationFunctionType
ALU = mybir.AluOpType

NC = 8
S = 16384
D = 1024
T = S // NC
NT = T // 128
DFF = 2816
NFC = DFF // 128
EPS = 1e-6
H = 8
DQK = 96
DV = 64
QL = 384
KVL = 256
MIXIN = 1184
ENGS = ("pe", "act", "dve", "pool", "sp")
FF_GROUPS = [(0, 3), (3, 6), (6, 9), (9, 12), (12, 15), (15, 18), (18, 20), (20, 22)]


class _Op:
    __slots__ = ("idx", "eng", "fn", "deps", "dma", "slot", "ms", "tok", "persist")

    def __init__(self, idx, eng, fn, deps, dma, slot):
        self.idx, self.eng, self.fn, self.deps, self.dma, self.slot = idx, eng, fn, deps, dma, slot
        self.ms = False
        self.tok = None
        self.persist = False


class Prog:
    def __init__(self, nc):
        self.nc = nc
        self.ops = []
        self.last_writer = {}
        self.readers = {}
        self.since_barrier = []

    def op(self, eng, fn, reads=(), writes=(), dma=0, slot=None):
        bank_reads = [k for k in reads if k[0] == "B"]
        if bank_reads:
            reads = [k for k in reads if k[0] != "B"]
            writes = list(writes) + bank_reads
        deps = set()
        for k in reads:
            w = self.last_writer.get(k)
            if w is not None:
                deps.add(w)
        for k in writes:
            w = self.last_writer.get(k)
            if w is not None:
                deps.add(w)
            deps.update(self.readers.get(k, ()))
        best = {}
        red = set()
        for dd in deps:
            q = self.ops[dd]
            if q.dma:
                red.add(dd)
            elif dd > best.get(q.eng, -1):
                best[q.eng] = dd
        red.update(best.values())
        deps = red
        idx = len(self.ops)
        o = _Op(idx, eng, fn, deps, dma, slot)
        allk = list(reads) + list(writes)
        o.persist = bool(dma) and len(allk) > 0 and all(k.startswith(self.PERSIST) for k in allk)
        self.ops.append(o)
        self.since_barrier.append(idx)
        for k in writes:
            self.last_writer[k] = idx
            self.readers[k] = []
        for k in reads:
            self.readers.setdefault(k, []).append(idx)
        return idx

    PERSIST = ("xs", "xo", "uTo", "qTo", "kTo", "vo_", "gbo", "uco")

    def barrier(self):
        prev = self.since_barrier
        self.since_barrier = []
        last = {}
        dmas = set()
        for i in prev:
            o = self.ops[i]
            if o.fn is None:
                continue
            if o.dma:
                if not o.persist:
                    dmas.add(i)
                else:
                    self.since_barrier.append(i)
            else:
                last[o.eng] = i
        for e in ENGS:
            deps = set(dmas)
            for e2, i in last.items():
                if e2 != e:
                    deps.add(i)
            idx = len(self.ops)
            self.ops.append(_Op(idx, e, None, deps, 0, None))
        self.last_writer = {k: v for k, v in self.last_writer.items() if k.startswith(self.PERSIST)}
        self.readers = {k: v for k, v in self.readers.items() if k.startswith(self.PERSIST)}

    def emit(self):
        nc = self.nc
        ops = self.ops
        for o in ops:
            for d in o.deps:
                q = ops[d]
                if q.dma or q.fn is None:
                    continue
                if q.eng == "pe" and o.eng == "pe" and not o.dma and o.fn is not None:
                    continue
                q.ms = True
        cnt = {e: 0 for e in ENGS}
        slots = {}
        for o in ops:
            if o.fn is None:
                continue
            if o.dma:
                c = slots.get(o.slot, 0) + 16 * o.dma
                slots[o.slot] = c
                o.tok = ("dma:" + o.slot, c)
            elif o.ms:
                cnt[o.eng] += 1
                o.tok = ("eng:" + o.eng, cnt[o.eng])
        names = ["eng:" + e for e in ENGS if cnt[e] > 0] + ["dma:" + s for s in slots]
        sems = {n: nc.alloc_semaphore(name="s%d" % i) for i, n in enumerate(names)}
        by_eng = {e: [o for o in ops if o.eng == e] for e in ENGS}

        def run(en, eng):
            seen = {}
            for o in by_eng[en]:
                need = {}
                for d in o.deps:
                    q = ops[d]
                    if q.tok is None:
                        continue
                    if q.eng == "pe" and en == "pe" and not q.dma and not o.dma:
                        continue
                    sn, c = q.tok
                    if c > need.get(sn, 0):
                        need[sn] = c
                for sn, c in need.items():
                    if seen.get(sn, 0) >= c:
                        continue
                    eng.wait_ge(sems[sn], c)
                    seen[sn] = c
                if o.fn is None:
                    continue
                if o.dma:
                    o.fn(eng, sems[o.tok[0]])
                else:
                    ins = o.fn(eng)
                    if o.ms:
                        ins.then_inc(sems[o.tok[0]], 1)

        with nc.Block() as block:
            @block.tensor
            def _(e):
                run("pe", e)

            @block.scalar
            def _(e):
                run("act", e)

            @block.vector
            def _(e):
                run("dve", e)

            @block.gpsimd
            def _(e):
                run("pool", e)

            @block.sync
            def _(e):
                run("sp", e)


class Arena:
    def __init__(self, ar, nbytes):
        self.ar = ar
        self.n = nbytes
        self.top = 0

    def alloc(self, shape, dt, parts=128):
        esz = 2 if dt == BF16 else 4
        free = int(np.prod(shape[1:]))
        nb = free * esz
        off = (self.top + 63) // 64 * 64
        assert off + nb <= self.n, ("SBUF arena overflow", off, nb, self.n)
        self.top = off + nb
        v = self.ar[0:shape[0], off:off + nb].bitcast(dt)
        if len(shape) == 3:
            v = v.rearrange("p (a b) -> p a b", a=shape[1])
        elif len(shape) == 4:
            v = v.rearrange("p (a b c) -> p a b c", a=shape[1], b=shape[2])
        return v


class Ctx:
    pass


def _dma(C, eng, out, in_, reads, writes, slot):
    C.p.op(eng, lambda e, s: e.dma_start(out=out, in_=in_).then_inc(s, 16), reads=reads, writes=writes, dma=1, slot=slot)


def wload(C, dst, src, wkey, eng="act"):
    shape = tuple(dst.shape)
    n = int(np.prod(shape[1:]))
    assert n <= 1536, shape
    i = C.stg_n % len(C.stg)
    C.stg_n += 1
    st = C.stg[i][0:shape[0], 0:n]
    if len(shape) == 3:
        st = st.rearrange("p (a b) -> p a b", a=shape[1])
    _dma(C, "sp", st, src, [], ["stg%d" % i], "stg%d" % i)
    if eng == "act":
        C.p.op(eng, lambda e: e.activation(out=dst, in_=st, func=AF.Copy), reads=["stg%d" % i], writes=[wkey])
    else:
        C.p.op(eng, lambda e: e.tensor_copy(out=dst, in_=st), reads=["stg%d" % i], writes=[wkey])


def xk(tb, k):
    return ["xnT%d_%d" % (4 * tb + i, k) for i in range(4)]


def ph_load_x(C, x_d):
    for j in range(NT):
        _dma(C, "sp", C.xs[:, j, :], x_d[:, j, :], [], ["xs%d" % j], "xs%d" % j)


def ph_store_x(C, x_d, keyname):
    for j in range(NT):
        _dma(C, "sp", x_d[:, j, :], C.xs[:, j, :], ["xs%d" % j], [keyname + "%d" % j], "xo%d" % (j % 4))
    C.outkeys += [keyname + "%d" % j for j in range(NT)]


def ph_consts(C):
    p = C.p
    A = C.A
    C.ident = A.alloc([128, 128], BF16)
    C.ones = A.alloc([128, 128], BF16)
    _dma(C, "pool", C.ident, C.d["ident"], [], ["ident"], "c_ident")
    _dma(C, "pool", C.ones, C.d["ones"], [], ["ones"], "c_ones")


def ph_norm(C, g_d, tag):
    p = C.p
    xs, xnT = C.xs, C.xnT
    gT = C.gT
    gfull = C.gfull
    _dma(C, "sp", gT, g_d, [], ["gT"], "gT")
    for k in range(8):
        p.op("pool", lambda e, k=k: e.tensor_scalar(out=gfull[:, k, :], in0=C.onesf, scalar1=gT[:, k:k + 1], scalar2=None, op0=ALU.mult),
             reads=["gT", "onesf"], writes=["gfull"])
    ss, rstd = C.ss, C.rstd
    for j in range(NT):
        col = (j // 2) + (8 if j % 2 else 0)
        jb = C.xb[j % 2]
        if j % 2 == 0:
            p.op("act", lambda e, j=j, col=col, jb=jb: e.activation(out=jb, in_=xs[:, j, :], func=AF.Square, accum_out=ss[:, col:col + 1]),
                 reads=["xs%d" % j], writes=["xb0", "ssA"])
        else:
            p.op("dve", lambda e, j=j, col=col, jb=jb: e.scalar_tensor_tensor(out=jb, in0=xs[:, j, :], scalar=1.0, in1=xs[:, j, :], op0=ALU.mult, op1=ALU.mult, accum_out=ss[:, col:col + 1]),
                 reads=["xs%d" % j], writes=["xb1", "ssD"])
    p.op("act", lambda e: e.activation(out=rstd, in_=ss, func=AF.Ln, bias=C.epsb, scale=1.0 / D), reads=["ssA", "ssD", "epsb"], writes=["rstd"])
    p.op("act", lambda e: e.activation(out=rstd, in_=rstd, func=AF.Exp, scale=-0.5), reads=["rstd"], writes=["rstd"])
    for j in range(NT):
        col = (j // 2) + (8 if j % 2 else 0)
        xb = C.xb[j % 2]
        bt = C.BT[j % 2]
        p.op("act", lambda e, j=j, col=col, xb=xb: e.activation(out=xb, in_=xs[:, j, :], func=AF.Copy, scale=rstd[:, col:col + 1]),
             reads=["xs%d" % j, "rstd"], writes=["xb%d" % (j % 2)])
        for k in range(8):
            p.op("pe", lambda e, k=k, xb=xb, bt=bt: e.transpose(out=bt[:, k * 128:(k + 1) * 128], in_=xb[:, k * 128:(k + 1) * 128], identity=C.ident),
                 reads=["xb%d" % (j % 2), "ident"], writes=["BT%d" % (j % 2)])
        p.op("dve", lambda e, j=j, bt=bt: e.tensor_tensor(out=xnT[:, :, j * 128:(j + 1) * 128], in0=bt.rearrange("p (k t) -> p k t", t=128), in1=gfull, op=ALU.mult),
             reads=["BT%d" % (j % 2), "gfull"], writes=["xnT%d_%d" % (j, k) for k in range(8)])


def ph_ffn(C, g_d, wg_d, wu_d, wd_d, tag):
    p = C.p
    ph_norm(C, g_d, tag)
    xs, xnT = C.xs, C.xnT
    NG = len(FF_GROUPS)

    def loads_gu(gi):
        c0, c1 = FF_GROUPS[gi]
        gs = c1 - c0
        sl = gi % 2
        wg, wu = C.wg[sl], C.wu[sl]
        out = []
        for kh in range(2):
            out.append(lambda kh=kh: wload(C, wg[:, kh * 4:kh * 4 + 4, 0:gs * 128], wg_d[:, kh * 4:kh * 4 + 4, c0 * 128:c1 * 128], "wg%d" % sl))
        for kh in range(2):
            out.append(lambda kh=kh: wload(C, wu[:, kh * 4:kh * 4 + 4, 0:gs * 128], wu_d[:, kh * 4:kh * 4 + 4, c0 * 128:c1 * 128], "wu%d" % sl))
        return out

    def loads_d(gi):
        c0, c1 = FF_GROUPS[gi]
        sl = gi % 2
        wd = C.wd[sl]
        return [lambda c=c: wload(C, wd[:, c, :], wd_d[:, c0 + c, :], "wd%d" % sl) for c in range(c1 - c0)]

    cnt = {"gu": 0, "dn": 0}
    PSY = [(C.B[4], "B4"), (C.B[5], "B5"), (C.SB[:, 0:512], "BT0"), (C.SB[:, 512:1024], "BT1")]

    def gu_units(gi):
        c0, c1 = FF_GROUPS[gi]
        gs = c1 - c0
        sl = gi % 2
        wg, wu, hT = C.wg[sl], C.wu[sl], C.hT[sl]
        units = []
        for tb in range(4):
            for c in range(gs):
                b = cnt["gu"] % 2
                cnt["gu"] += 1
                psg, psu = C.B[b], C.B[2 + b]
                u = []
                for k in range(8):
                    u.append(("pe", lambda e, k=k, c=c, tb=tb, psg=psg, wg=wg: e.matmul(psg, wg[:, k, c * 128:(c + 1) * 128], xnT[:, k, tb * 512:(tb + 1) * 512], start=(k == 0), stop=(k == 7)),
                              ["wg%d" % sl] + xk(tb, k), ["B%d" % b]))
                for k in range(8):
                    u.append(("pe", lambda e, k=k, c=c, tb=tb, psu=psu, wu=wu: e.matmul(psu, wu[:, k, c * 128:(c + 1) * 128], xnT[:, k, tb * 512:(tb + 1) * 512], start=(k == 0), stop=(k == 7)),
                              ["wu%d" % sl] + xk(tb, k), ["B%d" % (2 + b)]))
                sg = C.sg[b]
                u.append(("act", lambda e, sg=sg, psg=psg: e.activation(out=sg, in_=psg, func=AF.Silu), ["B%d" % b], ["sg%d" % b]))
                u.append(("dve", lambda e, sg=sg, psu=psu, hT=hT, c=c, tb=tb: e.tensor_tensor(out=hT[:, c, tb * 512:(tb + 1) * 512], in0=sg, in1=psu, op=ALU.mult),
                          ["sg%d" % b, "B%d" % (2 + b)], ["hT%d_%d_%d" % (sl, c, tb)]))
                units.append(u)
        return units

    def dn_units(gi):
        c0, c1 = FF_GROUPS[gi]
        gs = c1 - c0
        sl = gi % 2
        wd, hT = C.wd[sl], C.hT[sl]
        units = []
        for j in range(NT):
            for hf in range(2):
                psy, pkey = PSY[cnt["dn"] % 4]
                cnt["dn"] += 1
                u = []
                for c in range(gs):
                    u.append(("pe", lambda e, gs=gs, c=c, j=j, hf=hf, psy=psy, hT=hT, wd=wd: e.matmul(psy, hT[:, c, j * 128:(j + 1) * 128], wd[:, c, hf * 512:(hf + 1) * 512], start=(c == 0), stop=(c == gs - 1)),
                              ["hT%d_%d_%d" % (sl, c, j // 4), "wd%d" % sl], [pkey]))
                u.append(("dve", lambda e, j=j, hf=hf, psy=psy: e.scalar_tensor_tensor(out=xs[:, j, hf * 512:(hf + 1) * 512], in0=psy, scalar=0.5, in1=xs[:, j, hf * 512:(hf + 1) * 512], op0=ALU.mult, op1=ALU.add),
                          [pkey, "xs%d" % j], ["xs%d" % j]))
                units.append(u)
        return units

    def emit(u):
        for (eng, fn, rd, wr) in u:
            p.op(eng, fn, reads=rd, writes=wr)

    for f in loads_gu(0) + loads_gu(1) + loads_d(0):
        f()
    for u in gu_units(0):
        emit(u)
    for gi in range(NG):
        pend = []
        if gi + 2 < NG:
            pend += loads_gu(gi + 2)
        if gi + 1 < NG:
            pend += loads_d(gi + 1)
        dn = dn_units(gi)
        gu = gu_units(gi + 1) if gi + 1 < NG else []
        nd, ng = len(dn), len(gu)
        di = 0
        pi = 0
        for ui in range(ng):
            emit(gu[ui])
            tgt = (ui + 1) * nd // ng
            while di < tgt:
                emit(dn[di])
                di += 1
            ptgt = (ui + 1) * len(pend) // ng
            while pi < ptgt:
                pend[pi]()
                pi += 1
        while di < nd:
            emit(dn[di])
            di += 1
        while pi < len(pend):
            pend[pi]()
            pi += 1


def _rmsT(C, src_keys, sq, nchunk, rows, width, inv_n, out_rstd, bank, bkey):
    p = C.p
    for i in range(nchunk):
        p.op("pe", lambda e, i=i: e.matmul(bank[0:rows, 0:width], C.ones[0:rows, 0:rows], sq[0:rows, i, 0:width], start=(i == 0), stop=(i == nchunk - 1)),
             reads=["ones"] + src_keys, writes=[bkey])
    p.op("act", lambda e: e.activation(out=out_rstd[0:rows, 0:width], in_=bank[0:rows, 0:width], func=AF.Ln, bias=C.epsb[0:rows, :], scale=inv_n),
         reads=[bkey, "epsb"], writes=["rstdT"])
    p.op("act", lambda e: e.activation(out=out_rstd[0:rows, 0:width], in_=out_rstd[0:rows, 0:width], func=AF.Exp, scale=-0.5),
         reads=["rstdT"], writes=["rstdT"])


def ph_even_in(C, li):
    p, A, d = C.p, C.A, C.d
    C.xnT = A.alloc([128, 8, T], BF16)
    ph_norm(C, d["mixg"], "mix")
    xnT = C.xnT
    win = A.alloc([128, 8, MIXIN], BF16)
    for k in range(8):
        wload(C, win[:, k, :], d["win"][:, k, :], "win")
    wrope = A.alloc([128, 8, 96], BF16)
    p.op("pool", lambda e: e.memset(wrope, 0.0), writes=["wrope"])
    _dma(C, "pool", wrope[:, :, 64:96], d["win"][:, :, 1152:1184], [], ["wrope"], "wrope")
    wq = A.alloc([128, 3, 768], BF16)
    for i in range(3):
        wload(C, wq[:, i, :], d["wq"][:, i, :], "wq")
    wkp = A.alloc([128, 2, 8, 96], BF16)
    p.op("pool", lambda e: e.memset(wkp, 0.0), writes=["wkp"])
    wv = A.alloc([128, 2, 8, 64], BF16)
    wkv4 = d["wkv"].rearrange("p i (h c) -> p i h c", c=128)
    for i in range(2):
        _dma(C, "pool", wkp[:, i, :, 0:64], wkv4[:, i, :, 0:64], [], ["wkp"], "wkp")
        _dma(C, "pool", wv[:, i, :, :], wkv4[:, i, :, 64:128], [], ["wv"], "wv")
    rm = A.alloc([96, 96], BF16)
    _dma(C, "pool", rm, d["rm"], [], ["rm"], "rm")
    gql = A.alloc([128, 3], F32)
    gkl = A.alloc([128, 2], F32)
    gqh = A.alloc([96, 1], F32)
    gkh = A.alloc([96, 1], F32)
    _dma(C, "sp", gql, d["gql"], [], ["gsm"], "gsm")
    _dma(C, "sp", gkl, d["gkl"], [], ["gsm"], "gsm")
    _dma(C, "sp", gqh, d["gqh"], [], ["gsm"], "gsm")
    _dma(C, "sp", gkh, d["gkh"], [], ["gsm"], "gsm")
    cosb = [A.alloc([96, 512], F32) for _ in range(2)]
    sinb = [A.alloc([96, 512], F32) for _ in range(2)]
    ust = [A.alloc([128, 512], F32) for _ in range(2)]
    lat = A.alloc([128, 3, 512], F32)
    sq = A.alloc([128, 3, 512], BF16)
    rstdT = A.alloc([128, 512], F32)
    latn = A.alloc([128, 3, 512], BF16)
    kvn = A.alloc([128, 2, 512], BF16)
    hq = [A.alloc([96, 512], BF16) for _ in range(4)]
    t1 = [A.alloc([96, 512], F32) for _ in range(2)]
    t2 = [A.alloc([96, 512], F32) for _ in range(2)]
    sqh = [A.alloc([96, 512], BF16) for _ in range(2)]
    rsh = [A.alloc([96, 512], F32) for _ in range(2)]
    vst = [A.alloc([128, 8, 65], BF16) for _ in range(2)]
    for s in range(2):
        p.op("pool", lambda e, s=s: e.memset(vst[s], 1.0), writes=["vst%d" % s])
    nu = 0
    nh = 0
    for tb in range(4):
        tsl = slice(tb * 512, (tb + 1) * 512)
        cosT, sinT = cosb[tb % 2], sinb[tb % 2]
        ckey, skey2 = "cos%d" % (tb % 2), "sin%d" % (tb % 2)
        _dma(C, "sp", cosT[64:96, :], d["cosT"][64:96, tsl], [], [ckey], ckey)
        _dma(C, "sp", sinT[64:96, :], d["sinT"][64:96, tsl], [], [skey2], skey2)
        for m in range(4):
            bk = C.B[nu % 2]
            bkey = "B%d" % (nu % 2)
            st = ust[nu % 2]
            skey = "ust%d" % (nu % 2)
            nu += 1
            for k in range(8):
                p.op("pe", lambda e, tsl=tsl, k=k, m=m, bk=bk: e.matmul(bk, win[:, k, m * 128:(m + 1) * 128], xnT[:, k, tsl], start=(k == 0), stop=(k == 7)),
                     reads=["win"] + xk(tb, k), writes=[bkey])
            p.op("act", lambda e, bk=bk, st=st: e.activation(out=st, in_=bk, func=AF.Copy), reads=[bkey], writes=[skey])
            _dma(C, "sp", d["uT_o"][m][:, tsl], st, [skey], ["uTo_%d_%d" % (m, tb)], skey)
            C.outkeys.append("uTo_%d_%d" % (m, tb))
        for (nm, m0, nch, inv_n, gl, outn) in (("q", 4, 3, 1.0 / QL, gql, latn), ("kv", 7, 2, 1.0 / KVL, gkl, kvn)):
            for i in range(nch):
                bk = C.B[2 + i]
                bkey = "B%d" % (2 + i)
                m = m0 + i
                for k in range(8):
                    p.op("pe", lambda e, tsl=tsl, k=k, m=m, bk=bk: e.matmul(bk, win[:, k, m * 128:(m + 1) * 128], xnT[:, k, tsl], start=(k == 0), stop=(k == 7)),
                         reads=["win"] + xk(tb, k), writes=[bkey])
                p.op("act", lambda e, i=i, bk=bk: e.activation(out=lat[:, i, :], in_=bk, func=AF.Copy), reads=[bkey], writes=["lat"])
                p.op("act", lambda e, i=i, bk=bk: e.activation(out=sq[:, i, :], in_=bk, func=AF.Square), reads=[bkey], writes=["sq"])
            _rmsT(C, ["sq"], sq, nch, 128, 512, inv_n, rstdT, C.B[5], "B5")
            for i in range(nch):
                p.op("dve", lambda e, i=i, gl=gl, outn=outn: e.scalar_tensor_tensor(out=outn[:, i, :], in0=lat[:, i, :], scalar=gl[:, i:i + 1], in1=rstdT, op0=ALU.mult, op1=ALU.mult),
                     reads=["lat", "rstdT", "gsm"], writes=[nm + "n"])
        def head_stages(which, h, b, hqt, hkey):
            bk = C.B[b]
            bkey = "B%d" % b
            ssb, sskey = (C.B[5], "B5") if b == 0 else (C.B[3], "B3")
            rb, rkey = (C.B[4], "B4") if b == 0 else (C.B[2], "B2")
            sqb, rsb, t1b, t2b = sqh[b], rsh[b], t1[b], t2[b]
            st = [[] for _ in range(9)]
            if which == "q":
                for i in range(3):
                    st[0].append(("pe", lambda e, i=i, h=h, bk=bk: e.matmul(bk[0:96, :], wq[:, i, h * 96:(h + 1) * 96], latn[:, i, :], start=(i == 0), stop=(i == 2)),
                                  ["wq", "qn"], [bkey]))
                gh = gqh
            else:
                for k in range(8):
                    st[0].append(("pe", lambda e, tsl=tsl, k=k, bk=bk: e.matmul(bk[0:96, :], wrope[:, k, :], xnT[:, k, tsl], start=(k == 0), stop=False),
                                  ["wrope"] + xk(tb, k), [bkey]))
                for i in range(2):
                    st[0].append(("pe", lambda e, i=i, h=h, bk=bk: e.matmul(bk[0:96, :], wkp[:, i, h, :], kvn[:, i, :], start=False, stop=(i == 1)),
                                  ["wkp", "kvn"], [bkey]))
                gh = gkh
            st[1].append(("act", lambda e, bk=bk, sqb=sqb: e.activation(out=sqb, in_=bk[0:96, :], func=AF.Square), [bkey], ["sqh%d" % b]))
            st[2].append(("pe", lambda e, ssb=ssb, sqb=sqb: e.matmul(ssb[0:96, :], C.ones[0:96, 0:96], sqb, start=True, stop=True), ["ones", "sqh%d" % b], [sskey]))
            st[3].append(("act", lambda e, ssb=ssb, rsb=rsb: e.activation(out=rsb, in_=ssb[0:96, :], func=AF.Ln, bias=C.epsb[0:96, :], scale=1.0 / DQK), [sskey, "epsb"], ["rsh%d" % b]))
            st[3].append(("act", lambda e, rsb=rsb: e.activation(out=rsb, in_=rsb, func=AF.Exp, scale=-0.5), ["rsh%d" % b], ["rsh%d" % b]))
            st[4].append(("dve", lambda e, bk=bk, hqt=hqt, gh=gh, rsb=rsb: e.scalar_tensor_tensor(out=hqt, in0=bk[0:96, :], scalar=gh[:, 0:1], in1=rsb, op0=ALU.mult, op1=ALU.mult),
                          [bkey, "rsh%d" % b, "gsm"], [hkey]))
            st[5].append(("pe", lambda e, hqt=hqt, rb=rb: e.matmul(rb[0:96, :], rm, hqt, start=True, stop=True), ["rm", hkey], [rkey]))
            st[6].append(("pool", lambda e, cosT=cosT, hqt=hqt, t1b=t1b: e.tensor_tensor(out=t1b[64:96, :], in0=hqt[64:96, :], in1=cosT[64:96, :], op=ALU.mult),
                          [hkey, ckey], ["t1_%d" % b]))
            st[6].append(("dve", lambda e, sinT=sinT, rb=rb, t2b=t2b: e.tensor_tensor(out=t2b[64:96, :], in0=rb[64:96, :], in1=sinT[64:96, :], op=ALU.mult),
                          [rkey, skey2], ["t2_%d" % b]))
            st[7].append(("dve", lambda e, hqt=hqt, t1b=t1b, t2b=t2b: e.tensor_tensor(out=hqt[64:96, :], in0=t1b[64:96, :], in1=t2b[64:96, :], op=ALU.add),
                          ["t1_%d" % b, "t2_%d" % b], [hkey]))
            return st

        for which in ("q", "k"):
            for hp in range(H // 2):
                pp = nh % 2
                nh += 1
                sts = []
                for b in range(2):
                    h = hp * 2 + b
                    hqt = hq[pp * 2 + b]
                    hkey = "hq%d" % (pp * 2 + b)
                    sts.append((h, hqt, hkey, head_stages(which, h, b, hqt, hkey)))
                for si in range(9):
                    for (h, hqt, hkey, st) in sts:
                        for (eng, fn, rd, wr) in st[si]:
                            p.op(eng, fn, reads=rd, writes=wr)
                for (h, hqt, hkey, st) in sts:
                    if which == "q":
                        ok = "qTo_%d_%d" % (h, tb)
                        _dma(C, "sp", d["qT_o"][h][:, tsl], hqt, [hkey], [ok], hkey)
                        C.outkeys.append(ok)
                    else:
                        for jj in range(4):
                            ok = "kTo_%d_%d_%d" % (h, tb, jj)
                            _dma(C, "sp", d["kT_o"][tb * 4 + jj][:, h, :], hqt[:, jj * 128:(jj + 1) * 128], [hkey], [ok], hkey)
                            C.outkeys.append(ok)
        for jj in range(4):
            j = tb * 4 + jj
            b = j % 2
            bk = C.B[2 + b]
            bkey = "B%d" % (2 + b)
            for i in range(2):
                p.op("pe", lambda e, i=i, jj=jj, bk=bk: e.matmul(bk, kvn[:, i, jj * 128:(jj + 1) * 128], wv[:, i, :, :], start=(i == 0), stop=(i == 1)),
                     reads=["kvn", "wv"], writes=[bkey])
            p.op("dve", lambda e, b=b, bk=bk: e.tensor_copy(out=vst[b][:, :, 0:64], in_=bk.rearrange("p (h c) -> p h c", c=64)),
                 reads=[bkey], writes=["vst%d" % b])
            ok = "vo_%d" % j
            _dma(C, "sp", d["v_o"][:, j, :, :], vst[b], ["vst%d" % b], [ok], "vst%d" % b)
            C.outkeys.append(ok)


def ph_even_out(C, li):
    p, A, d = C.p, C.A, C.d
    scale = DQK ** -0.5
    catT = A.alloc([128, 8, T], BF16)
    mark = A.top
    uextb = [A.alloc([128, NT, 144], F32) for _ in range(2)]
    wa = A.alloc([128, NT, 144], F32)
    wb = A.alloc([128, NT, 144], F32)
    pooled = A.alloc([128, 4, T], BF16)
    invc = A.alloc([128, 4, 16], F32)
    _dma(C, "sp", invc, d["invc"], [], ["invc"], "invc")
    wpool = A.alloc([128, 4, 128], BF16)
    _dma(C, "pool", wpool, d["wpool"], [], ["wpool"], "wpool")
    pscale = A.alloc([128, 4], F32)
    _dma(C, "sp", pscale, d["pscale"], [], ["pscale"], "pscale")
    tmp16 = A.alloc([128, 16], F32)
    for g in range(4):
        w = 2 ** (g + 1)
        src = uextb[g % 2]
        _dma(C, "sp", src, d["uext"][:, g, :, :], [], ["uext%d" % (g % 2)], "uext%d" % (g % 2))
        bufs = [wa, wb]
        sh = 1
        cur = src
        curkey = "uext%d" % (g % 2)
        nb = 0
        while sh < w:
            dst = bufs[nb % 2]
            dkey = "w%d" % (nb % 2)
            nb += 1
            p.op("dve", lambda e, cur=cur, dst=dst, sh=sh: e.tensor_tensor(out=dst[:, :, 2 * sh - 1:144], in0=cur[:, :, 2 * sh - 1:144], in1=cur[:, :, sh - 1:144 - sh], op=ALU.add),
                 reads=[curkey], writes=[dkey])
            if sh > 1:
                pass
            cur = dst
            curkey = dkey
            sh *= 2
        pv = pooled[:, g, :].rearrange("p (j t) -> p j t", t=128)
        p.op("dve", lambda e, cur=cur, src=src, pv=pv, w=w: e.scalar_tensor_tensor(out=pv, in0=cur[:, :, 16:144], scalar=1.0 / w, in1=src[:, :, 16:144], op0=ALU.mult, op1=ALU.subtract),
             reads=[curkey, "uext%d" % (g % 2)], writes=["pooled%d" % g])
        p.op("dve", lambda e, cur=cur, g=g: e.tensor_tensor(out=tmp16, in0=cur[:, 0, 16:32], in1=invc[:, g, :], op=ALU.mult),
             reads=[curkey, "invc"], writes=["tmp16"])
        p.op("dve", lambda e, src=src, g=g: e.tensor_tensor(out=pooled[:, g, 0:16], in0=tmp16, in1=src[:, 0, 16:32], op=ALU.subtract),
             reads=["tmp16", "uext%d" % (g % 2), "pooled%d" % g], writes=["pooled%d" % g])
    n = 0
    for g in range(4):
        for tb in range(4):
            b = n % 2
            n += 1
            bk = C.B[b]
            p.op("pe", lambda e, g=g, tb=tb, bk=bk: e.matmul(bk, wpool[:, g, :], pooled[:, g, tb * 512:(tb + 1) * 512], start=True, stop=True),
                 reads=["wpool", "pooled%d" % g], writes=["B%d" % b])
            p.op("act", lambda e, g=g, tb=tb, bk=bk: e.activation(out=catT[:, g, tb * 512:(tb + 1) * 512], in_=bk, func=AF.Copy, scale=pscale[:, g:g + 1]),
                 reads=["B%d" % b, "pscale"], writes=["catT"])
    p.barrier()
    A.top = mark
    qT = A.alloc([96, H, T], BF16)
    for h in range(H):
        _dma(C, "sp", qT[:, h, :], d["qT"][h], [], ["qT"], "qT%d" % (h % 4))
    mask = A.alloc([128, 8, 128], BF16)
    _dma(C, "pool", mask, d["mask"], [], ["mask"], "mask")
    ph_load_x(C, d["x"])
    krow = [A.alloc([96, 8, 4, 128], BF16) for _ in range(2)]
    vrow = [A.alloc([128, 8, 4, 65], BF16) for _ in range(2)]
    atok = A.alloc([128, NT, 512], BF16)
    rden = A.alloc([128, 4], F32)
    nrow = 0
    ns = 0
    LAG = 1
    pT = [A.alloc([128, 2, 512], BF16) for _ in range(3)]
    SD = [(C.SA.rearrange("p (b c) -> p b c", b=2), ["B4", "B5"]),
          (C.SB.rearrange("p (b c) -> p b c", b=2), ["BT0", "BT1"])]
    for sb in range(4):
        for hg in range(2):
            steps = []
            for jr in range(4 * sb + 4):
                sl = nrow % 2
                nrow += 1
                kr, vr = krow[sl], vrow[sl]
                pre = [("sp", lambda e, s, kr=kr, jr=jr, hg=hg: [e.dma_start(out=kr[:, r, :, :], in_=d["kall"][jr][:, r, hg * 4:(hg + 1) * 4, :]).then_inc(s, 16) for r in range(8)],
                        [], ["krow%d" % sl], 8, "krow%d" % sl),
                       ("sp", lambda e, s, vr=vr, jr=jr, hg=hg: [e.dma_start(out=vr[:, r, :, :], in_=d["vall"][jr][:, r, hg * 4:(hg + 1) * 4, :]).then_inc(s, 16) for r in range(8)],
                        [], ["vrow%d" % sl], 8, "vrow%d" % sl)]
                i0 = max(0, jr - 4 * sb)
                diag = jr >= 4 * sb
                ncols = (4 - i0) * 128
                q0 = (4 * sb + i0) * 128
                kk = "krow%d" % sl
                for r in range(8):
                    for hp in range(2):
                        sd, sdkeys = SD[ns % 2]
                        pt = pT[ns % 3]
                        ptk = "pT%d" % (ns % 3)
                        ns += 1
                        S_ops = list(pre)
                        pre = []
                        for half in range(2):
                            hh = hp * 2 + half
                            h = hg * 4 + hh
                            bk = sd[:, half, :]
                            bkey = sdkeys[half]
                            if diag:
                                S_ops.append(("pe", lambda e, bk=bk, kr=kr, r=r, hh=hh, h=h, q0=q0: e.matmul(bk[:, 0:128], kr[:, r, hh, :], qT[:, h, q0:q0 + 128], start=True, stop=False),
                                              [kk, "qT"], [bkey], 0, None))
                                S_ops.append(("pe", lambda e, bk=bk, r=r: e.matmul(bk[:, 0:128], C.ident, mask[:, r, :], start=False, stop=True),
                                              ["ident", "mask"], [bkey], 0, None))
                                if ncols > 128:
                                    S_ops.append(("pe", lambda e, bk=bk, kr=kr, r=r, hh=hh, h=h, q0=q0, ncols=ncols: e.matmul(bk[:, 128:ncols], kr[:, r, hh, :], qT[:, h, q0 + 128:q0 + ncols], start=True, stop=True),
                                                  [kk, "qT"], [bkey], 0, None))
                            else:
                                S_ops.append(("pe", lambda e, bk=bk, kr=kr, r=r, hh=hh, h=h, q0=q0, ncols=ncols: e.matmul(bk[:, 0:ncols], kr[:, r, hh, :], qT[:, h, q0:q0 + ncols], start=True, stop=True),
                                              [kk, "qT"], [bkey], 0, None))
                        S_ops.append(("act", lambda e, sd=sd, pt=pt, ncols=ncols: e.activation(out=pt[:, :, 0:ncols], in_=sd[:, :, 0:ncols], func=AF.Exp, scale=scale),
                                      list(sdkeys), [ptk], 0, None))
                        PV_ops = []
                        for half in range(2):
                            hh = hp * 2 + half
                            for i in range(i0, 4):
                                first = (jr == 0 and r == 0 and hh == 0)
                                last = (jr == 4 * sb + i and r == 7)
                                PV_ops.append(("pe", lambda e, i0=i0, i=i, pt=pt, vr=vr, r=r, hh=hh, half=half, first=first, last=last: e.matmul(C.B[i][:, hh * 65:(hh + 1) * 65], pt[:, half, (i - i0) * 128:(i - i0 + 1) * 128], vr[:, r, hh, :], start=first, stop=last, skip_group_check=True),
                                               [ptk, "vrow%d" % sl], ["B%d" % i], 0, None))
                        steps.append((S_ops, PV_ops))
            for n in range(len(steps) + LAG):
                if n < len(steps):
                    for (eng, fn, rd, wr, dma, slot) in steps[n][0]:
                        p.op(eng, fn, reads=rd, writes=wr, dma=dma, slot=slot)
                if n - LAG >= 0:
                    for (eng, fn, rd, wr, dma, slot) in steps[n - LAG][1]:
                        p.op(eng, fn, reads=rd, writes=wr, dma=dma, slot=slot)
            for i in range(4):
                j = 4 * sb + i
                acc = C.B[i][:, 0:260].rearrange("p (h c) -> p h c", c=65)
                p.op("dve", lambda e, acc=acc: e.reciprocal(out=rden, in_=acc[:, :, 64]), reads=["B%d" % i], writes=["rden"])
                for hh in range(4):
                    h = hg * 4 + hh
                    p.op("dve", lambda e, acc=acc, hh=hh, h=h, j=j: e.tensor_scalar(out=atok[:, j, h * 64:(h + 1) * 64], in0=acc[:, hh, 0:64], scalar1=rden[:, hh:hh + 1], scalar2=None, op0=ALU.mult),
                         reads=["B%d" % i, "rden"], writes=["atok%d" % j])
    for j in range(NT):
        bt = C.BT[j % 2]
        for m in range(4):
            p.op("pe", lambda e, j=j, m=m, bt=bt: e.transpose(out=bt[:, m * 128:(m + 1) * 128], in_=atok[:, j, m * 128:(m + 1) * 128], identity=C.ident),
                 reads=["atok%d" % j, "ident"], writes=["BT%d" % (j % 2)])
        p.op("dve", lambda e, j=j, bt=bt: e.tensor_copy(out=catT[:, 4:8, j * 128:(j + 1) * 128], in_=bt[:, 0:512].rearrange("p (m t) -> p m t", t=128)),
             reads=["BT%d" % (j % 2)], writes=["catT"])
    p.barrier()
    A.top = mark
    _out_proj(C, catT, "catT", d["wout"])


def _out_proj(C, srcT, skey, w_d):
    p, A = C.p, C.A
    xs = C.xs
    wo = A.alloc([128, 8, 1024], BF16)
    for k in range(8):
        wload(C, wo[:, k, :], w_d[:, k, :], "wo")
    n = 0
    for j in range(NT):
        for hf in range(2):
            b = n % 2
            n += 1
            bk = C.B[b]
            for m in range(8):
                p.op("pe", lambda e, j=j, hf=hf, m=m, bk=bk: e.matmul(bk, srcT[:, m, j * 128:(j + 1) * 128], wo[:, m, hf * 512:(hf + 1) * 512], start=(m == 0), stop=(m == 7)),
                     reads=[skey, "wo"], writes=["B%d" % b])
            p.op("dve", lambda e, j=j, hf=hf, bk=bk: e.tensor_tensor(out=xs[:, j, hf * 512:(hf + 1) * 512], in0=bk, in1=xs[:, j, hf * 512:(hf + 1) * 512], op=ALU.add),
                 reads=["B%d" % b, "xs%d" % j], writes=["xs%d" % j])


def ph_odd_in(C, li):
    p, A, d = C.p, C.A, C.d
    C.xnT = A.alloc([128, 8, T], BF16)
    ph_norm(C, d["mixg"], "mix")
    xnT = C.xnT
    wst = [A.alloc([128, 8, 384], BF16) for _ in range(2)]
    gst = [A.alloc([128, 512], F32) for _ in range(2)]
    cst = [A.alloc([128, 512], F32) for _ in range(2)]
    ust = [A.alloc([128, 512], F32) for _ in range(2)]
    n = 0
    for m in range(8):
        sl = m % 2
        w = wst[sl]
        for q in range(3):
            wload(C, w[:, :, q * 128:(q + 1) * 128], d["wcin"][:, :, q * 1024 + m * 128:q * 1024 + (m + 1) * 128], "wst%d" % sl)
        for tb in range(4):
            tsl = slice(tb * 512, (tb + 1) * 512)
            b = n % 2
            n += 1
            banks = [C.B[b], C.B[2 + b], C.B[4 + b]]
            bkeys = ["B%d" % b, "B%d" % (2 + b), "B%d" % (4 + b)]
            for q in range(3):
                for k in range(8):
                    p.op("pe", lambda e, tsl=tsl, q=q, k=k, w=w, bk=banks[q]: e.matmul(bk, w[:, k, q * 128:(q + 1) * 128], xnT[:, k, tsl], start=(k == 0), stop=(k == 7)),
                         reads=["wst%d" % sl] + xk(tb, k), writes=[bkeys[q]])
            p.op("act", lambda e, b=b, bk=banks[0]: e.activation(out=gst[b], in_=bk, func=AF.Copy), reads=[bkeys[0]], writes=["gst%d" % b])
            p.op("act", lambda e, b=b, bk=banks[1]: e.activation(out=cst[b], in_=bk, func=AF.Copy), reads=[bkeys[1]], writes=["cst%d" % b])
            p.op("dve", lambda e, b=b, bk=banks[2]: e.tensor_tensor(out=ust[b], in0=cst[b], in1=bk, op=ALU.mult), reads=["cst%d" % b, bkeys[2]], writes=["ust%d" % b])
            ok = "gbo_%d_%d" % (m, tb)
            _dma(C, "sp", d["gb_o"][m][:, tsl], gst[b], ["gst%d" % b], [ok], "gst%d" % b)
            C.outkeys.append(ok)
            ok = "uco_%d_%d" % (m, tb)
            _dma(C, "sp", d["uc_o"][m][:, tsl], ust[b], ["ust%d" % b], [ok], "ust%d" % b)
            C.outkeys.append(ok)


def ph_odd_out(C, li):
    p, A, d = C.p, C.A, C.d
    vT = A.alloc([128, 8, T], BF16)
    cw = A.alloc([128, 8, 3], F32)
    _dma(C, "sp", cw, d["cw"], [], ["cw"], "cw")
    ue = [A.alloc([128, NT, 130], F32) for _ in range(2)]
    gb = [A.alloc([128, T], F32) for _ in range(2)]
    y = [A.alloc([128, NT, 128], F32) for _ in range(2)]
    for m in range(8):
        sl = m % 2
        _dma(C, "sp", ue[sl], d["ucext"][:, m, :, :], [], ["ue%d" % sl], "ue%d" % sl)
        _dma(C, "sp", gb[sl], d["gbT"][m], [], ["gb%d" % sl], "gb%d" % sl)
        if m == 1:
            ph_load_x(C, d["x"])
        eng = "dve"
        u, yy, g = ue[sl], y[sl], gb[sl]
        p.op(eng, lambda e, u=u, yy=yy, m=m: e.tensor_scalar(out=yy, in0=u[:, :, 0:128], scalar1=cw[:, m, 0:1], scalar2=None, op0=ALU.mult),
             reads=["ue%d" % sl, "cw"], writes=["y%d" % sl])
        p.op(eng, lambda e, u=u, yy=yy, m=m: e.scalar_tensor_tensor(out=yy, in0=u[:, :, 1:129], scalar=cw[:, m, 1:2], in1=yy, op0=ALU.mult, op1=ALU.add),
             reads=["ue%d" % sl, "cw", "y%d" % sl], writes=["y%d" % sl])
        p.op(eng, lambda e, u=u, yy=yy, m=m: e.scalar_tensor_tensor(out=yy, in0=u[:, :, 2:130], scalar=cw[:, m, 2:3], in1=yy, op0=ALU.mult, op1=ALU.add),
             reads=["ue%d" % sl, "cw", "y%d" % sl], writes=["y%d" % sl])
        p.op(eng, lambda e, yy=yy, g=g, m=m: e.tensor_tensor(out=vT[:, m, :], in0=yy.rearrange("p j t -> p (j t)"), in1=g, op=ALU.mult),
             reads=["y%d" % sl, "gb%d" % sl], writes=["vT"])
    _out_proj(C, vT, "vT", d["wcout"])


def build_launch(spec):
    nc = bass.Bass("TRN2", target_bir_lowering=False)
    C = Ctx()
    C.nc = nc
    C.p = Prog(nc)
    C.d = {}
    C.outkeys = []
    for name, (shape, dt) in spec["inputs"].items():
        C.d[name] = nc.dram_tensor(name, list(shape), dt, kind="ExternalInput").ap()
    for name, (shape, dt) in spec["outputs"].items():
        C.d[name] = nc.dram_tensor(name, list(shape), dt, kind="ExternalOutput").ap()
    import contextlib
    with contextlib.ExitStack() as st:
        nbytes = 204 * 1024
        ar = st.enter_context(nc.sbuf_tensor("arena", [128, nbytes], U8))
        C.A = Arena(ar, nbytes)
        C.B = [st.enter_context(nc.psum_tensor("bank%d" % i, [128, 512], F32))[:] for i in range(4)]
        C.SA = st.enter_context(nc.psum_tensor("bankSA", [128, 1024], F32))[:]
        C.SB = st.enter_context(nc.psum_tensor("bankSB", [128, 1024], F32))[:]
        C.B += [C.SA[:, 0:512], C.SA[:, 512:1024]]
        C.BT = [C.SB[:, 0:512].bitcast(BF16), C.SB[:, 512:1024].bitcast(BF16)]
        A = C.A
        C.xs = A.alloc([128, NT, D], F32)
        C.gT = A.alloc([128, 8], F32)
        C.ss = A.alloc([128, NT], F32)
        C.rstd = A.alloc([128, NT], F32)
        C.gfull = A.alloc([128, 8, 128], F32)
        C.onesf = A.alloc([128, 128], F32)
        C.p.op("pool", lambda e: e.memset(C.onesf, 1.0), writes=["onesf"])
        C.xb = [A.alloc([128, D], BF16) for _ in range(2)]
        C.epsb = A.alloc([128, 1], F32)
        C.stg = [A.alloc([128, 1536], F32) for _ in range(3)]
        C.stg_n = 0
        C.p.op("pool", lambda e: e.memset(C.epsb, EPS), writes=["epsb"])
        ph_consts(C)
        base = A.top
        for ph in spec["phases"]:
            if ph in (_load_x, _store_x):
                ph(C)
                continue
            C.p.barrier()
            A.top = base
            ph(C)
        C.p.op("sp", None, reads=list(C.outkeys))
        C.p.op("pool", None, reads=list(C.outkeys))
        C.p.emit()
    return nc


def alloc_ffn(C):
    A = C.A
    C.xnT = A.alloc([128, 8, T], BF16)
    C.wg = [A.alloc([128, 8, 384], BF16) for _ in range(2)]
    C.wu = [A.alloc([128, 8, 384], BF16) for _ in range(2)]
    C.wd = [A.alloc([128, 3, D], BF16) for _ in range(2)]
    C.hT = [A.alloc([128, 3, T], BF16) for _ in range(2)]
    C.sg = [A.alloc([128, 512], F32) for _ in range(2)]


def mk_ffn(which):
    def f(C):
        alloc_ffn(C)
        ph_ffn(C, C.d[which + "_g"], C.d[which + "_wg"], C.d[which + "_wu"], C.d[which + "_wd"], which)
    return f


FFN_IN = lambda w: {w + "_g": ((128, 8), F32), w + "_wg": ((128, 8, DFF), F32), w + "_wu": ((128, 8, DFF), F32), w + "_wd": ((128, NFC, D), F32)}
X_IO = ((128, NT, D), F32)
CONST_IN = {"ident": ((128, 128), F32), "ones": ((128, 128), F32)}
EVEN_IN_W = {"mixg": ((128, 8), F32), "win": ((128, 8, MIXIN), F32), "wq": ((128, 3, 768), F32), "wkv": ((128, 2, 1024), F32),
             "rm": ((96, 96), F32), "gql": ((128, 3), F32), "gkl": ((128, 2), F32), "gqh": ((96, 1), F32), "gkh": ((96, 1), F32),
             "cosT": ((96, T), F32), "sinT": ((96, T), F32)}
EVEN_IN_OUT = {"uT_o": ((4, 128, T), F32), "qT_o": ((H, 96, T), BF16), "kT_o": ((NT, 96, H, 128), BF16), "v_o": ((128, NT, H, 65), BF16)}
EVEN_OUT_IN = {"uext": ((128, 4, NT, 144), F32), "invc": ((128, 4, 16), F32), "wpool": ((128, 4, 128), F32), "pscale": ((128, 4), F32),
               "qT": ((H, 96, T), BF16), "mask": ((128, 8, 128), F32), "kall": ((NT, 96, 8, H, 128), BF16), "vall": ((NT, 128, 8, H, 65), BF16),
               "wout": ((128, 8, D), F32)}
ODD_IN_W = {"mixg": ((128, 8), F32), "wcin": ((128, 8, 3 * D), F32)}
ODD_IN_OUT = {"gb_o": ((8, 128, T), F32), "uc_o": ((8, 128, T), F32)}
ODD_OUT_IN = {"cw": ((128, 8, 3), F32), "ucext": ((128, 8, NT, 130), F32), "gbT": ((8, 128, T), F32), "wcout": ((128, 8, D), F32)}


def _load_x(C):
    ph_load_x(C, C.d["x"])


def _store_x(C):
    ph_store_x(C, C.d["x_o"], "xo")


def spec_A():
    return dict(inputs={"x": X_IO, **CONST_IN, **FFN_IN("f1"), **EVEN_IN_W},
                outputs={"x_o": X_IO, **EVEN_IN_OUT},
                phases=[_load_x, mk_ffn("f1"), _store_x, lambda C: ph_even_in(C, 0)])


def spec_B():
    return dict(inputs={"x": X_IO, **CONST_IN, **EVEN_OUT_IN, **FFN_IN("f2"), **FFN_IN("f1"), **ODD_IN_W},
                outputs={"x_o": X_IO, **ODD_IN_OUT},
                phases=[lambda C: ph_even_out(C, 0), mk_ffn("f2"), mk_ffn("f1"), _store_x, lambda C: ph_odd_in(C, 1)])


def spec_C():
    return dict(inputs={"x": X_IO, **CONST_IN, **ODD_OUT_IN, **FFN_IN("f2"), **FFN_IN("f1"), **EVEN_IN_W},
                outputs={"x_o": X_IO, **EVEN_IN_OUT},
                phases=[lambda C: ph_odd_out(C, 1), mk_ffn("f2"), mk_ffn("f1"), _store_x, lambda C: ph_even_in(C, 2)])


def spec_D():
    return dict(inputs={"x": X_IO, **CONST_IN, **ODD_OUT_IN, **FFN_IN("f2")},
                outputs={"x_o": X_IO},
                phases=[lambda C: ph_odd_out(C, 3), mk_ffn("f2"), _store_x])


_NC_CACHE = {}


def _get_nc(name):
    if name not in _NC_CACHE:
        _NC_CACHE[name] = build_launch({"A": spec_A, "B": spec_B, "C": spec_C, "D": spec_D}[name]())
    return _NC_CACHE[name]


def _kmaj(w, p=128):
    K, N = w.shape
    return np.ascontiguousarray(w.reshape(K // p, p, N).transpose(1, 0, 2))


def _vec(g, p=128):
    return np.ascontiguousarray(g.reshape(-1, p).T)


def _ffn_inputs(prefix, inputs, which, layer):
    return {prefix + "_g": _vec(inputs[which + "_norm"][layer]),
            prefix + "_wg": _kmaj(inputs[which + "_w_gate"][layer]),
            prefix + "_wu": _kmaj(inputs[which + "_w_up"][layer]),
            prefix + "_wd": _kmaj(inputs[which + "_w_down"][layer])}


def _consts():
    return {"ident": np.eye(128, dtype=np.float32), "ones": np.ones((128, 128), np.float32)}


def _rope_tables(c):
    pos = (np.arange(NT)[:, None] * NC + c) * 128 + np.arange(128)[None, :]
    pos = pos.reshape(-1).astype(np.float32)
    inv = (np.float32(10000.0) ** (-np.arange(0, 32, 2, dtype=np.float32) / np.float32(32))).astype(np.float32)
    ang = (pos[:, None] * inv[None, :]).astype(np.float32)
    cos = np.cos(ang).astype(np.float32).T
    sin = np.sin(ang).astype(np.float32).T
    cosT = np.zeros((96, T), np.float32)
    sinT = np.zeros((96, T), np.float32)
    cosT[64:80] = cos
    cosT[80:96] = cos
    sinT[64:80] = sin
    sinT[80:96] = sin
    return cosT, sinT


def _rm():
    rm = np.zeros((96, 96), np.float32)
    for i in range(16):
        rm[80 + i, 64 + i] = -1.0
        rm[64 + i, 80 + i] = 1.0
    return rm


def _even_in_inputs(inputs, i, c):
    cosT, sinT = _rope_tables(c)
    return {"mixg": _vec(inputs["mix_norm"][2 * i]), "win": _kmaj(inputs["a_w_in"][i]), "wq": _kmaj(inputs["a_w_q_up"][i]),
            "wkv": _kmaj(inputs["a_w_kv_up"][i]), "rm": _rm(), "gql": _vec(inputs["a_q_a_norm"][i]), "gkl": _vec(inputs["a_kv_a_norm"][i]),
            "gqh": np.ascontiguousarray(inputs["a_q_head_norm"][i].reshape(96, 1)), "gkh": np.ascontiguousarray(inputs["a_k_head_norm"][i].reshape(96, 1)),
            "cosT": cosT, "sinT": sinT}


def _mask(c):
    m = np.zeros((128, 8, 128), np.float32)
    for r in range(8):
        if r > c:
            m[:, r, :] = -30000.0
        elif r == c:
            m[:, r, :] = np.where(np.arange(128)[:, None] > np.arange(128)[None, :], -30000.0, 0.0)
    return m


def _even_out_inputs(inputs, i, c, res):
    uT = np.stack([r["uT_o"] for r in res])
    uT = uT.reshape(NC, 4, 128, NT, 128)
    glob = uT.transpose(1, 2, 3, 0, 4).reshape(4, 128, NT * NC, 128)
    ext = np.zeros((128, 4, NT, 144), np.float32)
    for j in range(NT):
        t = 8 * j + c
        ext[:, :, j, 16:144] = glob[:, :, t, :].transpose(1, 0, 2)
        if t > 0:
            ext[:, :, j, 0:16] = glob[:, :, t - 1, 112:128].transpose(1, 0, 2)
    invc = np.zeros((128, 4, 16), np.float32)
    for g in range(4):
        w = 2 ** (g + 1)
        if c == 0:
            invc[:, g, :] = 1.0 / np.minimum(np.arange(16) + 1, w).astype(np.float32)
        else:
            invc[:, g, :] = 1.0 / w
    kall = np.stack([r["kT_o"] for r in res])
    kall = np.ascontiguousarray(kall.transpose(1, 2, 0, 3, 4))
    vall = np.stack([r["v_o"] for r in res])
    vall = np.ascontiguousarray(vall.transpose(2, 1, 0, 3, 4))
    return {"uext": ext, "invc": invc, "wpool": np.ascontiguousarray(inputs["a_w_pool"][i].transpose(1, 0, 2)),
            "pscale": _vec(inputs["a_pool_scale"][i]), "qT": res[c]["qT_o"], "mask": _mask(c), "kall": kall, "vall": vall,
            "wout": _kmaj(inputs["a_w_out"][i])}, (kall, vall)


def _odd_in_inputs(inputs, i):
    return {"mixg": _vec(inputs["mix_norm"][2 * i + 1]), "wcin": _kmaj(inputs["c_w_in"][i])}


def _odd_out_inputs(inputs, i, c, res):
    uc = np.stack([r["uc_o"] for r in res]).reshape(NC, 8, 128, NT, 128)
    glob = uc.transpose(1, 2, 3, 0, 4).reshape(8, 128, NT * NC, 128)
    ext = np.zeros((128, 8, NT, 130), np.float32)
    for j in range(NT):
        t = 8 * j + c
        ext[:, :, j, 2:130] = glob[:, :, t, :].transpose(1, 0, 2)
        if t > 0:
            ext[:, :, j, 0:2] = glob[:, :, t - 1, 126:128].transpose(1, 0, 2)
    cw = np.ascontiguousarray(inputs["c_conv_w"][i].reshape(3, 8, 128).transpose(2, 1, 0))
    return {"cw": cw, "ucext": ext, "gbT": res[c]["gb_o"], "wcout": _kmaj(inputs["c_w_out"][i])}


def _run(name, in_maps):
    nc = _get_nc(name)
    res = run_bass_kernel_spmd(nc, in_maps, core_ids=list(range(NC)))
    return res.results


def kernel(**inputs):
    inputs = {k: np.asarray(v) for k, v in inputs.items()}
    x = inputs["x"][0].reshape(NT, NC, 128, D)
    xc = [np.ascontiguousarray(x[:, c].transpose(1, 0, 2)) for c in range(NC)]
    f1 = _ffn_inputs("f1", inputs, "ffn1", 0)
    res = _run("A", [{"x": xc[c], **_consts(), **f1, **_even_in_inputs(inputs, 0, c)} for c in range(NC)])
    f2 = _ffn_inputs("f2", inputs, "ffn2", 0)
    f1 = _ffn_inputs("f1", inputs, "ffn1", 1)
    oi = _odd_in_inputs(inputs, 0)
    maps = []
    for c in range(NC):
        eo, _ = _even_out_inputs(inputs, 0, c, res)
        maps.append({"x": res[c]["x_o"], **_consts(), **eo, **f2, **f1, **oi})
    res = _run("B", maps)
    f2 = _ffn_inputs("f2", inputs, "ffn2", 1)
    f1 = _ffn_inputs("f1", inputs, "ffn1", 2)
    maps = [{"x": res[c]["x_o"], **_consts(), **_odd_out_inputs(inputs, 0, c, res), **f2, **f1, **_even_in_inputs(inputs, 1, c)} for c in range(NC)]
    res = _run("C", maps)
    f2 = _ffn_inputs("f2", inputs, "ffn2", 2)
    f1 = _ffn_inputs("f1", inputs, "ffn1", 3)
    oi = _odd_in_inputs(inputs, 1)
    maps = []
    for c in range(NC):
        eo, _ = _even_out_inputs(inputs, 1, c, res)
        maps.append({"x": res[c]["x_o"], **_consts(), **eo, **f2, **f1, **oi})
    res = _run("B", maps)
    f2 = _ffn_inputs("f2", inputs, "ffn2", 3)
    maps = [{"x": res[c]["x_o"], **_consts(), **_odd_out_inputs(inputs, 1, c, res), **f2} for c in range(NC)]
    res = _run("D", maps)
    out = np.zeros((NT, NC, 128, D), np.float32)
    for c in range(NC):
        out[:, c] = res[c]["x_o"].transpose(1, 0, 2)
    return out.reshape(1, S, D)
```
